# Optimizing a Trainium2 kernel written in Bass

```python
import jax, jax.numpy as jnp
from jax import lax
import numpy as np

D_MODEL = 1024
BATCH = 32
SEQ = 256
DEPTH = 4
DEC_BATCH = 4
DEC_SEQ = 2048
PAST_LEN = 512

GRID_W = 64
HEAD_DIM = 64
N_MOD = 6
EPS = 1e-6
ROPE_BASE = 10000.0
NEG = -1e30
MLA_HEADS = 12
Q_LORA = 384
KV_LORA = 256
QK_NOPE = 64
QK_ROPE = 32
V_DIM = 64
MLA_SCALE = (QK_NOPE + QK_ROPE) ** -0.5
MLA_QBLOCK = 128
POOL_WINDOWS = (2, 4, 8, 16)
POOL_GROUP = D_MODEL // 16
POOL_DIM = 4 * POOL_GROUP
NA_HEADS = 8
NA_KH_MAX = 8
NA_KW = 16
NA_QCB = 16
NA_KCB = 32
SWA_HEADS = 8
SWA_KV_HEADS = 2
SWA_GROUP = SWA_HEADS // SWA_KV_HEADS
SWA_WINDOW = 128
SWA_BLOCK = 128
D_FF = 2816
CONV_W = 3
EVEN_IN = Q_LORA + KV_LORA + QK_ROPE + POOL_DIM
ODD_IN = 3 * NA_HEADS * HEAD_DIM + (SWA_HEADS + 2 * SWA_KV_HEADS) * HEAD_DIM
MIX_WIDTH = MLA_HEADS * V_DIM + POOL_DIM
N_EVEN = (DEPTH + 1) // 2
N_ODD = DEPTH // 2

kernel_name = 'hybrid_diffusion_trunk_step'


def rmsnorm(x, g):
    xf = x.astype(jnp.float32)
    y = xf * lax.rsqrt(jnp.mean(xf * xf, axis=-1, keepdims=True) + EPS)
    return (y * g.astype(jnp.float32)).astype(x.dtype)


def adaln(cvec, w_mod, b_mod):
    m = jax.nn.silu(cvec) @ w_mod + b_mod
    return m.reshape(cvec.shape[0], N_MOD, D_MODEL)[:, :, None, :]


def modulate(h, shift, scale):
    return h * (1 + scale) + shift


def rope_2d(x):
    n, r = x.shape[1], x.shape[-1]
    half = r // 2
    nf = half // 2
    t = jnp.arange(n)
    freqs = ROPE_BASE ** (-jnp.arange(nf, dtype=jnp.float32) / nf)

    def rot(xp, pos):
        ang = pos.astype(jnp.float32)[:, None] * freqs
        cos, sin = jnp.cos(ang)[:, None, :], jnp.sin(ang)[:, None, :]
        x1, x2 = xp[..., :nf], xp[..., nf:]
        return jnp.concatenate([x1 * cos - x2 * sin, x1 * sin + x2 * cos], axis=-1)

    out = jnp.concatenate([rot(x[..., :half], t // GRID_W), rot(x[..., half:], t % GRID_W)], axis=-1)
    return out.astype(x.dtype)


def attend(q, k, v, sink=None):
    d = q.shape[-1]
    s = jnp.einsum('bqhgd,bkhd->bhgqk', q, k).astype(jnp.float32) * (d ** -0.5)
    if sink is not None:
        sk = sink.reshape(1, SWA_KV_HEADS, SWA_GROUP, 1, 1).astype(jnp.float32)
        s = jnp.concatenate([s, jnp.broadcast_to(sk, s.shape[:-1] + (1,))], axis=-1)
    p = jax.nn.softmax(s, axis=-1)
    if sink is not None:
        p = p[..., :-1]
    return jnp.einsum('bhgqk,bkhd->bqhgd', p.astype(v.dtype), v)


def multiscale_pool(xp, w_pool, pool_scale):
    b, n, _ = xp.shape
    xf = xp.astype(jnp.float32)
    cs = jnp.concatenate([jnp.zeros_like(xf[:, :1]), jnp.cumsum(xf, axis=1)], axis=1)
    t = np.arange(n)
    outs = []
    for g, w in enumerate(POOL_WINDOWS):
        lo = np.clip(t - w // 2, 0, n)
        hi = np.clip(t + w // 2, 0, n)
        sl = slice(g * POOL_GROUP, (g + 1) * POOL_GROUP)
        seg = cs[:, :, sl]
        cnt = jnp.asarray((hi - lo).astype(np.float32))[None, :, None]
        outs.append((seg[:, hi] - seg[:, lo]) / cnt - xf[:, :, sl])
    p = jnp.stack(outs, axis=2)
    y = jnp.einsum('bngc,gcd->bngd', p, w_pool.astype(jnp.float32)).reshape(b, n, POOL_DIM)
    return (y * pool_scale.astype(jnp.float32)).astype(xp.dtype)


def mla_split(z):
    a = Q_LORA
    b_ = a + KV_LORA
    c_ = b_ + QK_ROPE
    return z[..., :a], z[..., a:b_], z[..., b_:c_], z[..., c_:]


def mla_queries(qa, g_q, w_uq, positional):
    b, n, _ = qa.shape
    q = (rmsnorm(qa, g_q) @ w_uq).reshape(b, n, MLA_HEADS, QK_NOPE + QK_ROPE)
    q_nope, q_rope = q[..., :QK_NOPE], q[..., QK_NOPE:]
    if positional:
        q_rope = rope_2d(q_rope)
    return q_nope, q_rope


def mla_expand(latent, w_ukv):
    b, n, _ = latent.shape
    kv = (latent[..., :KV_LORA] @ w_ukv).reshape(b, n, MLA_HEADS, QK_NOPE + V_DIM)
    return kv[..., :QK_NOPE], latent[..., KV_LORA:], kv[..., QK_NOPE:]


def mla_attend(q_nope, q_rope, k_nope, k_rope, v):
    s = (jnp.einsum('bqhd,bkhd->bhqk', q_nope, k_nope)
         + jnp.einsum('bqhr,bkr->bhqk', q_rope, k_rope)).astype(jnp.float32) * MLA_SCALE
    p = jax.nn.softmax(s, axis=-1).astype(v.dtype)
    return jnp.einsum('bhqk,bkhd->bqhd', p, v)


def even_context(h, w_in, g_q, g_kv, w_uq, w_ukv, w_pool, pool_scale, w_out):
    b, n, _ = h.shape
    qa, ckv, kr, xp = mla_split(h @ w_in)
    latent = jnp.concatenate([rmsnorm(ckv, g_kv), kr], axis=-1)
    q_nope, q_rope = mla_queries(qa, g_q, w_uq, False)
    k_nope, k_rope, v = mla_expand(latent, w_ukv)
    o = mla_attend(q_nope, q_rope, k_nope, k_rope, v).reshape(b, n, MLA_HEADS * V_DIM)
    y_pool = multiscale_pool(xp, w_pool, pool_scale)
    return jnp.concatenate([o, y_pool], axis=-1) @ w_out, latent


def even_latent(h, lat_ctx, w_in, g_q, g_kv, w_uq, w_ukv, w_pool, pool_scale, w_out):
    b, n, _ = h.shape
    qa, ckv, kr, xp = mla_split(h @ w_in)
    kr = rope_2d(kr[:, :, None, :])[:, :, 0, :]
    lat = jnp.concatenate([rmsnorm(ckv, g_kv), kr], axis=-1)
    q_nope, q_rope = mla_queries(qa, g_q, w_uq, True)
    k_nope, k_rope, v = mla_expand(jnp.concatenate([lat, lat_ctx.astype(lat.dtype)], axis=1), w_ukv)
    nb = n // MLA_QBLOCK
    qn_b = q_nope.reshape(b, nb, MLA_QBLOCK, MLA_HEADS, QK_NOPE).swapaxes(0, 1)
    qr_b = q_rope.reshape(b, nb, MLA_QBLOCK, MLA_HEADS, QK_ROPE).swapaxes(0, 1)
    o = lax.map(lambda qs: mla_attend(qs[0], qs[1], k_nope, k_rope, v), (qn_b, qr_b))
    o = o.swapaxes(0, 1).reshape(b, n, MLA_HEADS * V_DIM)
    y_pool = multiscale_pool(xp, w_pool, pool_scale)
    return jnp.concatenate([o, y_pool], axis=-1) @ w_out


def odd_split(z):
    b, n, _ = z.shape
    dna = NA_HEADS * HEAD_DIM
    dsq = SWA_HEADS * HEAD_DIM
    dkv = SWA_KV_HEADS * HEAD_DIM
    offs = [int(o) for o in np.cumsum([0, dna, dna, dna, dsq, dkv, dkv])]
    parts = [z[..., offs[i]:offs[i + 1]] for i in range(6)]
    q_na = parts[0].reshape(b, n, NA_HEADS, HEAD_DIM)
    k_na = parts[1].reshape(b, n, NA_HEADS, HEAD_DIM)
    v_na = parts[2].reshape(b, n, NA_HEADS, HEAD_DIM)
    q_sw = parts[3].reshape(b, n, SWA_KV_HEADS, SWA_GROUP, HEAD_DIM)
    k_sw = parts[4].reshape(b, n, SWA_KV_HEADS, HEAD_DIM)
    v_sw = parts[5].reshape(b, n, SWA_KV_HEADS, HEAD_DIM)
    return q_na, k_na, v_na, q_sw, k_sw, v_sw


def na_latent(q, k, v, k_ctx, v_ctx, rpb):
    b, n, hh, d = q.shape
    rows = n // GRID_W
    kh = min(NA_KH_MAX, rows)
    ncb = GRID_W // NA_QCB
    row_start = np.clip(np.arange(rows) - kh // 2, 0, rows - kh)
    dr_idx = row_start[:, None] + np.arange(kh)[None, :] - np.arange(rows)[:, None] + NA_KH_MAX - 1
    qcol = np.arange(GRID_W).reshape(ncb, NA_QCB)
    kcol_start = np.clip(np.arange(ncb) * NA_QCB - (NA_KCB - NA_QCB) // 2, 0, GRID_W - NA_KCB)
    kcol = kcol_start[:, None] + np.arange(NA_KCB)[None, :]
    qcol_start = np.clip(qcol - NA_KW // 2, 0, GRID_W - NA_KW)
    col_mask = (kcol[:, None, :] >= qcol_start[..., None]) & (kcol[:, None, :] < qcol_start[..., None] + NA_KW)
    dc_idx = np.clip(kcol[:, None, :] - qcol[..., None] + NA_KW - 1, 0, 2 * NA_KW - 2)
    kg = k.reshape(b, rows, GRID_W, hh, d)
    vg = v.reshape(b, rows, GRID_W, hh, d)
    qg = q.reshape(b, rows, ncb, NA_QCB, hh, d).swapaxes(0, 1)
    scale = d ** -0.5
    nloc = kh * NA_KCB
    mask = jnp.asarray(col_mask)[:, :, None, :]

    def row_block(args):
        q_r, r0, dr = args
        k_r = lax.dynamic_slice_in_dim(kg, r0, kh, axis=1)[:, :, kcol]
        v_r = lax.dynamic_slice_in_dim(vg, r0, kh, axis=1)[:, :, kcol]
        s_loc = jnp.einsum('bcqhd,bkcjhd->bhcqkj', q_r, k_r).astype(jnp.float32) * scale
        bias = rpb[:, dr[None, None, :, None], dc_idx[:, :, None, :]].astype(jnp.float32)
        s_loc = jnp.where(mask, s_loc + bias, NEG)
        s_ctx = jnp.einsum('bcqhd,blhd->bhcql', q_r, k_ctx).astype(jnp.float32) * scale
        s = jnp.concatenate([s_loc.reshape(b, hh, ncb, NA_QCB, nloc), s_ctx], axis=-1)
        p = jax.nn.softmax(s, axis=-1).astype(v.dtype)
        p_loc = p[..., :nloc].reshape(b, hh, ncb, NA_QCB, kh, NA_KCB)
        return (jnp.einsum('bhcqkj,bkcjhd->bcqhd', p_loc, v_r)
                + jnp.einsum('bhcql,blhd->bcqhd', p[..., nloc:], v_ctx))

    o = lax.map(row_block, (qg, jnp.asarray(row_start, jnp.int32), jnp.asarray(dr_idx, jnp.int32)))
    return o.swapaxes(0, 1).reshape(b, n, hh * d)


def swa_latent(q, k, v, k_ctx, v_ctx, sink):
    b, n, hk, g, d = q.shape
    nb = n // SWA_BLOCK
    span = SWA_BLOCK + 2 * SWA_WINDOW
    scale = d ** -0.5
    qb = q.reshape(b, nb, SWA_BLOCK, hk, g, d)
    padw = ((0, 0), (SWA_WINDOW, SWA_WINDOW), (0, 0), (0, 0))
    idx = np.arange(nb)[:, None] * SWA_BLOCK + np.arange(span)[None, :]
    kb = jnp.pad(k, padw)[:, idx]
    vb = jnp.pad(v, padw)[:, idx]
    tq = np.arange(nb)[:, None, None] * SWA_BLOCK + np.arange(SWA_BLOCK)[None, :, None]
    sk = np.arange(nb)[:, None, None] * SWA_BLOCK - SWA_WINDOW + np.arange(span)[None, None, :]
    band = (sk >= 0) & (sk < n) & (np.abs(tq - sk) <= SWA_WINDOW)
    s_loc = jnp.einsum('bnqhgd,bnkhd->bnhgqk', qb, kb).astype(jnp.float32) * scale
    s_loc = jnp.where(jnp.asarray(band)[None, :, None, None], s_loc, NEG)
    s_ctx = jnp.einsum('bnqhgd,blhd->bnhgql', qb, k_ctx).astype(jnp.float32) * scale
    s_sink = jnp.broadcast_to(sink.reshape(1, 1, hk, g, 1, 1).astype(jnp.float32), s_loc.shape[:-1] + (1,))
    s = jnp.concatenate([s_loc, s_ctx, s_sink], axis=-1)
    p = jax.nn.softmax(s, axis=-1).astype(v.dtype)
    o = (jnp.einsum('bnhgqk,bnkhd->bnqhgd', p[..., :span], vb)
         + jnp.einsum('bnhgql,blhd->bnqhgd', p[..., span:-1], v_ctx))
    return o.reshape(b, n, hk * g * d)


def odd_context(h, w_in, sink, w_out):
    b, n, _ = h.shape
    q_na, k_na, v_na, q_sw, k_sw, v_sw = odd_split(h @ w_in)
    o_na = attend(q_na[:, :, :, None], k_na, v_na).reshape(b, n, NA_HEADS * HEAD_DIM)
    o_sw = attend(q_sw, k_sw, v_sw, sink).reshape(b, n, SWA_HEADS * HEAD_DIM)
    out = jnp.concatenate([o_na, o_sw], axis=-1) @ w_out
    return out, jnp.stack([k_na, v_na], axis=2), jnp.stack([k_sw, v_sw], axis=2)


def odd_latent(h, na_kv_ctx, sw_kv_ctx, w_in, rpb, sink, w_out):
    b, n, _ = h.shape
    q_na, k_na, v_na, q_sw, k_sw, v_sw = odd_split(h @ w_in)
    na_kv_ctx = na_kv_ctx.astype(h.dtype)
    sw_kv_ctx = sw_kv_ctx.astype(h.dtype)
    o_na = na_latent(q_na, k_na, v_na, na_kv_ctx[:, :, 0], na_kv_ctx[:, :, 1], rpb)
    q_sw = rope_2d(q_sw.reshape(b, n, SWA_HEADS, HEAD_DIM)).reshape(b, n, SWA_KV_HEADS, SWA_GROUP, HEAD_DIM)
    k_sw = rope_2d(k_sw)
    o_sw = swa_latent(q_sw, k_sw, v_sw, sw_kv_ctx[:, :, 0], sw_kv_ctx[:, :, 1], sink)
    return jnp.concatenate([o_na, o_sw], axis=-1) @ w_out


def conv_ffn(h, w_up, conv_w, conv_b, w_down):
    n = h.shape[1]
    u = h @ w_up
    pad = CONV_W // 2
    up = jnp.pad(u, ((0, 0), (pad, pad), (0, 0)))
    u = sum(up[:, j:j + n] * conv_w[j] for j in range(CONV_W)) + conv_b
    a, gt = u[..., :D_FF], u[..., D_FF:]
    return (jax.nn.silu(gt) * a) @ w_down


def setup_inputs(seed: int = 0) -> dict:
    key = jax.random.key(seed)
    ks = iter(jax.random.split(key, 40))

    def nrm(shape, scale=1.0):
        return jax.random.normal(next(ks), shape, jnp.float32) * scale

    def gain(shape):
        return 1.0 + nrm(shape, 0.05)

    return {
        'x_prompt': nrm((BATCH, SEQ, D_MODEL)),
        'x_sample': nrm((DEC_BATCH, DEC_SEQ, D_MODEL)),
        'cache_mla_latent': nrm((DEC_BATCH, N_EVEN, PAST_LEN, KV_LORA + QK_ROPE)),
        'cache_na_kv': nrm((DEC_BATCH, N_ODD, PAST_LEN, 2, NA_HEADS, HEAD_DIM)),
        'cache_swa_kv': nrm((DEC_BATCH, N_ODD, PAST_LEN, 2, SWA_KV_HEADS, HEAD_DIM)),
        'c': nrm((DEC_BATCH, D_MODEL)),
        'c_ctx': nrm((D_MODEL,)),
        'w_mod': nrm((DEPTH, D_MODEL, N_MOD * D_MODEL), 0.5 * D_MODEL ** -0.5),
        'b_mod': nrm((DEPTH, N_MOD * D_MODEL), 0.02),
        'norm_mix': gain((DEPTH, D_MODEL)),
        'norm_ffn': gain((DEPTH, D_MODEL)),
        'norm_final': gain((D_MODEL,)),
        'w_in_even': nrm((N_EVEN, D_MODEL, EVEN_IN), D_MODEL ** -0.5),
        'mla_q_norm': gain((N_EVEN, Q_LORA)),
        'mla_kv_norm': gain((N_EVEN, KV_LORA)),
        'w_uq': nrm((N_EVEN, Q_LORA, MLA_HEADS * (QK_NOPE + QK_ROPE)), Q_LORA ** -0.5),
        'w_ukv': nrm((N_EVEN, KV_LORA, MLA_HEADS * (QK_NOPE + V_DIM)), KV_LORA ** -0.5),
        'w_pool': nrm((N_EVEN, 4, POOL_GROUP, POOL_GROUP), POOL_GROUP ** -0.5),
        'pool_scale': 1.0 + nrm((N_EVEN, POOL_DIM), 0.1),
        'w_out_even': nrm((N_EVEN, MIX_WIDTH, D_MODEL), MIX_WIDTH ** -0.5),
        'w_in_odd': nrm((N_ODD, D_MODEL, ODD_IN), D_MODEL ** -0.5),
        'na_rpb': nrm((N_ODD, NA_HEADS, 2 * NA_KH_MAX - 1, 2 * NA_KW - 1), 0.1),
        'swa_sink': nrm((N_ODD, SWA_HEADS), 0.5),
        'w_out_odd': nrm((N_ODD, MIX_WIDTH, D_MODEL), MIX_WIDTH ** -0.5),
        'w_up': nrm((DEPTH, D_MODEL, 2 * D_FF), D_MODEL ** -0.5),
        'conv_w': nrm((DEPTH, CONV_W, 2 * D_FF), CONV_W ** -0.5),
        'conv_b': nrm((DEPTH, 2 * D_FF), 0.01),
        'w_down': nrm((DEPTH, D_FF, D_MODEL), D_FF ** -0.5),
    }


def reference(x_prompt, x_sample, cache_mla_latent, cache_na_kv, cache_swa_kv, c, c_ctx,
              w_mod, b_mod, norm_mix, norm_ffn, norm_final,
              w_in_even, mla_q_norm, mla_kv_norm, w_uq, w_ukv, w_pool, pool_scale, w_out_even,
              w_in_odd, na_rpb, swa_sink, w_out_odd,
              w_up, conv_w, conv_b, w_down):
    xp, xs = x_prompt, x_sample
    mla_states, na_states, swa_states = [], [], []
    for l in range(DEPTH):
        e = l // 2
        mp = adaln(c_ctx[None, :], w_mod[l], b_mod[l])
        ms = adaln(c, w_mod[l], b_mod[l])
        hp = modulate(rmsnorm(xp, norm_mix[l]), mp[:, 0], mp[:, 1])
        hs = modulate(rmsnorm(xs, norm_mix[l]), ms[:, 0], ms[:, 1])
        if l % 2 == 0:
            op, lat = even_context(hp, w_in_even[e], mla_q_norm[e], mla_kv_norm[e], w_uq[e], w_ukv[e],
                                   w_pool[e], pool_scale[e], w_out_even[e])
            os_ = even_latent(hs, cache_mla_latent[:, e], w_in_even[e], mla_q_norm[e], mla_kv_norm[e],
                              w_uq[e], w_ukv[e], w_pool[e], pool_scale[e], w_out_even[e])
            mla_states.append(lat)
        else:
            op, kv_na, kv_sw = odd_context(hp, w_in_odd[e], swa_sink[e], w_out_odd[e])
            os_ = odd_latent(hs, cache_na_kv[:, e], cache_swa_kv[:, e], w_in_odd[e], na_rpb[e],
                             swa_sink[e], w_out_odd[e])
            na_states.append(kv_na)
            swa_states.append(kv_sw)
        xp = xp + mp[:, 2] * op
        xs = xs + ms[:, 2] * os_
        hp = modulate(rmsnorm(xp, norm_ffn[l]), mp[:, 3], mp[:, 4])
        hs = modulate(rmsnorm(xs, norm_ffn[l]), ms[:, 3], ms[:, 4])
        xp = xp + mp[:, 5] * conv_ffn(hp, w_up[l], conv_w[l], conv_b[l], w_down[l])
        xs = xs + ms[:, 5] * conv_ffn(hs, w_up[l], conv_w[l], conv_b[l], w_down[l])
    y_prompt = rmsnorm(xp, norm_final)
    y_sample = rmsnorm(xs, norm_final)
    new_mla_latent = jnp.stack(mla_states, axis=1)
    new_na_kv = jnp.stack(na_states, axis=1)
    new_swa_kv = jnp.stack(swa_states, axis=1)
    return (y_prompt, y_sample, new_mla_latent, new_na_kv, new_swa_kv)
```

```python
import numpy as np
import concourse.bass as bass
import concourse.mybir as mybir
from concourse.bass_utils import run_bass_kernel_spmd
from contextlib import ExitStack

F32, BF16 = mybir.dt.float32, mybir.dt.bfloat16
AF, ALU = mybir.ActivationFunctionType, mybir.AluOpType

D = 1024; T = 2048; DEPTH = 4; PAST = 512; NK = T + PAST
DFF = 2816; NFC = 22
EPS = 1e-6
MLA_SCALE = 96 ** -0.5
BM = 1024.0
NEGB = -30000.0
STAGES = {"even": True, "odd": True, "ffn": True}
STOP_AT = None


class _Stop(Exception):
    pass


def ck(name):
    if STOP_AT == name:
        raise _Stop()
NLAYERS = DEPTH


class Op:
    __slots__ = ("eng", "fn", "deps", "dma", "sem", "target", "pre", "need", "ticket")

    def __init__(self, eng, fn, dma):
        self.eng = eng; self.fn = fn; self.dma = dma; self.deps = []
        self.sem = None; self.target = 0; self.pre = None; self.need = False; self.ticket = 0


class Prog:
    ENGS = ("pe", "act", "dve", "pool", "sp")
    NDS = {"sp": 40, "pool": 40}

    def __init__(self):
        self.q = {e: [] for e in self.ENGS}
        self.lw = {}; self.rd = {}
        self.dsem_next = {e: 0 for e in self.NDS}
        self.dsem_tgt = {}
        self.dma_since = []

    def add(self, eng, fn, reads=(), writes=(), dma=False):
        op = Op(eng, fn, dma)
        deps = []
        for k in reads:
            w = self.lw.get(k)
            if w is not None: deps.append(w)
            if isinstance(k, tuple) and k[0] == "ps":
                for r in self.rd.get(k, ()):
                    if r.eng != eng: deps.append(r)
        for k in writes:
            w = self.lw.get(k)
            if w is not None: deps.append(w)
            for r in self.rd.get(k, ()): deps.append(r)
        seen = set(); dl = []
        for d in deps:
            if d is op or id(d) in seen: continue
            seen.add(id(d))
            if (not d.dma) and d.eng == eng and eng == "pe": continue
            dl.append(d)
        op.deps = dl
        if dma:
            i = self.dsem_next[eng]; self.dsem_next[eng] = (i + 1) % self.NDS[eng]
            key = (eng, i)
            prev = self.dsem_tgt.get(key, 0)
            op.sem = key; op.pre = prev; op.target = prev + 16
            self.dsem_tgt[key] = op.target
            self.dma_since.append(op)
        for k in reads:
            self.rd.setdefault(k, []).append(op)
        for k in writes:
            self.lw[k] = op; self.rd[k] = []
        self.q[eng].append(op)
        return op

    def barrier(self):
        col = Op("dve", "nop", False)
        for e in self.ENGS:
            if self.q[e]:
                last = None
                for o in reversed(self.q[e]):
                    if o.fn is not None and not o.dma:
                        last = o; break
                if last is not None: col.deps.append(last)
        col.deps.extend(self.dma_since)
        self.dma_since = []
        self.q["dve"].append(col)
        for e in self.ENGS:
            if e == "dve": continue
            w = Op(e, None, False); w.deps = [col]
            self.q[e].append(w)
        self.lw = {}; self.rd = {}

    def emit(self, nc, block, csem, dsems):
        for e in self.ENGS:
            for op in self.q[e]:
                for d in op.deps:
                    if not d.dma: d.need = True
        for e in self.ENGS:
            t = 0
            for op in self.q[e]:
                if op.need and not op.dma:
                    t += 1; op.ticket = t
        engobj = {"pe": nc.tensor, "act": nc.scalar, "dve": nc.vector, "pool": nc.gpsimd, "sp": nc.sync}
        self.nwaits = 0

        def run(e, eo):
            waited = {}

            def wait(key, sem, val):
                if val <= 0 or waited.get(key, 0) >= val: return
                waited[key] = val
                eo.wait_ge(sem, val); self.nwaits += 1

            for op in self.q[e]:
                for d in op.deps:
                    if d.dma: wait(d.sem, dsems[d.sem], d.target)
                    else: wait(d.eng, csem[d.eng], d.ticket)
                if op.dma:
                    wait(op.sem, dsems[op.sem], op.pre)
                    op.fn(eo).then_inc(dsems[op.sem], 16)
                elif op.fn is None:
                    pass
                else:
                    ins = eo.nop() if op.fn == "nop" else op.fn(eo)
                    if op.need: ins.then_inc(csem[e], 1)
            for key, tgt in self.dsem_tgt.items():
                if key[0] == e: wait(key, dsems[key], tgt)

        block.tensor(lambda eo: run("pe", eo))
        block.scalar(lambda eo: run("act", eo))
        block.vector(lambda eo: run("dve", eo))
        block.gpsimd(lambda eo: run("pool", eo))
        block.sync(lambda eo: run("sp", eo))


def _bf(x):
    import ml_dtypes
    return np.asarray(x, np.float32).astype(ml_dtypes.bfloat16).astype(np.float32)


def _rope_tables(R, sample):
    half = R // 2; nf = half // 2
    t = np.arange(T)
    freqs = (10000.0 ** (-np.arange(nf, dtype=np.float32) / nf)).astype(np.float32)
    C = np.ones((R, T), np.float32); S = np.zeros((R, T), np.float32)
    if sample:
        for hi, pos in enumerate((t // 64, t % 64)):
            ang = pos.astype(np.float32)[None, :] * freqs[:, None]
            c, s = np.cos(ang).astype(np.float32), np.sin(ang).astype(np.float32)
            b = hi * half
            C[b:b + nf] = c; C[b + nf:b + 2 * nf] = c
            S[b:b + nf] = -s; S[b + nf:b + 2 * nf] = s
    return C, S


def _rope_perm(R):
    half = R // 2; nf = half // 2
    p = np.arange(R)
    for b in (0, half):
        p[b:b + nf] = np.arange(b + nf, b + 2 * nf)
        p[b + nf:b + 2 * nf] = np.arange(b, b + nf)
    return p


VAR_OF = [0, 1] + [2, 3] * 6 + [4, 5]
VAR_REP = [0, 1, 2, 3, 14, 15]


def _struct_consts(sample):
    c = {}
    c["ident"] = np.eye(128, dtype=np.float32)
    Ce, Se = _rope_tables(32, sample)
    c["ropeE"] = np.stack([np.tile(Ce, (4, 1)), np.tile(Se, (4, 1))], 0)
    Co, So = _rope_tables(64, sample)
    c["ropeO"] = np.stack([np.tile(Co, (2, 1)), np.tile(So, (2, 1))], 0)
    km = np.zeros((9, NK), np.float32); qm = np.zeros((9, T), np.float32)
    km[8, :] = 1.0; qm[8, :] = -BM
    if sample:
        km[0, :] = 1.0; qm[0, :] = BM
    else:
        for s in range(8):
            km[s, s * 256:(s + 1) * 256] = 1.0
            qm[s, s * 256:(s + 1) * 256] = BM
    c["kmaskE"] = km; c["qmaskE"] = qm
    fl = 1.0 if sample else 0.0
    c["flags"] = np.tile(np.array([[fl, 1.0 - fl]], np.float32), (128, 1))
    n = T if sample else 256
    Pm = np.zeros((16, 128, 4, 3, 128), np.float32)
    tt = np.arange(T); tl = tt % n; base = tt - tl
    for g, w in enumerate((2, 4, 8, 16)):
        lo = np.clip(tl - w // 2, 0, n); hi = np.clip(tl + w // 2, 0, n)
        cnt = (hi - lo).astype(np.float32)
        for t in range(T):
            j = t // 128
            for s in range(base[t] + lo[t], base[t] + hi[t]):
                sb = s // 128 - (j - 1)
                Pm[j, s % 128, g, sb, t % 128] += 1.0 / cnt[t]
            Pm[j, t % 128, g, 1, t % 128] -= 1.0
    c["Pm"] = Pm
    swb = np.full((6, 128, 3, 128), NEGB, np.float32)
    for v, i in enumerate(VAR_REP):
        tq = i * 128 + np.arange(128)
        for cc in range(3):
            kb = i - 1 + cc
            if kb < 0 or kb > 15: continue
            ks = kb * 128 + np.arange(128)
            if sample:
                ok = np.abs(tq[None, :] - ks[:, None]) <= 128
            else:
                ok = (tq[None, :] // 256) == (ks[:, None] // 256)
            swb[v, :, cc, :] = np.where(ok, 0.0, NEGB)
    c["swbias"] = swb
    return c


def _na_bias(sample, rpb):
    out = np.full((2, 6, 128, 8, 5, 128), NEGB, np.float32)
    for v, i in enumerate(VAR_REP):
        start = min(max(i - 2, 0), 11)
        tq = i * 128 + np.arange(128)
        r, cq = tq // 64, tq % 64
        for cc in range(5):
            ks = (start + cc) * 128 + np.arange(128)
            if sample:
                rp, cp = ks // 64, ks % 64
                r0 = np.clip(r - 4, 0, 24); c0 = np.clip(cq - 8, 0, 48)
                ok = ((rp[:, None] >= r0[None, :]) & (rp[:, None] < r0[None, :] + 8) &
                      (cp[:, None] >= c0[None, :]) & (cp[:, None] < c0[None, :] + 16))
                dr = np.clip(rp[:, None] - r[None, :] + 7, 0, 14)
                dc = np.clip(cp[:, None] - cq[None, :] + 15, 0, 30)
                for l in range(2):
                    g = rpb[l][:, dr, dc]
                    out[l, v, :, :, cc, :] = np.where(ok[:, None, :], g.transpose(1, 0, 2), NEGB)
            else:
                ok = (tq[None, :] // 256) == (ks[:, None] // 256)
                out[:, v, :, :, cc, :] = np.where(ok, 0.0, NEGB)[None, :, None, :]
    return out


def _host_inputs(inp):
    f = lambda a: np.ascontiguousarray(np.asarray(a, np.float32))
    w = {}
    for k in ("w_mod", "b_mod", "norm_mix", "norm_ffn", "norm_final", "mla_q_norm", "mla_kv_norm",
              "pool_scale", "w_out_even", "w_out_odd", "swa_sink", "w_up", "conv_w", "conv_b", "w_down"):
        w[k] = f(inp[k])
    rowperm = np.concatenate([np.arange(512)] + [np.concatenate([512 + jj * 64 + np.arange(64), 512 + (4 + jj) * 64 + np.arange(64)])
                                                  for jj in range(4)])
    w["w_out_odd"] = f(w["w_out_odd"][:, rowperm, :])
    wie = f(inp["w_in_even"])
    w["wie_qa"] = f(wie[:, :, 0:384]); w["wie_tm"] = f(wie[:, :, 384:928])
    w["wie_kr"] = f(wie[:, :, 640:672]); w["wie_krs"] = f(wie[:, :, 640 + _rope_perm(32)])
    wuq = f(inp["w_uq"]).reshape(2, 384, 12, 96)
    w["wuq_n"] = f(wuq[..., 0:64].reshape(2, 384, 768))
    w["wuq_r"] = f(wuq[..., 64:96].reshape(2, 384, 384))
    w["wuq_rs"] = f(wuq[..., 64 + _rope_perm(32)].reshape(2, 384, 384))
    wukv = f(inp["w_ukv"]).reshape(2, 256, 12, 128)
    w["wukv_k"] = f(wukv[..., 0:64].reshape(2, 256, 768)); w["wukv_v"] = f(wukv[..., 64:128].reshape(2, 256, 768))
    wp = f(inp["w_pool"]); bd = np.zeros((2, 2, 128, 128), np.float32)
    for e in range(2):
        for pr in range(2):
            bd[e, pr, 0:64, 0:64] = wp[e, 2 * pr]; bd[e, pr, 64:128, 64:128] = wp[e, 2 * pr + 1]
    w["wpool_bd"] = bd
    wio = f(inp["w_in_odd"])
    w["wio_qna"] = f(wio[:, :, 0:512]); w["wio_kna"] = f(wio[:, :, 512:1024])
    w["wio_nakv"] = f(wio[:, :, 512:1536]); w["wio_swkv"] = f(wio[:, :, 2048:2304])
    qsw = wio[:, :, 1536:2048].reshape(2, 1024, 8, 64)
    order = [0, 4, 1, 5, 2, 6, 3, 7]
    w["wio_qsw"] = f(qsw[:, :, order, :].reshape(2, 1024, 512))
    w["wio_qsws"] = f(qsw[:, :, order, :][..., _rope_perm(64)].reshape(2, 1024, 512))
    ksw = wio[:, :, 2048:2176].reshape(2, 1024, 2, 64)
    w["wio_ksw"] = f(ksw.reshape(2, 1024, 128)); w["wio_ksws"] = f(ksw[..., _rope_perm(64)].reshape(2, 1024, 128))
    sc = {True: _struct_consts(True), False: _struct_consts(False)}
    rpb = f(inp["na_rpb"])
    nab = {True: _na_bias(True, rpb), False: _na_bias(False, rpb)}
    xp = f(inp["x_prompt"]); xs = f(inp["x_sample"])
    maps = []
    for core in range(8):
        sample = core >= 4
        m = dict(w)
        m.update(sc[sample]); m["nabias"] = nab[sample]
        if sample:
            b = core - 4
            m["xT"] = f(xs[b].T)
            m["cvec"] = f(inp["c"])[b]
            m["c_mla"] = f(inp["cache_mla_latent"])[b]
            m["c_na"] = f(inp["cache_na_kv"])[b].reshape(2, 512, 2, 512)
            m["c_sw"] = f(inp["cache_swa_kv"])[b].reshape(2, 512, 2, 128)
        else:
            m["xT"] = f(xp[8 * core:8 * core + 8].reshape(T, D).T)
            m["cvec"] = f(inp["c_ctx"])
            m["c_mla"] = np.zeros((2, 512, 288), np.float32)
            m["c_na"] = np.zeros((2, 512, 2, 512), np.float32)
            m["c_sw"] = np.zeros((2, 512, 2, 128), np.float32)
        maps.append(m)
    return maps


IN_SHAPES = {
    "xT": [D, T], "cvec": [D], "c_mla": [2, 512, 288], "c_na": [2, 512, 2, 512], "c_sw": [2, 512, 2, 128],
    "w_mod": [4, D, 6 * D], "b_mod": [4, 6 * D], "norm_mix": [4, D], "norm_ffn": [4, D], "norm_final": [D],
    "mla_q_norm": [2, 384], "mla_kv_norm": [2, 256], "pool_scale": [2, 256],
    "w_out_even": [2, D, D], "w_out_odd": [2, D, D], "swa_sink": [2, 8],
    "w_up": [4, D, 2 * DFF], "conv_w": [4, 3, 2 * DFF], "conv_b": [4, 2 * DFF], "w_down": [4, DFF, D],
    "wie_qa": [2, D, 384], "wie_tm": [2, D, 544], "wie_kr": [2, D, 32], "wie_krs": [2, D, 32],
    "wuq_n": [2, 384, 768], "wuq_r": [2, 384, 384], "wuq_rs": [2, 384, 384],
    "wukv_k": [2, 256, 768], "wukv_v": [2, 256, 768], "wpool_bd": [2, 2, 128, 128],
    "wio_qna": [2, D, 512], "wio_kna": [2, D, 512], "wio_nakv": [2, D, 1024], "wio_swkv": [2, D, 256],
    "wio_qsw": [2, D, 512], "wio_qsws": [2, D, 512], "wio_ksw": [2, D, 128], "wio_ksws": [2, D, 128],
    "ident": [128, 128], "ropeE": [2, 128, T], "ropeO": [2, 128, T], "kmaskE": [9, NK], "qmaskE": [9, T],
    "flags": [128, 2], "Pm": [16, 128, 4, 3, 128], "swbias": [6, 128, 3, 128], "nabias": [2, 6, 128, 8, 5, 128],
}
OUT_SHAPES = {"yT": [D, T], "lat": [2, T, 288], "nakv": [2, T, 1024], "swkv": [2, T, 256]}


class KB:
    def __init__(self, nc, big, psums, ins, outs):
        self.nc = nc; self.big = big; self.P = psums; self.I = ins; self.O = outs
        self.pg = Prog()
        self.top = 0
        self.rr = {"mm": 0, "acc": 0, "tr": 0}
        self.RINGS_DEFAULT = {"mm": (0, 4), "acc": (4, 3), "tr": (7, 1)}
        self.rings = dict(self.RINGS_DEFAULT)
        self.uid = 0

    def alloc(self, nbytes):
        off = self.top; self.top += (nbytes + 7) // 8 * 2
        assert self.top * 4 <= 212000, f"SBUF overflow {self.top * 4}"
        return off

    def _shape(self, v, shape):
        if len(shape) == 1: return v
        if len(shape) == 2: return v.rearrange("p (a b) -> p a b", a=shape[0])
        if len(shape) == 3: return v.rearrange("p (a b c) -> p a b c", a=shape[0], b=shape[1])
        return v.rearrange("p (a b c d) -> p a b c d", a=shape[0], b=shape[1], c=shape[2])

    def f32(self, shape):
        n = int(np.prod(shape)); off = self.alloc(n * 4)
        return self._shape(self.big[:, off:off + n], shape)

    def bf(self, shape):
        n = int(np.prod(shape)); off = self.alloc(n * 2)
        return self._shape(self.big[:, off:off + (n + 1) // 2].bitcast(BF16)[:, 0:n], shape)

    def key(self, name):
        self.uid += 1
        return (name, self.uid)

    def ps(self, ring):
        lo, n = self.rings[ring]
        i = lo + self.rr[ring] % n; self.rr[ring] += 1
        return self.P[i], ("ps", i)

    def mm(self, out, lhsT, rhs, start, stop, reads, writes):
        return self.pg.add("pe", lambda e: e.matmul(out, lhsT, rhs, start=start, stop=stop), reads, writes)

    def tr(self, out, in_, ident, reads, writes):
        return self.pg.add("pe", lambda e: e.transpose(out, in_, ident), reads, writes)

    def act(self, out, in_, func, reads, writes, scale=1.0, bias=0.0, accum=None):
        if accum is None:
            return self.pg.add("act", lambda e: e.activation(out=out, in_=in_, func=func, bias=bias, scale=scale), reads, writes)
        return self.pg.add("act", lambda e: e.activation(out=out, in_=in_, func=func, bias=bias, scale=scale, accum_out=accum), reads, writes)

    def ts(self, eng, out, in0, s1, s2, op0, op1, reads, writes):
        if s2 is None:
            return self.pg.add(eng, lambda e: e.tensor_scalar(out=out, in0=in0, scalar1=s1, scalar2=None, op0=op0), reads, writes)
        return self.pg.add(eng, lambda e: e.tensor_scalar(out=out, in0=in0, scalar1=s1, scalar2=s2, op0=op0, op1=op1), reads, writes)

    def stt(self, eng, out, in0, scalar, in1, op0, op1, reads, writes):
        return self.pg.add(eng, lambda e: e.scalar_tensor_tensor(out=out, in0=in0, scalar=scalar, in1=in1, op0=op0, op1=op1), reads, writes)

    def tt(self, eng, out, in0, in1, op, reads, writes):
        return self.pg.add(eng, lambda e: e.tensor_tensor(out=out, in0=in0, in1=in1, op=op), reads, writes)

    def cp(self, eng, out, in_, reads, writes):
        if eng == "act":
            return self.pg.add("act", lambda e: e.copy(out=out, in_=in_), reads, writes)
        return self.pg.add(eng, lambda e: e.tensor_copy(out=out, in_=in_), reads, writes)

    def recip(self, out, in_, reads, writes):
        return self.pg.add("dve", lambda e: e.reciprocal(out=out, in_=in_), reads, writes)

    def memset(self, eng, ap, val, writes):
        return self.pg.add(eng, lambda e: e.memset(ap, val), (), writes)

    def dma(self, q, out, in_, reads, writes):
        return self.pg.add(q, lambda e: e.dma_start(out=out, in_=in_), reads, writes, dma=True)

    def wload(self, dst, src_ap, key):
        return self.dma("pool", dst, src_ap.rearrange("(kc p) n -> p kc n", p=128), (), (key,))


VOFF = {}
def _voff():
    o = 0
    for name, n in (("bmod", 192), ("nmix", 32), ("nffn", 32), ("nfin", 8), ("conv", 704), ("gq", 6),
                    ("pscale", 4), ("cvec", 8), ("gkv", 512), ("sink", 16), ("flags", 2)):
        VOFF[name] = (o, n); o += n
    return o
NV = _voff()


def _pack_vecs(inp, cvec, flags):
    v = np.zeros((128, NV), np.float32)
    def put(name, arr):
        o, n = VOFF[name]; v[:, o:o + n] = np.asarray(arr, np.float32).reshape(128, n)
    f = lambda a: np.asarray(a, np.float32)
    put("bmod", f(inp["b_mod"]).reshape(4, 48, 128).transpose(2, 0, 1))
    put("nmix", f(inp["norm_mix"]).reshape(4, 8, 128).transpose(2, 0, 1))
    put("nffn", f(inp["norm_ffn"]).reshape(4, 8, 128).transpose(2, 0, 1))
    put("nfin", f(inp["norm_final"]).reshape(8, 128).transpose(1, 0))
    cw = np.concatenate([f(inp["conv_w"]), f(inp["conv_b"])[:, None, :]], 1)
    put("conv", cw.reshape(4, 4, 44, 128).transpose(3, 0, 1, 2))
    put("gq", f(inp["mla_q_norm"]).reshape(2, 3, 128).transpose(2, 0, 1))
    put("pscale", f(inp["pool_scale"]).reshape(2, 2, 128).transpose(2, 0, 1))
    put("cvec", f(cvec).reshape(8, 128).transpose(1, 0))
    put("gkv", np.broadcast_to(f(inp["mla_kv_norm"]).reshape(1, 512), (128, 512)))
    put("sink", np.broadcast_to(f(inp["swa_sink"]).reshape(1, 16), (128, 16)))
    put("flags", flags)
    return v


def build_program():
    nc = bass.Bass("TRN2", target_bir_lowering=False)
    shapes = dict(IN_SHAPES); shapes["vecs"] = [128, NV]
    for k in ("cvec", "b_mod", "norm_mix", "norm_ffn", "norm_final", "mla_q_norm", "mla_kv_norm", "pool_scale",
              "swa_sink", "conv_w", "conv_b", "flags"):
        shapes.pop(k)
    I = {k: nc.dram_tensor(k, s, F32, kind="ExternalInput").ap() for k, s in shapes.items()}
    O = {k: nc.dram_tensor(k, s, F32, kind="ExternalOutput").ap() for k, s in OUT_SHAPES.items()}
    es = ExitStack()
    with es:
        big = es.enter_context(nc.sbuf_tensor("big", [128, 53000], F32))
        P = [es.enter_context(nc.psum_tensor(f"psb{i}", [128, 512], F32)) for i in range(8)]
        csem = {e: es.enter_context(nc.semaphore(f"c_{e}")) for e in Prog.ENGS}
        dsems = {(q, i): es.enter_context(nc.semaphore(f"d_{q}{i}")) for q in Prog.NDS for i in range(Prog.NDS[q])}
        block = es.enter_context(nc.Block())
        kb = KB(nc, big, [p[:, :] for p in P], I, O)
        _emit_all(kb)
        kb.pg.emit(nc, block, csem, dsems)
    return nc, kb


def _emit_all(kb):
    I, O, pg = kb.I, kb.O, kb.pg
    xres = kb.f32([8, T]); hT = kb.bf([8, T])
    vecs = kb.f32([NV])
    ones_bf = kb.bf([128]); ident_bf = kb.bf([128])
    scv = kb.bf([8]); modT_all = kb.f32([4, 48]); prm_all = kb.f32([4, 6, 8]); convx_all = kb.f32([4, 4, 44])
    prm = prm_all[:, 0]; convx = convx_all[:, 0]
    MARK = kb.top
    kb.xres, kb.hT, kb.vecs, kb.ones_bf, kb.ident_bf, kb.prm = xres, hT, vecs, ones_bf, ident_bf, prm
    kb.MARK = MARK

    def vv(name, l=None, per=None):
        o, n = VOFF[name]
        if l is None: return vecs[:, o:o + n]
        return vecs[:, o + l * per:o + (l + 1) * per]
    kb.vv = vv

    XK = lambda c, tb: ("x", c, tb)
    HK = lambda c, tb: ("h", c, tb)
    kb.XK, kb.HK = XK, HK
    kb.dma("sp", vecs, I["vecs"][:, :], (), ("vecs",))
    for c in range(8):
        kb.dma("sp", xres[:, c, :], I["xT"][c * 128:(c + 1) * 128, :], (), [XK(c, tb) for tb in range(4)])
    kb.dma("pool", ident_bf, I["ident"][:, :], (), ("ident",))
    kb.memset("dve", ones_bf, 1.0, ("ones",))
    kb.act(scv, vv("cvec"), AF.Silu, ("vecs",), ("scv",))

    def ring_setup(nslots, nel):
        kb.ring = [kb.bf([nel]) for _ in range(nslots)]
        kb.ring_i = 0; kb.ring_nel = nel

    def ring_next():
        i = kb.ring_i % len(kb.ring); kb.ring_i += 1
        return kb.ring[i], ("wr", i)
    kb.ring_setup, kb.ring_next = ring_setup, ring_next

    def wview(slot, kc, n):
        return slot[:, 0:kc * n].rearrange("p (a b) -> p a b", a=kc)
    kb.wview = wview

    def norm_mod(Acol, Bcol, sq, rs, tmp, dst_fn, final=False):
        for tb in range(4):
            sl = slice(tb * 512, (tb + 1) * 512)
            pst, pk = kb.ps("mm")
            for c in range(8):
                s = sq[:, c % 2, :]
                kb.act(s, xres[:, c, sl], AF.Square, (XK(c, tb),), (("sq", c % 2),))
                kb.mm(pst, ones_bf, s, c == 0, c == 7, (("sq", c % 2), "ones"), (pk,))
            kb.act(rs, pst, AF.Sqrt, (pk,), ("rs",), scale=1.0 / D, bias=EPS)
            kb.recip(rs, rs, ("rs",), ("rs",))
            for c in range(8):
                tm = tmp[:, c % 2, :]
                kb.stt("dve", tm, xres[:, c, sl], Acol(c), rs, ALU.mult, ALU.mult,
                       (XK(c, tb), "rs", "prm", "vecs"), (("tmp", c % 2),))
                dst_fn(c, tb, tm, ("tmp", c % 2), Bcol(c) if Bcol else 0.0)
    kb.norm_mod = norm_mod

    def to_hT(c, tb, tm, tk, bias):
        kb.act(hT[:, c, tb * 512:(tb + 1) * 512], tm, AF.Identity, (tk, "prm"), (HK(c, tb),), bias=bias)

    def adaln_gen(layers, aring, pst, pk, pw=512):
        cnt = 0
        for l in layers:
            modT = modT_all[:, l]; prm = prm_all[:, l]; convx = convx_all[:, l]
            for piece in range(6144 // pw):
                slot, sk = aring[cnt % len(aring)], ("awr", cnt % len(aring)); cnt += 1
                wv = wview(slot, 8, pw)
                kb.wload(wv, I["w_mod"][l][:, piece * pw:(piece + 1) * pw], sk)
                for cc in range(pw // 128):
                    j = piece * (pw // 128) + cc
                    for kc in range(8):
                        kb.mm(pst[:, j:j + 1], wv[:, kc, cc * 128:(cc + 1) * 128], scv[:, kc:kc + 1], kc == 0, kc == 7,
                              (sk, "scv"), (pk,))
                yield
            mk = ("modT", l); pk_ = ("prm", l)
            kb.tt("dve", modT, pst[:, 0:48], vv("bmod", l, 48), ALU.add, (pk, "vecs"), (mk,))
            for (row, sc_i, g) in ((0, 1, "nmix"), (3, 4, "nffn")):
                kb.stt("dve", prm[:, row, :], modT[:, sc_i * 8:(sc_i + 1) * 8], 1.0, vv(g, l, 8), ALU.add, ALU.mult,
                       (mk, "vecs"), (pk_,))
            for (row, m_i) in ((1, 0), (2, 2), (4, 3), (5, 5)):
                kb.cp("dve", prm[:, row, :], modT[:, m_i * 8:(m_i + 1) * 8], (mk,), (pk_,))
            o, _ = VOFF["conv"]; cb = o + l * 176
            fo, _ = VOFF["flags"]
            w0 = vecs[:, cb:cb + 44]; w2 = vecs[:, cb + 88:cb + 132]
            kb.ts("dve", convx[:, 0, :], w0, vecs[:, fo:fo + 1], None, ALU.mult, None, ("vecs",), (("convx", l),))
            kb.ts("dve", convx[:, 1, :], w2, vecs[:, fo:fo + 1], None, ALU.mult, None, ("vecs",), (("convx", l),))
            kb.ts("dve", convx[:, 2, :], w0, vecs[:, fo + 1:fo + 2], -1.0, ALU.mult, ALU.mult, ("vecs",), (("convx", l),))
            kb.ts("dve", convx[:, 3, :], w2, vecs[:, fo + 1:fo + 2], -1.0, ALU.mult, ALU.mult, ("vecs",), (("convx", l),))
            yield

    def adaln_first():
        kb.top = MARK
        aring = [kb.bf([8 * 512]) for _ in range(3)]
        pst, pk = kb.ps("acc")
        for _ in adaln_gen([0], aring, pst, pk):
            pass
        pg.barrier()

    def norm_phase(row_a, row_b):
        kb.top = MARK
        sq = kb.bf([2, 512]); rs = kb.f32([512]); tmp = kb.f32([2, 512])
        norm_mod(lambda c: kb.prm[:, row_a, c:c + 1], lambda c: kb.prm[:, row_b, c:c + 1], sq, rs, tmp, to_hT)
        pg.barrier()

    def ffn(l):
        kb.top = MARK
        o, _ = VOFF["conv"]; cb = o + l * 176
        cw = lambda k, c: vecs[:, cb + k * 44 + c:cb + k * 44 + c + 1]
        cx = lambda k, c: kb.convx[:, k, c:c + 1]
        ring_setup(2 if (l == 0 and NLAYERS > 1) else 3, 22 * 256)
        kb.rings = {"mm": (0, 7), "acc": (0, 7), "tr": (7, 1)}
        actb = kb.bf([NFC, 1024]); ta_r = kb.f32([4, 512]); tg_r = kb.f32([4, 512]); es = kb.f32([4, 2]); eh = kb.f32([44])
        agen = None
        if l == 0 and NLAYERS > 1:
            kb.rings = {"mm": (0, 6), "acc": (0, 6), "tr": (7, 1)}
            aring = [kb.bf([8 * 384]) for _ in range(2)]
            agen = adaln_gen(list(range(1, NLAYERS)), aring, kb.P[6], ("ps", 6), pw=384)
        for sb in range(2):
            for c in range(NFC):
                if c % 2 == 0:
                    slot, sk = ring_next(); wv = wview(slot, 8, 512)
                    kb.dma("pool", wv[:, :, 0:256], I["w_up"][l][:, c * 128:c * 128 + 256].rearrange("(kc p) n -> p kc n", p=128), (), (sk,))
                    kb.dma("pool", wv[:, :, 256:512], I["w_up"][l][:, DFF + c * 128:DFF + c * 128 + 256].rearrange("(kc p) n -> p kc n", p=128), (), (sk,))
                hp, hk = kb.ps("tr")
                tiles = {}; tts = {}
                for tb2 in range(2):
                    tb = sb * 2 + tb2; t0 = tb * 512
                    ri = (c % 2) * 2 + tb2
                    for gi in range(2):
                        col0 = gi * 256 + (c % 2) * 128
                        pst, pk = kb.ps("mm")
                        for kc in range(8):
                            kb.mm(pst, wv[:, kc, col0:col0 + 128], hT[:, kc, t0:t0 + 512], kc == 0, kc == 7,
                                  (sk, HK(kc, tb)), (pk,))
                        hcol = None
                        if sb == 0 and tb2 == 1: hcol = 1024
                        if hcol is not None:
                            for kc in range(8):
                                kb.mm(hp[:, gi:gi + 1], wv[:, kc, col0:col0 + 128], hT[:, kc, hcol:hcol + 1], kc == 0, kc == 7,
                                      (sk, HK(kc, hcol // 512)), (hk,))
                        tiles[(tb2, gi)] = (pst, pk)
                        tts[(tb2, gi)] = ((ta_r if gi == 0 else tg_r)[:, ri, :], ("tconv", gi, ri))
                    ccs = [gi * NFC + c for gi in range(2)]
                    for gi in range(2):
                        (pst, pk), (tt_, tk) = tiles[(tb2, gi)], tts[(tb2, gi)]
                        kb.act(tt_, pst, AF.Identity, (pk, "vecs"), (tk,), scale=cw(1, ccs[gi]), bias=cw(3, ccs[gi]))
                        if tb2 == 0:
                            kb.cp("act", es[:, (c % 2) * 2 + gi, 0:1], pst[:, 511:512], (pk,), (("es", (c % 2) * 2 + gi),))
                        if sb == 0 and tb2 == 1:
                            kb.cp("act", eh[:, ccs[gi]:ccs[gi] + 1], pst[:, 511:512], (pk,), (("eh", ccs[gi]),))
                    for gi in range(2):
                        (pst, pk), (tt_, tk) = tiles[(tb2, gi)], tts[(tb2, gi)]
                        kb.stt("dve", tt_[:, 1:512], pst[:, 0:511], cw(0, ccs[gi]), tt_[:, 1:512], ALU.mult, ALU.add, (pk, tk, "vecs"), (tk,))
                    for gi in range(2):
                        (pst, pk), (tt_, tk) = tiles[(tb2, gi)], tts[(tb2, gi)]
                        kb.stt("dve", tt_[:, 0:511], pst[:, 1:512], cw(2, ccs[gi]), tt_[:, 0:511], ALU.mult, ALU.add, (pk, tk, "vecs"), (tk,))
                    for gi in range(2):
                        (pst, pk), (tt_, tk) = tiles[(tb2, gi)], tts[(tb2, gi)]
                        kb.stt("dve", tt_[:, 256:257], pst[:, 255:256], cx(2, ccs[gi]), tt_[:, 256:257], ALU.mult, ALU.add, (pk, tk), (tk,))
                    for gi in range(2):
                        (pst, pk), (tt_, tk) = tiles[(tb2, gi)], tts[(tb2, gi)]
                        kb.stt("dve", tt_[:, 255:256], pst[:, 256:257], cx(3, ccs[gi]), tt_[:, 255:256], ALU.mult, ALU.add, (pk, tk), (tk,))
                    for gi in range(2):
                        (pst, pk), (tt_, tk) = tiles[(tb2, gi)], tts[(tb2, gi)]
                        if tb2 == 0 and sb == 1:
                            kb.act(tt_[:, 0:1], eh[:, ccs[gi]:ccs[gi] + 1], AF.Identity, (("eh", ccs[gi]), tk), (tk,), scale=cx(0, ccs[gi]), bias=tt_[:, 0:1])
                        if tb2 == 1:
                            ek = ("es", (c % 2) * 2 + gi)
                            kb.act(tt_[:, 0:1], es[:, (c % 2) * 2 + gi, 0:1], AF.Identity, (ek, tk), (tk,), scale=cx(0, ccs[gi]), bias=tt_[:, 0:1])
                        if tb2 == 1 and sb == 0:
                            kb.act(tt_[:, 511:512], hp[:, gi:gi + 1], AF.Identity, (hk, tk), (tk,), scale=cx(1, ccs[gi]), bias=tt_[:, 511:512])
                for gi in range(2):
                    (p1, k1), (t0_, tk0) = tiles[(1, gi)], tts[(0, gi)]
                    kb.act(t0_[:, 511:512], p1[:, 0:1], AF.Identity, (k1, tk0), (tk0,), scale=cx(1, gi * NFC + c), bias=t0_[:, 511:512])
                for tb2 in range(2):
                    (ta, tak), (tg, tgk) = tts[(tb2, 0)], tts[(tb2, 1)]
                    kb.act(tg, tg, AF.Silu, (tgk,), (tgk,))
                    kb.tt("dve", actb[:, c, tb2 * 512:(tb2 + 1) * 512], ta, tg, ALU.mult, (tak, tgk), (("actb", c, tb2),))
                if agen is not None and c % 2 == 1:
                    next(agen, None)
            for dp in range(4):
                slot, sk = ring_next(); wd = wview(slot, NFC, 256)
                kb.wload(wd, I["w_down"][l][:, dp * 256:(dp + 1) * 256], sk)
                for d2 in range(2):
                    dch = dp * 2 + d2
                    for tb2 in range(2):
                        tb = sb * 2 + tb2
                        pst, pk = kb.ps("acc")
                        for fc in range(NFC):
                            kb.mm(pst, wd[:, fc, d2 * 128:(d2 + 1) * 128], actb[:, fc, tb2 * 512:(tb2 + 1) * 512], fc == 0, fc == NFC - 1,
                                  (sk, ("actb", fc, tb2)), (pk,))
                        xs = xres[:, dch, tb * 512:(tb + 1) * 512]
                        kb.stt("dve", xs, pst, kb.prm[:, 5, dch:dch + 1], xs, ALU.mult, ALU.add, (pk, XK(dch, tb)), (XK(dch, tb),))
                if agen is not None:
                    next(agen, None)
        if agen is not None:
            for _ in agen:
                pass
        pg.barrier()
        kb.rings = dict(kb.RINGS_DEFAULT)

    def final_out():
        kb.top = MARK
        sq = kb.bf([2, 512]); rs = kb.f32([512]); tmp = kb.f32([2, 512]); yst = kb.f32([4, 512])
        cnt = [0]
        def to_out(c, tb, tm, tk, bias):
            i = cnt[0] % 4; cnt[0] += 1
            kb.cp("act", yst[:, i, :], tm, (tk,), (("yst", i),))
            kb.dma("sp", O["yT"][c * 128:(c + 1) * 128, tb * 512:(tb + 1) * 512], yst[:, i, :], (("yst", i),), ())
        o, _ = VOFF["nfin"]
        norm_mod(lambda c: vecs[:, o + c:o + c + 1], None, sq, rs, tmp, to_out)

    from_mixers = _mixers(kb)
    adaln_first()
    try:
        for l in range(NLAYERS):
            kb.prm = prm_all[:, l]; kb.convx = convx_all[:, l]
            norm_phase(0, 1)
            if l % 2 == 0 and STAGES["even"]:
                from_mixers["even"](l, l // 2)
            if l % 2 == 1 and STAGES["odd"]:
                from_mixers["odd"](l, l // 2)
            if STAGES["ffn"]:
                norm_phase(3, 4)
                ffn(l)
    except _Stop:
        pg.barrier()
    final_out()


def _mixers(kb):
    I, O, pg = kb.I, kb.O, kb.pg
    xres, hT, vecs, ones_bf, ident_bf, prm = kb.xres, kb.hT, kb.vecs, kb.ones_bf, kb.ident_bf, kb.prm
    XK, HK, vv, MARK = kb.XK, kb.HK, kb.vv, kb.MARK
    wview = kb.wview
    fo = VOFF["flags"][0]

    def bfv(pst):
        return pst[:, 0:256].bitcast(BF16)

    def out_proj(wname, e, extra, src=None):
        src = hT if src is None else src
        kb.ring_setup(2, 8 * 512)
        for piece in range(2):
            slot, sk = kb.ring_next(); wo = wview(slot, 8, 512)
            kb.wload(wo, I[wname][e][:, piece * 512:(piece + 1) * 512], sk)
            for d4 in range(4):
                dch = piece * 4 + d4
                for tb in range(4):
                    sl = slice(tb * 512, (tb + 1) * 512)
                    pst, pk = kb.ps("acc")
                    for kc in range(8):
                        if extra is not None and kc >= 6:
                            rhs, rk = extra[:, kc - 6, sl], ("yp", kc - 6, tb)
                        else:
                            rhs, rk = src[:, kc, sl], HK(kc, tb)
                        kb.mm(pst, wo[:, kc, d4 * 128:(d4 + 1) * 128], rhs, kc == 0, kc == 7, (sk, rk), (pk,))
                    xs = xres[:, dch, sl]
                    kb.stt("dve", xs, pst, kb.prm[:, 2, dch:dch + 1], xs, ALU.mult, ALU.add, (pk, XK(dch, tb)), (XK(dch, tb),))
        pg.barrier()

    def rope_combine(dst, dk, pa, pak, pb, pbk, tab, tabk, t1, t2, np_, pre=1.0):
        kb.stt("dve", t1[0:np_, :], pa[0:np_, :], pre, tab[0:np_, 0, :], ALU.mult, ALU.mult, (pak, tabk), ("rt1",))
        kb.stt("dve", t2[0:np_, :], pb[0:np_, :], pre, tab[0:np_, 1, :], ALU.mult, ALU.mult, (pbk, tabk), ("rt2",))
        kb.tt("dve", dst, t1[0:np_, :], t2[0:np_, :], ALU.add, ("rt1", "rt2"), (dk,))

    def even(l, e):
        kb.top = MARK
        qnT = kb.bf([3, T]); qrT = kb.bf([3, T]); cT = kb.bf([2, NK]); krT = kb.bf([NK]); ypT = kb.bf([2, T])
        MARK_E = kb.top
        wtm = kb.bf([8, 544]); wpl = kb.bf([2, 128]); sk = "wtm"; skp = "wpl"
        xp_tok = kb.bf([16, 384]); pooledT = kb.bf([2, T]); lat_st = kb.f32([2, 288]); ctok = kb.bf([2, 256])
        sqt = kb.f32([2, 256]); ssq = kb.f32([2]); ctxl = kb.bf([4, 288]); Pms = [kb.bf([4, 3, 128]) for _ in range(2)]
        kb.memset("dve", xp_tok, 0.0, [("xp", j) for j in range(16)])
        kb.wload(wtm, I["wie_tm"][e], sk)
        kb.dma("pool", ctxl, I["c_mla"][e].rearrange("(b p) n -> p b n", p=128), (), ("ctxl",))
        for j in range(16):
            i = j % 2; tb = j // 4
            p1, k1 = kb.ps("mm"); p2, k2 = kb.ps("mm")
            for kc in range(8):
                lh = hT[:, kc, j * 128:(j + 1) * 128]
                kb.mm(p1[:, 0:288], lh, wtm[:, kc, 0:288], kc == 0, kc == 7, (sk, HK(kc, tb)), (k1,))
                kb.mm(p2[:, 0:256], lh, wtm[:, kc, 288:544], kc == 0, kc == 7, (sk, HK(kc, tb)), (k2,))
            kb.act(sqt[:, i, :], p1[:, 0:256], AF.Square, (k1,), (("sqt", i),))
            pg.add("dve", lambda en, o=ssq[:, i:i + 1], a=sqt[:, i, :]: en.reduce_sum(out=o, in_=a, axis=mybir.AxisListType.X),
                   (("sqt", i),), (("ssq", i),))
            kb.act(ssq[:, i:i + 1], ssq[:, i:i + 1], AF.Sqrt, (("ssq", i),), (("ssq", i),), scale=1.0 / 256, bias=EPS)
            kb.recip(ssq[:, i:i + 1], ssq[:, i:i + 1], (("ssq", i),), (("ssq", i),))
            go = VOFF["gkv"][0] + e * 256
            kb.stt("dve", lat_st[:, i, 0:256], p1[:, 0:256], ssq[:, i:i + 1], vecs[:, go:go + 256], ALU.mult, ALU.mult,
                   (k1, ("ssq", i), "vecs"), (("lat", i),))
            kb.cp("act", lat_st[:, i, 256:288], p1[:, 256:288], (k1,), (("lat", i),))
            kb.dma("sp", O["lat"][e, j * 128:(j + 1) * 128, :], lat_st[:, i, :], (("lat", i),), ())
            kb.cp("dve", ctok[:, i, :], lat_st[:, i, 0:256], (("lat", i),), (("ctok", i),))
            ptr, tk = kb.ps("tr"); pb = bfv(ptr)
            for cc in range(2):
                kb.tr(pb[:, cc * 128:(cc + 1) * 128], ctok[:, i, cc * 128:(cc + 1) * 128], ident_bf, (("ctok", i), "ident"), (tk,))
            kb.cp("act", cT[:, :, j * 128:(j + 1) * 128], pb[:, 0:256].rearrange("p (a b) -> p a b", a=2), (tk,), (("cT", j),))
            for half in range(2):
                dst = xp_tok[:, j, half * 192:(half + 1) * 192].rearrange("p (a b) -> p a b", a=3)[:, 0:3:2, :]
                src = p2[:, half * 128:(half + 1) * 128].rearrange("p (a b) -> p a b", a=2)
                kb.cp("act", dst, src, (k2,), (("xp", j),))
        ck("e_tm")
        for blk in range(4):
            ptr, tk = kb.ps("tr"); pb = bfv(ptr)
            for cc in range(2):
                kb.tr(pb[:, cc * 128:(cc + 1) * 128], ctxl[:, blk, cc * 128:(cc + 1) * 128], ident_bf, ("ctxl", "ident"), (tk,))
            kb.tr(pb[0:32, 256:384], ctxl[:, blk, 256:288], ident_bf, ("ctxl", "ident"), (tk,))
            kb.cp("act", cT[:, :, T + blk * 128:T + (blk + 1) * 128], pb[:, 0:256].rearrange("p (a b) -> p a b", a=2), (tk,), (("cT", 16 + blk),))
            kb.cp("dve", krT[0:32, T + blk * 128:T + (blk + 1) * 128], pb[0:32, 256:384], (tk,), (("krT", 4),))
        ck("e_ctx")
        kb.dma("pool", wpl, I["wpool_bd"][e].rearrange("a k m -> k a m"), (), (skp,))
        for j in range(16):
            pm = Pms[j % 2]; pmk = ("Pm", j % 2)
            kb.dma("pool", pm, I["Pm"][j], (), (pmk,))
            for pr in range(2):
                pst, pk = kb.ps("mm")
                todo = [(gg, sbi) for gg in range(2) for sbi in range(3) if 0 <= j - 1 + sbi <= 15]
                for n_, (gg, sbi) in enumerate(todo):
                    sbk = j - 1 + sbi; c0 = pr * 192 + gg * 64
                    kb.mm(pst[:, 0:128], xp_tok[:, sbk, c0:c0 + 128], pm[:, pr * 2 + gg, sbi, :], n_ == 0, n_ == len(todo) - 1,
                          (("xp", sbk), pmk), (pk,))
                kb.cp("act", pooledT[:, pr, j * 128:(j + 1) * 128], pst[:, 0:128], (pk,), (("pooled", pr, j // 4),))
        pso = VOFF["pscale"][0] + e * 2
        for pr in range(2):
            for tb in range(4):
                sl = slice(tb * 512, (tb + 1) * 512)
                pst, pk = kb.ps("mm")
                kb.mm(pst, wpl[:, pr, :], pooledT[:, pr, sl], True, True, (skp, ("pooled", pr, tb)), (pk,))
                kb.ts("dve", ypT[:, pr, sl], pst, vecs[:, pso + pr:pso + pr + 1], None, ALU.mult, None, (pk, "vecs"), (("yp", pr, tb),))
        ck("e_pool")
        pg.barrier()
        kb.top = MARK_E
        kb.ring_setup(2, 8 * 544)
        qa_f = kb.f32([3, 512]); sq = kb.bf([2, 512]); rs = kb.f32([512]); t1 = kb.f32([512]); t2 = kb.f32([512])
        tabE = [kb.f32([2, 512]) for _ in range(2)]
        slot, sk1 = kb.ring_next(); wkr = wview(slot, 8, 448)
        kb.dma("pool", wkr[:, :, 0:32], I["wie_kr"][e].rearrange("(kc p) n -> p kc n", p=128), (), (sk1,))
        kb.dma("pool", wkr[:, :, 32:64], I["wie_krs"][e].rearrange("(kc p) n -> p kc n", p=128), (), (sk1,))
        kb.dma("pool", wkr[:, :, 64:448], I["wie_qa"][e].rearrange("(kc p) n -> p kc n", p=128), (), (sk1,))
        slot, sk2 = kb.ring_next(); wqr = wview(slot, 3, 768)
        kb.dma("pool", wqr[:, :, 0:384], I["wuq_r"][e].rearrange("(kc p) n -> p kc n", p=128), (), (sk2,))
        kb.dma("pool", wqr[:, :, 384:768], I["wuq_rs"][e].rearrange("(kc p) n -> p kc n", p=128), (), (sk2,))
        gq0 = VOFF["gq"][0] + e * 3
        for tb in range(4):
            sl = slice(tb * 512, (tb + 1) * 512)
            tab = tabE[tb % 2]; tabk = ("tabE", tb % 2)
            kb.dma("sp", tab, I["ropeE"][:, :, sl].rearrange("a p t -> p a t"), (), (tabk,))
            pa, pak = kb.ps("mm"); pb_, pbk = kb.ps("mm")
            for kc in range(8):
                kb.mm(pa[0:32, :], wkr[:, kc, 0:32], hT[:, kc, sl], kc == 0, kc == 7, (sk1, HK(kc, tb)), (pak,))
            for kc in range(8):
                kb.mm(pb_[0:32, :], wkr[:, kc, 32:64], hT[:, kc, sl], kc == 0, kc == 7, (sk1, HK(kc, tb)), (pbk,))
            rope_combine(krT[0:32, sl], ("krT", tb), pa, pak, pb_, pbk, tab, tabk, t1, t2, 32)
            pn, pnk = kb.ps("acc")
            for m in range(3):
                pq, pqk = kb.ps("mm")
                for kc in range(8):
                    kb.mm(pq, wkr[:, kc, 64 + m * 128:64 + (m + 1) * 128], hT[:, kc, sl], kc == 0, kc == 7, (sk1, HK(kc, tb)), (pqk,))
                kb.cp("act", qa_f[:, m, :], pq, (pqk,), (("qa_f", m),))
                kb.act(sq[:, m % 2, :], qa_f[:, m, :], AF.Square, (("qa_f", m),), (("sq", m % 2),))
                kb.mm(pn, ones_bf, sq[:, m % 2, :], m == 0, m == 2, (("sq", m % 2), "ones"), (pnk,))
            kb.act(rs, pn, AF.Sqrt, (pnk,), ("rs",), scale=1.0 / 384, bias=EPS)
            kb.recip(rs, rs, ("rs",), ("rs",))
            for m in range(3):
                kb.stt("dve", qnT[:, m, sl], qa_f[:, m, :], vecs[:, gq0 + m:gq0 + m + 1], rs, ALU.mult, ALU.mult,
                       (("qa_f", m), "rs", "vecs"), (("qnT", m, tb),))
            for m in range(3):
                pa, pak = kb.ps("mm"); pb_, pbk = kb.ps("mm")
                for kc in range(3):
                    kb.mm(pa, wqr[:, kc, m * 128:(m + 1) * 128], qnT[:, kc, sl], kc == 0, kc == 2, (sk2, ("qnT", kc, tb)), (pak,))
                for kc in range(3):
                    kb.mm(pb_, wqr[:, kc, 384 + m * 128:384 + (m + 1) * 128], qnT[:, kc, sl], kc == 0, kc == 2, (sk2, ("qnT", kc, tb)), (pbk,))
                rope_combine(qrT[:, m, sl], ("qrT", m, tb), pa, pak, pb_, pbk, tab, tabk, t1, t2, 128)
        ck("e_ea2")
        pg.barrier()
        kb.top = MARK_E
        kb.rings = {"mm": (0, 4), "acc": (4, 4), "tr": (7, 1)}
        wqn = kb.bf([3, 768]); wk = kb.bf([2, 768]); wvv = kb.bf([2, 768])
        KT = kb.bf([2, NK]); QT = kb.bf([2, T]); Vh = kb.bf([2, 20, 128]); PT = kb.bf([6, 512])
        rDs = kb.f32([2, 512]); rDt = kb.f32([512]); sel = kb.f32([2, 128])
        kb.memset("dve", rDt, 0.0, ("rDt",)); kb.memset("dve", sel, 0.0, ("sel",))
        kb.memset("dve", sel[64:65, 0, 0:64], 1.0, ("sel",))
        kb.memset("dve", sel[32:33, 1, 64:128], 1.0, ("sel",))
        kb.wload(wqn, I["wuq_n"][e], "wqn"); kb.wload(wk, I["wukv_k"][e], "wk"); kb.wload(wvv, I["wukv_v"][e], "wvv")
        for b in range(2):
            kb.dma("pool", KT[96:105, b, :], I["kmaskE"][:, :], (), (("KTm", b),))
            kb.dma("pool", QT[96:105, b, :], I["qmaskE"][:, :], (), (("QTm", b),))
            kb.dma("sp", KT[64:96, b, :], krT[0:32, :], (), (("KTr", b),))
            kb.memset("dve", Vh[:, b, :, :], 0.0, (("Vh", b),))
            oc = 64 if b == 0 else 32
            kb.memset("dve", Vh[:, b, :, oc:oc + 1], 1.0, (("Vh", b),))

        def proj(h, b):
            kb.dma("sp", QT[64:96, b, :], qrT[(h % 4) * 32:(h % 4) * 32 + 32, h // 4, :], (), (("QTr", b),))
            for tb in range(4):
                sl = slice(tb * 512, (tb + 1) * 512)
                pst, pk = kb.ps("mm")
                for kc in range(3):
                    kb.mm(pst[0:64, :], wqn[:, kc, h * 64:(h + 1) * 64], qnT[:, kc, sl], kc == 0, kc == 2, ("wqn",), (pk,))
                kb.cp("act", QT[0:64, b, sl], pst[0:64, :], (pk,), (("QTn", b),))
            for k5 in range(5):
                sl = slice(k5 * 512, (k5 + 1) * 512)
                pst, pk = kb.ps("mm")
                for cc in range(2):
                    kb.mm(pst[0:64, :], wk[:, cc, h * 64:(h + 1) * 64], cT[:, cc, sl], cc == 0, cc == 1, ("wk",), (pk,))
                kb.cp("dve", KT[0:64, b, sl], pst[0:64, :], (pk,), (("KTn", b),))
            for g0, nb in ((0, 8), (8, 8), (16, 4)):
                pst, pk = kb.ps("mm")
                for i in range(nb):
                    kblk = g0 + i
                    for cc in range(2):
                        kb.mm(pst[:, i * 64:(i + 1) * 64], cT[:, cc, kblk * 128:(kblk + 1) * 128], wvv[:, cc, h * 64:(h + 1) * 64],
                              cc == 0, cc == 1, ("wvv",), (pk,))
                kb.cp("act" if g0 == 8 else "dve", Vh[:, b, g0:g0 + nb, b * 64:b * 64 + 64], pst[:, 0:nb * 64].rearrange("p (a b) -> p a b", a=nb),
                      (pk,), (("Vh", b),))

        def attn(h, b):
            rd_q = (("QTn", b), ("QTr", b), ("QTm", b)); rd_k = (("KTn", b), ("KTr", b), ("KTm", b))
            hp = slice(b * 64, b * 64 + 64)
            p0 = 64 if b == 0 else 32
            for qc in range(4):
                accO, ok_ = kb.ps("acc")
                pend = []

                def pv(kc, slot):
                    kb.mm(accO, Vh[:, b, kc, :], PT[:, slot, :], kc == 0, kc == 19, (("PT", slot), ("Vh", b)), (ok_,))
                for kc in range(20):
                    slot = (qc * 20 + kc) % 6
                    pst, pk = kb.ps("mm")
                    kb.mm(pst, KT[0:105, b, kc * 128:(kc + 1) * 128], QT[0:105, b, qc * 512:(qc + 1) * 512], True, True, rd_q + rd_k, (pk,))
                    kb.act(PT[:, slot, :], pst, AF.Exp, (pk,), (("PT", slot),), scale=MLA_SCALE)
                    pend.append((kc, slot))
                    if len(pend) > 1:
                        pv(*pend.pop(0))
                while pend:
                    pv(*pend.pop(0))
                rd = rDs[:, qc % 2, :]; rk = ("rD", qc % 2)
                kb.recip(rDt[p0:p0 + 1, :], accO[p0:p0 + 1, :], (ok_,), ("rDt",))
                bc, bk = kb.ps("acc")
                kb.mm(bc, sel[:, b, :], rDt, True, True, ("sel", "rDt"), (bk,))
                kb.cp("act", rd[hp, :], bc[hp, :], (bk,), (rk,))
                kb.tt("dve", hT[hp, h // 2, qc * 512:(qc + 1) * 512], accO[hp, :], rd[hp, :], ALU.mult, (ok_, rk), (HK(h // 2, qc),))

        ck("e_ebsetup")
        proj(0, 0)
        ck("e_proj0")
        for h in range(12):
            if h + 1 < 12: proj(h + 1, (h + 1) % 2)
            attn(h, h % 2)
            ck("e_attn%d" % h)
        pg.barrier()
        kb.rings = dict(kb.RINGS_DEFAULT)
        kb.top = MARK_E
        out_proj("w_out_even", e, ypT)

    def odd(l, e):
        kb.top = MARK
        QM = kb.bf([8, T]); ctxones = kb.bf([128]); zrow = kb.f32([128])
        kb.memset("dve", ctxones, 1.0, ("ctxones",))
        kb.ts("dve", ctxones, ctxones, vecs[:, fo:fo + 1], None, ALU.mult, None, ("ctxones", "vecs"), ("ctxones",))
        kb.memset("dve", zrow, 0.0, ("zrow",))
        MARK_O = kb.top
        KTna = kb.bf([4, NK]); Vna = kb.bf([20, 512])
        MARK_O2 = kb.top
        kb.ring_setup(2, 8 * 512)
        stg = kb.f32([2, 512]); ctxl = kb.bf([4, 512])
        kb.dma("pool", Vna[:, 16:20, :], I["c_na"][e][:, 1, :].rearrange("(b p) n -> p b n", p=128), (), ("Vna_ctx",))
        kb.dma("pool", ctxl, I["c_na"][e][:, 0, :].rearrange("(b p) n -> p b n", p=128), (), ("ctxl",))
        for piece in range(2):
            slot, sk = kb.ring_next(); wv = wview(slot, 8, 512)
            kb.wload(wv, I["wio_nakv"][e][:, piece * 512:(piece + 1) * 512], sk)
            for j in range(16):
                pst, pk = kb.ps("mm")
                for kc in range(8):
                    kb.mm(pst, hT[:, kc, j * 128:(j + 1) * 128], wv[:, kc, :], kc == 0, kc == 7, (sk, HK(kc, j // 4)), (pk,))
                kb.cp("act", stg[:, j % 2, :], pst, (pk,), (("stg", j % 2),))
                kb.dma("sp", O["nakv"][e, j * 128:(j + 1) * 128, piece * 512:(piece + 1) * 512], stg[:, j % 2, :], (("stg", j % 2),), ())
                if piece == 1:
                    kb.cp("dve", Vna[:, j, :], pst, (pk,), (("Vna", j),))
        for blk in range(4):
            ptr, tk = kb.ps("tr"); pb = bfv(ptr)
            for m in range(4):
                kb.tr(pb[:, m * 128:(m + 1) * 128], ctxl[:, blk, m * 128:(m + 1) * 128], ident_bf, ("ctxl", "ident"), (tk,))
            kb.cp("act", KTna[:, :, T + blk * 128:T + (blk + 1) * 128], pb.rearrange("p (a b) -> p a b", a=4), (tk,), (("KTna_ctx", blk),))
        for wname, isq in (("wio_qna", True), ("wio_kna", False)):
            slot, sk = kb.ring_next(); wv = wview(slot, 8, 512)
            kb.wload(wv, I[wname][e], sk)
            for m in range(4):
                for tb in range(4):
                    sl = slice(tb * 512, (tb + 1) * 512)
                    pst, pk = kb.ps("mm")
                    for kc in range(8):
                        kb.mm(pst, wv[:, kc, m * 128:(m + 1) * 128], hT[:, kc, sl], kc == 0, kc == 7, (sk, HK(kc, tb)), (pk,))
                    if isq:
                        kb.act(QM[:, m, sl], pst, AF.Copy, (pk,), [("QM", m, tb * 4 + i) for i in range(4)], scale=0.125)
                    else:
                        kb.cp("dve", KTna[:, m, sl], pst, (pk,), (("KTna", m, tb),))
        pg.barrier()
        kb.top = MARK_O2
        kb.rings = {"mm": (0, 6), "acc": (6, 2), "tr": (7, 1)}
        PT = kb.bf([6, 512]); nab = [kb.bf([2, 5, 128]) for _ in range(2)]; rDs = kb.f32([2, 128])
        units = [(j, i, hh) for j in range(4) for i in range(16) for hh in range(2)]
        st = {}

        def na_S(k):
            j, i, hh = units[k]
            if hh == 0:
                nb_ = nab[(k // 2) % 2]; nk = ("nab", (k // 2) % 2)
                kb.dma("pool", nb_, I["nabias"][e, VAR_OF[i]][:, 2 * j:2 * j + 2, :, :], (), (nk,))
                start = min(max(i - 2, 0), 11)
                tiles = [(start + c, c) for c in range(5)] + [(16 + c, None) for c in range(4)]
                accb, ak = kb.ps("acc")
                st[(j, i)] = (nb_, nk, tiles, accb, ak)
            nb_, nk, tiles, accb, ak = st[(j, i)]
            hp = slice(hh * 64, hh * 64 + 64)
            banks = []
            for t, (kblk, c) in enumerate(tiles):
                if t % 4 == 0:
                    banks.append(kb.ps("mm"))
                pst, pk = banks[-1]; col = slice((t % 4) * 128, (t % 4) * 128 + 128)
                kb.mm(pst[:, col], KTna[hp, j, kblk * 128:(kblk + 1) * 128], QM[hp, j, i * 128:(i + 1) * 128], True, True,
                      (("QM", j, i),), (pk,))
            kb.tt("dve", banks[0][0], banks[0][0], nb_[:, hh, 0:4, :].rearrange("p a b -> p (a b)"), ALU.add, (banks[0][1], nk), (banks[0][1],))
            kb.tt("dve", banks[1][0][:, 0:128], banks[1][0][:, 0:128], nb_[:, hh, 4, :], ALU.add, (banks[1][1], nk), (banks[1][1],))
            slots = []
            for bi, (pst, pk) in enumerate(banks):
                ncol = min(4, 9 - bi * 4) * 128
                sl_ = (k * 3 + bi) % 6
                kb.act(PT[:, sl_, 0:ncol], pst[:, 0:ncol], AF.Exp, (pk,), (("PT", sl_),))
                slots.append(sl_)
            st[(j, i, hh)] = slots

        def na_OD(k):
            j, i, hh = units[k]
            nb_, nk, tiles, accb, ak = st[(j, i)]
            slots = st[(j, i, hh)]
            Oh = accb[:, hh * 128:(hh + 1) * 128]; Dh = accb[:, 256 + hh * 128:256 + (hh + 1) * 128]
            for t, (kblk, c) in enumerate(tiles):
                sl_ = slots[t // 4]
                kb.mm(Oh, Vna[:, kblk, j * 128:(j + 1) * 128], PT[:, sl_, (t % 4) * 128:(t % 4) * 128 + 128], t == 0, t == 8,
                      (("PT", sl_),), (ak,))
            for t, (kblk, c) in enumerate(tiles):
                sl_ = slots[t // 4]
                kb.mm(Dh, ones_bf if c is not None else ctxones, PT[:, sl_, (t % 4) * 128:(t % 4) * 128 + 128], t == 0, t == 8,
                      (("PT", sl_), "ones", "ctxones"), (ak,))
            if hh == 1:
                for h2 in range(2):
                    hp = slice(h2 * 64, h2 * 64 + 64)
                    rd = rDs[:, (k + h2) % 2, :]; rk = ("rD", (k + h2) % 2)
                    kb.recip(rd[hp, :], accb[hp, 256 + h2 * 128:256 + (h2 + 1) * 128], (ak,), (rk,))
                    kb.tt("dve", QM[hp, j, i * 128:(i + 1) * 128], accb[hp, h2 * 128:(h2 + 1) * 128], rd[hp, :], ALU.mult, (ak, rk), (("QM", j, i),))

        na_S(0)
        for k in range(len(units)):
            if k + 1 < len(units): na_S(k + 1)
            na_OD(k)
        pg.barrier()
        kb.rings = dict(kb.RINGS_DEFAULT)
        kb.top = MARK_O
        KTsw = kb.bf([NK]); Vsw = kb.bf([20, 128])
        MARK_S = kb.top
        kb.ring_setup(3, 8 * 512)
        stg = kb.f32([2, 256]); ctxk = kb.bf([4, 128]); t1 = kb.f32([512]); t2 = kb.f32([512])
        tabO = [kb.f32([2, 512]) for _ in range(2)]
        kb.dma("pool", Vsw[:, 16:20, :], I["c_sw"][e][:, 1, :].rearrange("(b p) n -> p b n", p=128), (), ("Vsw_ctx",))
        kb.dma("pool", ctxk, I["c_sw"][e][:, 0, :].rearrange("(b p) n -> p b n", p=128), (), ("ctxk",))
        slot, sk = kb.ring_next(); wv = wview(slot, 8, 512)
        kb.wload(wv[:, :, 0:256], I["wio_swkv"][e], sk)
        kb.dma("pool", wv[:, :, 256:384], I["wio_ksw"][e].rearrange("(kc p) n -> p kc n", p=128), (), (sk,))
        kb.dma("pool", wv[:, :, 384:512], I["wio_ksws"][e].rearrange("(kc p) n -> p kc n", p=128), (), (sk,))
        for j in range(16):
            pst, pk = kb.ps("mm")
            for kc in range(8):
                kb.mm(pst[:, 0:256], hT[:, kc, j * 128:(j + 1) * 128], wv[:, kc, 0:256], kc == 0, kc == 7, (sk, HK(kc, j // 4)), (pk,))
            kb.cp("act", stg[:, j % 2, :], pst[:, 0:256], (pk,), (("stg", j % 2),))
            kb.dma("sp", O["swkv"][e, j * 128:(j + 1) * 128, :], stg[:, j % 2, :], (("stg", j % 2),), ())
            kb.cp("dve", Vsw[:, j, :], pst[:, 128:256], (pk,), (("Vsw", j),))
        ptr, tk = kb.ps("tr"); pb = bfv(ptr)
        for blk in range(4):
            kb.tr(pb[:, blk * 128:(blk + 1) * 128], ctxk[:, blk, :], ident_bf, ("ctxk", "ident"), (tk,))
        kb.cp("act", KTsw[:, T:NK], pb, (tk,), ("KTsw_ctx",))
        slot, skq = kb.ring_next(); wq = wview(slot, 8, 512)
        kb.wload(wq, I["wio_qsw"][e], skq)
        slot, skqs = kb.ring_next(); wqs = wview(slot, 8, 512)
        kb.wload(wqs, I["wio_qsws"][e], skqs)
        for tb in range(4):
            sl = slice(tb * 512, (tb + 1) * 512)
            tab = tabO[tb % 2]; tabk = ("tabO", tb % 2)
            kb.dma("sp", tab, I["ropeO"][:, :, sl].rearrange("a p t -> p a t"), (), (tabk,))
            pa, pak = kb.ps("mm"); pb_, pbk = kb.ps("mm")
            for kc in range(8):
                kb.mm(pa, wv[:, kc, 256:384], hT[:, kc, sl], kc == 0, kc == 7, (sk, HK(kc, tb)), (pak,))
            for kc in range(8):
                kb.mm(pb_, wv[:, kc, 384:512], hT[:, kc, sl], kc == 0, kc == 7, (sk, HK(kc, tb)), (pbk,))
            rope_combine(KTsw[:, sl], ("KTsw", tb), pa, pak, pb_, pbk, tab, tabk, t1, t2, 128)
            for m in range(4):
                pa, pak = kb.ps("mm"); pb_, pbk = kb.ps("mm")
                for kc in range(8):
                    kb.mm(pa, wq[:, kc, m * 128:(m + 1) * 128], hT[:, kc, sl], kc == 0, kc == 7, (skq, HK(kc, tb)), (pak,))
                for kc in range(8):
                    kb.mm(pb_, wqs[:, kc, m * 128:(m + 1) * 128], hT[:, kc, sl], kc == 0, kc == 7, (skqs, HK(kc, tb)), (pbk,))
                rope_combine(QM[:, 4 + m, sl], ("QMs", m, tb), pa, pak, pb_, pbk, tab, tabk, t1, t2, 128, pre=0.125)
        pg.barrier()
        kb.top = MARK_S
        kb.rings = {"mm": (0, 4), "acc": (4, 4), "tr": (7, 1)}
        PT = kb.bf([14, 512]); swb = kb.bf([6, 3, 128]); esink = kb.bf([2, 512]); rDs = kb.f32([2, 512])
        kb.dma("pool", swb, I["swbias"].rearrange("v k c q -> k v c q"), (), ("swb",))
        so = VOFF["sink"][0] + e * 8
        for g in range(2):
            for hd in range(4):
                kb.act(esink[0:1, g, hd * 128:(hd + 1) * 128], zrow[0:1, 0:128], AF.Exp, ("zrow", "vecs"), ("esink",),
                       bias=vecs[0:1, so + 4 * g + hd:so + 4 * g + hd + 1])
        sunits = [(i, g) for i in range(16) for g in range(2)]
        sst = {}

        def sw_S(k):
            i, g = sunits[k]
            hp = slice(g * 64, g * 64 + 64)
            chunks = [(min(max(i - 1 + c, 0), 15), c) for c in range(3)] + [(16 + c, None) for c in range(4)]
            slots = []
            for t, (kblk, c) in enumerate(chunks):
                pst, pk = kb.ps("mm")
                kb.mm(pst, KTsw[hp, kblk * 128:(kblk + 1) * 128], QM[hp, 4:8, i * 128:(i + 1) * 128], True, True, (), (pk,))
                if c is not None:
                    p3 = pst.rearrange("p (a b) -> p a b", a=4)
                    kb.tt("dve", p3, p3, swb[:, VAR_OF[i], c, :].unsqueeze(1).to_broadcast([128, 4, 128]), ALU.add, (pk, "swb"), (pk,))
                sl_ = (k * 7 + t) % 14
                kb.act(PT[:, sl_, :], pst, AF.Exp, (pk,), (("PT", sl_),))
                slots.append(sl_)
            sst[k] = (chunks, slots)

        def sw_OD(k):
            i, g = sunits[k]
            hp = slice(g * 64, g * 64 + 64)
            chunks, slots = sst[k]
            accO, ok_ = kb.ps("acc"); accD, dk_ = kb.ps("acc")
            for t, (kblk, c) in enumerate(chunks):
                kb.mm(accO, Vsw[:, kblk, :], PT[:, slots[t], :], t == 0, t == 6, (("PT", slots[t]),), (ok_,))
            for t, (kblk, c) in enumerate(chunks):
                kb.mm(accD, ones_bf if c is not None else ctxones, PT[:, slots[t], :], t == 0, False, (("PT", slots[t]), "ones", "ctxones"), (dk_,))
            kb.mm(accD, ones_bf[0:1, :], esink[0:1, g, :], False, True, ("esink", "ones"), (dk_,))
            rd = rDs[:, k % 2, :]; rk = ("rD", k % 2)
            kb.recip(rd[hp, :], accD[hp, :], (dk_,), (rk,))
            kb.tt("dve", QM[hp, 4:8, i * 128:(i + 1) * 128], accO[hp, :].rearrange("p (a b) -> p a b", a=4),
                  rd[hp, :].rearrange("p (a b) -> p a b", a=4), ALU.mult, (ok_, rk), (("QMo", i, g),))

        sw_S(0)
        for k in range(len(sunits)):
            if k + 1 < len(sunits): sw_S(k + 1)
            sw_OD(k)
        pg.barrier()
        kb.rings = dict(kb.RINGS_DEFAULT)
        kb.top = MARK_O
        out_proj("w_out_odd", e, None, src=QM)

    return {"even": even, "odd": odd}


_CACHE = {}


def kernel(**inputs):
    maps = _host_inputs(inputs)
    for core, m in enumerate(maps):
        sample = core >= 4
        fl = m.pop("flags")
        m["vecs"] = _pack_vecs(inputs, m.pop("cvec"), fl)
        for k in ("b_mod", "norm_mix", "norm_ffn", "norm_final", "mla_q_norm", "mla_kv_norm", "pool_scale",
                  "swa_sink", "conv_w", "conv_b"):
            m.pop(k, None)
    if "nc" not in _CACHE:
        _CACHE["nc"] = build_program()[0]
    nc = _CACHE["nc"]
    res = run_bass_kernel_spmd(nc, maps, core_ids=list(range(8)))
    R = res.results
    y_prompt = np.concatenate([np.asarray(R[c]["yT"]).T.reshape(8, 256, D) for c in range(4)], 0)
    y_sample = np.stack([np.asarray(R[4 + b]["yT"]).T for b in range(4)], 0)
    lat = np.concatenate([np.asarray(R[c]["lat"]).reshape(2, 8, 256, 288).transpose(1, 0, 2, 3) for c in range(4)], 0)
    na = np.concatenate([np.asarray(R[c]["nakv"]).reshape(2, 8, 256, 2, 8, 64).transpose(1, 0, 2, 3, 4, 5) for c in range(4)], 0)
    sw = np.concatenate([np.asarray(R[c]["swkv"]).reshape(2, 8, 256, 2, 2, 64).transpose(1, 0, 2, 3, 4, 5) for c in range(4)], 0)
    f = lambda a: np.ascontiguousarray(a, dtype=np.float32)
    return (f(y_prompt), f(y_sample), f(lat), f(na), f(sw))
```

```python
import numpy as np
import concourse.bass as bass
import concourse.mybir as mybir
from concourse.bass_utils import run_bass_kernel_spmd
from contextlib import ExitStack

F32, BF16 = mybir.dt.float32, mybir.dt.bfloat16
AF, ALU = mybir.ActivationFunctionType, mybir.AluOpType

D = 1024; T = 2048; DEPTH = 4; PAST = 512; NK = T + PAST
DFF = 2816; NFC = 22
EPS = 1e-6
MLA_SCALE = 96 ** -0.5
BM = 1024.0
NEGB = -30000.0
STAGES = {"even": True, "odd": True, "ffn": True}
STOP_AT = None


class _Stop(Exception):
    pass


def ck(name):
    if STOP_AT == name:
        raise _Stop()
NLAYERS = DEPTH


class Op:
    __slots__ = ("eng", "fn", "deps", "dma", "sem", "target", "pre", "need", "ticket")

    def __init__(self, eng, fn, dma):
        self.eng = eng; self.fn = fn; self.dma = dma; self.deps = []
        self.sem = None; self.target = 0; self.pre = None; self.need = False; self.ticket = 0


class Prog:
    ENGS = ("pe", "act", "dve", "pool", "sp")
    NDS = {"sp": 40, "pool": 40}

    def __init__(self):
        self.q = {e: [] for e in self.ENGS}
        self.lw = {}; self.rd = {}
        self.dsem_next = {e: 0 for e in self.NDS}
        self.dsem_tgt = {}
        self.dma_since = []

    def add(self, eng, fn, reads=(), writes=(), dma=False):
        op = Op(eng, fn, dma)
        deps = []
        for k in reads:
            w = self.lw.get(k)
            if w is not None: deps.append(w)
            if isinstance(k, tuple) and k[0] == "ps":
                for r in self.rd.get(k, ()):
                    if r.eng != eng: deps.append(r)
        for k in writes:
            w = self.lw.get(k)
            if w is not None: deps.append(w)
            for r in self.rd.get(k, ()): deps.append(r)
        seen = set(); dl = []
        for d in deps:
            if d is op or id(d) in seen: continue
            seen.add(id(d))
            if (not d.dma) and d.eng == eng and eng == "pe": continue
            dl.append(d)
        op.deps = dl
        if dma:
            i = self.dsem_next[eng]; self.dsem_next[eng] = (i + 1) % self.NDS[eng]
            key = (eng, i)
            prev = self.dsem_tgt.get(key, 0)
            op.sem = key; op.pre = prev; op.target = prev + 16
            self.dsem_tgt[key] = op.target
            self.dma_since.append(op)
        for k in reads:
            self.rd.setdefault(k, []).append(op)
        for k in writes:
            self.lw[k] = op; self.rd[k] = []
        self.q[eng].append(op)
        return op

    def barrier(self):
        col = Op("dve", "nop", False)
        for e in self.ENGS:
            if self.q[e]:
                last = None
                for o in reversed(self.q[e]):
                    if o.fn is not None and not o.dma:
                        last = o; break
                if last is not None: col.deps.append(last)
        col.deps.extend(self.dma_since)
        self.dma_since = []
        self.q["dve"].append(col)
        for e in self.ENGS:
            if e == "dve": continue
            w = Op(e, None, False); w.deps = [col]
            self.q[e].append(w)
        self.lw = {}; self.rd = {}

    def emit(self, nc, block, csem, dsems):
        for e in self.ENGS:
            for op in self.q[e]:
                for d in op.deps:
                    if not d.dma: d.need = True
        for e in self.ENGS:
            t = 0
            for op in self.q[e]:
                if op.need and not op.dma:
                    t += 1; op.ticket = t
        engobj = {"pe": nc.tensor, "act": nc.scalar, "dve": nc.vector, "pool": nc.gpsimd, "sp": nc.sync}
        self.nwaits = 0

        def run(e, eo):
            waited = {}

            def wait(key, sem, val):
                if val <= 0 or waited.get(key, 0) >= val: return
                waited[key] = val
                eo.wait_ge(sem, val); self.nwaits += 1

            for op in self.q[e]:
                for d in op.deps:
                    if d.dma: wait(d.sem, dsems[d.sem], d.target)
                    else: wait(d.eng, csem[d.eng], d.ticket)
                if op.dma:
                    wait(op.sem, dsems[op.sem], op.pre)
                    op.fn(eo).then_inc(dsems[op.sem], 16)
                elif op.fn is None:
                    pass
                else:
                    ins = eo.nop() if op.fn == "nop" else op.fn(eo)
                    if op.need: ins.then_inc(csem[e], 1)
            for key, tgt in self.dsem_tgt.items():
                if key[0] == e: wait(key, dsems[key], tgt)

        block.tensor(lambda eo: run("pe", eo))
        block.scalar(lambda eo: run("act", eo))
        block.vector(lambda eo: run("dve", eo))
        block.gpsimd(lambda eo: run("pool", eo))
        block.sync(lambda eo: run("sp", eo))


def _bf(x):
    import ml_dtypes
    return np.asarray(x, np.float32).astype(ml_dtypes.bfloat16).astype(np.float32)


def _rope_tables(R, sample):
    half = R // 2; nf = half // 2
    t = np.arange(T)
    freqs = (10000.0 ** (-np.arange(nf, dtype=np.float32) / nf)).astype(np.float32)
    C = np.ones((R, T), np.float32); S = np.zeros((R, T), np.float32)
    if sample:
        for hi, pos in enumerate((t // 64, t % 64)):
            ang = pos.astype(np.float32)[None, :] * freqs[:, None]
            c, s = np.cos(ang).astype(np.float32), np.sin(ang).astype(np.float32)
            b = hi * half
            C[b:b + nf] = c; C[b + nf:b + 2 * nf] = c
            S[b:b + nf] = -s; S[b + nf:b + 2 * nf] = s
    return C, S


def _rope_perm(R):
    half = R // 2; nf = half // 2
    p = np.arange(R)
    for b in (0, half):
        p[b:b + nf] = np.arange(b + nf, b + 2 * nf)
        p[b + nf:b + 2 * nf] = np.arange(b, b + nf)
    return p


VAR_OF = [0, 1] + [2, 3] * 6 + [4, 5]
VAR_REP = [0, 1, 2, 3, 14, 15]


def _struct_consts(sample):
    c = {}
    c["ident"] = np.eye(128, dtype=np.float32)
    Ce, Se = _rope_tables(32, sample)
    c["ropeE"] = np.stack([np.tile(Ce, (4, 1)), np.tile(Se, (4, 1))], 0)
    Co, So = _rope_tables(64, sample)
    c["ropeO"] = np.stack([np.tile(Co, (2, 1)), np.tile(So, (2, 1))], 0)
    km = np.zeros((9, NK), np.float32); qm = np.zeros((9, T), np.float32)
    km[8, :] = 1.0; qm[8, :] = -BM
    if sample:
        km[0, :] = 1.0; qm[0, :] = BM
    else:
        for s in range(8):
            km[s, s * 256:(s + 1) * 256] = 1.0
            qm[s, s * 256:(s + 1) * 256] = BM
    c["kmaskE"] = km; c["qmaskE"] = qm
    fl = 1.0 if sample else 0.0
    c["flags"] = np.tile(np.array([[fl, 1.0 - fl]], np.float32), (128, 1))
    n = T if sample else 256
    Pm = np.zeros((16, 128, 4, 3, 128), np.float32)
    tt = np.arange(T); tl = tt % n; base = tt - tl
    for g, w in enumerate((2, 4, 8, 16)):
        lo = np.clip(tl - w // 2, 0, n); hi = np.clip(tl + w // 2, 0, n)
        cnt = (hi - lo).astype(np.float32)
        for t in range(T):
            j = t // 128
            for s in range(base[t] + lo[t], base[t] + hi[t]):
                sb = s // 128 - (j - 1)
                Pm[j, s % 128, g, sb, t % 128] += 1.0 / cnt[t]
            Pm[j, t % 128, g, 1, t % 128] -= 1.0
    c["Pm"] = Pm
    swb = np.full((6, 128, 3, 128), NEGB, np.float32)
    for v, i in enumerate(VAR_REP):
        tq = i * 128 + np.arange(128)
        for cc in range(3):
            kb = i - 1 + cc
            if kb < 0 or kb > 15: continue
            ks = kb * 128 + np.arange(128)
            if sample:
                ok = np.abs(tq[None, :] - ks[:, None]) <= 128
            else:
                ok = (tq[None, :] // 256) == (ks[:, None] // 256)
            swb[v, :, cc, :] = np.where(ok, 0.0, NEGB)
    c["swbias"] = swb
    return c


def _na_bias(sample, rpb):
    out = np.full((2, 6, 128, 8, 5, 128), NEGB, np.float32)
    for v, i in enumerate(VAR_REP):
        start = min(max(i - 2, 0), 11)
        tq = i * 128 + np.arange(128)
        r, cq = tq // 64, tq % 64
        for cc in range(5):
            ks = (start + cc) * 128 + np.arange(128)
            if sample:
                rp, cp = ks // 64, ks % 64
                r0 = np.clip(r - 4, 0, 24); c0 = np.clip(cq - 8, 0, 48)
                ok = ((rp[:, None] >= r0[None, :]) & (rp[:, None] < r0[None, :] + 8) &
                      (cp[:, None] >= c0[None, :]) & (cp[:, None] < c0[None, :] + 16))
                dr = np.clip(rp[:, None] - r[None, :] + 7, 0, 14)
                dc = np.clip(cp[:, None] - cq[None, :] + 15, 0, 30)
                for l in range(2):
                    g = rpb[l][:, dr, dc]
                    out[l, v, :, :, cc, :] = np.where(ok[:, None, :], g.transpose(1, 0, 2), NEGB)
            else:
                ok = (tq[None, :] // 256) == (ks[:, None] // 256)
                out[:, v, :, :, cc, :] = np.where(ok, 0.0, NEGB)[None, :, None, :]
    return out


def _host_inputs(inp):
    f = lambda a: np.ascontiguousarray(np.asarray(a, np.float32))
    w = {}
    for k in ("w_mod", "b_mod", "norm_mix", "norm_ffn", "norm_final", "mla_q_norm", "mla_kv_norm",
              "pool_scale", "w_out_even", "w_out_odd", "swa_sink", "w_up", "conv_w", "conv_b", "w_down"):
        w[k] = f(inp[k])
    rowperm = np.concatenate([np.arange(512)] + [np.concatenate([512 + jj * 64 + np.arange(64), 512 + (4 + jj) * 64 + np.arange(64)])
                                                  for jj in range(4)])
    w["w_out_odd"] = f(w["w_out_odd"][:, rowperm, :])
    wie = f(inp["w_in_even"])
    w["wie_qa"] = f(wie[:, :, 0:384]); w["wie_tm"] = f(wie[:, :, 384:928])
    w["wie_kr"] = f(wie[:, :, 640:672]); w["wie_krs"] = f(wie[:, :, 640 + _rope_perm(32)])
    wuq = f(inp["w_uq"]).reshape(2, 384, 12, 96)
    w["wuq_n"] = f(wuq[..., 0:64].reshape(2, 384, 768))
    w["wuq_r"] = f(wuq[..., 64:96].reshape(2, 384, 384))
    w["wuq_rs"] = f(wuq[..., 64 + _rope_perm(32)].reshape(2, 384, 384))
    wukv = f(inp["w_ukv"]).reshape(2, 256, 12, 128)
    w["wukv_k"] = f(wukv[..., 0:64].reshape(2, 256, 768)); w["wukv_v"] = f(wukv[..., 64:128].reshape(2, 256, 768))
    wp = f(inp["w_pool"]); bd = np.zeros((2, 2, 128, 128), np.float32)
    for e in range(2):
        for pr in range(2):
            bd[e, pr, 0:64, 0:64] = wp[e, 2 * pr]; bd[e, pr, 64:128, 64:128] = wp[e, 2 * pr + 1]
    w["wpool_bd"] = bd
    wio = f(inp["w_in_odd"])
    w["wio_qna"] = f(wio[:, :, 0:512]); w["wio_kna"] = f(wio[:, :, 512:1024])
    w["wio_nakv"] = f(wio[:, :, 512:1536]); w["wio_swkv"] = f(wio[:, :, 2048:2304])
    qsw = wio[:, :, 1536:2048].reshape(2, 1024, 8, 64)
    order = [0, 4, 1, 5, 2, 6, 3, 7]
    w["wio_qsw"] = f(qsw[:, :, order, :].reshape(2, 1024, 512))
    w["wio_qsws"] = f(qsw[:, :, order, :][..., _rope_perm(64)].reshape(2, 1024, 512))
    ksw = wio[:, :, 2048:2176].reshape(2, 1024, 2, 64)
    w["wio_ksw"] = f(ksw.reshape(2, 1024, 128)); w["wio_ksws"] = f(ksw[..., _rope_perm(64)].reshape(2, 1024, 128))
    sc = {True: _struct_consts(True), False: _struct_consts(False)}
    rpb = f(inp["na_rpb"])
    nab = {True: _na_bias(True, rpb), False: _na_bias(False, rpb)}
    xp = f(inp["x_prompt"]); xs = f(inp["x_sample"])
    maps = []
    for core in range(8):
        sample = core >= 4
        m = dict(w)
        m.update(sc[sample]); m["nabias"] = nab[sample]
        if sample:
            b = core - 4
            m["xT"] = f(xs[b].T)
            m["cvec"] = f(inp["c"])[b]
            m["c_mla"] = f(inp["cache_mla_latent"])[b]
            m["c_na"] = f(inp["cache_na_kv"])[b].reshape(2, 512, 2, 512)
            m["c_sw"] = f(inp["cache_swa_kv"])[b].reshape(2, 512, 2, 128)
        else:
            m["xT"] = f(xp[8 * core:8 * core + 8].reshape(T, D).T)
            m["cvec"] = f(inp["c_ctx"])
            m["c_mla"] = np.zeros((2, 512, 288), np.float32)
            m["c_na"] = np.zeros((2, 512, 2, 512), np.float32)
            m["c_sw"] = np.zeros((2, 512, 2, 128), np.float32)
        maps.append(m)
    return maps


IN_SHAPES = {
    "xT": [D, T], "cvec": [D], "c_mla": [2, 512, 288], "c_na": [2, 512, 2, 512], "c_sw": [2, 512, 2, 128],
    "w_mod": [4, D, 6 * D], "b_mod": [4, 6 * D], "norm_mix": [4, D], "norm_ffn": [4, D], "norm_final": [D],
    "mla_q_norm": [2, 384], "mla_kv_norm": [2, 256], "pool_scale": [2, 256],
    "w_out_even": [2, D, D], "w_out_odd": [2, D, D], "swa_sink": [2, 8],
    "w_up": [4, D, 2 * DFF], "conv_w": [4, 3, 2 * DFF], "conv_b": [4, 2 * DFF], "w_down": [4, DFF, D],
    "wie_qa": [2, D, 384], "wie_tm": [2, D, 544], "wie_kr": [2, D, 32], "wie_krs": [2, D, 32],
    "wuq_n": [2, 384, 768], "wuq_r": [2, 384, 384], "wuq_rs": [2, 384, 384],
    "wukv_k": [2, 256, 768], "wukv_v": [2, 256, 768], "wpool_bd": [2, 2, 128, 128],
    "wio_qna": [2, D, 512], "wio_kna": [2, D, 512], "wio_nakv": [2, D, 1024], "wio_swkv": [2, D, 256],
    "wio_qsw": [2, D, 512], "wio_qsws": [2, D, 512], "wio_ksw": [2, D, 128], "wio_ksws": [2, D, 128],
    "ident": [128, 128], "ropeE": [2, 128, T], "ropeO": [2, 128, T], "kmaskE": [9, NK], "qmaskE": [9, T],
    "flags": [128, 2], "Pm": [16, 128, 4, 3, 128], "swbias": [6, 128, 3, 128], "nabias": [2, 6, 128, 8, 5, 128],
}
OUT_SHAPES = {"yT": [D, T], "lat": [2, T, 288], "nakv": [2, T, 1024], "swkv": [2, T, 256]}


class KB:
    def __init__(self, nc, big, psums, ins, outs):
        self.nc = nc; self.big = big; self.P = psums; self.I = ins; self.O = outs
        self.pg = Prog()
        self.top = 0
        self.rr = {"mm": 0, "acc": 0, "tr": 0}
        self.RINGS_DEFAULT = {"mm": (0, 4), "acc": (4, 3), "tr": (7, 1)}
        self.rings = dict(self.RINGS_DEFAULT)
        self.uid = 0

    def alloc(self, nbytes):
        off = self.top; self.top += (nbytes + 7) // 8 * 2
        assert self.top * 4 <= 212800, f"SBUF overflow {self.top * 4}"
        return off

    def _shape(self, v, shape):
        if len(shape) == 1: return v
        if len(shape) == 2: return v.rearrange("p (a b) -> p a b", a=shape[0])
        if len(shape) == 3: return v.rearrange("p (a b c) -> p a b c", a=shape[0], b=shape[1])
        return v.rearrange("p (a b c d) -> p a b c d", a=shape[0], b=shape[1], c=shape[2])

    def f32(self, shape):
        n = int(np.prod(shape)); off = self.alloc(n * 4)
        return self._shape(self.big[:, off:off + n], shape)

    def bf(self, shape):
        n = int(np.prod(shape)); off = self.alloc(n * 2)
        return self._shape(self.big[:, off:off + (n + 1) // 2].bitcast(BF16)[:, 0:n], shape)

    def key(self, name):
        self.uid += 1
        return (name, self.uid)

    def ps(self, ring):
        lo, n = self.rings[ring]
        i = lo + self.rr[ring] % n; self.rr[ring] += 1
        return self.P[i], ("ps", i)

    def mm(self, out, lhsT, rhs, start, stop, reads, writes):
        return self.pg.add("pe", lambda e: e.matmul(out, lhsT, rhs, start=start, stop=stop), reads, writes)

    def tr(self, out, in_, ident, reads, writes):
        return self.pg.add("pe", lambda e: e.transpose(out, in_, ident), reads, writes)

    def act(self, out, in_, func, reads, writes, scale=1.0, bias=0.0, accum=None):
        if accum is None:
            return self.pg.add("act", lambda e: e.activation(out=out, in_=in_, func=func, bias=bias, scale=scale), reads, writes)
        return self.pg.add("act", lambda e: e.activation(out=out, in_=in_, func=func, bias=bias, scale=scale, accum_out=accum), reads, writes)

    def ts(self, eng, out, in0, s1, s2, op0, op1, reads, writes):
        if s2 is None:
            return self.pg.add(eng, lambda e: e.tensor_scalar(out=out, in0=in0, scalar1=s1, scalar2=None, op0=op0), reads, writes)
        return self.pg.add(eng, lambda e: e.tensor_scalar(out=out, in0=in0, scalar1=s1, scalar2=s2, op0=op0, op1=op1), reads, writes)

    def stt(self, eng, out, in0, scalar, in1, op0, op1, reads, writes):
        return self.pg.add(eng, lambda e: e.scalar_tensor_tensor(out=out, in0=in0, scalar=scalar, in1=in1, op0=op0, op1=op1), reads, writes)

    def tt(self, eng, out, in0, in1, op, reads, writes):
        return self.pg.add(eng, lambda e: e.tensor_tensor(out=out, in0=in0, in1=in1, op=op), reads, writes)

    def cp(self, eng, out, in_, reads, writes):
        if eng == "act":
            return self.pg.add("act", lambda e: e.copy(out=out, in_=in_), reads, writes)
        return self.pg.add(eng, lambda e: e.tensor_copy(out=out, in_=in_), reads, writes)

    def recip(self, out, in_, reads, writes):
        return self.pg.add("dve", lambda e: e.reciprocal(out=out, in_=in_), reads, writes)

    def memset(self, eng, ap, val, writes):
        return self.pg.add(eng, lambda e: e.memset(ap, val), (), writes)

    def dma(self, q, out, in_, reads, writes):
        return self.pg.add(q, lambda e: e.dma_start(out=out, in_=in_), reads, writes, dma=True)

    def wload(self, dst, src_ap, key):
        return self.dma("pool", dst, src_ap.rearrange("(kc p) n -> p kc n", p=128), (), (key,))


VOFF = {}
def _voff():
    o = 0
    for name, n in (("bmod", 192), ("nmix", 32), ("nffn", 32), ("nfin", 8), ("conv", 704), ("gq", 6),
                    ("pscale", 4), ("cvec", 8), ("gkv", 512), ("sink", 16), ("flags", 2)):
        VOFF[name] = (o, n); o += n
    return o
NV = _voff()


def _pack_vecs(inp, cvec, flags):
    v = np.zeros((128, NV), np.float32)
    def put(name, arr):
        o, n = VOFF[name]; v[:, o:o + n] = np.asarray(arr, np.float32).reshape(128, n)
    f = lambda a: np.asarray(a, np.float32)
    put("bmod", f(inp["b_mod"]).reshape(4, 48, 128).transpose(2, 0, 1))
    put("nmix", f(inp["norm_mix"]).reshape(4, 8, 128).transpose(2, 0, 1))
    put("nffn", f(inp["norm_ffn"]).reshape(4, 8, 128).transpose(2, 0, 1))
    put("nfin", f(inp["norm_final"]).reshape(8, 128).transpose(1, 0))
    cw = np.concatenate([f(inp["conv_w"]), f(inp["conv_b"])[:, None, :]], 1)
    put("conv", cw.reshape(4, 4, 44, 128).transpose(3, 0, 1, 2))
    put("gq", f(inp["mla_q_norm"]).reshape(2, 3, 128).transpose(2, 0, 1))
    put("pscale", f(inp["pool_scale"]).reshape(2, 2, 128).transpose(2, 0, 1))
    put("cvec", f(cvec).reshape(8, 128).transpose(1, 0))
    put("gkv", np.broadcast_to(f(inp["mla_kv_norm"]).reshape(1, 512), (128, 512)))
    put("sink", np.broadcast_to(f(inp["swa_sink"]).reshape(1, 16), (128, 16)))
    put("flags", flags)
    return v


def build_program():
    nc = bass.Bass("TRN2", target_bir_lowering=False)
    shapes = dict(IN_SHAPES); shapes["vecs"] = [128, NV]
    for k in ("cvec", "b_mod", "norm_mix", "norm_ffn", "norm_final", "mla_q_norm", "mla_kv_norm", "pool_scale",
              "swa_sink", "conv_w", "conv_b", "flags"):
        shapes.pop(k)
    I = {k: nc.dram_tensor(k, s, F32, kind="ExternalInput").ap() for k, s in shapes.items()}
    O = {k: nc.dram_tensor(k, s, F32, kind="ExternalOutput").ap() for k, s in OUT_SHAPES.items()}
    es = ExitStack()
    with es:
        big = es.enter_context(nc.sbuf_tensor("big", [128, 53200], F32))
        P = [es.enter_context(nc.psum_tensor(f"psb{i}", [128, 512], F32)) for i in range(8)]
        csem = {e: es.enter_context(nc.semaphore(f"c_{e}")) for e in Prog.ENGS}
        dsems = {(q, i): es.enter_context(nc.semaphore(f"d_{q}{i}")) for q in Prog.NDS for i in range(Prog.NDS[q])}
        block = es.enter_context(nc.Block())
        kb = KB(nc, big, [p[:, :] for p in P], I, O)
        _emit_all(kb)
        kb.pg.emit(nc, block, csem, dsems)
    return nc, kb


def _emit_all(kb):
    I, O, pg = kb.I, kb.O, kb.pg
    xres = kb.f32([8, T]); hT = kb.bf([8, T])
    vecs = kb.f32([NV])
    ones_bf = kb.bf([128]); ident_bf = kb.bf([128])
    scv = kb.bf([8]); modT_all = kb.f32([4, 48]); prm_all = kb.f32([4, 6, 8]); convx_all = kb.f32([4, 4, 44])
    prm = prm_all[:, 0]; convx = convx_all[:, 0]
    MARK = kb.top
    kb.xres, kb.hT, kb.vecs, kb.ones_bf, kb.ident_bf, kb.prm = xres, hT, vecs, ones_bf, ident_bf, prm
    kb.MARK = MARK

    def vv(name, l=None, per=None):
        o, n = VOFF[name]
        if l is None: return vecs[:, o:o + n]
        return vecs[:, o + l * per:o + (l + 1) * per]
    kb.vv = vv

    XK = lambda c, tb: ("x", c, tb)
    HK = lambda c, tb: ("h", c, tb)
    kb.XK, kb.HK = XK, HK
    kb.dma("sp", vecs, I["vecs"][:, :], (), ("vecs",))
    for c in range(8):
        kb.dma("sp", xres[:, c, :], I["xT"][c * 128:(c + 1) * 128, :], (), [XK(c, tb) for tb in range(4)])
    kb.dma("pool", ident_bf, I["ident"][:, :], (), ("ident",))
    kb.memset("dve", ones_bf, 1.0, ("ones",))
    kb.act(scv, vv("cvec"), AF.Silu, ("vecs",), ("scv",))

    def ring_setup(nslots, nel):
        kb.ring = [kb.bf([nel]) for _ in range(nslots)]
        kb.ring_i = 0; kb.ring_nel = nel

    def ring_next():
        i = kb.ring_i % len(kb.ring); kb.ring_i += 1
        return kb.ring[i], ("wr", i)
    kb.ring_setup, kb.ring_next = ring_setup, ring_next

    def wview(slot, kc, n):
        return slot[:, 0:kc * n].rearrange("p (a b) -> p a b", a=kc)
    kb.wview = wview

    def norm_mod(Acol, Bcol, sq, rs, tmp, dst_fn, final=False):
        for tb in range(4):
            sl = slice(tb * 512, (tb + 1) * 512)
            pst, pk = kb.ps("mm")
            for c in range(8):
                s = sq[:, c % 2, :]
                kb.act(s, xres[:, c, sl], AF.Square, (XK(c, tb),), (("sq", c % 2),))
                kb.mm(pst, ones_bf, s, c == 0, c == 7, (("sq", c % 2), "ones"), (pk,))
            kb.act(rs, pst, AF.Sqrt, (pk,), ("rs",), scale=1.0 / D, bias=EPS)
            kb.recip(rs, rs, ("rs",), ("rs",))
            for c in range(8):
                tm = tmp[:, c % 2, :]
                kb.stt("dve", tm, xres[:, c, sl], Acol(c), rs, ALU.mult, ALU.mult,
                       (XK(c, tb), "rs", "prm", "vecs"), (("tmp", c % 2),))
                dst_fn(c, tb, tm, ("tmp", c % 2), Bcol(c) if Bcol else 0.0)
    kb.norm_mod = norm_mod

    def to_hT(c, tb, tm, tk, bias):
        kb.act(hT[:, c, tb * 512:(tb + 1) * 512], tm, AF.Identity, (tk, "prm"), (HK(c, tb),), bias=bias)

    def adaln_gen(layers, aring, pst, pk, pw=512):
        cnt = 0
        for l in layers:
            modT = modT_all[:, l]; prm = prm_all[:, l]; convx = convx_all[:, l]
            for piece in range(6144 // pw):
                slot, sk = aring[cnt % len(aring)], ("awr", cnt % len(aring)); cnt += 1
                wv = wview(slot, 8, pw)
                kb.wload(wv, I["w_mod"][l][:, piece * pw:(piece + 1) * pw], sk)
                for cc in range(pw // 128):
                    j = piece * (pw // 128) + cc
                    for kc in range(8):
                        kb.mm(pst[:, j:j + 1], wv[:, kc, cc * 128:(cc + 1) * 128], scv[:, kc:kc + 1], kc == 0, kc == 7,
                              (sk, "scv"), (pk,))
                yield
            mk = ("modT", l); pk_ = ("prm", l)
            kb.tt("dve", modT, pst[:, 0:48], vv("bmod", l, 48), ALU.add, (pk, "vecs"), (mk,))
            for (row, sc_i, g) in ((0, 1, "nmix"), (3, 4, "nffn")):
                kb.stt("dve", prm[:, row, :], modT[:, sc_i * 8:(sc_i + 1) * 8], 1.0, vv(g, l, 8), ALU.add, ALU.mult,
                       (mk, "vecs"), (pk_,))
            for (row, m_i) in ((1, 0), (2, 2), (4, 3), (5, 5)):
                kb.cp("dve", prm[:, row, :], modT[:, m_i * 8:(m_i + 1) * 8], (mk,), (pk_,))
            o, _ = VOFF["conv"]; cb = o + l * 176
            fo, _ = VOFF["flags"]
            w0 = vecs[:, cb:cb + 44]; w2 = vecs[:, cb + 88:cb + 132]
            kb.ts("dve", convx[:, 0, :], w0, vecs[:, fo:fo + 1], None, ALU.mult, None, ("vecs",), (("convx", l),))
            kb.ts("dve", convx[:, 1, :], w2, vecs[:, fo:fo + 1], None, ALU.mult, None, ("vecs",), (("convx", l),))
            kb.ts("dve", convx[:, 2, :], w0, vecs[:, fo + 1:fo + 2], -1.0, ALU.mult, ALU.mult, ("vecs",), (("convx", l),))
            kb.ts("dve", convx[:, 3, :], w2, vecs[:, fo + 1:fo + 2], -1.0, ALU.mult, ALU.mult, ("vecs",), (("convx", l),))
            yield

    def adaln_first():
        kb.top = MARK
        aring = [kb.bf([8 * 512]) for _ in range(3)]
        pst, pk = kb.ps("acc")
        for _ in adaln_gen([0], aring, pst, pk):
            pass
        pg.barrier()

    def norm_phase(row_a, row_b):
        kb.top = MARK
        sq = kb.bf([2, 512]); rs = kb.f32([512]); tmp = kb.f32([2, 512])
        norm_mod(lambda c: kb.prm[:, row_a, c:c + 1], lambda c: kb.prm[:, row_b, c:c + 1], sq, rs, tmp, to_hT)
        pg.barrier()

    def ffn(l):
        kb.top = MARK
        o, _ = VOFF["conv"]; cb = o + l * 176
        cw = lambda k, c: vecs[:, cb + k * 44 + c:cb + k * 44 + c + 1]
        cx = lambda k, c: kb.convx[:, k, c:c + 1]
        ring_setup(2 if (l == 0 and NLAYERS > 1) else 3, 22 * 256)
        kb.rings = {"mm": (0, 7), "acc": (0, 7), "tr": (7, 1)}
        actb = kb.bf([NFC, 1024]); ta_r = kb.f32([4, 512]); tg_r = kb.f32([4, 512]); es = kb.f32([4, 2]); eh = kb.f32([44])
        agen = None
        if l == 0 and NLAYERS > 1:
            kb.rings = {"mm": (0, 6), "acc": (0, 6), "tr": (7, 1)}
            aring = [kb.bf([8 * 384]) for _ in range(2)]
            agen = adaln_gen(list(range(1, NLAYERS)), aring, kb.P[6], ("ps", 6), pw=384)
        for sb in range(2):
            for c in range(NFC):
                if c % 2 == 0:
                    slot, sk = ring_next(); wv = wview(slot, 8, 512)
                    kb.dma("pool", wv[:, :, 0:256], I["w_up"][l][:, c * 128:c * 128 + 256].rearrange("(kc p) n -> p kc n", p=128), (), (sk,))
                    kb.dma("pool", wv[:, :, 256:512], I["w_up"][l][:, DFF + c * 128:DFF + c * 128 + 256].rearrange("(kc p) n -> p kc n", p=128), (), (sk,))
                hp, hk = kb.ps("tr")
                tiles = {}; tts = {}
                for tb2 in range(2):
                    tb = sb * 2 + tb2; t0 = tb * 512
                    ri = (c % 2) * 2 + tb2
                    for gi in range(2):
                        col0 = gi * 256 + (c % 2) * 128
                        pst, pk = kb.ps("mm")
                        for kc in range(8):
                            kb.mm(pst, wv[:, kc, col0:col0 + 128], hT[:, kc, t0:t0 + 512], kc == 0, kc == 7,
                                  (sk, HK(kc, tb)), (pk,))
                        hcol = None
                        if sb == 0 and tb2 == 1: hcol = 1024
                        if hcol is not None:
                            for kc in range(8):
                                kb.mm(hp[:, gi:gi + 1], wv[:, kc, col0:col0 + 128], hT[:, kc, hcol:hcol + 1], kc == 0, kc == 7,
                                      (sk, HK(kc, hcol // 512)), (hk,))
                        tiles[(tb2, gi)] = (pst, pk)
                        tts[(tb2, gi)] = ((ta_r if gi == 0 else tg_r)[:, ri, :], ("tconv", gi, ri))
                    ccs = [gi * NFC + c for gi in range(2)]
                    for gi in range(2):
                        (pst, pk), (tt_, tk) = tiles[(tb2, gi)], tts[(tb2, gi)]
                        kb.act(tt_, pst, AF.Identity, (pk, "vecs"), (tk,), scale=cw(1, ccs[gi]), bias=cw(3, ccs[gi]))
                        if tb2 == 0:
                            kb.cp("act", es[:, (c % 2) * 2 + gi, 0:1], pst[:, 511:512], (pk,), (("es", (c % 2) * 2 + gi),))
                        if sb == 0 and tb2 == 1:
                            kb.cp("act", eh[:, ccs[gi]:ccs[gi] + 1], pst[:, 511:512], (pk,), (("eh", ccs[gi]),))
                    for gi in range(2):
                        (pst, pk), (tt_, tk) = tiles[(tb2, gi)], tts[(tb2, gi)]
                        kb.stt("dve", tt_[:, 1:512], pst[:, 0:511], cw(0, ccs[gi]), tt_[:, 1:512], ALU.mult, ALU.add, (pk, tk, "vecs"), (tk,))
                    for gi in range(2):
                        (pst, pk), (tt_, tk) = tiles[(tb2, gi)], tts[(tb2, gi)]
                        kb.stt("dve", tt_[:, 0:511], pst[:, 1:512], cw(2, ccs[gi]), tt_[:, 0:511], ALU.mult, ALU.add, (pk, tk, "vecs"), (tk,))
                    for gi in range(2):
                        (pst, pk), (tt_, tk) = tiles[(tb2, gi)], tts[(tb2, gi)]
                        kb.stt("dve", tt_[:, 256:257], pst[:, 255:256], cx(2, ccs[gi]), tt_[:, 256:257], ALU.mult, ALU.add, (pk, tk), (tk,))
                    for gi in range(2):
                        (pst, pk), (tt_, tk) = tiles[(tb2, gi)], tts[(tb2, gi)]
                        kb.stt("dve", tt_[:, 255:256], pst[:, 256:257], cx(3, ccs[gi]), tt_[:, 255:256], ALU.mult, ALU.add, (pk, tk), (tk,))
                    for gi in range(2):
                        (pst, pk), (tt_, tk) = tiles[(tb2, gi)], tts[(tb2, gi)]
                        if tb2 == 0 and sb == 1:
                            kb.act(tt_[:, 0:1], eh[:, ccs[gi]:ccs[gi] + 1], AF.Identity, (("eh", ccs[gi]), tk), (tk,), scale=cx(0, ccs[gi]), bias=tt_[:, 0:1])
                        if tb2 == 1:
                            ek = ("es", (c % 2) * 2 + gi)
                            kb.act(tt_[:, 0:1], es[:, (c % 2) * 2 + gi, 0:1], AF.Identity, (ek, tk), (tk,), scale=cx(0, ccs[gi]), bias=tt_[:, 0:1])
                        if tb2 == 1 and sb == 0:
                            kb.act(tt_[:, 511:512], hp[:, gi:gi + 1], AF.Identity, (hk, tk), (tk,), scale=cx(1, ccs[gi]), bias=tt_[:, 511:512])
                for gi in range(2):
                    (p1, k1), (t0_, tk0) = tiles[(1, gi)], tts[(0, gi)]
                    kb.act(t0_[:, 511:512], p1[:, 0:1], AF.Identity, (k1, tk0), (tk0,), scale=cx(1, gi * NFC + c), bias=t0_[:, 511:512])
                for tb2 in range(2):
                    (ta, tak), (tg, tgk) = tts[(tb2, 0)], tts[(tb2, 1)]
                    kb.act(tg, tg, AF.Silu, (tgk,), (tgk,))
                    kb.tt("dve", actb[:, c, tb2 * 512:(tb2 + 1) * 512], ta, tg, ALU.mult, (tak, tgk), (("actb", c, tb2),))
                if agen is not None and c % 2 == 1:
                    next(agen, None)
            for dp in range(4):
                slot, sk = ring_next(); wd = wview(slot, NFC, 256)
                kb.wload(wd, I["w_down"][l][:, dp * 256:(dp + 1) * 256], sk)
                for d2 in range(2):
                    dch = dp * 2 + d2
                    for tb2 in range(2):
                        tb = sb * 2 + tb2
                        pst, pk = kb.ps("acc")
                        for fc in range(NFC):
                            kb.mm(pst, wd[:, fc, d2 * 128:(d2 + 1) * 128], actb[:, fc, tb2 * 512:(tb2 + 1) * 512], fc == 0, fc == NFC - 1,
                                  (sk, ("actb", fc, tb2)), (pk,))
                        xs = xres[:, dch, tb * 512:(tb + 1) * 512]
                        kb.stt("dve", xs, pst, kb.prm[:, 5, dch:dch + 1], xs, ALU.mult, ALU.add, (pk, XK(dch, tb)), (XK(dch, tb),))
                if agen is not None:
                    next(agen, None)
        if agen is not None:
            for _ in agen:
                pass
        pg.barrier()
        kb.rings = dict(kb.RINGS_DEFAULT)

    def final_out():
        kb.top = MARK
        sq = kb.bf([2, 512]); rs = kb.f32([512]); tmp = kb.f32([2, 512]); yst = kb.f32([4, 512])
        cnt = [0]
        def to_out(c, tb, tm, tk, bias):
            i = cnt[0] % 4; cnt[0] += 1
            kb.cp("act", yst[:, i, :], tm, (tk,), (("yst", i),))
            kb.dma("sp", O["yT"][c * 128:(c + 1) * 128, tb * 512:(tb + 1) * 512], yst[:, i, :], (("yst", i),), ())
        o, _ = VOFF["nfin"]
        norm_mod(lambda c: vecs[:, o + c:o + c + 1], None, sq, rs, tmp, to_out)

    from_mixers = _mixers(kb)
    adaln_first()
    try:
        for l in range(NLAYERS):
            kb.prm = prm_all[:, l]; kb.convx = convx_all[:, l]
            norm_phase(0, 1)
            if l % 2 == 0 and STAGES["even"]:
                from_mixers["even"](l, l // 2)
            if l % 2 == 1 and STAGES["odd"]:
                from_mixers["odd"](l, l // 2)
            if STAGES["ffn"]:
                norm_phase(3, 4)
                ffn(l)
    except _Stop:
        pg.barrier()
    final_out()


def _mixers(kb):
    I, O, pg = kb.I, kb.O, kb.pg
    xres, hT, vecs, ones_bf, ident_bf, prm = kb.xres, kb.hT, kb.vecs, kb.ones_bf, kb.ident_bf, kb.prm
    XK, HK, vv, MARK = kb.XK, kb.HK, kb.vv, kb.MARK
    wview = kb.wview
    fo = VOFF["flags"][0]

    def bfv(pst):
        return pst[:, 0:256].bitcast(BF16)

    def out_proj(wname, e, extra, src=None):
        src = hT if src is None else src
        kb.ring_setup(2, 8 * 512)
        for piece in range(2):
            slot, sk = kb.ring_next(); wo = wview(slot, 8, 512)
            kb.wload(wo, I[wname][e][:, piece * 512:(piece + 1) * 512], sk)
            for d4 in range(4):
                dch = piece * 4 + d4
                for tb in range(4):
                    sl = slice(tb * 512, (tb + 1) * 512)
                    pst, pk = kb.ps("acc")
                    for kc in range(8):
                        if extra is not None and kc >= 6:
                            rhs, rk = extra[:, kc - 6, sl], ("yp", kc - 6, tb)
                        else:
                            rhs, rk = src[:, kc, sl], HK(kc, tb)
                        kb.mm(pst, wo[:, kc, d4 * 128:(d4 + 1) * 128], rhs, kc == 0, kc == 7, (sk, rk), (pk,))
                    xs = xres[:, dch, sl]
                    kb.stt("dve", xs, pst, kb.prm[:, 2, dch:dch + 1], xs, ALU.mult, ALU.add, (pk, XK(dch, tb)), (XK(dch, tb),))
        pg.barrier()

    def rope_combine(dst, dk, pa, pak, pb, pbk, tab, tabk, t1, t2, np_, pre=1.0):
        kb.stt("dve", t1[0:np_, :], pa[0:np_, :], pre, tab[0:np_, 0, :], ALU.mult, ALU.mult, (pak, tabk), ("rt1",))
        kb.stt("dve", t2[0:np_, :], pb[0:np_, :], pre, tab[0:np_, 1, :], ALU.mult, ALU.mult, (pbk, tabk), ("rt2",))
        kb.tt("dve", dst, t1[0:np_, :], t2[0:np_, :], ALU.add, ("rt1", "rt2"), (dk,))

    def even(l, e):
        kb.top = MARK
        qnT = kb.bf([3, T]); qrT = kb.bf([3, T]); cT = kb.bf([2, NK]); krT = kb.bf([NK]); ypT = kb.bf([2, T])
        MARK_E = kb.top
        wtm = kb.bf([8, 544]); wpl = kb.bf([2, 128]); sk = "wtm"; skp = "wpl"
        xp_tok = kb.bf([16, 384]); pooledT = kb.bf([2, T]); lat_st = kb.f32([2, 288]); ctok = kb.bf([2, 256])
        sqt = kb.f32([2, 256]); ssq = kb.f32([2]); ctxl = kb.bf([4, 288]); Pms = [kb.bf([4, 3, 128]) for _ in range(2)]
        kb.memset("dve", xp_tok, 0.0, [("xp", j) for j in range(16)])
        kb.wload(wtm, I["wie_tm"][e], sk)
        kb.dma("pool", ctxl, I["c_mla"][e].rearrange("(b p) n -> p b n", p=128), (), ("ctxl",))
        for j in range(16):
            i = j % 2; tb = j // 4
            p1, k1 = kb.ps("mm"); p2, k2 = kb.ps("mm")
            for kc in range(8):
                lh = hT[:, kc, j * 128:(j + 1) * 128]
                kb.mm(p1[:, 0:288], lh, wtm[:, kc, 0:288], kc == 0, kc == 7, (sk, HK(kc, tb)), (k1,))
                kb.mm(p2[:, 0:256], lh, wtm[:, kc, 288:544], kc == 0, kc == 7, (sk, HK(kc, tb)), (k2,))
            kb.act(sqt[:, i, :], p1[:, 0:256], AF.Square, (k1,), (("sqt", i),))
            pg.add("dve", lambda en, o=ssq[:, i:i + 1], a=sqt[:, i, :]: en.reduce_sum(out=o, in_=a, axis=mybir.AxisListType.X),
                   (("sqt", i),), (("ssq", i),))
            kb.act(ssq[:, i:i + 1], ssq[:, i:i + 1], AF.Sqrt, (("ssq", i),), (("ssq", i),), scale=1.0 / 256, bias=EPS)
            kb.recip(ssq[:, i:i + 1], ssq[:, i:i + 1], (("ssq", i),), (("ssq", i),))
            go = VOFF["gkv"][0] + e * 256
            kb.stt("dve", lat_st[:, i, 0:256], p1[:, 0:256], ssq[:, i:i + 1], vecs[:, go:go + 256], ALU.mult, ALU.mult,
                   (k1, ("ssq", i), "vecs"), (("lat", i),))
            kb.cp("act", lat_st[:, i, 256:288], p1[:, 256:288], (k1,), (("lat", i),))
            kb.dma("sp", O["lat"][e, j * 128:(j + 1) * 128, :], lat_st[:, i, :], (("lat", i),), ())
            kb.cp("dve", ctok[:, i, :], lat_st[:, i, 0:256], (("lat", i),), (("ctok", i),))
            ptr, tk = kb.ps("tr"); pb = bfv(ptr)
            for cc in range(2):
                kb.tr(pb[:, cc * 128:(cc + 1) * 128], ctok[:, i, cc * 128:(cc + 1) * 128], ident_bf, (("ctok", i), "ident"), (tk,))
            kb.cp("act", cT[:, :, j * 128:(j + 1) * 128], pb[:, 0:256].rearrange("p (a b) -> p a b", a=2), (tk,), (("cT", j),))
            for half in range(2):
                dst = xp_tok[:, j, half * 192:(half + 1) * 192].rearrange("p (a b) -> p a b", a=3)[:, 0:3:2, :]
                src = p2[:, half * 128:(half + 1) * 128].rearrange("p (a b) -> p a b", a=2)
                kb.cp("act", dst, src, (k2,), (("xp", j),))
        ck("e_tm")
        for blk in range(4):
            ptr, tk = kb.ps("tr"); pb = bfv(ptr)
            for cc in range(2):
                kb.tr(pb[:, cc * 128:(cc + 1) * 128], ctxl[:, blk, cc * 128:(cc + 1) * 128], ident_bf, ("ctxl", "ident"), (tk,))
            kb.tr(pb[0:32, 256:384], ctxl[:, blk, 256:288], ident_bf, ("ctxl", "ident"), (tk,))
            kb.cp("act", cT[:, :, T + blk * 128:T + (blk + 1) * 128], pb[:, 0:256].rearrange("p (a b) -> p a b", a=2), (tk,), (("cT", 16 + blk),))
            kb.cp("dve", krT[0:32, T + blk * 128:T + (blk + 1) * 128], pb[0:32, 256:384], (tk,), (("krT", 4),))
        ck("e_ctx")
        kb.dma("pool", wpl, I["wpool_bd"][e].rearrange("a k m -> k a m"), (), (skp,))
        for j in range(16):
            pm = Pms[j % 2]; pmk = ("Pm", j % 2)
            kb.dma("pool", pm, I["Pm"][j], (), (pmk,))
            for pr in range(2):
                pst, pk = kb.ps("mm")
                todo = [(gg, sbi) for gg in range(2) for sbi in range(3) if 0 <= j - 1 + sbi <= 15]
                for n_, (gg, sbi) in enumerate(todo):
                    sbk = j - 1 + sbi; c0 = pr * 192 + gg * 64
                    kb.mm(pst[:, 0:128], xp_tok[:, sbk, c0:c0 + 128], pm[:, pr * 2 + gg, sbi, :], n_ == 0, n_ == len(todo) - 1,
                          (("xp", sbk), pmk), (pk,))
                kb.cp("act", pooledT[:, pr, j * 128:(j + 1) * 128], pst[:, 0:128], (pk,), (("pooled", pr, j // 4),))
        pso = VOFF["pscale"][0] + e * 2
        for pr in range(2):
            for tb in range(4):
                sl = slice(tb * 512, (tb + 1) * 512)
                pst, pk = kb.ps("mm")
                kb.mm(pst, wpl[:, pr, :], pooledT[:, pr, sl], True, True, (skp, ("pooled", pr, tb)), (pk,))
                kb.ts("dve", ypT[:, pr, sl], pst, vecs[:, pso + pr:pso + pr + 1], None, ALU.mult, None, (pk, "vecs"), (("yp", pr, tb),))
        ck("e_pool")
        pg.barrier()
        kb.top = MARK_E
        kb.ring_setup(2, 8 * 544)
        qa_f = kb.f32([3, 512]); sq = kb.bf([2, 512]); rs = kb.f32([512]); t1 = kb.f32([512]); t2 = kb.f32([512])
        tabE = [kb.f32([2, 512]) for _ in range(2)]
        slot, sk1 = kb.ring_next(); wkr = wview(slot, 8, 448)
        kb.dma("pool", wkr[:, :, 0:32], I["wie_kr"][e].rearrange("(kc p) n -> p kc n", p=128), (), (sk1,))
        kb.dma("pool", wkr[:, :, 32:64], I["wie_krs"][e].rearrange("(kc p) n -> p kc n", p=128), (), (sk1,))
        kb.dma("pool", wkr[:, :, 64:448], I["wie_qa"][e].rearrange("(kc p) n -> p kc n", p=128), (), (sk1,))
        slot, sk2 = kb.ring_next(); wqr = wview(slot, 3, 768)
        kb.dma("pool", wqr[:, :, 0:384], I["wuq_r"][e].rearrange("(kc p) n -> p kc n", p=128), (), (sk2,))
        kb.dma("pool", wqr[:, :, 384:768], I["wuq_rs"][e].rearrange("(kc p) n -> p kc n", p=128), (), (sk2,))
        gq0 = VOFF["gq"][0] + e * 3
        for tb in range(4):
            sl = slice(tb * 512, (tb + 1) * 512)
            tab = tabE[tb % 2]; tabk = ("tabE", tb % 2)
            kb.dma("sp", tab, I["ropeE"][:, :, sl].rearrange("a p t -> p a t"), (), (tabk,))
            pa, pak = kb.ps("mm"); pb_, pbk = kb.ps("mm")
            for kc in range(8):
                kb.mm(pa[0:32, :], wkr[:, kc, 0:32], hT[:, kc, sl], kc == 0, kc == 7, (sk1, HK(kc, tb)), (pak,))
            for kc in range(8):
                kb.mm(pb_[0:32, :], wkr[:, kc, 32:64], hT[:, kc, sl], kc == 0, kc == 7, (sk1, HK(kc, tb)), (pbk,))
            rope_combine(krT[0:32, sl], ("krT", tb), pa, pak, pb_, pbk, tab, tabk, t1, t2, 32)
            pn, pnk = kb.ps("acc")
            for m in range(3):
                pq, pqk = kb.ps("mm")
                for kc in range(8):
                    kb.mm(pq, wkr[:, kc, 64 + m * 128:64 + (m + 1) * 128], hT[:, kc, sl], kc == 0, kc == 7, (sk1, HK(kc, tb)), (pqk,))
                kb.cp("act", qa_f[:, m, :], pq, (pqk,), (("qa_f", m),))
                kb.act(sq[:, m % 2, :], qa_f[:, m, :], AF.Square, (("qa_f", m),), (("sq", m % 2),))
                kb.mm(pn, ones_bf, sq[:, m % 2, :], m == 0, m == 2, (("sq", m % 2), "ones"), (pnk,))
            kb.act(rs, pn, AF.Sqrt, (pnk,), ("rs",), scale=1.0 / 384, bias=EPS)
            kb.recip(rs, rs, ("rs",), ("rs",))
            for m in range(3):
                kb.stt("dve", qnT[:, m, sl], qa_f[:, m, :], vecs[:, gq0 + m:gq0 + m + 1], rs, ALU.mult, ALU.mult,
                       (("qa_f", m), "rs", "vecs"), (("qnT", m, tb),))
            for m in range(3):
                pa, pak = kb.ps("mm"); pb_, pbk = kb.ps("mm")
                for kc in range(3):
                    kb.mm(pa, wqr[:, kc, m * 128:(m + 1) * 128], qnT[:, kc, sl], kc == 0, kc == 2, (sk2, ("qnT", kc, tb)), (pak,))
                for kc in range(3):
                    kb.mm(pb_, wqr[:, kc, 384 + m * 128:384 + (m + 1) * 128], qnT[:, kc, sl], kc == 0, kc == 2, (sk2, ("qnT", kc, tb)), (pbk,))
                rope_combine(qrT[:, m, sl], ("qrT", m, tb), pa, pak, pb_, pbk, tab, tabk, t1, t2, 128)
        ck("e_ea2")
        pg.barrier()
        kb.top = MARK_E
        kb.rings = {"mm": (0, 4), "acc": (4, 4), "tr": (7, 1)}
        wqn = kb.bf([3, 768]); wk = kb.bf([2, 768]); wvv = kb.bf([2, 768])
        KT = kb.bf([2, NK]); QT = kb.bf([2, T]); Vh = kb.bf([2, 20, 128]); PT = kb.bf([6, 512])
        rDs = kb.f32([2, 512]); rDt = kb.f32([2, 512]); sel = kb.f32([2, 128]); fin_state = []
        kb.memset("dve", rDt, 0.0, ("rDt",)); kb.memset("dve", sel, 0.0, ("sel",))
        kb.memset("dve", sel[64:65, 0, 0:64], 1.0, ("sel",))
        kb.memset("dve", sel[32:33, 1, 64:128], 1.0, ("sel",))
        kb.wload(wqn, I["wuq_n"][e], "wqn"); kb.wload(wk, I["wukv_k"][e], "wk"); kb.wload(wvv, I["wukv_v"][e], "wvv")
        for b in range(2):
            kb.dma("pool", KT[96:105, b, :], I["kmaskE"][:, :], (), (("KTm", b),))
            kb.dma("pool", QT[96:105, b, :], I["qmaskE"][:, :], (), (("QTm", b),))
            kb.dma("sp", KT[64:96, b, :], krT[0:32, :], (), (("KTr", b),))
            kb.memset("dve", Vh[:, b, :, :], 0.0, (("Vh", b),))
            oc = 64 if b == 0 else 32
            kb.memset("dve", Vh[:, b, :, oc:oc + 1], 1.0, (("Vh", b),))

        def proj(h, b):
            kb.dma("sp", QT[64:96, b, :], qrT[(h % 4) * 32:(h % 4) * 32 + 32, h // 4, :], (), (("QTr", b),))
            for tb in range(4):
                sl = slice(tb * 512, (tb + 1) * 512)
                pst, pk = kb.ps("mm")
                for kc in range(3):
                    kb.mm(pst[0:64, :], wqn[:, kc, h * 64:(h + 1) * 64], qnT[:, kc, sl], kc == 0, kc == 2, ("wqn",), (pk,))
                kb.cp("act", QT[0:64, b, sl], pst[0:64, :], (pk,), (("QTn", b),))
            for k5 in range(5):
                sl = slice(k5 * 512, (k5 + 1) * 512)
                pst, pk = kb.ps("mm")
                for cc in range(2):
                    kb.mm(pst[0:64, :], wk[:, cc, h * 64:(h + 1) * 64], cT[:, cc, sl], cc == 0, cc == 1, ("wk",), (pk,))
                kb.cp("dve", KT[0:64, b, sl], pst[0:64, :], (pk,), (("KTn", b),))
            for g0, nb in ((0, 8), (8, 8), (16, 4)):
                pst, pk = kb.ps("mm")
                for i in range(nb):
                    kblk = g0 + i
                    for cc in range(2):
                        kb.mm(pst[:, i * 64:(i + 1) * 64], cT[:, cc, kblk * 128:(kblk + 1) * 128], wvv[:, cc, h * 64:(h + 1) * 64],
                              cc == 0, cc == 1, ("wvv",), (pk,))
                kb.cp("act" if g0 == 8 else "dve", Vh[:, b, g0:g0 + nb, b * 64:b * 64 + 64], pst[:, 0:nb * 64].rearrange("p (a b) -> p a b", a=nb),
                      (pk,), (("Vh", b),))

        def attn(h, b):
            rd_q = (("QTn", b), ("QTr", b), ("QTm", b)); rd_k = (("KTn", b), ("KTr", b), ("KTm", b))
            hp = slice(b * 64, b * 64 + 64)
            p0 = 64 if b == 0 else 32
            for qc in range(4):
                accO, ok_ = kb.ps("acc")
                pend = []

                def pv(kc, slot):
                    kb.mm(accO, Vh[:, b, kc, :], PT[:, slot, :], kc == 0, kc == 19, (("PT", slot), ("Vh", b)), (ok_,))
                for kc in range(20):
                    slot = (qc * 20 + kc) % 6
                    pst, pk = kb.ps("mm")
                    kb.mm(pst, KT[0:105, b, kc * 128:(kc + 1) * 128], QT[0:105, b, qc * 512:(qc + 1) * 512], True, True, rd_q + rd_k, (pk,))
                    kb.act(PT[:, slot, :], pst, AF.Exp, (pk,), (("PT", slot),), scale=MLA_SCALE)
                    pend.append((kc, slot))
                    if len(pend) > 1:
                        pv(*pend.pop(0))
                    if kc == 4 and fin_state:
                        fin_state.pop(0)()
                while pend:
                    pv(*pend.pop(0))
                ri = (h * 4 + qc) % 2
                rd = rDs[:, ri, :]; rk = ("rD", ri)
                rdt = rDt[:, ri, :]; rtk = ("rDt", ri)
                kb.recip(rdt[p0:p0 + 1, :], accO[p0:p0 + 1, :], (ok_,), (rtk,))

                def fin(accO=accO, ok_=ok_, rd=rd, rk=rk, rdt=rdt, rtk=rtk, qc=qc):
                    bc, bk = kb.ps("acc")
                    kb.mm(bc, sel[:, b, :], rdt, True, True, ("sel", rtk), (bk,))
                    kb.cp("act", rd[hp, :], bc[hp, :], (bk,), (rk,))
                    kb.tt("dve", hT[hp, h // 2, qc * 512:(qc + 1) * 512], accO[hp, :], rd[hp, :], ALU.mult, (ok_, rk), (HK(h // 2, qc),))
                fin_state.append(fin)

        ck("e_ebsetup")
        proj(0, 0)
        ck("e_proj0")
        for h in range(12):
            if h + 1 < 12: proj(h + 1, (h + 1) % 2)
            attn(h, h % 2)
            ck("e_attn%d" % h)
        while fin_state:
            fin_state.pop(0)()
        pg.barrier()
        kb.rings = dict(kb.RINGS_DEFAULT)
        kb.top = MARK_E
        out_proj("w_out_even", e, ypT)

    def odd(l, e):
        kb.top = MARK
        QM = kb.bf([8, T]); ctxones = kb.bf([128]); zrow = kb.f32([128])
        kb.memset("dve", ctxones, 1.0, ("ctxones",))
        kb.ts("dve", ctxones, ctxones, vecs[:, fo:fo + 1], None, ALU.mult, None, ("ctxones", "vecs"), ("ctxones",))
        kb.memset("dve", zrow, 0.0, ("zrow",))
        MARK_O = kb.top
        KTna = kb.bf([4, NK]); Vna = kb.bf([20, 512])
        MARK_O2 = kb.top
        kb.ring_setup(2, 8 * 512)
        stg = kb.f32([2, 512]); ctxl = kb.bf([4, 512])
        kb.dma("pool", Vna[:, 16:20, :], I["c_na"][e][:, 1, :].rearrange("(b p) n -> p b n", p=128), (), ("Vna_ctx",))
        kb.dma("pool", ctxl, I["c_na"][e][:, 0, :].rearrange("(b p) n -> p b n", p=128), (), ("ctxl",))
        for piece in range(2):
            slot, sk = kb.ring_next(); wv = wview(slot, 8, 512)
            kb.wload(wv, I["wio_nakv"][e][:, piece * 512:(piece + 1) * 512], sk)
            for j in range(16):
                pst, pk = kb.ps("mm")
                for kc in range(8):
                    kb.mm(pst, hT[:, kc, j * 128:(j + 1) * 128], wv[:, kc, :], kc == 0, kc == 7, (sk, HK(kc, j // 4)), (pk,))
                kb.cp("act", stg[:, j % 2, :], pst, (pk,), (("stg", j % 2),))
                kb.dma("sp", O["nakv"][e, j * 128:(j + 1) * 128, piece * 512:(piece + 1) * 512], stg[:, j % 2, :], (("stg", j % 2),), ())
                if piece == 1:
                    kb.cp("dve", Vna[:, j, :], pst, (pk,), (("Vna", j),))
        for blk in range(4):
            ptr, tk = kb.ps("tr"); pb = bfv(ptr)
            for m in range(4):
                kb.tr(pb[:, m * 128:(m + 1) * 128], ctxl[:, blk, m * 128:(m + 1) * 128], ident_bf, ("ctxl", "ident"), (tk,))
            kb.cp("act", KTna[:, :, T + blk * 128:T + (blk + 1) * 128], pb.rearrange("p (a b) -> p a b", a=4), (tk,), (("KTna_ctx", blk),))
        for wname, isq in (("wio_qna", True), ("wio_kna", False)):
            slot, sk = kb.ring_next(); wv = wview(slot, 8, 512)
            kb.wload(wv, I[wname][e], sk)
            for m in range(4):
                for tb in range(4):
                    sl = slice(tb * 512, (tb + 1) * 512)
                    pst, pk = kb.ps("mm")
                    for kc in range(8):
                        kb.mm(pst, wv[:, kc, m * 128:(m + 1) * 128], hT[:, kc, sl], kc == 0, kc == 7, (sk, HK(kc, tb)), (pk,))
                    if isq:
                        kb.act(QM[:, m, sl], pst, AF.Copy, (pk,), [("QM", m, tb * 4 + i) for i in range(4)], scale=0.125)
                    else:
                        kb.cp("dve", KTna[:, m, sl], pst, (pk,), (("KTna", m, tb),))
        pg.barrier()
        kb.top = MARK_O2
        kb.rings = {"mm": (0, 6), "acc": (6, 2), "tr": (7, 1)}
        PT = kb.bf([6, 512]); nab = [kb.bf([2, 5, 128]) for _ in range(2)]; rDs = kb.f32([2, 128])
        units = [(j, i, hh) for j in range(4) for i in range(16) for hh in range(2)]
        st = {}

        def na_S(k):
            j, i, hh = units[k]
            if hh == 0:
                nb_ = nab[(k // 2) % 2]; nk = ("nab", (k // 2) % 2)
                kb.dma("pool", nb_, I["nabias"][e, VAR_OF[i]][:, 2 * j:2 * j + 2, :, :], (), (nk,))
                start = min(max(i - 2, 0), 11)
                tiles = [(start + c, c) for c in range(5)] + [(16 + c, None) for c in range(4)]
                accb, ak = kb.ps("acc")
                st[(j, i)] = (nb_, nk, tiles, accb, ak)
            nb_, nk, tiles, accb, ak = st[(j, i)]
            hp = slice(hh * 64, hh * 64 + 64)
            banks = []
            for t, (kblk, c) in enumerate(tiles):
                if t % 4 == 0:
                    banks.append(kb.ps("mm"))
                pst, pk = banks[-1]; col = slice((t % 4) * 128, (t % 4) * 128 + 128)
                kb.mm(pst[:, col], KTna[hp, j, kblk * 128:(kblk + 1) * 128], QM[hp, j, i * 128:(i + 1) * 128], True, True,
                      (("QM", j, i),), (pk,))
            kb.tt("dve", banks[0][0], banks[0][0], nb_[:, hh, 0:4, :].rearrange("p a b -> p (a b)"), ALU.add, (banks[0][1], nk), (banks[0][1],))
            kb.tt("dve", banks[1][0][:, 0:128], banks[1][0][:, 0:128], nb_[:, hh, 4, :], ALU.add, (banks[1][1], nk), (banks[1][1],))
            slots = []
            for bi, (pst, pk) in enumerate(banks):
                ncol = min(4, 9 - bi * 4) * 128
                sl_ = (k * 3 + bi) % 6
                kb.act(PT[:, sl_, 0:ncol], pst[:, 0:ncol], AF.Exp, (pk,), (("PT", sl_),))
                slots.append(sl_)
            st[(j, i, hh)] = slots

        def na_OD(k):
            j, i, hh = units[k]
            nb_, nk, tiles, accb, ak = st[(j, i)]
            slots = st[(j, i, hh)]
            Oh = accb[:, hh * 128:(hh + 1) * 128]; Dh = accb[:, 256 + hh * 128:256 + (hh + 1) * 128]
            for t, (kblk, c) in enumerate(tiles):
                sl_ = slots[t // 4]
                kb.mm(Oh, Vna[:, kblk, j * 128:(j + 1) * 128], PT[:, sl_, (t % 4) * 128:(t % 4) * 128 + 128], t == 0, t == 8,
                      (("PT", sl_),), (ak,))
            for t, (kblk, c) in enumerate(tiles):
                sl_ = slots[t // 4]
                kb.mm(Dh, ones_bf if c is not None else ctxones, PT[:, sl_, (t % 4) * 128:(t % 4) * 128 + 128], t == 0, t == 8,
                      (("PT", sl_), "ones", "ctxones"), (ak,))
            if hh == 1:
                for h2 in range(2):
                    hp = slice(h2 * 64, h2 * 64 + 64)
                    rd = rDs[:, (k + h2) % 2, :]; rk = ("rD", (k + h2) % 2)
                    kb.recip(rd[hp, :], accb[hp, 256 + h2 * 128:256 + (h2 + 1) * 128], (ak,), (rk,))
                    kb.tt("dve", QM[hp, j, i * 128:(i + 1) * 128], accb[hp, h2 * 128:(h2 + 1) * 128], rd[hp, :], ALU.mult, (ak, rk), (("QM", j, i),))

        na_S(0)
        for k in range(len(units)):
            if k + 1 < len(units): na_S(k + 1)
            na_OD(k)
        pg.barrier()
        kb.rings = dict(kb.RINGS_DEFAULT)
        kb.top = MARK_O
        KTsw = kb.bf([NK]); Vsw = kb.bf([20, 128])
        MARK_S = kb.top
        kb.ring_setup(3, 8 * 512)
        stg = kb.f32([2, 256]); ctxk = kb.bf([4, 128]); t1 = kb.f32([512]); t2 = kb.f32([512])
        tabO = [kb.f32([2, 512]) for _ in range(2)]
        kb.dma("pool", Vsw[:, 16:20, :], I["c_sw"][e][:, 1, :].rearrange("(b p) n -> p b n", p=128), (), ("Vsw_ctx",))
        kb.dma("pool", ctxk, I["c_sw"][e][:, 0, :].rearrange("(b p) n -> p b n", p=128), (), ("ctxk",))
        slot, sk = kb.ring_next(); wv = wview(slot, 8, 512)
        kb.wload(wv[:, :, 0:256], I["wio_swkv"][e], sk)
        kb.dma("pool", wv[:, :, 256:384], I["wio_ksw"][e].rearrange("(kc p) n -> p kc n", p=128), (), (sk,))
        kb.dma("pool", wv[:, :, 384:512], I["wio_ksws"][e].rearrange("(kc p) n -> p kc n", p=128), (), (sk,))
        for j in range(16):
            pst, pk = kb.ps("mm")
            for kc in range(8):
                kb.mm(pst[:, 0:256], hT[:, kc, j * 128:(j + 1) * 128], wv[:, kc, 0:256], kc == 0, kc == 7, (sk, HK(kc, j // 4)), (pk,))
            kb.cp("act", stg[:, j % 2, :], pst[:, 0:256], (pk,), (("stg", j % 2),))
            kb.dma("sp", O["swkv"][e, j * 128:(j + 1) * 128, :], stg[:, j % 2, :], (("stg", j % 2),), ())
            kb.cp("dve", Vsw[:, j, :], pst[:, 128:256], (pk,), (("Vsw", j),))
        ptr, tk = kb.ps("tr"); pb = bfv(ptr)
        for blk in range(4):
            kb.tr(pb[:, blk * 128:(blk + 1) * 128], ctxk[:, blk, :], ident_bf, ("ctxk", "ident"), (tk,))
        kb.cp("act", KTsw[:, T:NK], pb, (tk,), ("KTsw_ctx",))
        slot, skq = kb.ring_next(); wq = wview(slot, 8, 512)
        kb.wload(wq, I["wio_qsw"][e], skq)
        slot, skqs = kb.ring_next(); wqs = wview(slot, 8, 512)
        kb.wload(wqs, I["wio_qsws"][e], skqs)
        for tb in range(4):
            sl = slice(tb * 512, (tb + 1) * 512)
            tab = tabO[tb % 2]; tabk = ("tabO", tb % 2)
            kb.dma("sp", tab, I["ropeO"][:, :, sl].rearrange("a p t -> p a t"), (), (tabk,))
            pa, pak = kb.ps("mm"); pb_, pbk = kb.ps("mm")
            for kc in range(8):
                kb.mm(pa, wv[:, kc, 256:384], hT[:, kc, sl], kc == 0, kc == 7, (sk, HK(kc, tb)), (pak,))
            for kc in range(8):
                kb.mm(pb_, wv[:, kc, 384:512], hT[:, kc, sl], kc == 0, kc == 7, (sk, HK(kc, tb)), (pbk,))
            rope_combine(KTsw[:, sl], ("KTsw", tb), pa, pak, pb_, pbk, tab, tabk, t1, t2, 128)
            for m in range(4):
                pa, pak = kb.ps("mm"); pb_, pbk = kb.ps("mm")
                for kc in range(8):
                    kb.mm(pa, wq[:, kc, m * 128:(m + 1) * 128], hT[:, kc, sl], kc == 0, kc == 7, (skq, HK(kc, tb)), (pak,))
                for kc in range(8):
                    kb.mm(pb_, wqs[:, kc, m * 128:(m + 1) * 128], hT[:, kc, sl], kc == 0, kc == 7, (skqs, HK(kc, tb)), (pbk,))
                rope_combine(QM[:, 4 + m, sl], ("QMs", m, tb), pa, pak, pb_, pbk, tab, tabk, t1, t2, 128, pre=0.125)
        pg.barrier()
        kb.top = MARK_S
        kb.rings = {"mm": (0, 4), "acc": (4, 4), "tr": (7, 1)}
        PT = kb.bf([14, 512]); swb = kb.bf([6, 3, 128]); esink = kb.bf([2, 512]); rDs = kb.f32([2, 512])
        kb.dma("pool", swb, I["swbias"].rearrange("v k c q -> k v c q"), (), ("swb",))
        so = VOFF["sink"][0] + e * 8
        for g in range(2):
            for hd in range(4):
                kb.act(esink[0:1, g, hd * 128:(hd + 1) * 128], zrow[0:1, 0:128], AF.Exp, ("zrow", "vecs"), ("esink",),
                       bias=vecs[0:1, so + 4 * g + hd:so + 4 * g + hd + 1])
        sunits = [(i, g) for i in range(16) for g in range(2)]
        sst = {}

        def sw_S(k):
            i, g = sunits[k]
            hp = slice(g * 64, g * 64 + 64)
            chunks = [(min(max(i - 1 + c, 0), 15), c) for c in range(3)] + [(16 + c, None) for c in range(4)]
            slots = []
            for t, (kblk, c) in enumerate(chunks):
                pst, pk = kb.ps("mm")
                kb.mm(pst, KTsw[hp, kblk * 128:(kblk + 1) * 128], QM[hp, 4:8, i * 128:(i + 1) * 128], True, True, (), (pk,))
                if c is not None:
                    p3 = pst.rearrange("p (a b) -> p a b", a=4)
                    kb.tt("dve", p3, p3, swb[:, VAR_OF[i], c, :].unsqueeze(1).to_broadcast([128, 4, 128]), ALU.add, (pk, "swb"), (pk,))
                sl_ = (k * 7 + t) % 14
                kb.act(PT[:, sl_, :], pst, AF.Exp, (pk,), (("PT", sl_),))
                slots.append(sl_)
            sst[k] = (chunks, slots)

        def sw_OD(k):
            i, g = sunits[k]
            hp = slice(g * 64, g * 64 + 64)
            chunks, slots = sst[k]
            accO, ok_ = kb.ps("acc"); accD, dk_ = kb.ps("acc")
            for t, (kblk, c) in enumerate(chunks):
                kb.mm(accO, Vsw[:, kblk, :], PT[:, slots[t], :], t == 0, t == 6, (("PT", slots[t]),), (ok_,))
            for t, (kblk, c) in enumerate(chunks):
                kb.mm(accD, ones_bf if c is not None else ctxones, PT[:, slots[t], :], t == 0, False, (("PT", slots[t]), "ones", "ctxones"), (dk_,))
            kb.mm(accD, ones_bf[0:1, :], esink[0:1, g, :], False, True, ("esink", "ones"), (dk_,))
            rd = rDs[:, k % 2, :]; rk = ("rD", k % 2)
            kb.recip(rd[hp, :], accD[hp, :], (dk_,), (rk,))
            kb.tt("dve", QM[hp, 4:8, i * 128:(i + 1) * 128], accO[hp, :].rearrange("p (a b) -> p a b", a=4),
                  rd[hp, :].rearrange("p (a b) -> p a b", a=4), ALU.mult, (ok_, rk), (("QMo", i, g),))

        sw_S(0)
        for k in range(len(sunits)):
            if k + 1 < len(sunits): sw_S(k + 1)
            sw_OD(k)
        pg.barrier()
        kb.rings = dict(kb.RINGS_DEFAULT)
        kb.top = MARK_O
        out_proj("w_out_odd", e, None, src=QM)

    return {"even": even, "odd": odd}


_CACHE = {}


def kernel(**inputs):
    maps = _host_inputs(inputs)
    for core, m in enumerate(maps):
        sample = core >= 4
        fl = m.pop("flags")
        m["vecs"] = _pack_vecs(inputs, m.pop("cvec"), fl)
        for k in ("b_mod", "norm_mix", "norm_ffn", "norm_final", "mla_q_norm", "mla_kv_norm", "pool_scale",
                  "swa_sink", "conv_w", "conv_b"):
            m.pop(k, None)
    if "nc" not in _CACHE:
        _CACHE["nc"] = build_program()[0]
    nc = _CACHE["nc"]
    res = run_bass_kernel_spmd(nc, maps, core_ids=list(range(8)))
    R = res.results
    y_prompt = np.concatenate([np.asarray(R[c]["yT"]).T.reshape(8, 256, D) for c in range(4)], 0)
    y_sample = np.stack([np.asarray(R[4 + b]["yT"]).T for b in range(4)], 0)
    lat = np.concatenate([np.asarray(R[c]["lat"]).reshape(2, 8, 256, 288).transpose(1, 0, 2, 3) for c in range(4)], 0)
    na = np.concatenate([np.asarray(R[c]["nakv"]).reshape(2, 8, 256, 2, 8, 64).transpose(1, 0, 2, 3, 4, 5) for c in range(4)], 0)
    sw = np.concatenate([np.asarray(R[c]["swkv"]).reshape(2, 8, 256, 2, 2, 64).transpose(1, 0, 2, 3, 4, 5) for c in range(4)], 0)
    f = lambda a: np.ascontiguousarray(a, dtype=np.float32)
    return (f(y_prompt), f(y_sample), f(lat), f(na), f(sw))
```

```python
import numpy as np
import concourse.bass as bass
import concourse.mybir as mybir
from concourse.bass_utils import run_bass_kernel_spmd
from contextlib import ExitStack

F32, BF16 = mybir.dt.float32, mybir.dt.bfloat16
AF, ALU = mybir.ActivationFunctionType, mybir.AluOpType

D = 1024; T = 2048; DEPTH = 4; PAST = 512; NK = T + PAST
DFF = 2816; NFC = 22
EPS = 1e-6
MLA_SCALE = 96 ** -0.5
BM = 1024.0
NEGB = -30000.0
STAGES = {"even": True, "odd": True, "ffn": True}
STOP_AT = None


class _Stop(Exception):
    pass


def ck(name):
    if STOP_AT == name:
        raise _Stop()
NLAYERS = DEPTH


class Op:
    __slots__ = ("eng", "fn", "deps", "dma", "sem", "target", "pre", "need", "ticket")

    def __init__(self, eng, fn, dma):
        self.eng = eng; self.fn = fn; self.dma = dma; self.deps = []
        self.sem = None; self.target = 0; self.pre = None; self.need = False; self.ticket = 0


class Prog:
    ENGS = ("pe", "act", "dve", "pool", "sp")
    NDS = {"sp": 40, "pool": 40}

    def __init__(self):
        self.q = {e: [] for e in self.ENGS}
        self.lw = {}; self.rd = {}
        self.dsem_next = {e: 0 for e in self.NDS}
        self.dsem_tgt = {}
        self.dma_since = []

    def add(self, eng, fn, reads=(), writes=(), dma=False):
        op = Op(eng, fn, dma)
        deps = []
        for k in reads:
            w = self.lw.get(k)
            if w is not None: deps.append(w)
            if isinstance(k, tuple) and k[0] == "ps":
                for r in self.rd.get(k, ()):
                    if r.eng != eng: deps.append(r)
        for k in writes:
            w = self.lw.get(k)
            if w is not None: deps.append(w)
            for r in self.rd.get(k, ()): deps.append(r)
        seen = set(); dl = []
        for d in deps:
            if d is op or id(d) in seen: continue
            seen.add(id(d))
            if (not d.dma) and d.eng == eng and eng == "pe": continue
            dl.append(d)
        op.deps = dl
        if dma:
            i = self.dsem_next[eng]; self.dsem_next[eng] = (i + 1) % self.NDS[eng]
            key = (eng, i)
            prev = self.dsem_tgt.get(key, 0)
            op.sem = key; op.pre = prev; op.target = prev + 16
            self.dsem_tgt[key] = op.target
            self.dma_since.append(op)
        for k in reads:
            self.rd.setdefault(k, []).append(op)
        for k in writes:
            self.lw[k] = op; self.rd[k] = []
        self.q[eng].append(op)
        return op

    def barrier(self):
        col = Op("dve", "nop", False)
        for e in self.ENGS:
            if self.q[e]:
                last = None
                for o in reversed(self.q[e]):
                    if o.fn is not None and not o.dma:
                        last = o; break
                if last is not None: col.deps.append(last)
        col.deps.extend(self.dma_since)
        self.dma_since = []
        self.q["dve"].append(col)
        for e in self.ENGS:
            if e == "dve": continue
            w = Op(e, None, False); w.deps = [col]
            self.q[e].append(w)
        self.lw = {}; self.rd = {}

    def emit(self, nc, block, csem, dsems):
        for e in self.ENGS:
            for op in self.q[e]:
                for d in op.deps:
                    if not d.dma: d.need = True
        for e in self.ENGS:
            t = 0
            for op in self.q[e]:
                if op.need and not op.dma:
                    t += 1; op.ticket = t
        engobj = {"pe": nc.tensor, "act": nc.scalar, "dve": nc.vector, "pool": nc.gpsimd, "sp": nc.sync}
        self.nwaits = 0

        def run(e, eo):
            waited = {}

            def wait(key, sem, val):
                if val <= 0 or waited.get(key, 0) >= val: return
                waited[key] = val
                eo.wait_ge(sem, val); self.nwaits += 1

            for op in self.q[e]:
                for d in op.deps:
                    if d.dma: wait(d.sem, dsems[d.sem], d.target)
                    else: wait(d.eng, csem[d.eng], d.ticket)
                if op.dma:
                    wait(op.sem, dsems[op.sem], op.pre)
                    op.fn(eo).then_inc(dsems[op.sem], 16)
                elif op.fn is None:
                    pass
                else:
                    ins = eo.nop() if op.fn == "nop" else op.fn(eo)
                    if op.need: ins.then_inc(csem[e], 1)
            for key, tgt in self.dsem_tgt.items():
                if key[0] == e: wait(key, dsems[key], tgt)

        block.tensor(lambda eo: run("pe", eo))
        block.scalar(lambda eo: run("act", eo))
        block.vector(lambda eo: run("dve", eo))
        block.gpsimd(lambda eo: run("pool", eo))
        block.sync(lambda eo: run("sp", eo))


def _bf(x):
    import ml_dtypes
    return np.asarray(x, np.float32).astype(ml_dtypes.bfloat16).astype(np.float32)


def _rope_tables(R, sample):
    half = R // 2; nf = half // 2
    t = np.arange(T)
    freqs = (10000.0 ** (-np.arange(nf, dtype=np.float32) / nf)).astype(np.float32)
    C = np.ones((R, T), np.float32); S = np.zeros((R, T), np.float32)
    if sample:
        for hi, pos in enumerate((t // 64, t % 64)):
            ang = pos.astype(np.float32)[None, :] * freqs[:, None]
            c, s = np.cos(ang).astype(np.float32), np.sin(ang).astype(np.float32)
            b = hi * half
            C[b:b + nf] = c; C[b + nf:b + 2 * nf] = c
            S[b:b + nf] = -s; S[b + nf:b + 2 * nf] = s
    return C, S


def _rope_perm(R):
    half = R // 2; nf = half // 2
    p = np.arange(R)
    for b in (0, half):
        p[b:b + nf] = np.arange(b + nf, b + 2 * nf)
        p[b + nf:b + 2 * nf] = np.arange(b, b + nf)
    return p


VAR_OF = [0, 1] + [2, 3] * 6 + [4, 5]
VAR_REP = [0, 1, 2, 3, 14, 15]


def _struct_consts(sample):
    c = {}
    c["ident"] = np.eye(128, dtype=np.float32)
    Ce, Se = _rope_tables(32, sample)
    c["ropeE"] = np.stack([np.tile(Ce, (4, 1)), np.tile(Se, (4, 1))], 0)
    Co, So = _rope_tables(64, sample)
    c["ropeO"] = np.stack([np.tile(Co, (2, 1)), np.tile(So, (2, 1))], 0)
    km = np.zeros((9, NK), np.float32); qm = np.zeros((9, T), np.float32)
    km[8, :] = 1.0; qm[8, :] = -BM
    if sample:
        km[0, :] = 1.0; qm[0, :] = BM
    else:
        for s in range(8):
            km[s, s * 256:(s + 1) * 256] = 1.0
            qm[s, s * 256:(s + 1) * 256] = BM
    c["kmaskE"] = km; c["qmaskE"] = qm
    fl = 1.0 if sample else 0.0
    c["flags"] = np.tile(np.array([[fl, 1.0 - fl]], np.float32), (128, 1))
    n = T if sample else 256
    Pm = np.zeros((16, 128, 4, 3, 128), np.float32)
    tt = np.arange(T); tl = tt % n; base = tt - tl
    for g, w in enumerate((2, 4, 8, 16)):
        lo = np.clip(tl - w // 2, 0, n); hi = np.clip(tl + w // 2, 0, n)
        cnt = (hi - lo).astype(np.float32)
        for t in range(T):
            j = t // 128
            for s in range(base[t] + lo[t], base[t] + hi[t]):
                sb = s // 128 - (j - 1)
                Pm[j, s % 128, g, sb, t % 128] += 1.0 / cnt[t]
            Pm[j, t % 128, g, 1, t % 128] -= 1.0
    c["Pm"] = Pm
    swb = np.full((6, 128, 3, 128), NEGB, np.float32)
    for v, i in enumerate(VAR_REP):
        tq = i * 128 + np.arange(128)
        for cc in range(3):
            kb = i - 1 + cc
            if kb < 0 or kb > 15: continue
            ks = kb * 128 + np.arange(128)
            if sample:
                ok = np.abs(tq[None, :] - ks[:, None]) <= 128
            else:
                ok = (tq[None, :] // 256) == (ks[:, None] // 256)
            swb[v, :, cc, :] = np.where(ok, 0.0, NEGB)
    c["swbias"] = swb
    return c


def _na_bias(sample, rpb):
    out = np.full((2, 6, 128, 8, 5, 128), NEGB, np.float32)
    for v, i in enumerate(VAR_REP):
        start = min(max(i - 2, 0), 11)
        tq = i * 128 + np.arange(128)
        r, cq = tq // 64, tq % 64
        for cc in range(5):
            ks = (start + cc) * 128 + np.arange(128)
            if sample:
                rp, cp = ks // 64, ks % 64
                r0 = np.clip(r - 4, 0, 24); c0 = np.clip(cq - 8, 0, 48)
                ok = ((rp[:, None] >= r0[None, :]) & (rp[:, None] < r0[None, :] + 8) &
                      (cp[:, None] >= c0[None, :]) & (cp[:, None] < c0[None, :] + 16))
                dr = np.clip(rp[:, None] - r[None, :] + 7, 0, 14)
                dc = np.clip(cp[:, None] - cq[None, :] + 15, 0, 30)
                for l in range(2):
                    g = rpb[l][:, dr, dc]
                    out[l, v, :, :, cc, :] = np.where(ok[:, None, :], g.transpose(1, 0, 2), NEGB)
            else:
                ok = (tq[None, :] // 256) == (ks[:, None] // 256)
                out[:, v, :, :, cc, :] = np.where(ok, 0.0, NEGB)[None, :, None, :]
    return out


def _host_inputs(inp):
    f = lambda a: np.ascontiguousarray(np.asarray(a, np.float32))
    w = {}
    for k in ("w_mod", "b_mod", "norm_mix", "norm_ffn", "norm_final", "mla_q_norm", "mla_kv_norm",
              "pool_scale", "w_out_even", "w_out_odd", "swa_sink", "w_up", "conv_w", "conv_b", "w_down"):
        w[k] = f(inp[k])
    rowperm = np.concatenate([np.arange(512)] + [np.concatenate([512 + jj * 64 + np.arange(64), 512 + (4 + jj) * 64 + np.arange(64)])
                                                  for jj in range(4)])
    w["w_out_odd"] = f(w["w_out_odd"][:, rowperm, :])
    wie = f(inp["w_in_even"])
    w["wie_qa"] = f(wie[:, :, 0:384]); w["wie_tm"] = f(wie[:, :, 384:928])
    w["wie_kr"] = f(wie[:, :, 640:672]); w["wie_krs"] = f(wie[:, :, 640 + _rope_perm(32)])
    wuq = f(inp["w_uq"]).reshape(2, 384, 12, 96)
    w["wuq_n"] = f(wuq[..., 0:64].reshape(2, 384, 768))
    w["wuq_r"] = f(wuq[..., 64:96].reshape(2, 384, 384))
    w["wuq_rs"] = f(wuq[..., 64 + _rope_perm(32)].reshape(2, 384, 384))
    wukv = f(inp["w_ukv"]).reshape(2, 256, 12, 128)
    w["wukv_k"] = f(wukv[..., 0:64].reshape(2, 256, 768)); w["wukv_v"] = f(wukv[..., 64:128].reshape(2, 256, 768))
    wp = f(inp["w_pool"]); bd = np.zeros((2, 2, 128, 128), np.float32)
    for e in range(2):
        for pr in range(2):
            bd[e, pr, 0:64, 0:64] = wp[e, 2 * pr]; bd[e, pr, 64:128, 64:128] = wp[e, 2 * pr + 1]
    w["wpool_bd"] = bd
    wio = f(inp["w_in_odd"])
    w["wio_qna"] = f(wio[:, :, 0:512]); w["wio_kna"] = f(wio[:, :, 512:1024])
    w["wio_nakv"] = f(wio[:, :, 512:1536]); w["wio_swkv"] = f(wio[:, :, 2048:2304])
    qsw = wio[:, :, 1536:2048].reshape(2, 1024, 8, 64)
    order = [0, 4, 1, 5, 2, 6, 3, 7]
    w["wio_qsw"] = f(qsw[:, :, order, :].reshape(2, 1024, 512))
    w["wio_qsws"] = f(qsw[:, :, order, :][..., _rope_perm(64)].reshape(2, 1024, 512))
    ksw = wio[:, :, 2048:2176].reshape(2, 1024, 2, 64)
    w["wio_ksw"] = f(ksw.reshape(2, 1024, 128)); w["wio_ksws"] = f(ksw[..., _rope_perm(64)].reshape(2, 1024, 128))
    sc = {True: _struct_consts(True), False: _struct_consts(False)}
    rpb = f(inp["na_rpb"])
    nab = {True: _na_bias(True, rpb), False: _na_bias(False, rpb)}
    xp = f(inp["x_prompt"]); xs = f(inp["x_sample"])
    maps = []
    for core in range(8):
        sample = core >= 4
        m = dict(w)
        m.update(sc[sample]); m["nabias"] = nab[sample]
        if sample:
            b = core - 4
            m["xT"] = f(xs[b].T)
            m["cvec"] = f(inp["c"])[b]
            m["c_mla"] = f(inp["cache_mla_latent"])[b]
            m["c_na"] = f(inp["cache_na_kv"])[b].reshape(2, 512, 2, 512)
            m["c_sw"] = f(inp["cache_swa_kv"])[b].reshape(2, 512, 2, 128)
        else:
            m["xT"] = f(xp[8 * core:8 * core + 8].reshape(T, D).T)
            m["cvec"] = f(inp["c_ctx"])
            m["c_mla"] = np.zeros((2, 512, 288), np.float32)
            m["c_na"] = np.zeros((2, 512, 2, 512), np.float32)
            m["c_sw"] = np.zeros((2, 512, 2, 128), np.float32)
        maps.append(m)
    return maps


IN_SHAPES = {
    "xT": [D, T], "cvec": [D], "c_mla": [2, 512, 288], "c_na": [2, 512, 2, 512], "c_sw": [2, 512, 2, 128],
    "w_mod": [4, D, 6 * D], "b_mod": [4, 6 * D], "norm_mix": [4, D], "norm_ffn": [4, D], "norm_final": [D],
    "mla_q_norm": [2, 384], "mla_kv_norm": [2, 256], "pool_scale": [2, 256],
    "w_out_even": [2, D, D], "w_out_odd": [2, D, D], "swa_sink": [2, 8],
    "w_up": [4, D, 2 * DFF], "conv_w": [4, 3, 2 * DFF], "conv_b": [4, 2 * DFF], "w_down": [4, DFF, D],
    "wie_qa": [2, D, 384], "wie_tm": [2, D, 544], "wie_kr": [2, D, 32], "wie_krs": [2, D, 32],
    "wuq_n": [2, 384, 768], "wuq_r": [2, 384, 384], "wuq_rs": [2, 384, 384],
    "wukv_k": [2, 256, 768], "wukv_v": [2, 256, 768], "wpool_bd": [2, 2, 128, 128],
    "wio_qna": [2, D, 512], "wio_kna": [2, D, 512], "wio_nakv": [2, D, 1024], "wio_swkv": [2, D, 256],
    "wio_qsw": [2, D, 512], "wio_qsws": [2, D, 512], "wio_ksw": [2, D, 128], "wio_ksws": [2, D, 128],
    "ident": [128, 128], "ropeE": [2, 128, T], "ropeO": [2, 128, T], "kmaskE": [9, NK], "qmaskE": [9, T],
    "flags": [128, 2], "Pm": [16, 128, 4, 3, 128], "swbias": [6, 128, 3, 128], "nabias": [2, 6, 128, 8, 5, 128],
}
OUT_SHAPES = {"yT": [D, T], "lat": [2, T, 288], "nakv": [2, T, 1024], "swkv": [2, T, 256]}


class KB:
    def __init__(self, nc, big, psums, ins, outs):
        self.nc = nc; self.big = big; self.P = psums; self.I = ins; self.O = outs
        self.pg = Prog()
        self.top = 0
        self.rr = {"mm": 0, "acc": 0, "tr": 0}
        self.RINGS_DEFAULT = {"mm": (0, 4), "acc": (4, 3), "tr": (7, 1)}
        self.rings = dict(self.RINGS_DEFAULT)
        self.uid = 0

    def alloc(self, nbytes):
        off = self.top; self.top += (nbytes + 7) // 8 * 2
        assert self.top * 4 <= 212800, f"SBUF overflow {self.top * 4}"
        return off

    def _shape(self, v, shape):
        if len(shape) == 1: return v
        if len(shape) == 2: return v.rearrange("p (a b) -> p a b", a=shape[0])
        if len(shape) == 3: return v.rearrange("p (a b c) -> p a b c", a=shape[0], b=shape[1])
        return v.rearrange("p (a b c d) -> p a b c d", a=shape[0], b=shape[1], c=shape[2])

    def f32(self, shape):
        n = int(np.prod(shape)); off = self.alloc(n * 4)
        return self._shape(self.big[:, off:off + n], shape)

    def bf(self, shape):
        n = int(np.prod(shape)); off = self.alloc(n * 2)
        return self._shape(self.big[:, off:off + (n + 1) // 2].bitcast(BF16)[:, 0:n], shape)

    def key(self, name):
        self.uid += 1
        return (name, self.uid)

    def ps(self, ring):
        lo, n = self.rings[ring]
        i = lo + self.rr[ring] % n; self.rr[ring] += 1
        return self.P[i], ("ps", i)

    def mm(self, out, lhsT, rhs, start, stop, reads, writes):
        return self.pg.add("pe", lambda e: e.matmul(out, lhsT, rhs, start=start, stop=stop), reads, writes)

    def tr(self, out, in_, ident, reads, writes):
        return self.pg.add("pe", lambda e: e.transpose(out, in_, ident), reads, writes)

    def act(self, out, in_, func, reads, writes, scale=1.0, bias=0.0, accum=None):
        if accum is None:
            return self.pg.add("act", lambda e: e.activation(out=out, in_=in_, func=func, bias=bias, scale=scale), reads, writes)
        return self.pg.add("act", lambda e: e.activation(out=out, in_=in_, func=func, bias=bias, scale=scale, accum_out=accum), reads, writes)

    def ts(self, eng, out, in0, s1, s2, op0, op1, reads, writes):
        if s2 is None:
            return self.pg.add(eng, lambda e: e.tensor_scalar(out=out, in0=in0, scalar1=s1, scalar2=None, op0=op0), reads, writes)
        return self.pg.add(eng, lambda e: e.tensor_scalar(out=out, in0=in0, scalar1=s1, scalar2=s2, op0=op0, op1=op1), reads, writes)

    def stt(self, eng, out, in0, scalar, in1, op0, op1, reads, writes):
        return self.pg.add(eng, lambda e: e.scalar_tensor_tensor(out=out, in0=in0, scalar=scalar, in1=in1, op0=op0, op1=op1), reads, writes)

    def tt(self, eng, out, in0, in1, op, reads, writes):
        return self.pg.add(eng, lambda e: e.tensor_tensor(out=out, in0=in0, in1=in1, op=op), reads, writes)

    def cp(self, eng, out, in_, reads, writes):
        if eng == "act":
            return self.pg.add("act", lambda e: e.copy(out=out, in_=in_), reads, writes)
        return self.pg.add(eng, lambda e: e.tensor_copy(out=out, in_=in_), reads, writes)

    def recip(self, out, in_, reads, writes):
        return self.pg.add("dve", lambda e: e.reciprocal(out=out, in_=in_), reads, writes)

    def memset(self, eng, ap, val, writes):
        return self.pg.add(eng, lambda e: e.memset(ap, val), (), writes)

    def dma(self, q, out, in_, reads, writes):
        return self.pg.add(q, lambda e: e.dma_start(out=out, in_=in_), reads, writes, dma=True)

    def wload(self, dst, src_ap, key):
        return self.dma("pool", dst, src_ap.rearrange("(kc p) n -> p kc n", p=128), (), (key,))


VOFF = {}
def _voff():
    o = 0
    for name, n in (("bmod", 192), ("nmix", 32), ("nffn", 32), ("nfin", 8), ("conv", 704), ("gq", 6),
                    ("pscale", 4), ("cvec", 8), ("gkv", 512), ("sink", 16), ("flags", 2)):
        VOFF[name] = (o, n); o += n
    return o
NV = _voff()


def _pack_vecs(inp, cvec, flags):
    v = np.zeros((128, NV), np.float32)
    def put(name, arr):
        o, n = VOFF[name]; v[:, o:o + n] = np.asarray(arr, np.float32).reshape(128, n)
    f = lambda a: np.asarray(a, np.float32)
    put("bmod", f(inp["b_mod"]).reshape(4, 48, 128).transpose(2, 0, 1))
    put("nmix", f(inp["norm_mix"]).reshape(4, 8, 128).transpose(2, 0, 1))
    put("nffn", f(inp["norm_ffn"]).reshape(4, 8, 128).transpose(2, 0, 1))
    put("nfin", f(inp["norm_final"]).reshape(8, 128).transpose(1, 0))
    cw = np.concatenate([f(inp["conv_w"]), f(inp["conv_b"])[:, None, :]], 1)
    put("conv", cw.reshape(4, 4, 44, 128).transpose(3, 0, 1, 2))
    put("gq", f(inp["mla_q_norm"]).reshape(2, 3, 128).transpose(2, 0, 1))
    put("pscale", f(inp["pool_scale"]).reshape(2, 2, 128).transpose(2, 0, 1))
    put("cvec", f(cvec).reshape(8, 128).transpose(1, 0))
    put("gkv", np.broadcast_to(f(inp["mla_kv_norm"]).reshape(1, 512), (128, 512)))
    put("sink", np.broadcast_to(f(inp["swa_sink"]).reshape(1, 16), (128, 16)))
    put("flags", flags)
    return v


def build_program():
    nc = bass.Bass("TRN2", target_bir_lowering=False)
    shapes = dict(IN_SHAPES); shapes["vecs"] = [128, NV]
    for k in ("cvec", "b_mod", "norm_mix", "norm_ffn", "norm_final", "mla_q_norm", "mla_kv_norm", "pool_scale",
              "swa_sink", "conv_w", "conv_b", "flags"):
        shapes.pop(k)
    I = {k: nc.dram_tensor(k, s, F32, kind="ExternalInput").ap() for k, s in shapes.items()}
    O = {k: nc.dram_tensor(k, s, F32, kind="ExternalOutput").ap() for k, s in OUT_SHAPES.items()}
    es = ExitStack()
    with es:
        big = es.enter_context(nc.sbuf_tensor("big", [128, 53200], F32))
        P = [es.enter_context(nc.psum_tensor(f"psb{i}", [128, 512], F32)) for i in range(8)]
        csem = {e: es.enter_context(nc.semaphore(f"c_{e}")) for e in Prog.ENGS}
        dsems = {(q, i): es.enter_context(nc.semaphore(f"d_{q}{i}")) for q in Prog.NDS for i in range(Prog.NDS[q])}
        block = es.enter_context(nc.Block())
        kb = KB(nc, big, [p[:, :] for p in P], I, O)
        _emit_all(kb)
        kb.pg.emit(nc, block, csem, dsems)
    return nc, kb


def _emit_all(kb):
    I, O, pg = kb.I, kb.O, kb.pg
    xres = kb.f32([8, T]); hT = kb.bf([8, T])
    vecs = kb.f32([NV])
    ones_bf = kb.bf([128]); ident_bf = kb.bf([128])
    scv = kb.bf([8]); modT_all = kb.f32([4, 48]); prm_all = kb.f32([4, 6, 8]); convx_all = kb.f32([4, 4, 44])
    prm = prm_all[:, 0]; convx = convx_all[:, 0]
    MARK = kb.top
    kb.xres, kb.hT, kb.vecs, kb.ones_bf, kb.ident_bf, kb.prm = xres, hT, vecs, ones_bf, ident_bf, prm
    kb.MARK = MARK

    def vv(name, l=None, per=None):
        o, n = VOFF[name]
        if l is None: return vecs[:, o:o + n]
        return vecs[:, o + l * per:o + (l + 1) * per]
    kb.vv = vv

    XK = lambda c, tb: ("x", c, tb)
    HK = lambda c, tb: ("h", c, tb)
    kb.XK, kb.HK = XK, HK
    kb.dma("sp", vecs, I["vecs"][:, :], (), ("vecs",))
    for c in range(8):
        kb.dma("sp", xres[:, c, :], I["xT"][c * 128:(c + 1) * 128, :], (), [XK(c, tb) for tb in range(4)])
    kb.dma("pool", ident_bf, I["ident"][:, :], (), ("ident",))
    kb.memset("dve", ones_bf, 1.0, ("ones",))
    kb.act(scv, vv("cvec"), AF.Silu, ("vecs",), ("scv",))

    def ring_setup(nslots, nel):
        kb.ring = [kb.bf([nel]) for _ in range(nslots)]
        kb.ring_i = 0; kb.ring_nel = nel

    def ring_next():
        i = kb.ring_i % len(kb.ring); kb.ring_i += 1
        return kb.ring[i], ("wr", i)
    kb.ring_setup, kb.ring_next = ring_setup, ring_next

    def wview(slot, kc, n):
        return slot[:, 0:kc * n].rearrange("p (a b) -> p a b", a=kc)
    kb.wview = wview

    def norm_mod(Acol, Bcol, sq, rs, tmp, dst_fn, final=False):
        for tb in range(4):
            sl = slice(tb * 512, (tb + 1) * 512)
            pst, pk = kb.ps("mm")
            for c in range(8):
                s = sq[:, c % 2, :]
                kb.act(s, xres[:, c, sl], AF.Square, (XK(c, tb),), (("sq", c % 2),))
                kb.mm(pst, ones_bf, s, c == 0, c == 7, (("sq", c % 2), "ones"), (pk,))
            kb.act(rs, pst, AF.Sqrt, (pk,), ("rs",), scale=1.0 / D, bias=EPS)
            kb.recip(rs, rs, ("rs",), ("rs",))
            for c in range(8):
                tm = tmp[:, c % 2, :]
                kb.stt("dve", tm, xres[:, c, sl], Acol(c), rs, ALU.mult, ALU.mult,
                       (XK(c, tb), "rs", "prm", "vecs"), (("tmp", c % 2),))
                dst_fn(c, tb, tm, ("tmp", c % 2), Bcol(c) if Bcol else 0.0)
    kb.norm_mod = norm_mod

    def to_hT(c, tb, tm, tk, bias):
        kb.act(hT[:, c, tb * 512:(tb + 1) * 512], tm, AF.Identity, (tk, "prm"), (HK(c, tb),), bias=bias)

    def adaln_gen(layers, aring, pst, pk, pw=512):
        cnt = 0
        for l in layers:
            modT = modT_all[:, l]; prm = prm_all[:, l]; convx = convx_all[:, l]
            for piece in range(6144 // pw):
                slot, sk = aring[cnt % len(aring)], ("awr", cnt % len(aring)); cnt += 1
                wv = wview(slot, 8, pw)
                kb.wload(wv, I["w_mod"][l][:, piece * pw:(piece + 1) * pw], sk)
                for cc in range(pw // 128):
                    j = piece * (pw // 128) + cc
                    for kc in range(8):
                        kb.mm(pst[:, j:j + 1], wv[:, kc, cc * 128:(cc + 1) * 128], scv[:, kc:kc + 1], kc == 0, kc == 7,
                              (sk, "scv"), (pk,))
                yield
            mk = ("modT", l); pk_ = ("prm", l)
            kb.tt("dve", modT, pst[:, 0:48], vv("bmod", l, 48), ALU.add, (pk, "vecs"), (mk,))
            for (row, sc_i, g) in ((0, 1, "nmix"), (3, 4, "nffn")):
                kb.stt("dve", prm[:, row, :], modT[:, sc_i * 8:(sc_i + 1) * 8], 1.0, vv(g, l, 8), ALU.add, ALU.mult,
                       (mk, "vecs"), (pk_,))
            for (row, m_i) in ((1, 0), (2, 2), (4, 3), (5, 5)):
                kb.cp("dve", prm[:, row, :], modT[:, m_i * 8:(m_i + 1) * 8], (mk,), (pk_,))
            o, _ = VOFF["conv"]; cb = o + l * 176
            fo, _ = VOFF["flags"]
            w0 = vecs[:, cb:cb + 44]; w2 = vecs[:, cb + 88:cb + 132]
            kb.ts("dve", convx[:, 0, :], w0, vecs[:, fo:fo + 1], None, ALU.mult, None, ("vecs",), (("convx", l),))
            kb.ts("dve", convx[:, 1, :], w2, vecs[:, fo:fo + 1], None, ALU.mult, None, ("vecs",), (("convx", l),))
            kb.ts("dve", convx[:, 2, :], w0, vecs[:, fo + 1:fo + 2], -1.0, ALU.mult, ALU.mult, ("vecs",), (("convx", l),))
            kb.ts("dve", convx[:, 3, :], w2, vecs[:, fo + 1:fo + 2], -1.0, ALU.mult, ALU.mult, ("vecs",), (("convx", l),))
            yield

    def adaln_first():
        kb.top = MARK
        aring = [kb.bf([8 * 512]) for _ in range(3)]
        pst, pk = kb.ps("acc")
        for _ in adaln_gen([0], aring, pst, pk):
            pass
        pg.barrier()

    def norm_phase(row_a, row_b):
        kb.top = MARK
        sq = kb.bf([2, 512]); rs = kb.f32([512]); tmp = kb.f32([2, 512])
        norm_mod(lambda c: kb.prm[:, row_a, c:c + 1], lambda c: kb.prm[:, row_b, c:c + 1], sq, rs, tmp, to_hT)
        pg.barrier()

    def ffn(l):
        kb.top = MARK
        o, _ = VOFF["conv"]; cb = o + l * 176
        cw = lambda k, c: vecs[:, cb + k * 44 + c:cb + k * 44 + c + 1]
        cx = lambda k, c: kb.convx[:, k, c:c + 1]
        ring_setup(2 if (l == 0 and NLAYERS > 1) else 3, 22 * 256)
        kb.rings = {"mm": (0, 7), "acc": (0, 7), "tr": (7, 1)}
        actb = kb.bf([NFC, 1024]); ta_r = kb.f32([4, 512]); tg_r = kb.f32([4, 512]); es = kb.f32([4, 2]); eh = kb.f32([44])
        agen = None
        if l == 0 and NLAYERS > 1:
            kb.rings = {"mm": (0, 6), "acc": (0, 6), "tr": (7, 1)}
            aring = [kb.bf([8 * 384]) for _ in range(2)]
            agen = adaln_gen(list(range(1, NLAYERS)), aring, kb.P[6], ("ps", 6), pw=384)
        for sb in range(2):
            for c in range(NFC):
                if c % 2 == 0:
                    slot, sk = ring_next(); wv = wview(slot, 8, 512)
                    kb.dma("pool", wv[:, :, 0:256], I["w_up"][l][:, c * 128:c * 128 + 256].rearrange("(kc p) n -> p kc n", p=128), (), (sk,))
                    kb.dma("pool", wv[:, :, 256:512], I["w_up"][l][:, DFF + c * 128:DFF + c * 128 + 256].rearrange("(kc p) n -> p kc n", p=128), (), (sk,))
                hp, hk = kb.ps("tr")
                tiles = {}; tts = {}
                for tb2 in range(2):
                    tb = sb * 2 + tb2; t0 = tb * 512
                    ri = (c % 2) * 2 + tb2
                    for gi in range(2):
                        col0 = gi * 256 + (c % 2) * 128
                        pst, pk = kb.ps("mm")
                        for kc in range(8):
                            kb.mm(pst, wv[:, kc, col0:col0 + 128], hT[:, kc, t0:t0 + 512], kc == 0, kc == 7,
                                  (sk, HK(kc, tb)), (pk,))
                        hcol = None
                        if sb == 0 and tb2 == 1: hcol = 1024
                        if hcol is not None:
                            for kc in range(8):
                                kb.mm(hp[:, gi:gi + 1], wv[:, kc, col0:col0 + 128], hT[:, kc, hcol:hcol + 1], kc == 0, kc == 7,
                                      (sk, HK(kc, hcol // 512)), (hk,))
                        tiles[(tb2, gi)] = (pst, pk)
                        tts[(tb2, gi)] = ((ta_r if gi == 0 else tg_r)[:, ri, :], ("tconv", gi, ri))
                    ccs = [gi * NFC + c for gi in range(2)]
                    for gi in range(2):
                        (pst, pk), (tt_, tk) = tiles[(tb2, gi)], tts[(tb2, gi)]
                        kb.act(tt_, pst, AF.Identity, (pk, "vecs"), (tk,), scale=cw(1, ccs[gi]), bias=cw(3, ccs[gi]))
                        if tb2 == 0:
                            kb.cp("act", es[:, (c % 2) * 2 + gi, 0:1], pst[:, 511:512], (pk,), (("es", (c % 2) * 2 + gi),))
                        if sb == 0 and tb2 == 1:
                            kb.cp("act", eh[:, ccs[gi]:ccs[gi] + 1], pst[:, 511:512], (pk,), (("eh", ccs[gi]),))
                    for gi in range(2):
                        (pst, pk), (tt_, tk) = tiles[(tb2, gi)], tts[(tb2, gi)]
                        kb.stt("dve", tt_[:, 1:512], pst[:, 0:511], cw(0, ccs[gi]), tt_[:, 1:512], ALU.mult, ALU.add, (pk, tk, "vecs"), (tk,))
                    for gi in range(2):
                        (pst, pk), (tt_, tk) = tiles[(tb2, gi)], tts[(tb2, gi)]
                        kb.stt("dve", tt_[:, 0:511], pst[:, 1:512], cw(2, ccs[gi]), tt_[:, 0:511], ALU.mult, ALU.add, (pk, tk, "vecs"), (tk,))
                    for gi in range(2):
                        (pst, pk), (tt_, tk) = tiles[(tb2, gi)], tts[(tb2, gi)]
                        kb.stt("dve", tt_[:, 256:257], pst[:, 255:256], cx(2, ccs[gi]), tt_[:, 256:257], ALU.mult, ALU.add, (pk, tk), (tk,))
                    for gi in range(2):
                        (pst, pk), (tt_, tk) = tiles[(tb2, gi)], tts[(tb2, gi)]
                        kb.stt("dve", tt_[:, 255:256], pst[:, 256:257], cx(3, ccs[gi]), tt_[:, 255:256], ALU.mult, ALU.add, (pk, tk), (tk,))
                    for gi in range(2):
                        (pst, pk), (tt_, tk) = tiles[(tb2, gi)], tts[(tb2, gi)]
                        if tb2 == 0 and sb == 1:
                            kb.act(tt_[:, 0:1], eh[:, ccs[gi]:ccs[gi] + 1], AF.Identity, (("eh", ccs[gi]), tk), (tk,), scale=cx(0, ccs[gi]), bias=tt_[:, 0:1])
                        if tb2 == 1:
                            ek = ("es", (c % 2) * 2 + gi)
                            kb.act(tt_[:, 0:1], es[:, (c % 2) * 2 + gi, 0:1], AF.Identity, (ek, tk), (tk,), scale=cx(0, ccs[gi]), bias=tt_[:, 0:1])
                        if tb2 == 1 and sb == 0:
                            kb.act(tt_[:, 511:512], hp[:, gi:gi + 1], AF.Identity, (hk, tk), (tk,), scale=cx(1, ccs[gi]), bias=tt_[:, 511:512])
                for gi in range(2):
                    (p1, k1), (t0_, tk0) = tiles[(1, gi)], tts[(0, gi)]
                    kb.act(t0_[:, 511:512], p1[:, 0:1], AF.Identity, (k1, tk0), (tk0,), scale=cx(1, gi * NFC + c), bias=t0_[:, 511:512])
                for tb2 in range(2):
                    (ta, tak), (tg, tgk) = tts[(tb2, 0)], tts[(tb2, 1)]
                    kb.act(tg, tg, AF.Silu, (tgk,), (tgk,))
                    kb.tt("dve", actb[:, c, tb2 * 512:(tb2 + 1) * 512], ta, tg, ALU.mult, (tak, tgk), (("actb", c, tb2),))
                if agen is not None and c % 2 == 1:
                    next(agen, None)
            for dp in range(4):
                slot, sk = ring_next(); wd = wview(slot, NFC, 256)
                kb.wload(wd, I["w_down"][l][:, dp * 256:(dp + 1) * 256], sk)
                for d2 in range(2):
                    dch = dp * 2 + d2
                    for tb2 in range(2):
                        tb = sb * 2 + tb2
                        pst, pk = kb.ps("acc")
                        for fc in range(NFC):
                            kb.mm(pst, wd[:, fc, d2 * 128:(d2 + 1) * 128], actb[:, fc, tb2 * 512:(tb2 + 1) * 512], fc == 0, fc == NFC - 1,
                                  (sk, ("actb", fc, tb2)), (pk,))
                        xs = xres[:, dch, tb * 512:(tb + 1) * 512]
                        kb.stt("dve", xs, pst, kb.prm[:, 5, dch:dch + 1], xs, ALU.mult, ALU.add, (pk, XK(dch, tb)), (XK(dch, tb),))
                if agen is not None:
                    next(agen, None)
        if agen is not None:
            for _ in agen:
                pass
        pg.barrier()
        kb.rings = dict(kb.RINGS_DEFAULT)

    def final_out():
        kb.top = MARK
        sq = kb.bf([2, 512]); rs = kb.f32([512]); tmp = kb.f32([2, 512]); yst = kb.f32([4, 512])
        cnt = [0]
        def to_out(c, tb, tm, tk, bias):
            i = cnt[0] % 4; cnt[0] += 1
            kb.cp("act", yst[:, i, :], tm, (tk,), (("yst", i),))
            kb.dma("sp", O["yT"][c * 128:(c + 1) * 128, tb * 512:(tb + 1) * 512], yst[:, i, :], (("yst", i),), ())
        o, _ = VOFF["nfin"]
        norm_mod(lambda c: vecs[:, o + c:o + c + 1], None, sq, rs, tmp, to_out)

    from_mixers = _mixers(kb)
    adaln_first()
    try:
        for l in range(NLAYERS):
            kb.prm = prm_all[:, l]; kb.convx = convx_all[:, l]
            norm_phase(0, 1)
            if l % 2 == 0 and STAGES["even"]:
                from_mixers["even"](l, l // 2)
            if l % 2 == 1 and STAGES["odd"]:
                from_mixers["odd"](l, l // 2)
            if STAGES["ffn"]:
                norm_phase(3, 4)
                ffn(l)
    except _Stop:
        pg.barrier()
    final_out()


def _mixers(kb):
    I, O, pg = kb.I, kb.O, kb.pg
    xres, hT, vecs, ones_bf, ident_bf, prm = kb.xres, kb.hT, kb.vecs, kb.ones_bf, kb.ident_bf, kb.prm
    XK, HK, vv, MARK = kb.XK, kb.HK, kb.vv, kb.MARK
    wview = kb.wview
    fo = VOFF["flags"][0]

    def bfv(pst):
        return pst[:, 0:256].bitcast(BF16)

    def out_proj(wname, e, extra, src=None):
        src = hT if src is None else src
        kb.ring_setup(2, 8 * 512)
        for piece in range(2):
            slot, sk = kb.ring_next(); wo = wview(slot, 8, 512)
            kb.wload(wo, I[wname][e][:, piece * 512:(piece + 1) * 512], sk)
            for d4 in range(4):
                dch = piece * 4 + d4
                for tb in range(4):
                    sl = slice(tb * 512, (tb + 1) * 512)
                    pst, pk = kb.ps("acc")
                    for kc in range(8):
                        if extra is not None and kc >= 6:
                            rhs, rk = extra[:, kc - 6, sl], ("yp", kc - 6, tb)
                        else:
                            rhs, rk = src[:, kc, sl], HK(kc, tb)
                        kb.mm(pst, wo[:, kc, d4 * 128:(d4 + 1) * 128], rhs, kc == 0, kc == 7, (sk, rk), (pk,))
                    xs = xres[:, dch, sl]
                    kb.stt("dve", xs, pst, kb.prm[:, 2, dch:dch + 1], xs, ALU.mult, ALU.add, (pk, XK(dch, tb)), (XK(dch, tb),))
        pg.barrier()

    def rope_combine(dst, dk, pa, pak, pb, pbk, tab, tabk, t1, t2, np_, pre=1.0):
        kb.stt("dve", t1[0:np_, :], pa[0:np_, :], pre, tab[0:np_, 0, :], ALU.mult, ALU.mult, (pak, tabk), ("rt1",))
        kb.stt("dve", t2[0:np_, :], pb[0:np_, :], pre, tab[0:np_, 1, :], ALU.mult, ALU.mult, (pbk, tabk), ("rt2",))
        kb.tt("dve", dst, t1[0:np_, :], t2[0:np_, :], ALU.add, ("rt1", "rt2"), (dk,))

    def even(l, e):
        kb.top = MARK
        qnT = kb.bf([3, T]); qrT = kb.bf([3, T]); cT = kb.bf([2, NK]); krT = kb.bf([NK]); ypT = kb.bf([2, T])
        MARK_E = kb.top
        wtm = kb.bf([8, 544]); wpl = kb.bf([2, 128]); sk = "wtm"; skp = "wpl"
        xp_tok = kb.bf([16, 384]); pooledT = kb.bf([2, T]); lat_st = kb.f32([2, 288]); ctok = kb.bf([2, 256])
        sqt = kb.f32([2, 256]); ssq = kb.f32([2]); ctxl = kb.bf([4, 288]); Pms = [kb.bf([4, 3, 128]) for _ in range(2)]
        kb.memset("dve", xp_tok, 0.0, [("xp", j) for j in range(16)])
        kb.wload(wtm, I["wie_tm"][e], sk)
        kb.dma("pool", ctxl, I["c_mla"][e].rearrange("(b p) n -> p b n", p=128), (), ("ctxl",))
        for j in range(16):
            i = j % 2; tb = j // 4
            p1, k1 = kb.ps("mm"); p2, k2 = kb.ps("mm")
            for kc in range(8):
                lh = hT[:, kc, j * 128:(j + 1) * 128]
                kb.mm(p1[:, 0:288], lh, wtm[:, kc, 0:288], kc == 0, kc == 7, (sk, HK(kc, tb)), (k1,))
                kb.mm(p2[:, 0:256], lh, wtm[:, kc, 288:544], kc == 0, kc == 7, (sk, HK(kc, tb)), (k2,))
            kb.act(sqt[:, i, :], p1[:, 0:256], AF.Square, (k1,), (("sqt", i),))
            pg.add("dve", lambda en, o=ssq[:, i:i + 1], a=sqt[:, i, :]: en.reduce_sum(out=o, in_=a, axis=mybir.AxisListType.X),
                   (("sqt", i),), (("ssq", i),))
            kb.act(ssq[:, i:i + 1], ssq[:, i:i + 1], AF.Sqrt, (("ssq", i),), (("ssq", i),), scale=1.0 / 256, bias=EPS)
            kb.recip(ssq[:, i:i + 1], ssq[:, i:i + 1], (("ssq", i),), (("ssq", i),))
            go = VOFF["gkv"][0] + e * 256
            kb.stt("dve", lat_st[:, i, 0:256], p1[:, 0:256], ssq[:, i:i + 1], vecs[:, go:go + 256], ALU.mult, ALU.mult,
                   (k1, ("ssq", i), "vecs"), (("lat", i),))
            kb.cp("act", lat_st[:, i, 256:288], p1[:, 256:288], (k1,), (("lat", i),))
            kb.dma("sp", O["lat"][e, j * 128:(j + 1) * 128, :], lat_st[:, i, :], (("lat", i),), ())
            kb.cp("dve", ctok[:, i, :], lat_st[:, i, 0:256], (("lat", i),), (("ctok", i),))
            ptr, tk = kb.ps("tr"); pb = bfv(ptr)
            for cc in range(2):
                kb.tr(pb[:, cc * 128:(cc + 1) * 128], ctok[:, i, cc * 128:(cc + 1) * 128], ident_bf, (("ctok", i), "ident"), (tk,))
            kb.cp("act", cT[:, :, j * 128:(j + 1) * 128], pb[:, 0:256].rearrange("p (a b) -> p a b", a=2), (tk,), (("cT", j),))
            for half in range(2):
                dst = xp_tok[:, j, half * 192:(half + 1) * 192].rearrange("p (a b) -> p a b", a=3)[:, 0:3:2, :]
                src = p2[:, half * 128:(half + 1) * 128].rearrange("p (a b) -> p a b", a=2)
                kb.cp("act", dst, src, (k2,), (("xp", j),))
        ck("e_tm")
        for blk in range(4):
            ptr, tk = kb.ps("tr"); pb = bfv(ptr)
            for cc in range(2):
                kb.tr(pb[:, cc * 128:(cc + 1) * 128], ctxl[:, blk, cc * 128:(cc + 1) * 128], ident_bf, ("ctxl", "ident"), (tk,))
            kb.tr(pb[0:32, 256:384], ctxl[:, blk, 256:288], ident_bf, ("ctxl", "ident"), (tk,))
            kb.cp("act", cT[:, :, T + blk * 128:T + (blk + 1) * 128], pb[:, 0:256].rearrange("p (a b) -> p a b", a=2), (tk,), (("cT", 16 + blk),))
            kb.cp("dve", krT[0:32, T + blk * 128:T + (blk + 1) * 128], pb[0:32, 256:384], (tk,), (("krT", 4),))
        ck("e_ctx")
        kb.dma("pool", wpl, I["wpool_bd"][e].rearrange("a k m -> k a m"), (), (skp,))
        for j in range(16):
            pm = Pms[j % 2]; pmk = ("Pm", j % 2)
            kb.dma("pool", pm, I["Pm"][j], (), (pmk,))
            for pr in range(2):
                pst, pk = kb.ps("mm")
                todo = [(gg, sbi) for gg in range(2) for sbi in range(3) if 0 <= j - 1 + sbi <= 15]
                for n_, (gg, sbi) in enumerate(todo):
                    sbk = j - 1 + sbi; c0 = pr * 192 + gg * 64
                    kb.mm(pst[:, 0:128], xp_tok[:, sbk, c0:c0 + 128], pm[:, pr * 2 + gg, sbi, :], n_ == 0, n_ == len(todo) - 1,
                          (("xp", sbk), pmk), (pk,))
                kb.cp("act", pooledT[:, pr, j * 128:(j + 1) * 128], pst[:, 0:128], (pk,), (("pooled", pr, j // 4),))
        pso = VOFF["pscale"][0] + e * 2
        for pr in range(2):
            for tb in range(4):
                sl = slice(tb * 512, (tb + 1) * 512)
                pst, pk = kb.ps("mm")
                kb.mm(pst, wpl[:, pr, :], pooledT[:, pr, sl], True, True, (skp, ("pooled", pr, tb)), (pk,))
                kb.ts("dve", ypT[:, pr, sl], pst, vecs[:, pso + pr:pso + pr + 1], None, ALU.mult, None, (pk, "vecs"), (("yp", pr, tb),))
        ck("e_pool")
        pg.barrier()
        kb.top = MARK_E
        kb.ring_setup(2, 8 * 544)
        qa_f = kb.f32([3, 512]); sq = kb.bf([2, 512]); rs = kb.f32([512]); t1 = kb.f32([512]); t2 = kb.f32([512])
        tabE = [kb.f32([2, 512]) for _ in range(2)]
        slot, sk1 = kb.ring_next(); wkr = wview(slot, 8, 448)
        kb.dma("pool", wkr[:, :, 0:32], I["wie_kr"][e].rearrange("(kc p) n -> p kc n", p=128), (), (sk1,))
        kb.dma("pool", wkr[:, :, 32:64], I["wie_krs"][e].rearrange("(kc p) n -> p kc n", p=128), (), (sk1,))
        kb.dma("pool", wkr[:, :, 64:448], I["wie_qa"][e].rearrange("(kc p) n -> p kc n", p=128), (), (sk1,))
        slot, sk2 = kb.ring_next(); wqr = wview(slot, 3, 768)
        kb.dma("pool", wqr[:, :, 0:384], I["wuq_r"][e].rearrange("(kc p) n -> p kc n", p=128), (), (sk2,))
        kb.dma("pool", wqr[:, :, 384:768], I["wuq_rs"][e].rearrange("(kc p) n -> p kc n", p=128), (), (sk2,))
        gq0 = VOFF["gq"][0] + e * 3
        for tb in range(4):
            sl = slice(tb * 512, (tb + 1) * 512)
            tab = tabE[tb % 2]; tabk = ("tabE", tb % 2)
            kb.dma("sp", tab, I["ropeE"][:, :, sl].rearrange("a p t -> p a t"), (), (tabk,))
            pa, pak = kb.ps("mm"); pb_, pbk = kb.ps("mm")
            for kc in range(8):
                kb.mm(pa[0:32, :], wkr[:, kc, 0:32], hT[:, kc, sl], kc == 0, kc == 7, (sk1, HK(kc, tb)), (pak,))
            for kc in range(8):
                kb.mm(pb_[0:32, :], wkr[:, kc, 32:64], hT[:, kc, sl], kc == 0, kc == 7, (sk1, HK(kc, tb)), (pbk,))
            rope_combine(krT[0:32, sl], ("krT", tb), pa, pak, pb_, pbk, tab, tabk, t1, t2, 32)
            pn, pnk = kb.ps("acc")
            for m in range(3):
                pq, pqk = kb.ps("mm")
                for kc in range(8):
                    kb.mm(pq, wkr[:, kc, 64 + m * 128:64 + (m + 1) * 128], hT[:, kc, sl], kc == 0, kc == 7, (sk1, HK(kc, tb)), (pqk,))
                kb.cp("act", qa_f[:, m, :], pq, (pqk,), (("qa_f", m),))
                kb.act(sq[:, m % 2, :], qa_f[:, m, :], AF.Square, (("qa_f", m),), (("sq", m % 2),))
                kb.mm(pn, ones_bf, sq[:, m % 2, :], m == 0, m == 2, (("sq", m % 2), "ones"), (pnk,))
            kb.act(rs, pn, AF.Sqrt, (pnk,), ("rs",), scale=1.0 / 384, bias=EPS)
            kb.recip(rs, rs, ("rs",), ("rs",))
            for m in range(3):
                kb.stt("dve", qnT[:, m, sl], qa_f[:, m, :], vecs[:, gq0 + m:gq0 + m + 1], rs, ALU.mult, ALU.mult,
                       (("qa_f", m), "rs", "vecs"), (("qnT", m, tb),))
            for m in range(3):
                pa, pak = kb.ps("mm"); pb_, pbk = kb.ps("mm")
                for kc in range(3):
                    kb.mm(pa, wqr[:, kc, m * 128:(m + 1) * 128], qnT[:, kc, sl], kc == 0, kc == 2, (sk2, ("qnT", kc, tb)), (pak,))
                for kc in range(3):
                    kb.mm(pb_, wqr[:, kc, 384 + m * 128:384 + (m + 1) * 128], qnT[:, kc, sl], kc == 0, kc == 2, (sk2, ("qnT", kc, tb)), (pbk,))
                rope_combine(qrT[:, m, sl], ("qrT", m, tb), pa, pak, pb_, pbk, tab, tabk, t1, t2, 128)
        ck("e_ea2")
        pg.barrier()
        kb.top = MARK_E
        kb.rings = {"mm": (0, 4), "acc": (4, 4), "tr": (7, 1)}
        wqn = kb.bf([3, 768]); wk = kb.bf([2, 768]); wvv = kb.bf([2, 768])
        KT = kb.bf([2, NK]); QT = kb.bf([2, T]); Vh = kb.bf([2, 20, 128]); PT = kb.bf([6, 512])
        rDs = kb.f32([2, 512]); rDt = kb.f32([2, 512]); sel = kb.f32([2, 128]); fin_state = []
        kb.memset("dve", rDt, 0.0, ("rDt",)); kb.memset("dve", sel, 0.0, ("sel",))
        kb.memset("dve", sel[64:65, 0, 0:64], 1.0, ("sel",))
        kb.memset("dve", sel[32:33, 1, 64:128], 1.0, ("sel",))
        kb.wload(wqn, I["wuq_n"][e], "wqn"); kb.wload(wk, I["wukv_k"][e], "wk"); kb.wload(wvv, I["wukv_v"][e], "wvv")
        for b in range(2):
            kb.dma("pool", KT[96:105, b, :], I["kmaskE"][:, :], (), (("KTm", b),))
            kb.dma("pool", QT[96:105, b, :], I["qmaskE"][:, :], (), (("QTm", b),))
            kb.dma("sp", KT[64:96, b, :], krT[0:32, :], (), (("KTr", b),))
            kb.memset("dve", Vh[:, b, :, :], 0.0, (("Vh", b),))
            oc = 64 if b == 0 else 32
            kb.memset("dve", Vh[:, b, :, oc:oc + 1], 1.0, (("Vh", b),))

        def proj(h, b):
            kb.dma("sp", QT[64:96, b, :], qrT[(h % 4) * 32:(h % 4) * 32 + 32, h // 4, :], (), (("QTr", b),))
            for tb in range(4):
                sl = slice(tb * 512, (tb + 1) * 512)
                pst, pk = kb.ps("mm")
                for kc in range(3):
                    kb.mm(pst[0:64, :], wqn[:, kc, h * 64:(h + 1) * 64], qnT[:, kc, sl], kc == 0, kc == 2, ("wqn",), (pk,))
                kb.cp("dve", QT[0:64, b, sl], pst[0:64, :], (pk,), (("QTn", b),))
            for k5 in range(5):
                sl = slice(k5 * 512, (k5 + 1) * 512)
                pst, pk = kb.ps("mm")
                for cc in range(2):
                    kb.mm(pst[0:64, :], wk[:, cc, h * 64:(h + 1) * 64], cT[:, cc, sl], cc == 0, cc == 1, ("wk",), (pk,))
                kb.cp("dve", KT[0:64, b, sl], pst[0:64, :], (pk,), (("KTn", b),))
            for g0, nb in ((0, 8), (8, 8), (16, 4)):
                pst, pk = kb.ps("mm")
                for i in range(nb):
                    kblk = g0 + i
                    for cc in range(2):
                        kb.mm(pst[:, i * 64:(i + 1) * 64], cT[:, cc, kblk * 128:(kblk + 1) * 128], wvv[:, cc, h * 64:(h + 1) * 64],
                              cc == 0, cc == 1, ("wvv",), (pk,))
                kb.cp("dve", Vh[:, b, g0:g0 + nb, b * 64:b * 64 + 64], pst[:, 0:nb * 64].rearrange("p (a b) -> p a b", a=nb),
                      (pk,), (("Vh", b),))

        def attn(h, b):
            rd_q = (("QTn", b), ("QTr", b), ("QTm", b)); rd_k = (("KTn", b), ("KTr", b), ("KTm", b))
            hp = slice(b * 64, b * 64 + 64)
            p0 = 64 if b == 0 else 32
            for qc in range(4):
                accO, ok_ = kb.ps("acc")
                pend = []

                def pv(kc, slot):
                    kb.mm(accO, Vh[:, b, kc, :], PT[:, slot, :], kc == 0, kc == 19, (("PT", slot), ("Vh", b)), (ok_,))
                for kc in range(20):
                    slot = (qc * 20 + kc) % 6
                    pst, pk = kb.ps("mm")
                    kb.mm(pst, KT[0:105, b, kc * 128:(kc + 1) * 128], QT[0:105, b, qc * 512:(qc + 1) * 512], True, True, rd_q + rd_k, (pk,))
                    kb.act(PT[:, slot, :], pst, AF.Exp, (pk,), (("PT", slot),), scale=MLA_SCALE)
                    pend.append((kc, slot))
                    if len(pend) > 2:
                        pv(*pend.pop(0))
                    if kc == 4 and fin_state:
                        fin_state.pop(0)()
                while pend:
                    pv(*pend.pop(0))
                ri = (h * 4 + qc) % 2
                rd = rDs[:, ri, :]; rk = ("rD", ri)
                rdt = rDt[:, ri, :]; rtk = ("rDt", ri)
                kb.recip(rdt[p0:p0 + 1, :], accO[p0:p0 + 1, :], (ok_,), (rtk,))

                def fin(accO=accO, ok_=ok_, rd=rd, rk=rk, rdt=rdt, rtk=rtk, qc=qc):
                    bc, bk = kb.ps("acc")
                    kb.mm(bc, sel[:, b, :], rdt, True, True, ("sel", rtk), (bk,))
                    kb.cp("dve", rd[hp, :], bc[hp, :], (bk,), (rk,))
                    kb.tt("dve", hT[hp, h // 2, qc * 512:(qc + 1) * 512], accO[hp, :], rd[hp, :], ALU.mult, (ok_, rk), (HK(h // 2, qc),))
                fin_state.append(fin)

        ck("e_ebsetup")
        proj(0, 0)
        ck("e_proj0")
        for h in range(12):
            if h + 1 < 12: proj(h + 1, (h + 1) % 2)
            attn(h, h % 2)
            ck("e_attn%d" % h)
        while fin_state:
            fin_state.pop(0)()
        pg.barrier()
        kb.rings = dict(kb.RINGS_DEFAULT)
        kb.top = MARK_E
        out_proj("w_out_even", e, ypT)

    def odd(l, e):
        kb.top = MARK
        QM = kb.bf([8, T]); ctxones = kb.bf([128]); zrow = kb.f32([128])
        kb.memset("dve", ctxones, 1.0, ("ctxones",))
        kb.ts("dve", ctxones, ctxones, vecs[:, fo:fo + 1], None, ALU.mult, None, ("ctxones", "vecs"), ("ctxones",))
        kb.memset("dve", zrow, 0.0, ("zrow",))
        MARK_O = kb.top
        KTna = kb.bf([4, NK]); Vna = kb.bf([20, 512])
        MARK_O2 = kb.top
        kb.ring_setup(2, 8 * 512)
        stg = kb.f32([2, 512]); ctxl = kb.bf([4, 512])
        kb.dma("pool", Vna[:, 16:20, :], I["c_na"][e][:, 1, :].rearrange("(b p) n -> p b n", p=128), (), ("Vna_ctx",))
        kb.dma("pool", ctxl, I["c_na"][e][:, 0, :].rearrange("(b p) n -> p b n", p=128), (), ("ctxl",))
        for piece in range(2):
            slot, sk = kb.ring_next(); wv = wview(slot, 8, 512)
            kb.wload(wv, I["wio_nakv"][e][:, piece * 512:(piece + 1) * 512], sk)
            for j in range(16):
                pst, pk = kb.ps("mm")
                for kc in range(8):
                    kb.mm(pst, hT[:, kc, j * 128:(j + 1) * 128], wv[:, kc, :], kc == 0, kc == 7, (sk, HK(kc, j // 4)), (pk,))
                kb.cp("act", stg[:, j % 2, :], pst, (pk,), (("stg", j % 2),))
                kb.dma("sp", O["nakv"][e, j * 128:(j + 1) * 128, piece * 512:(piece + 1) * 512], stg[:, j % 2, :], (("stg", j % 2),), ())
                if piece == 1:
                    kb.cp("dve", Vna[:, j, :], pst, (pk,), (("Vna", j),))
        for blk in range(4):
            ptr, tk = kb.ps("tr"); pb = bfv(ptr)
            for m in range(4):
                kb.tr(pb[:, m * 128:(m + 1) * 128], ctxl[:, blk, m * 128:(m + 1) * 128], ident_bf, ("ctxl", "ident"), (tk,))
            kb.cp("act", KTna[:, :, T + blk * 128:T + (blk + 1) * 128], pb.rearrange("p (a b) -> p a b", a=4), (tk,), (("KTna_ctx", blk),))
        for wname, isq in (("wio_qna", True), ("wio_kna", False)):
            slot, sk = kb.ring_next(); wv = wview(slot, 8, 512)
            kb.wload(wv, I[wname][e], sk)
            for m in range(4):
                for tb in range(4):
                    sl = slice(tb * 512, (tb + 1) * 512)
                    pst, pk = kb.ps("mm")
                    for kc in range(8):
                        kb.mm(pst, wv[:, kc, m * 128:(m + 1) * 128], hT[:, kc, sl], kc == 0, kc == 7, (sk, HK(kc, tb)), (pk,))
                    if isq:
                        kb.act(QM[:, m, sl], pst, AF.Copy, (pk,), [("QM", m, tb * 4 + i) for i in range(4)], scale=0.125)
                    else:
                        kb.cp("dve", KTna[:, m, sl], pst, (pk,), (("KTna", m, tb),))
        pg.barrier()
        kb.top = MARK_O2
        kb.rings = {"mm": (0, 6), "acc": (6, 2), "tr": (7, 1)}
        PT = kb.bf([6, 512]); nab = [kb.bf([2, 5, 128]) for _ in range(2)]; rDs = kb.f32([2, 128])
        units = [(j, i, hh) for j in range(4) for i in range(16) for hh in range(2)]
        st = {}

        def na_S(k):
            j, i, hh = units[k]
            if hh == 0:
                nb_ = nab[(k // 2) % 2]; nk = ("nab", (k // 2) % 2)
                kb.dma("pool", nb_, I["nabias"][e, VAR_OF[i]][:, 2 * j:2 * j + 2, :, :], (), (nk,))
                start = min(max(i - 2, 0), 11)
                tiles = [(start + c, c) for c in range(5)] + [(16 + c, None) for c in range(4)]
                accb, ak = kb.ps("acc")
                st[(j, i)] = (nb_, nk, tiles, accb, ak)
            nb_, nk, tiles, accb, ak = st[(j, i)]
            hp = slice(hh * 64, hh * 64 + 64)
            banks = []
            for t, (kblk, c) in enumerate(tiles):
                if t % 4 == 0:
                    banks.append(kb.ps("mm"))
                pst, pk = banks[-1]; col = slice((t % 4) * 128, (t % 4) * 128 + 128)
                kb.mm(pst[:, col], KTna[hp, j, kblk * 128:(kblk + 1) * 128], QM[hp, j, i * 128:(i + 1) * 128], True, True,
                      (("QM", j, i),), (pk,))
            kb.tt("dve", banks[0][0], banks[0][0], nb_[:, hh, 0:4, :].rearrange("p a b -> p (a b)"), ALU.add, (banks[0][1], nk), (banks[0][1],))
            kb.tt("dve", banks[1][0][:, 0:128], banks[1][0][:, 0:128], nb_[:, hh, 4, :], ALU.add, (banks[1][1], nk), (banks[1][1],))
            slots = []
            for bi, (pst, pk) in enumerate(banks):
                ncol = min(4, 9 - bi * 4) * 128
                sl_ = (k * 3 + bi) % 6
                kb.act(PT[:, sl_, 0:ncol], pst[:, 0:ncol], AF.Exp, (pk,), (("PT", sl_),))
                slots.append(sl_)
            st[(j, i, hh)] = slots

        def na_OD(k):
            j, i, hh = units[k]
            nb_, nk, tiles, accb, ak = st[(j, i)]
            slots = st[(j, i, hh)]
            Oh = accb[:, hh * 128:(hh + 1) * 128]; Dh = accb[:, 256 + hh * 128:256 + (hh + 1) * 128]
            for t, (kblk, c) in enumerate(tiles):
                sl_ = slots[t // 4]
                kb.mm(Oh, Vna[:, kblk, j * 128:(j + 1) * 128], PT[:, sl_, (t % 4) * 128:(t % 4) * 128 + 128], t == 0, t == 8,
                      (("PT", sl_),), (ak,))
            for t, (kblk, c) in enumerate(tiles):
                sl_ = slots[t // 4]
                kb.mm(Dh, ones_bf if c is not None else ctxones, PT[:, sl_, (t % 4) * 128:(t % 4) * 128 + 128], t == 0, t == 8,
                      (("PT", sl_), "ones", "ctxones"), (ak,))
            if hh == 1:
                for h2 in range(2):
                    hp = slice(h2 * 64, h2 * 64 + 64)
                    rd = rDs[:, (k + h2) % 2, :]; rk = ("rD", (k + h2) % 2)
                    kb.recip(rd[hp, :], accb[hp, 256 + h2 * 128:256 + (h2 + 1) * 128], (ak,), (rk,))
                    kb.tt("dve", QM[hp, j, i * 128:(i + 1) * 128], accb[hp, h2 * 128:(h2 + 1) * 128], rd[hp, :], ALU.mult, (ak, rk), (("QM", j, i),))

        na_S(0)
        for k in range(len(units)):
            if k + 1 < len(units): na_S(k + 1)
            na_OD(k)
        pg.barrier()
        kb.rings = dict(kb.RINGS_DEFAULT)
        kb.top = MARK_O
        KTsw = kb.bf([NK]); Vsw = kb.bf([20, 128])
        MARK_S = kb.top
        kb.ring_setup(3, 8 * 512)
        stg = kb.f32([2, 256]); ctxk = kb.bf([4, 128]); t1 = kb.f32([512]); t2 = kb.f32([512])
        tabO = [kb.f32([2, 512]) for _ in range(2)]
        kb.dma("pool", Vsw[:, 16:20, :], I["c_sw"][e][:, 1, :].rearrange("(b p) n -> p b n", p=128), (), ("Vsw_ctx",))
        kb.dma("pool", ctxk, I["c_sw"][e][:, 0, :].rearrange("(b p) n -> p b n", p=128), (), ("ctxk",))
        slot, sk = kb.ring_next(); wv = wview(slot, 8, 512)
        kb.wload(wv[:, :, 0:256], I["wio_swkv"][e], sk)
        kb.dma("pool", wv[:, :, 256:384], I["wio_ksw"][e].rearrange("(kc p) n -> p kc n", p=128), (), (sk,))
        kb.dma("pool", wv[:, :, 384:512], I["wio_ksws"][e].rearrange("(kc p) n -> p kc n", p=128), (), (sk,))
        for j in range(16):
            pst, pk = kb.ps("mm")
            for kc in range(8):
                kb.mm(pst[:, 0:256], hT[:, kc, j * 128:(j + 1) * 128], wv[:, kc, 0:256], kc == 0, kc == 7, (sk, HK(kc, j // 4)), (pk,))
            kb.cp("act", stg[:, j % 2, :], pst[:, 0:256], (pk,), (("stg", j % 2),))
            kb.dma("sp", O["swkv"][e, j * 128:(j + 1) * 128, :], stg[:, j % 2, :], (("stg", j % 2),), ())
            kb.cp("dve", Vsw[:, j, :], pst[:, 128:256], (pk,), (("Vsw", j),))
        ptr, tk = kb.ps("tr"); pb = bfv(ptr)
        for blk in range(4):
            kb.tr(pb[:, blk * 128:(blk + 1) * 128], ctxk[:, blk, :], ident_bf, ("ctxk", "ident"), (tk,))
        kb.cp("act", KTsw[:, T:NK], pb, (tk,), ("KTsw_ctx",))
        slot, skq = kb.ring_next(); wq = wview(slot, 8, 512)
        kb.wload(wq, I["wio_qsw"][e], skq)
        slot, skqs = kb.ring_next(); wqs = wview(slot, 8, 512)
        kb.wload(wqs, I["wio_qsws"][e], skqs)
        for tb in range(4):
            sl = slice(tb * 512, (tb + 1) * 512)
            tab = tabO[tb % 2]; tabk = ("tabO", tb % 2)
            kb.dma("sp", tab, I["ropeO"][:, :, sl].rearrange("a p t -> p a t"), (), (tabk,))
            pa, pak = kb.ps("mm"); pb_, pbk = kb.ps("mm")
            for kc in range(8):
                kb.mm(pa, wv[:, kc, 256:384], hT[:, kc, sl], kc == 0, kc == 7, (sk, HK(kc, tb)), (pak,))
            for kc in range(8):
                kb.mm(pb_, wv[:, kc, 384:512], hT[:, kc, sl], kc == 0, kc == 7, (sk, HK(kc, tb)), (pbk,))
            rope_combine(KTsw[:, sl], ("KTsw", tb), pa, pak, pb_, pbk, tab, tabk, t1, t2, 128)
            for m in range(4):
                pa, pak = kb.ps("mm"); pb_, pbk = kb.ps("mm")
                for kc in range(8):
                    kb.mm(pa, wq[:, kc, m * 128:(m + 1) * 128], hT[:, kc, sl], kc == 0, kc == 7, (skq, HK(kc, tb)), (pak,))
                for kc in range(8):
                    kb.mm(pb_, wqs[:, kc, m * 128:(m + 1) * 128], hT[:, kc, sl], kc == 0, kc == 7, (skqs, HK(kc, tb)), (pbk,))
                rope_combine(QM[:, 4 + m, sl], ("QMs", m, tb), pa, pak, pb_, pbk, tab, tabk, t1, t2, 128, pre=0.125)
        pg.barrier()
        kb.top = MARK_S
        kb.rings = {"mm": (0, 4), "acc": (4, 4), "tr": (7, 1)}
        PT = kb.bf([14, 512]); swb = kb.bf([6, 3, 128]); esink = kb.bf([2, 512]); rDs = kb.f32([2, 512])
        kb.dma("pool", swb, I["swbias"].rearrange("v k c q -> k v c q"), (), ("swb",))
        so = VOFF["sink"][0] + e * 8
        for g in range(2):
            for hd in range(4):
                kb.act(esink[0:1, g, hd * 128:(hd + 1) * 128], zrow[0:1, 0:128], AF.Exp, ("zrow", "vecs"), ("esink",),
                       bias=vecs[0:1, so + 4 * g + hd:so + 4 * g + hd + 1])
        sunits = [(i, g) for i in range(16) for g in range(2)]
        sst = {}

        def sw_S(k):
            i, g = sunits[k]
            hp = slice(g * 64, g * 64 + 64)
            chunks = [(min(max(i - 1 + c, 0), 15), c) for c in range(3)] + [(16 + c, None) for c in range(4)]
            slots = []
            for t, (kblk, c) in enumerate(chunks):
                pst, pk = kb.ps("mm")
                kb.mm(pst, KTsw[hp, kblk * 128:(kblk + 1) * 128], QM[hp, 4:8, i * 128:(i + 1) * 128], True, True, (), (pk,))
                if c is not None:
                    p3 = pst.rearrange("p (a b) -> p a b", a=4)
                    kb.tt("dve", p3, p3, swb[:, VAR_OF[i], c, :].unsqueeze(1).to_broadcast([128, 4, 128]), ALU.add, (pk, "swb"), (pk,))
                sl_ = (k * 7 + t) % 14
                kb.act(PT[:, sl_, :], pst, AF.Exp, (pk,), (("PT", sl_),))
                slots.append(sl_)
            sst[k] = (chunks, slots)

        def sw_OD(k):
            i, g = sunits[k]
            hp = slice(g * 64, g * 64 + 64)
            chunks, slots = sst[k]
            accO, ok_ = kb.ps("acc"); accD, dk_ = kb.ps("acc")
            for t, (kblk, c) in enumerate(chunks):
                kb.mm(accO, Vsw[:, kblk, :], PT[:, slots[t], :], t == 0, t == 6, (("PT", slots[t]),), (ok_,))
            for t, (kblk, c) in enumerate(chunks):
                kb.mm(accD, ones_bf if c is not None else ctxones, PT[:, slots[t], :], t == 0, False, (("PT", slots[t]), "ones", "ctxones"), (dk_,))
            kb.mm(accD, ones_bf[0:1, :], esink[0:1, g, :], False, True, ("esink", "ones"), (dk_,))
            rd = rDs[:, k % 2, :]; rk = ("rD", k % 2)
            kb.recip(rd[hp, :], accD[hp, :], (dk_,), (rk,))
            kb.tt("dve", QM[hp, 4:8, i * 128:(i + 1) * 128], accO[hp, :].rearrange("p (a b) -> p a b", a=4),
                  rd[hp, :].rearrange("p (a b) -> p a b", a=4), ALU.mult, (ok_, rk), (("QMo", i, g),))

        sw_S(0)
        for k in range(len(sunits)):
            if k + 1 < len(sunits): sw_S(k + 1)
            sw_OD(k)
        pg.barrier()
        kb.rings = dict(kb.RINGS_DEFAULT)
        kb.top = MARK_O
        out_proj("w_out_odd", e, None, src=QM)

    return {"even": even, "odd": odd}


_CACHE = {}


def kernel(**inputs):
    maps = _host_inputs(inputs)
    for core, m in enumerate(maps):
        sample = core >= 4
        fl = m.pop("flags")
        m["vecs"] = _pack_vecs(inputs, m.pop("cvec"), fl)
        for k in ("b_mod", "norm_mix", "norm_ffn", "norm_final", "mla_q_norm", "mla_kv_norm", "pool_scale",
                  "swa_sink", "conv_w", "conv_b"):
            m.pop(k, None)
    if "nc" not in _CACHE:
        _CACHE["nc"] = build_program()[0]
    nc = _CACHE["nc"]
    res = run_bass_kernel_spmd(nc, maps, core_ids=list(range(8)))
    R = res.results
    y_prompt = np.concatenate([np.asarray(R[c]["yT"]).T.reshape(8, 256, D) for c in range(4)], 0)
    y_sample = np.stack([np.asarray(R[4 + b]["yT"]).T for b in range(4)], 0)
    lat = np.concatenate([np.asarray(R[c]["lat"]).reshape(2, 8, 256, 288).transpose(1, 0, 2, 3) for c in range(4)], 0)
    na = np.concatenate([np.asarray(R[c]["nakv"]).reshape(2, 8, 256, 2, 8, 64).transpose(1, 0, 2, 3, 4, 5) for c in range(4)], 0)
    sw = np.concatenate([np.asarray(R[c]["swkv"]).reshape(2, 8, 256, 2, 2, 64).transpose(1, 0, 2, 3, 4, 5) for c in range(4)], 0)
    f = lambda a: np.ascontiguousarray(a, dtype=np.float32)
    return (f(y_prompt), f(y_sample), f(lat), f(na), f(sw))
```

```python
import numpy as np
import concourse.bass as bass
import concourse.mybir as mybir
from concourse.bass_utils import run_bass_kernel_spmd
from contextlib import ExitStack

F32, BF16 = mybir.dt.float32, mybir.dt.bfloat16
AF, ALU = mybir.ActivationFunctionType, mybir.AluOpType

D = 1024; T = 2048; DEPTH = 4; PAST = 512; NK = T + PAST
DFF = 2816; NFC = 22
EPS = 1e-6
MLA_SCALE = 96 ** -0.5
BM = 1024.0
NEGB = -30000.0
STAGES = {"even": True, "odd": True, "ffn": True}
STOP_AT = None


class _Stop(Exception):
    pass


def ck(name):
    if STOP_AT == name:
        raise _Stop()
NLAYERS = DEPTH


class Op:
    __slots__ = ("eng", "fn", "deps", "dma", "sem", "target", "pre", "need", "ticket")

    def __init__(self, eng, fn, dma):
        self.eng = eng; self.fn = fn; self.dma = dma; self.deps = []
        self.sem = None; self.target = 0; self.pre = None; self.need = False; self.ticket = 0


class Prog:
    ENGS = ("pe", "act", "dve", "pool", "sp")
    NDS = {"sp": 40, "pool": 40}

    def __init__(self):
        self.q = {e: [] for e in self.ENGS}
        self.lw = {}; self.rd = {}
        self.dsem_next = {e: 0 for e in self.NDS}
        self.dsem_tgt = {}
        self.dma_since = []

    def add(self, eng, fn, reads=(), writes=(), dma=False):
        op = Op(eng, fn, dma)
        deps = []
        for k in reads:
            w = self.lw.get(k)
            if w is not None: deps.append(w)
            if isinstance(k, tuple) and k[0] == "ps":
                for r in self.rd.get(k, ()):
                    if r.eng != eng: deps.append(r)
        for k in writes:
            w = self.lw.get(k)
            if w is not None: deps.append(w)
            for r in self.rd.get(k, ()): deps.append(r)
        seen = set(); dl = []
        for d in deps:
            if d is op or id(d) in seen: continue
            seen.add(id(d))
            if (not d.dma) and d.eng == eng and eng == "pe": continue
            dl.append(d)
        op.deps = dl
        if dma:
            i = self.dsem_next[eng]; self.dsem_next[eng] = (i + 1) % self.NDS[eng]
            key = (eng, i)
            prev = self.dsem_tgt.get(key, 0)
            op.sem = key; op.pre = prev; op.target = prev + 16
            self.dsem_tgt[key] = op.target
            self.dma_since.append(op)
        for k in reads:
            self.rd.setdefault(k, []).append(op)
        for k in writes:
            self.lw[k] = op; self.rd[k] = []
        self.q[eng].append(op)
        return op

    def barrier(self):
        col = Op("dve", "nop", False)
        for e in self.ENGS:
            if self.q[e]:
                last = None
                for o in reversed(self.q[e]):
                    if o.fn is not None and not o.dma:
                        last = o; break
                if last is not None: col.deps.append(last)
        col.deps.extend(self.dma_since)
        self.dma_since = []
        self.q["dve"].append(col)
        for e in self.ENGS:
            if e == "dve": continue
            w = Op(e, None, False); w.deps = [col]
            self.q[e].append(w)
        self.lw = {}; self.rd = {}

    def emit(self, nc, block, csem, dsems):
        for e in self.ENGS:
            for op in self.q[e]:
                for d in op.deps:
                    if not d.dma: d.need = True
        for e in self.ENGS:
            t = 0
            for op in self.q[e]:
                if op.need and not op.dma:
                    t += 1; op.ticket = t
        engobj = {"pe": nc.tensor, "act": nc.scalar, "dve": nc.vector, "pool": nc.gpsimd, "sp": nc.sync}
        self.nwaits = 0

        def run(e, eo):
            waited = {}

            def wait(key, sem, val):
                if val <= 0 or waited.get(key, 0) >= val: return
                waited[key] = val
                eo.wait_ge(sem, val); self.nwaits += 1

            for op in self.q[e]:
                for d in op.deps:
                    if d.dma: wait(d.sem, dsems[d.sem], d.target)
                    else: wait(d.eng, csem[d.eng], d.ticket)
                if op.dma:
                    wait(op.sem, dsems[op.sem], op.pre)
                    op.fn(eo).then_inc(dsems[op.sem], 16)
                elif op.fn is None:
                    pass
                else:
                    ins = eo.nop() if op.fn == "nop" else op.fn(eo)
                    if op.need: ins.then_inc(csem[e], 1)
            for key, tgt in self.dsem_tgt.items():
                if key[0] == e: wait(key, dsems[key], tgt)

        block.tensor(lambda eo: run("pe", eo))
        block.scalar(lambda eo: run("act", eo))
        block.vector(lambda eo: run("dve", eo))
        block.gpsimd(lambda eo: run("pool", eo))
        block.sync(lambda eo: run("sp", eo))


def _bf(x):
    import ml_dtypes
    return np.asarray(x, np.float32).astype(ml_dtypes.bfloat16).astype(np.float32)


def _rope_tables(R, sample):
    half = R // 2; nf = half // 2
    t = np.arange(T)
    freqs = (10000.0 ** (-np.arange(nf, dtype=np.float32) / nf)).astype(np.float32)
    C = np.ones((R, T), np.float32); S = np.zeros((R, T), np.float32)
    if sample:
        for hi, pos in enumerate((t // 64, t % 64)):
            ang = pos.astype(np.float32)[None, :] * freqs[:, None]
            c, s = np.cos(ang).astype(np.float32), np.sin(ang).astype(np.float32)
            b = hi * half
            C[b:b + nf] = c; C[b + nf:b + 2 * nf] = c
            S[b:b + nf] = -s; S[b + nf:b + 2 * nf] = s
    return C, S


def _rope_perm(R):
    half = R // 2; nf = half // 2
    p = np.arange(R)
    for b in (0, half):
        p[b:b + nf] = np.arange(b + nf, b + 2 * nf)
        p[b + nf:b + 2 * nf] = np.arange(b, b + nf)
    return p


VAR_OF = [0, 1] + [2, 3] * 6 + [4, 5]
VAR_REP = [0, 1, 2, 3, 14, 15]


def _struct_consts(sample):
    c = {}
    c["ident"] = np.eye(128, dtype=np.float32)
    Ce, Se = _rope_tables(32, sample)
    c["ropeE"] = np.stack([np.tile(Ce, (4, 1)), np.tile(Se, (4, 1))], 0)
    Co, So = _rope_tables(64, sample)
    c["ropeO"] = np.stack([np.tile(Co, (2, 1)), np.tile(So, (2, 1))], 0)
    km = np.zeros((9, NK), np.float32); qm = np.zeros((9, T), np.float32)
    km[8, :] = 1.0; qm[8, :] = -BM
    if sample:
        km[0, :] = 1.0; qm[0, :] = BM
    else:
        for s in range(8):
            km[s, s * 256:(s + 1) * 256] = 1.0
            qm[s, s * 256:(s + 1) * 256] = BM
    c["kmaskE"] = km; c["qmaskE"] = qm
    fl = 1.0 if sample else 0.0
    c["flags"] = np.tile(np.array([[fl, 1.0 - fl]], np.float32), (128, 1))
    n = T if sample else 256
    Pm = np.zeros((16, 128, 4, 3, 128), np.float32)
    tt = np.arange(T); tl = tt % n; base = tt - tl
    for g, w in enumerate((2, 4, 8, 16)):
        lo = np.clip(tl - w // 2, 0, n); hi = np.clip(tl + w // 2, 0, n)
        cnt = (hi - lo).astype(np.float32)
        for t in range(T):
            j = t // 128
            for s in range(base[t] + lo[t], base[t] + hi[t]):
                sb = s // 128 - (j - 1)
                Pm[j, s % 128, g, sb, t % 128] += 1.0 / cnt[t]
            Pm[j, t % 128, g, 1, t % 128] -= 1.0
    c["Pm"] = Pm
    swb = np.full((6, 128, 3, 128), NEGB, np.float32)
    for v, i in enumerate(VAR_REP):
        tq = i * 128 + np.arange(128)
        for cc in range(3):
            kb = i - 1 + cc
            if kb < 0 or kb > 15: continue
            ks = kb * 128 + np.arange(128)
            if sample:
                ok = np.abs(tq[None, :] - ks[:, None]) <= 128
            else:
                ok = (tq[None, :] // 256) == (ks[:, None] // 256)
            swb[v, :, cc, :] = np.where(ok, 0.0, NEGB)
    c["swbias"] = swb
    return c


def _na_bias(sample, rpb):
    out = np.full((2, 6, 128, 8, 5, 128), NEGB, np.float32)
    for v, i in enumerate(VAR_REP):
        start = min(max(i - 2, 0), 11)
        tq = i * 128 + np.arange(128)
        r, cq = tq // 64, tq % 64
        for cc in range(5):
            ks = (start + cc) * 128 + np.arange(128)
            if sample:
                rp, cp = ks // 64, ks % 64
                r0 = np.clip(r - 4, 0, 24); c0 = np.clip(cq - 8, 0, 48)
                ok = ((rp[:, None] >= r0[None, :]) & (rp[:, None] < r0[None, :] + 8) &
                      (cp[:, None] >= c0[None, :]) & (cp[:, None] < c0[None, :] + 16))
                dr = np.clip(rp[:, None] - r[None, :] + 7, 0, 14)
                dc = np.clip(cp[:, None] - cq[None, :] + 15, 0, 30)
                for l in range(2):
                    g = rpb[l][:, dr, dc]
                    out[l, v, :, :, cc, :] = np.where(ok[:, None, :], g.transpose(1, 0, 2), NEGB)
            else:
                ok = (tq[None, :] // 256) == (ks[:, None] // 256)
                out[:, v, :, :, cc, :] = np.where(ok, 0.0, NEGB)[None, :, None, :]
    return out


def _host_inputs(inp):
    f = lambda a: np.ascontiguousarray(np.asarray(a, np.float32))
    w = {}
    for k in ("w_mod", "b_mod", "norm_mix", "norm_ffn", "norm_final", "mla_q_norm", "mla_kv_norm",
              "pool_scale", "w_out_even", "w_out_odd", "swa_sink", "w_up", "conv_w", "conv_b", "w_down"):
        w[k] = f(inp[k])
    rowperm = np.concatenate([np.arange(512)] + [np.concatenate([512 + jj * 64 + np.arange(64), 512 + (4 + jj) * 64 + np.arange(64)])
                                                  for jj in range(4)])
    w["w_out_odd"] = f(w["w_out_odd"][:, rowperm, :])
    wie = f(inp["w_in_even"])
    w["wie_qa"] = f(wie[:, :, 0:384]); w["wie_tm"] = f(wie[:, :, 384:928])
    w["wie_kr"] = f(wie[:, :, 640:672]); w["wie_krs"] = f(wie[:, :, 640 + _rope_perm(32)])
    wuq = f(inp["w_uq"]).reshape(2, 384, 12, 96)
    w["wuq_n"] = f(wuq[..., 0:64].reshape(2, 384, 768))
    w["wuq_r"] = f(wuq[..., 64:96].reshape(2, 384, 384))
    w["wuq_rs"] = f(wuq[..., 64 + _rope_perm(32)].reshape(2, 384, 384))
    wukv = f(inp["w_ukv"]).reshape(2, 256, 12, 128)
    w["wukv_k"] = f(wukv[..., 0:64].reshape(2, 256, 768)); w["wukv_v"] = f(wukv[..., 64:128].reshape(2, 256, 768))
    wp = f(inp["w_pool"]); bd = np.zeros((2, 2, 128, 128), np.float32)
    for e in range(2):
        for pr in range(2):
            bd[e, pr, 0:64, 0:64] = wp[e, 2 * pr]; bd[e, pr, 64:128, 64:128] = wp[e, 2 * pr + 1]
    w["wpool_bd"] = bd
    wio = f(inp["w_in_odd"])
    w["wio_qna"] = f(wio[:, :, 0:512]); w["wio_kna"] = f(wio[:, :, 512:1024])
    w["wio_nakv"] = f(wio[:, :, 512:1536]); w["wio_swkv"] = f(wio[:, :, 2048:2304])
    qsw = wio[:, :, 1536:2048].reshape(2, 1024, 8, 64)
    order = [0, 4, 1, 5, 2, 6, 3, 7]
    w["wio_qsw"] = f(qsw[:, :, order, :].reshape(2, 1024, 512))
    w["wio_qsws"] = f(qsw[:, :, order, :][..., _rope_perm(64)].reshape(2, 1024, 512))
    ksw = wio[:, :, 2048:2176].reshape(2, 1024, 2, 64)
    w["wio_ksw"] = f(ksw.reshape(2, 1024, 128)); w["wio_ksws"] = f(ksw[..., _rope_perm(64)].reshape(2, 1024, 128))
    sc = {True: _struct_consts(True), False: _struct_consts(False)}
    rpb = f(inp["na_rpb"])
    nab = {True: _na_bias(True, rpb), False: _na_bias(False, rpb)}
    xp = f(inp["x_prompt"]); xs = f(inp["x_sample"])
    maps = []
    for core in range(8):
        sample = core >= 4
        m = dict(w)
        m.update(sc[sample]); m["nabias"] = nab[sample]
        if sample:
            b = core - 4
            m["xT"] = f(xs[b].T)
            m["cvec"] = f(inp["c"])[b]
            m["c_mla"] = f(inp["cache_mla_latent"])[b]
            m["c_na"] = f(inp["cache_na_kv"])[b].reshape(2, 512, 2, 512)
            m["c_sw"] = f(inp["cache_swa_kv"])[b].reshape(2, 512, 2, 128)
        else:
            m["xT"] = f(xp[8 * core:8 * core + 8].reshape(T, D).T)
            m["cvec"] = f(inp["c_ctx"])
            m["c_mla"] = np.zeros((2, 512, 288), np.float32)
            m["c_na"] = np.zeros((2, 512, 2, 512), np.float32)
            m["c_sw"] = np.zeros((2, 512, 2, 128), np.float32)
        maps.append(m)
    return maps


IN_SHAPES = {
    "xT": [D, T], "cvec": [D], "c_mla": [2, 512, 288], "c_na": [2, 512, 2, 512], "c_sw": [2, 512, 2, 128],
    "w_mod": [4, D, 6 * D], "b_mod": [4, 6 * D], "norm_mix": [4, D], "norm_ffn": [4, D], "norm_final": [D],
    "mla_q_norm": [2, 384], "mla_kv_norm": [2, 256], "pool_scale": [2, 256],
    "w_out_even": [2, D, D], "w_out_odd": [2, D, D], "swa_sink": [2, 8],
    "w_up": [4, D, 2 * DFF], "conv_w": [4, 3, 2 * DFF], "conv_b": [4, 2 * DFF], "w_down": [4, DFF, D],
    "wie_qa": [2, D, 384], "wie_tm": [2, D, 544], "wie_kr": [2, D, 32], "wie_krs": [2, D, 32],
    "wuq_n": [2, 384, 768], "wuq_r": [2, 384, 384], "wuq_rs": [2, 384, 384],
    "wukv_k": [2, 256, 768], "wukv_v": [2, 256, 768], "wpool_bd": [2, 2, 128, 128],
    "wio_qna": [2, D, 512], "wio_kna": [2, D, 512], "wio_nakv": [2, D, 1024], "wio_swkv": [2, D, 256],
    "wio_qsw": [2, D, 512], "wio_qsws": [2, D, 512], "wio_ksw": [2, D, 128], "wio_ksws": [2, D, 128],
    "ident": [128, 128], "ropeE": [2, 128, T], "ropeO": [2, 128, T], "kmaskE": [9, NK], "qmaskE": [9, T],
    "flags": [128, 2], "Pm": [16, 128, 4, 3, 128], "swbias": [6, 128, 3, 128], "nabias": [2, 6, 128, 8, 5, 128],
}
OUT_SHAPES = {"yT": [D, T], "lat": [2, T, 288], "nakv": [2, T, 1024], "swkv": [2, T, 256]}


class KB:
    def __init__(self, nc, big, psums, ins, outs):
        self.nc = nc; self.big = big; self.P = psums; self.I = ins; self.O = outs
        self.pg = Prog()
        self.top = 0
        self.rr = {"mm": 0, "acc": 0, "tr": 0}
        self.RINGS_DEFAULT = {"mm": (0, 4), "acc": (4, 3), "tr": (7, 1)}
        self.rings = dict(self.RINGS_DEFAULT)
        self.uid = 0

    def alloc(self, nbytes):
        off = self.top; self.top += (nbytes + 7) // 8 * 2
        assert self.top * 4 <= 212800, f"SBUF overflow {self.top * 4}"
        return off

    def _shape(self, v, shape):
        if len(shape) == 1: return v
        if len(shape) == 2: return v.rearrange("p (a b) -> p a b", a=shape[0])
        if len(shape) == 3: return v.rearrange("p (a b c) -> p a b c", a=shape[0], b=shape[1])
        return v.rearrange("p (a b c d) -> p a b c d", a=shape[0], b=shape[1], c=shape[2])

    def f32(self, shape):
        n = int(np.prod(shape)); off = self.alloc(n * 4)
        return self._shape(self.big[:, off:off + n], shape)

    def bf(self, shape):
        n = int(np.prod(shape)); off = self.alloc(n * 2)
        return self._shape(self.big[:, off:off + (n + 1) // 2].bitcast(BF16)[:, 0:n], shape)

    def key(self, name):
        self.uid += 1
        return (name, self.uid)

    def ps(self, ring):
        lo, n = self.rings[ring]
        i = lo + self.rr[ring] % n; self.rr[ring] += 1
        return self.P[i], ("ps", i)

    def mm(self, out, lhsT, rhs, start, stop, reads, writes):
        return self.pg.add("pe", lambda e: e.matmul(out, lhsT, rhs, start=start, stop=stop), reads, writes)

    def tr(self, out, in_, ident, reads, writes):
        return self.pg.add("pe", lambda e: e.transpose(out, in_, ident), reads, writes)

    def act(self, out, in_, func, reads, writes, scale=1.0, bias=0.0, accum=None):
        if accum is None:
            return self.pg.add("act", lambda e: e.activation(out=out, in_=in_, func=func, bias=bias, scale=scale), reads, writes)
        return self.pg.add("act", lambda e: e.activation(out=out, in_=in_, func=func, bias=bias, scale=scale, accum_out=accum), reads, writes)

    def ts(self, eng, out, in0, s1, s2, op0, op1, reads, writes):
        if s2 is None:
            return self.pg.add(eng, lambda e: e.tensor_scalar(out=out, in0=in0, scalar1=s1, scalar2=None, op0=op0), reads, writes)
        return self.pg.add(eng, lambda e: e.tensor_scalar(out=out, in0=in0, scalar1=s1, scalar2=s2, op0=op0, op1=op1), reads, writes)

    def stt(self, eng, out, in0, scalar, in1, op0, op1, reads, writes):
        return self.pg.add(eng, lambda e: e.scalar_tensor_tensor(out=out, in0=in0, scalar=scalar, in1=in1, op0=op0, op1=op1), reads, writes)

    def tt(self, eng, out, in0, in1, op, reads, writes):
        return self.pg.add(eng, lambda e: e.tensor_tensor(out=out, in0=in0, in1=in1, op=op), reads, writes)

    def cp(self, eng, out, in_, reads, writes):
        if eng == "act":
            return self.pg.add("act", lambda e: e.copy(out=out, in_=in_), reads, writes)
        return self.pg.add(eng, lambda e: e.tensor_copy(out=out, in_=in_), reads, writes)

    def recip(self, out, in_, reads, writes):
        return self.pg.add("dve", lambda e: e.reciprocal(out=out, in_=in_), reads, writes)

    def memset(self, eng, ap, val, writes):
        return self.pg.add(eng, lambda e: e.memset(ap, val), (), writes)

    def dma(self, q, out, in_, reads, writes):
        return self.pg.add(q, lambda e: e.dma_start(out=out, in_=in_), reads, writes, dma=True)

    def wload(self, dst, src_ap, key):
        return self.dma("pool", dst, src_ap.rearrange("(kc p) n -> p kc n", p=128), (), (key,))


VOFF = {}
def _voff():
    o = 0
    for name, n in (("bmod", 192), ("nmix", 32), ("nffn", 32), ("nfin", 8), ("conv", 704), ("gq", 6),
                    ("pscale", 4), ("cvec", 8), ("gkv", 512), ("sink", 16), ("flags", 2)):
        VOFF[name] = (o, n); o += n
    return o
NV = _voff()


def _pack_vecs(inp, cvec, flags):
    v = np.zeros((128, NV), np.float32)
    def put(name, arr):
        o, n = VOFF[name]; v[:, o:o + n] = np.asarray(arr, np.float32).reshape(128, n)
    f = lambda a: np.asarray(a, np.float32)
    put("bmod", f(inp["b_mod"]).reshape(4, 48, 128).transpose(2, 0, 1))
    put("nmix", f(inp["norm_mix"]).reshape(4, 8, 128).transpose(2, 0, 1))
    put("nffn", f(inp["norm_ffn"]).reshape(4, 8, 128).transpose(2, 0, 1))
    put("nfin", f(inp["norm_final"]).reshape(8, 128).transpose(1, 0))
    cw = np.concatenate([f(inp["conv_w"]), f(inp["conv_b"])[:, None, :]], 1)
    put("conv", cw.reshape(4, 4, 44, 128).transpose(3, 0, 1, 2))
    put("gq", f(inp["mla_q_norm"]).reshape(2, 3, 128).transpose(2, 0, 1))
    put("pscale", f(inp["pool_scale"]).reshape(2, 2, 128).transpose(2, 0, 1))
    put("cvec", f(cvec).reshape(8, 128).transpose(1, 0))
    put("gkv", np.broadcast_to(f(inp["mla_kv_norm"]).reshape(1, 512), (128, 512)))
    put("sink", np.broadcast_to(f(inp["swa_sink"]).reshape(1, 16), (128, 16)))
    put("flags", flags)
    return v


def build_program():
    nc = bass.Bass("TRN2", target_bir_lowering=False)
    shapes = dict(IN_SHAPES); shapes["vecs"] = [128, NV]
    for k in ("cvec", "b_mod", "norm_mix", "norm_ffn", "norm_final", "mla_q_norm", "mla_kv_norm", "pool_scale",
              "swa_sink", "conv_w", "conv_b", "flags"):
        shapes.pop(k)
    I = {k: nc.dram_tensor(k, s, F32, kind="ExternalInput").ap() for k, s in shapes.items()}
    O = {k: nc.dram_tensor(k, s, F32, kind="ExternalOutput").ap() for k, s in OUT_SHAPES.items()}
    es = ExitStack()
    with es:
        big = es.enter_context(nc.sbuf_tensor("big", [128, 53200], F32))
        P = [es.enter_context(nc.psum_tensor(f"psb{i}", [128, 512], F32)) for i in range(8)]
        csem = {e: es.enter_context(nc.semaphore(f"c_{e}")) for e in Prog.ENGS}
        dsems = {(q, i): es.enter_context(nc.semaphore(f"d_{q}{i}")) for q in Prog.NDS for i in range(Prog.NDS[q])}
        block = es.enter_context(nc.Block())
        kb = KB(nc, big, [p[:, :] for p in P], I, O)
        _emit_all(kb)
        kb.pg.emit(nc, block, csem, dsems)
    return nc, kb


def _emit_all(kb):
    I, O, pg = kb.I, kb.O, kb.pg
    xres = kb.f32([8, T]); hT = kb.bf([8, T])
    vecs = kb.f32([NV])
    ones_bf = kb.bf([128]); ident_bf = kb.bf([128])
    scv = kb.bf([8]); modT_all = kb.f32([4, 48]); prm_all = kb.f32([4, 6, 8]); convx_all = kb.f32([4, 4, 44])
    prm = prm_all[:, 0]; convx = convx_all[:, 0]
    MARK = kb.top
    kb.xres, kb.hT, kb.vecs, kb.ones_bf, kb.ident_bf, kb.prm = xres, hT, vecs, ones_bf, ident_bf, prm
    kb.MARK = MARK

    def vv(name, l=None, per=None):
        o, n = VOFF[name]
        if l is None: return vecs[:, o:o + n]
        return vecs[:, o + l * per:o + (l + 1) * per]
    kb.vv = vv

    XK = lambda c, tb: ("x", c, tb)
    HK = lambda c, tb: ("h", c, tb)
    kb.XK, kb.HK = XK, HK
    kb.dma("sp", vecs, I["vecs"][:, :], (), ("vecs",))
    for c in range(8):
        kb.dma("sp", xres[:, c, :], I["xT"][c * 128:(c + 1) * 128, :], (), [XK(c, tb) for tb in range(4)])
    kb.dma("pool", ident_bf, I["ident"][:, :], (), ("ident",))
    kb.memset("dve", ones_bf, 1.0, ("ones",))
    kb.act(scv, vv("cvec"), AF.Silu, ("vecs",), ("scv",))

    def ring_setup(nslots, nel):
        kb.ring = [kb.bf([nel]) for _ in range(nslots)]
        kb.ring_i = 0; kb.ring_nel = nel

    def ring_next():
        i = kb.ring_i % len(kb.ring); kb.ring_i += 1
        return kb.ring[i], ("wr", i)
    kb.ring_setup, kb.ring_next = ring_setup, ring_next

    def wview(slot, kc, n):
        return slot[:, 0:kc * n].rearrange("p (a b) -> p a b", a=kc)
    kb.wview = wview

    def norm_mod(Acol, Bcol, sq, rs, tmp, dst_fn, final=False):
        nsq = sq.shape[1]; ntm = tmp.shape[1]
        two_rs = len(rs.shape) == 3
        rss = []
        for tb in range(4):
            sl = slice(tb * 512, (tb + 1) * 512)
            pst, pk = kb.ps("mm")
            for c in range(8):
                i = (tb * 8 + c) % nsq
                s = sq[:, i, :]
                kb.act(s, xres[:, c, sl], AF.Square, (XK(c, tb),), (("sq", i),))
                kb.mm(pst, ones_bf, s, c == 0, c == 7, (("sq", i), "ones"), (pk,))
            r = rs[:, tb % 2, :] if two_rs else rs
            rk = ("rs", tb % 2) if two_rs else "rs"
            kb.act(r, pst, AF.Sqrt, (pk,), (rk,), scale=1.0 / D, bias=EPS)
            kb.recip(r, r, (rk,), (rk,))
            for c in range(8):
                i = (tb * 8 + c) % ntm
                tm = tmp[:, i, :]
                kb.stt("dve", tm, xres[:, c, sl], Acol(c), r, ALU.mult, ALU.mult,
                       (XK(c, tb), rk, "prm", "vecs"), (("tmp", i),))
                dst_fn(c, tb, tm, ("tmp", i), Bcol(c) if Bcol else 0.0)
    kb.norm_mod = norm_mod

    def to_hT(c, tb, tm, tk, bias):
        kb.act(hT[:, c, tb * 512:(tb + 1) * 512], tm, AF.Identity, (tk, "prm"), (HK(c, tb),), bias=bias)

    def adaln_gen(layers, aring, pst, pk, pw=512):
        cnt = 0
        for l in layers:
            modT = modT_all[:, l]; prm = prm_all[:, l]; convx = convx_all[:, l]
            for piece in range(6144 // pw):
                slot, sk = aring[cnt % len(aring)], ("awr", cnt % len(aring)); cnt += 1
                wv = wview(slot, 8, pw)
                kb.wload(wv, I["w_mod"][l][:, piece * pw:(piece + 1) * pw], sk)
                for cc in range(pw // 128):
                    j = piece * (pw // 128) + cc
                    for kc in range(8):
                        kb.mm(pst[:, j:j + 1], wv[:, kc, cc * 128:(cc + 1) * 128], scv[:, kc:kc + 1], kc == 0, kc == 7,
                              (sk, "scv"), (pk,))
                yield
            mk = ("modT", l); pk_ = ("prm", l)
            kb.tt("dve", modT, pst[:, 0:48], vv("bmod", l, 48), ALU.add, (pk, "vecs"), (mk,))
            for (row, sc_i, g) in ((0, 1, "nmix"), (3, 4, "nffn")):
                kb.stt("dve", prm[:, row, :], modT[:, sc_i * 8:(sc_i + 1) * 8], 1.0, vv(g, l, 8), ALU.add, ALU.mult,
                       (mk, "vecs"), (pk_,))
            for (row, m_i) in ((1, 0), (2, 2), (4, 3), (5, 5)):
                kb.cp("dve", prm[:, row, :], modT[:, m_i * 8:(m_i + 1) * 8], (mk,), (pk_,))
            o, _ = VOFF["conv"]; cb = o + l * 176
            fo, _ = VOFF["flags"]
            w0 = vecs[:, cb:cb + 44]; w2 = vecs[:, cb + 88:cb + 132]
            kb.ts("dve", convx[:, 0, :], w0, vecs[:, fo:fo + 1], None, ALU.mult, None, ("vecs",), (("convx", l),))
            kb.ts("dve", convx[:, 1, :], w2, vecs[:, fo:fo + 1], None, ALU.mult, None, ("vecs",), (("convx", l),))
            kb.ts("dve", convx[:, 2, :], w0, vecs[:, fo + 1:fo + 2], -1.0, ALU.mult, ALU.mult, ("vecs",), (("convx", l),))
            kb.ts("dve", convx[:, 3, :], w2, vecs[:, fo + 1:fo + 2], -1.0, ALU.mult, ALU.mult, ("vecs",), (("convx", l),))
            yield

    def adaln_first():
        kb.top = MARK
        aring = [kb.bf([8 * 512]) for _ in range(3)]
        pst, pk = kb.ps("acc")
        for _ in adaln_gen([0], aring, pst, pk):
            pass
        pg.barrier()

    def norm_phase(row_a, row_b):
        kb.top = MARK
        sq = kb.bf([4, 512]); rs = kb.f32([2, 512]); tmp = kb.f32([4, 512])
        norm_mod(lambda c: kb.prm[:, row_a, c:c + 1], lambda c: kb.prm[:, row_b, c:c + 1], sq, rs, tmp, to_hT)
        pg.barrier()

    def ffn(l):
        kb.top = MARK
        o, _ = VOFF["conv"]; cb = o + l * 176
        cw = lambda k, c: vecs[:, cb + k * 44 + c:cb + k * 44 + c + 1]
        cx = lambda k, c: kb.convx[:, k, c:c + 1]
        ring_setup(2 if (l == 0 and NLAYERS > 1) else 3, 22 * 256)
        kb.rings = {"mm": (0, 7), "acc": (0, 7), "tr": (7, 1)}
        actb = kb.bf([NFC, 1024]); ta_r = kb.f32([4, 512]); tg_r = kb.f32([4, 512]); es = kb.f32([4, 2]); eh = kb.f32([44])
        agen = None
        if l == 0 and NLAYERS > 1:
            kb.rings = {"mm": (0, 6), "acc": (0, 6), "tr": (7, 1)}
            aring = [kb.bf([8 * 384]) for _ in range(2)]
            agen = adaln_gen(list(range(1, NLAYERS)), aring, kb.P[6], ("ps", 6), pw=384)
        for sb in range(2):
            for c in range(NFC):
                if c % 2 == 0:
                    slot, sk = ring_next(); wv = wview(slot, 8, 512)
                    kb.dma("pool", wv[:, :, 0:256], I["w_up"][l][:, c * 128:c * 128 + 256].rearrange("(kc p) n -> p kc n", p=128), (), (sk,))
                    kb.dma("pool", wv[:, :, 256:512], I["w_up"][l][:, DFF + c * 128:DFF + c * 128 + 256].rearrange("(kc p) n -> p kc n", p=128), (), (sk,))
                hp, hk = kb.ps("tr")
                tiles = {}; tts = {}
                for tb2 in range(2):
                    tb = sb * 2 + tb2; t0 = tb * 512
                    ri = (c % 2) * 2 + tb2
                    for gi in range(2):
                        col0 = gi * 256 + (c % 2) * 128
                        pst, pk = kb.ps("mm")
                        for kc in range(8):
                            kb.mm(pst, wv[:, kc, col0:col0 + 128], hT[:, kc, t0:t0 + 512], kc == 0, kc == 7,
                                  (sk, HK(kc, tb)), (pk,))
                        hcol = None
                        if sb == 0 and tb2 == 1: hcol = 1024
                        if hcol is not None:
                            for kc in range(8):
                                kb.mm(hp[:, gi:gi + 1], wv[:, kc, col0:col0 + 128], hT[:, kc, hcol:hcol + 1], kc == 0, kc == 7,
                                      (sk, HK(kc, hcol // 512)), (hk,))
                        tiles[(tb2, gi)] = (pst, pk)
                        tts[(tb2, gi)] = ((ta_r if gi == 0 else tg_r)[:, ri, :], ("tconv", gi, ri))
                    ccs = [gi * NFC + c for gi in range(2)]
                    for gi in range(2):
                        (pst, pk), (tt_, tk) = tiles[(tb2, gi)], tts[(tb2, gi)]
                        kb.act(tt_, pst, AF.Identity, (pk, "vecs"), (tk,), scale=cw(1, ccs[gi]), bias=cw(3, ccs[gi]))
                        if tb2 == 0:
                            kb.cp("act", es[:, (c % 2) * 2 + gi, 0:1], pst[:, 511:512], (pk,), (("es", (c % 2) * 2 + gi),))
                        if sb == 0 and tb2 == 1:
                            kb.cp("act", eh[:, ccs[gi]:ccs[gi] + 1], pst[:, 511:512], (pk,), (("eh", ccs[gi]),))
                    for gi in range(2):
                        (pst, pk), (tt_, tk) = tiles[(tb2, gi)], tts[(tb2, gi)]
                        kb.stt("dve", tt_[:, 1:512], pst[:, 0:511], cw(0, ccs[gi]), tt_[:, 1:512], ALU.mult, ALU.add, (pk, tk, "vecs"), (tk,))
                    for gi in range(2):
                        (pst, pk), (tt_, tk) = tiles[(tb2, gi)], tts[(tb2, gi)]
                        kb.stt("dve", tt_[:, 0:511], pst[:, 1:512], cw(2, ccs[gi]), tt_[:, 0:511], ALU.mult, ALU.add, (pk, tk, "vecs"), (tk,))
                    for gi in range(2):
                        (pst, pk), (tt_, tk) = tiles[(tb2, gi)], tts[(tb2, gi)]
                        kb.stt("dve", tt_[:, 256:257], pst[:, 255:256], cx(2, ccs[gi]), tt_[:, 256:257], ALU.mult, ALU.add, (pk, tk), (tk,))
                    for gi in range(2):
                        (pst, pk), (tt_, tk) = tiles[(tb2, gi)], tts[(tb2, gi)]
                        kb.stt("dve", tt_[:, 255:256], pst[:, 256:257], cx(3, ccs[gi]), tt_[:, 255:256], ALU.mult, ALU.add, (pk, tk), (tk,))
                    for gi in range(2):
                        (pst, pk), (tt_, tk) = tiles[(tb2, gi)], tts[(tb2, gi)]
                        if tb2 == 0 and sb == 1:
                            kb.act(tt_[:, 0:1], eh[:, ccs[gi]:ccs[gi] + 1], AF.Identity, (("eh", ccs[gi]), tk), (tk,), scale=cx(0, ccs[gi]), bias=tt_[:, 0:1])
                        if tb2 == 1:
                            ek = ("es", (c % 2) * 2 + gi)
                            kb.act(tt_[:, 0:1], es[:, (c % 2) * 2 + gi, 0:1], AF.Identity, (ek, tk), (tk,), scale=cx(0, ccs[gi]), bias=tt_[:, 0:1])
                        if tb2 == 1 and sb == 0:
                            kb.act(tt_[:, 511:512], hp[:, gi:gi + 1], AF.Identity, (hk, tk), (tk,), scale=cx(1, ccs[gi]), bias=tt_[:, 511:512])
                for gi in range(2):
                    (p1, k1), (t0_, tk0) = tiles[(1, gi)], tts[(0, gi)]
                    kb.act(t0_[:, 511:512], p1[:, 0:1], AF.Identity, (k1, tk0), (tk0,), scale=cx(1, gi * NFC + c), bias=t0_[:, 511:512])
                for tb2 in range(2):
                    (ta, tak), (tg, tgk) = tts[(tb2, 0)], tts[(tb2, 1)]
                    kb.act(tg, tg, AF.Silu, (tgk,), (tgk,))
                    kb.tt("dve", actb[:, c, tb2 * 512:(tb2 + 1) * 512], ta, tg, ALU.mult, (tak, tgk), (("actb", c, tb2),))
                if agen is not None and c % 2 == 1:
                    next(agen, None)
            for dp in range(4):
                slot, sk = ring_next(); wd = wview(slot, NFC, 256)
                kb.wload(wd, I["w_down"][l][:, dp * 256:(dp + 1) * 256], sk)
                for d2 in range(2):
                    dch = dp * 2 + d2
                    for tb2 in range(2):
                        tb = sb * 2 + tb2
                        pst, pk = kb.ps("acc")
                        for fc in range(NFC):
                            kb.mm(pst, wd[:, fc, d2 * 128:(d2 + 1) * 128], actb[:, fc, tb2 * 512:(tb2 + 1) * 512], fc == 0, fc == NFC - 1,
                                  (sk, ("actb", fc, tb2)), (pk,))
                        xs = xres[:, dch, tb * 512:(tb + 1) * 512]
                        kb.stt("dve", xs, pst, kb.prm[:, 5, dch:dch + 1], xs, ALU.mult, ALU.add, (pk, XK(dch, tb)), (XK(dch, tb),))
                if agen is not None:
                    next(agen, None)
        if agen is not None:
            for _ in agen:
                pass
        pg.barrier()
        kb.rings = dict(kb.RINGS_DEFAULT)

    def final_out():
        kb.top = MARK
        sq = kb.bf([4, 512]); rs = kb.f32([2, 512]); tmp = kb.f32([4, 512]); yst = kb.f32([4, 512])
        cnt = [0]
        def to_out(c, tb, tm, tk, bias):
            i = cnt[0] % 4; cnt[0] += 1
            kb.cp("act", yst[:, i, :], tm, (tk,), (("yst", i),))
            kb.dma("sp", O["yT"][c * 128:(c + 1) * 128, tb * 512:(tb + 1) * 512], yst[:, i, :], (("yst", i),), ())
        o, _ = VOFF["nfin"]
        norm_mod(lambda c: vecs[:, o + c:o + c + 1], None, sq, rs, tmp, to_out)

    from_mixers = _mixers(kb)
    adaln_first()
    try:
        for l in range(NLAYERS):
            kb.prm = prm_all[:, l]; kb.convx = convx_all[:, l]
            norm_phase(0, 1)
            if l % 2 == 0 and STAGES["even"]:
                from_mixers["even"](l, l // 2)
            if l % 2 == 1 and STAGES["odd"]:
                from_mixers["odd"](l, l // 2)
            if STAGES["ffn"]:
                norm_phase(3, 4)
                ffn(l)
    except _Stop:
        pg.barrier()
    final_out()


def _mixers(kb):
    I, O, pg = kb.I, kb.O, kb.pg
    xres, hT, vecs, ones_bf, ident_bf, prm = kb.xres, kb.hT, kb.vecs, kb.ones_bf, kb.ident_bf, kb.prm
    XK, HK, vv, MARK = kb.XK, kb.HK, kb.vv, kb.MARK
    wview = kb.wview
    fo = VOFF["flags"][0]

    def bfv(pst):
        return pst[:, 0:256].bitcast(BF16)

    def out_proj(wname, e, extra, src=None):
        src = hT if src is None else src
        kb.ring_setup(2, 8 * 512)
        for piece in range(2):
            slot, sk = kb.ring_next(); wo = wview(slot, 8, 512)
            kb.wload(wo, I[wname][e][:, piece * 512:(piece + 1) * 512], sk)
            for d4 in range(4):
                dch = piece * 4 + d4
                for tb in range(4):
                    sl = slice(tb * 512, (tb + 1) * 512)
                    pst, pk = kb.ps("acc")
                    for kc in range(8):
                        if extra is not None and kc >= 6:
                            rhs, rk = extra[:, kc - 6, sl], ("yp", kc - 6, tb)
                        else:
                            rhs, rk = src[:, kc, sl], HK(kc, tb)
                        kb.mm(pst, wo[:, kc, d4 * 128:(d4 + 1) * 128], rhs, kc == 0, kc == 7, (sk, rk), (pk,))
                    xs = xres[:, dch, sl]
                    kb.stt("dve", xs, pst, kb.prm[:, 2, dch:dch + 1], xs, ALU.mult, ALU.add, (pk, XK(dch, tb)), (XK(dch, tb),))
        pg.barrier()

    def rope_combine(dst, dk, pa, pak, pb, pbk, tab, tabk, t1, t2, np_, pre=1.0):
        kb.stt("dve", t1[0:np_, :], pa[0:np_, :], pre, tab[0:np_, 0, :], ALU.mult, ALU.mult, (pak, tabk), ("rt1",))
        kb.stt("dve", t2[0:np_, :], pb[0:np_, :], pre, tab[0:np_, 1, :], ALU.mult, ALU.mult, (pbk, tabk), ("rt2",))
        kb.tt("dve", dst, t1[0:np_, :], t2[0:np_, :], ALU.add, ("rt1", "rt2"), (dk,))

    def even(l, e):
        kb.top = MARK
        qnT = kb.bf([3, T]); qrT = kb.bf([3, T]); cT = kb.bf([2, NK]); krT = kb.bf([NK]); ypT = kb.bf([2, T])
        MARK_E = kb.top
        wtm = kb.bf([8, 544]); wpl = kb.bf([2, 128]); sk = "wtm"; skp = "wpl"
        xp_tok = kb.bf([16, 384]); pooledT = kb.bf([2, T]); lat_st = kb.f32([2, 288]); ctok = kb.bf([2, 256])
        sqt = kb.f32([2, 256]); ssq = kb.f32([2]); ctxl = kb.bf([4, 288]); Pms = [kb.bf([4, 3, 128]) for _ in range(2)]
        kb.memset("dve", xp_tok, 0.0, [("xp", j) for j in range(16)])
        kb.wload(wtm, I["wie_tm"][e], sk)
        kb.dma("pool", ctxl, I["c_mla"][e].rearrange("(b p) n -> p b n", p=128), (), ("ctxl",))
        for j in range(16):
            i = j % 2; tb = j // 4
            p1, k1 = kb.ps("mm"); p2, k2 = kb.ps("mm")
            for kc in range(8):
                lh = hT[:, kc, j * 128:(j + 1) * 128]
                kb.mm(p1[:, 0:288], lh, wtm[:, kc, 0:288], kc == 0, kc == 7, (sk, HK(kc, tb)), (k1,))
                kb.mm(p2[:, 0:256], lh, wtm[:, kc, 288:544], kc == 0, kc == 7, (sk, HK(kc, tb)), (k2,))
            kb.act(sqt[:, i, :], p1[:, 0:256], AF.Square, (k1,), (("sqt", i),))
            pg.add("dve", lambda en, o=ssq[:, i:i + 1], a=sqt[:, i, :]: en.reduce_sum(out=o, in_=a, axis=mybir.AxisListType.X),
                   (("sqt", i),), (("ssq", i),))
            kb.act(ssq[:, i:i + 1], ssq[:, i:i + 1], AF.Sqrt, (("ssq", i),), (("ssq", i),), scale=1.0 / 256, bias=EPS)
            kb.recip(ssq[:, i:i + 1], ssq[:, i:i + 1], (("ssq", i),), (("ssq", i),))
            go = VOFF["gkv"][0] + e * 256
            kb.stt("dve", lat_st[:, i, 0:256], p1[:, 0:256], ssq[:, i:i + 1], vecs[:, go:go + 256], ALU.mult, ALU.mult,
                   (k1, ("ssq", i), "vecs"), (("lat", i),))
            kb.cp("act", lat_st[:, i, 256:288], p1[:, 256:288], (k1,), (("lat", i),))
            kb.dma("sp", O["lat"][e, j * 128:(j + 1) * 128, :], lat_st[:, i, :], (("lat", i),), ())
            kb.cp("dve", ctok[:, i, :], lat_st[:, i, 0:256], (("lat", i),), (("ctok", i),))
            ptr, tk = kb.ps("tr"); pb = bfv(ptr)
            for cc in range(2):
                kb.tr(pb[:, cc * 128:(cc + 1) * 128], ctok[:, i, cc * 128:(cc + 1) * 128], ident_bf, (("ctok", i), "ident"), (tk,))
            kb.cp("act", cT[:, :, j * 128:(j + 1) * 128], pb[:, 0:256].rearrange("p (a b) -> p a b", a=2), (tk,), (("cT", j),))
            for half in range(2):
                dst = xp_tok[:, j, half * 192:(half + 1) * 192].rearrange("p (a b) -> p a b", a=3)[:, 0:3:2, :]
                src = p2[:, half * 128:(half + 1) * 128].rearrange("p (a b) -> p a b", a=2)
                kb.cp("act", dst, src, (k2,), (("xp", j),))
        ck("e_tm")
        for blk in range(4):
            ptr, tk = kb.ps("tr"); pb = bfv(ptr)
            for cc in range(2):
                kb.tr(pb[:, cc * 128:(cc + 1) * 128], ctxl[:, blk, cc * 128:(cc + 1) * 128], ident_bf, ("ctxl", "ident"), (tk,))
            kb.tr(pb[0:32, 256:384], ctxl[:, blk, 256:288], ident_bf, ("ctxl", "ident"), (tk,))
            kb.cp("act", cT[:, :, T + blk * 128:T + (blk + 1) * 128], pb[:, 0:256].rearrange("p (a b) -> p a b", a=2), (tk,), (("cT", 16 + blk),))
            kb.cp("dve", krT[0:32, T + blk * 128:T + (blk + 1) * 128], pb[0:32, 256:384], (tk,), (("krT", 4),))
        ck("e_ctx")
        kb.dma("pool", wpl, I["wpool_bd"][e].rearrange("a k m -> k a m"), (), (skp,))
        for j in range(16):
            pm = Pms[j % 2]; pmk = ("Pm", j % 2)
            kb.dma("pool", pm, I["Pm"][j], (), (pmk,))
            for pr in range(2):
                pst, pk = kb.ps("mm")
                todo = [(gg, sbi) for gg in range(2) for sbi in range(3) if 0 <= j - 1 + sbi <= 15]
                for n_, (gg, sbi) in enumerate(todo):
                    sbk = j - 1 + sbi; c0 = pr * 192 + gg * 64
                    kb.mm(pst[:, 0:128], xp_tok[:, sbk, c0:c0 + 128], pm[:, pr * 2 + gg, sbi, :], n_ == 0, n_ == len(todo) - 1,
                          (("xp", sbk), pmk), (pk,))
                kb.cp("act", pooledT[:, pr, j * 128:(j + 1) * 128], pst[:, 0:128], (pk,), (("pooled", pr, j // 4),))
        pso = VOFF["pscale"][0] + e * 2
        for pr in range(2):
            for tb in range(4):
                sl = slice(tb * 512, (tb + 1) * 512)
                pst, pk = kb.ps("mm")
                kb.mm(pst, wpl[:, pr, :], pooledT[:, pr, sl], True, True, (skp, ("pooled", pr, tb)), (pk,))
                kb.ts("dve", ypT[:, pr, sl], pst, vecs[:, pso + pr:pso + pr + 1], None, ALU.mult, None, (pk, "vecs"), (("yp", pr, tb),))
        ck("e_pool")
        pg.barrier()
        kb.top = MARK_E
        kb.ring_setup(2, 8 * 544)
        qa_f = kb.f32([3, 512]); sq = kb.bf([2, 512]); rs = kb.f32([512]); t1 = kb.f32([512]); t2 = kb.f32([512])
        tabE = [kb.f32([2, 512]) for _ in range(2)]
        slot, sk1 = kb.ring_next(); wkr = wview(slot, 8, 448)
        kb.dma("pool", wkr[:, :, 0:32], I["wie_kr"][e].rearrange("(kc p) n -> p kc n", p=128), (), (sk1,))
        kb.dma("pool", wkr[:, :, 32:64], I["wie_krs"][e].rearrange("(kc p) n -> p kc n", p=128), (), (sk1,))
        kb.dma("pool", wkr[:, :, 64:448], I["wie_qa"][e].rearrange("(kc p) n -> p kc n", p=128), (), (sk1,))
        slot, sk2 = kb.ring_next(); wqr = wview(slot, 3, 768)
        kb.dma("pool", wqr[:, :, 0:384], I["wuq_r"][e].rearrange("(kc p) n -> p kc n", p=128), (), (sk2,))
        kb.dma("pool", wqr[:, :, 384:768], I["wuq_rs"][e].rearrange("(kc p) n -> p kc n", p=128), (), (sk2,))
        gq0 = VOFF["gq"][0] + e * 3
        for tb in range(4):
            sl = slice(tb * 512, (tb + 1) * 512)
            tab = tabE[tb % 2]; tabk = ("tabE", tb % 2)
            kb.dma("sp", tab, I["ropeE"][:, :, sl].rearrange("a p t -> p a t"), (), (tabk,))
            pa, pak = kb.ps("mm"); pb_, pbk = kb.ps("mm")
            for kc in range(8):
                kb.mm(pa[0:32, :], wkr[:, kc, 0:32], hT[:, kc, sl], kc == 0, kc == 7, (sk1, HK(kc, tb)), (pak,))
            for kc in range(8):
                kb.mm(pb_[0:32, :], wkr[:, kc, 32:64], hT[:, kc, sl], kc == 0, kc == 7, (sk1, HK(kc, tb)), (pbk,))
            rope_combine(krT[0:32, sl], ("krT", tb), pa, pak, pb_, pbk, tab, tabk, t1, t2, 32)
            pn, pnk = kb.ps("acc")
            for m in range(3):
                pq, pqk = kb.ps("mm")
                for kc in range(8):
                    kb.mm(pq, wkr[:, kc, 64 + m * 128:64 + (m + 1) * 128], hT[:, kc, sl], kc == 0, kc == 7, (sk1, HK(kc, tb)), (pqk,))
                kb.cp("act", qa_f[:, m, :], pq, (pqk,), (("qa_f", m),))
                kb.act(sq[:, m % 2, :], qa_f[:, m, :], AF.Square, (("qa_f", m),), (("sq", m % 2),))
                kb.mm(pn, ones_bf, sq[:, m % 2, :], m == 0, m == 2, (("sq", m % 2), "ones"), (pnk,))
            kb.act(rs, pn, AF.Sqrt, (pnk,), ("rs",), scale=1.0 / 384, bias=EPS)
            kb.recip(rs, rs, ("rs",), ("rs",))
            for m in range(3):
                kb.stt("dve", qnT[:, m, sl], qa_f[:, m, :], vecs[:, gq0 + m:gq0 + m + 1], rs, ALU.mult, ALU.mult,
                       (("qa_f", m), "rs", "vecs"), (("qnT", m, tb),))
            for m in range(3):
                pa, pak = kb.ps("mm"); pb_, pbk = kb.ps("mm")
                for kc in range(3):
                    kb.mm(pa, wqr[:, kc, m * 128:(m + 1) * 128], qnT[:, kc, sl], kc == 0, kc == 2, (sk2, ("qnT", kc, tb)), (pak,))
                for kc in range(3):
                    kb.mm(pb_, wqr[:, kc, 384 + m * 128:384 + (m + 1) * 128], qnT[:, kc, sl], kc == 0, kc == 2, (sk2, ("qnT", kc, tb)), (pbk,))
                rope_combine(qrT[:, m, sl], ("qrT", m, tb), pa, pak, pb_, pbk, tab, tabk, t1, t2, 128)
        ck("e_ea2")
        pg.barrier()
        kb.top = MARK_E
        kb.rings = {"mm": (0, 4), "acc": (4, 4), "tr": (7, 1)}
        wqn = kb.bf([3, 768]); wk = kb.bf([2, 768]); wvv = kb.bf([2, 768])
        KT = kb.bf([2, NK]); QT = kb.bf([2, T]); Vh = kb.bf([2, 20, 128]); PT = kb.bf([6, 512])
        rDs = kb.f32([2, 512]); rDt = kb.f32([2, 512]); sel = kb.f32([2, 128]); fin_state = []
        kb.memset("dve", rDt, 0.0, ("rDt",)); kb.memset("dve", sel, 0.0, ("sel",))
        kb.memset("dve", sel[64:65, 0, 0:64], 1.0, ("sel",))
        kb.memset("dve", sel[32:33, 1, 64:128], 1.0, ("sel",))
        kb.wload(wqn, I["wuq_n"][e], "wqn"); kb.wload(wk, I["wukv_k"][e], "wk"); kb.wload(wvv, I["wukv_v"][e], "wvv")
        for b in range(2):
            kb.dma("pool", KT[96:105, b, :], I["kmaskE"][:, :], (), (("KTm", b),))
            kb.dma("pool", QT[96:105, b, :], I["qmaskE"][:, :], (), (("QTm", b),))
            kb.dma("sp", KT[64:96, b, :], krT[0:32, :], (), (("KTr", b),))
            kb.memset("dve", Vh[:, b, :, :], 0.0, (("Vh", b),))
            oc = 64 if b == 0 else 32
            kb.memset("dve", Vh[:, b, :, oc:oc + 1], 1.0, (("Vh", b),))

        def proj(h, b):
            kb.dma("sp", QT[64:96, b, :], qrT[(h % 4) * 32:(h % 4) * 32 + 32, h // 4, :], (), (("QTr", b),))
            for tb in range(4):
                sl = slice(tb * 512, (tb + 1) * 512)
                pst, pk = kb.ps("mm")
                for kc in range(3):
                    kb.mm(pst[0:64, :], wqn[:, kc, h * 64:(h + 1) * 64], qnT[:, kc, sl], kc == 0, kc == 2, ("wqn",), (pk,))
                kb.cp("dve", QT[0:64, b, sl], pst[0:64, :], (pk,), (("QTn", b),))
            for k5 in range(5):
                sl = slice(k5 * 512, (k5 + 1) * 512)
                pst, pk = kb.ps("mm")
                for cc in range(2):
                    kb.mm(pst[0:64, :], wk[:, cc, h * 64:(h + 1) * 64], cT[:, cc, sl], cc == 0, cc == 1, ("wk",), (pk,))
                kb.cp("dve", KT[0:64, b, sl], pst[0:64, :], (pk,), (("KTn", b),))
            for g0, nb in ((0, 8), (8, 8), (16, 4)):
                pst, pk = kb.ps("mm")
                for i in range(nb):
                    kblk = g0 + i
                    for cc in range(2):
                        kb.mm(pst[:, i * 64:(i + 1) * 64], cT[:, cc, kblk * 128:(kblk + 1) * 128], wvv[:, cc, h * 64:(h + 1) * 64],
                              cc == 0, cc == 1, ("wvv",), (pk,))
                kb.cp("dve", Vh[:, b, g0:g0 + nb, b * 64:b * 64 + 64], pst[:, 0:nb * 64].rearrange("p (a b) -> p a b", a=nb),
                      (pk,), (("Vh", b),))

        def attn(h, b):
            rd_q = (("QTn", b), ("QTr", b), ("QTm", b)); rd_k = (("KTn", b), ("KTr", b), ("KTm", b))
            hp = slice(b * 64, b * 64 + 64)
            p0 = 64 if b == 0 else 32
            for qc in range(4):
                accO, ok_ = kb.ps("acc")
                pend = []

                def pv(kc, slot):
                    kb.mm(accO, Vh[:, b, kc, :], PT[:, slot, :], kc == 0, kc == 19, (("PT", slot), ("Vh", b)), (ok_,))
                for kc in range(20):
                    slot = (qc * 20 + kc) % 6
                    pst, pk = kb.ps("mm")
                    kb.mm(pst, KT[0:105, b, kc * 128:(kc + 1) * 128], QT[0:105, b, qc * 512:(qc + 1) * 512], True, True, rd_q + rd_k, (pk,))
                    kb.act(PT[:, slot, :], pst, AF.Exp, (pk,), (("PT", slot),), scale=MLA_SCALE)
                    pend.append((kc, slot))
                    if len(pend) > 2:
                        pv(*pend.pop(0))
                    if kc == 4 and fin_state:
                        fin_state.pop(0)()
                while pend:
                    pv(*pend.pop(0))
                ri = (h * 4 + qc) % 2
                rd = rDs[:, ri, :]; rk = ("rD", ri)
                rdt = rDt[:, ri, :]; rtk = ("rDt", ri)
                kb.recip(rdt[p0:p0 + 1, :], accO[p0:p0 + 1, :], (ok_,), (rtk,))

                def fin(accO=accO, ok_=ok_, rd=rd, rk=rk, rdt=rdt, rtk=rtk, qc=qc):
                    bc, bk = kb.ps("acc")
                    kb.mm(bc, sel[:, b, :], rdt, True, True, ("sel", rtk), (bk,))
                    kb.cp("dve", rd[hp, :], bc[hp, :], (bk,), (rk,))
                    kb.tt("dve", hT[hp, h // 2, qc * 512:(qc + 1) * 512], accO[hp, :], rd[hp, :], ALU.mult, (ok_, rk), (HK(h // 2, qc),))
                fin_state.append(fin)

        ck("e_ebsetup")
        proj(0, 0)
        ck("e_proj0")
        for h in range(12):
            if h + 1 < 12: proj(h + 1, (h + 1) % 2)
            attn(h, h % 2)
            ck("e_attn%d" % h)
        while fin_state:
            fin_state.pop(0)()
        pg.barrier()
        kb.rings = dict(kb.RINGS_DEFAULT)
        kb.top = MARK_E
        out_proj("w_out_even", e, ypT)

    def odd(l, e):
        kb.top = MARK
        QM = kb.bf([8, T]); ctxones = kb.bf([128]); zrow = kb.f32([128])
        kb.memset("dve", ctxones, 1.0, ("ctxones",))
        kb.ts("dve", ctxones, ctxones, vecs[:, fo:fo + 1], None, ALU.mult, None, ("ctxones", "vecs"), ("ctxones",))
        kb.memset("dve", zrow, 0.0, ("zrow",))
        MARK_O = kb.top
        KTna = kb.bf([4, NK]); Vna = kb.bf([20, 512])
        MARK_O2 = kb.top
        kb.ring_setup(2, 8 * 512)
        stg = kb.f32([2, 512]); ctxl = kb.bf([4, 512])
        kb.dma("pool", Vna[:, 16:20, :], I["c_na"][e][:, 1, :].rearrange("(b p) n -> p b n", p=128), (), ("Vna_ctx",))
        kb.dma("pool", ctxl, I["c_na"][e][:, 0, :].rearrange("(b p) n -> p b n", p=128), (), ("ctxl",))
        for piece in range(2):
            slot, sk = kb.ring_next(); wv = wview(slot, 8, 512)
            kb.wload(wv, I["wio_nakv"][e][:, piece * 512:(piece + 1) * 512], sk)
            for j in range(16):
                pst, pk = kb.ps("mm")
                for kc in range(8):
                    kb.mm(pst, hT[:, kc, j * 128:(j + 1) * 128], wv[:, kc, :], kc == 0, kc == 7, (sk, HK(kc, j // 4)), (pk,))
                kb.cp("act", stg[:, j % 2, :], pst, (pk,), (("stg", j % 2),))
                kb.dma("sp", O["nakv"][e, j * 128:(j + 1) * 128, piece * 512:(piece + 1) * 512], stg[:, j % 2, :], (("stg", j % 2),), ())
                if piece == 1:
                    kb.cp("dve", Vna[:, j, :], pst, (pk,), (("Vna", j),))
        for blk in range(4):
            ptr, tk = kb.ps("tr"); pb = bfv(ptr)
            for m in range(4):
                kb.tr(pb[:, m * 128:(m + 1) * 128], ctxl[:, blk, m * 128:(m + 1) * 128], ident_bf, ("ctxl", "ident"), (tk,))
            kb.cp("act", KTna[:, :, T + blk * 128:T + (blk + 1) * 128], pb.rearrange("p (a b) -> p a b", a=4), (tk,), (("KTna_ctx", blk),))
        for wname, isq in (("wio_qna", True), ("wio_kna", False)):
            slot, sk = kb.ring_next(); wv = wview(slot, 8, 512)
            kb.wload(wv, I[wname][e], sk)
            for m in range(4):
                for tb in range(4):
                    sl = slice(tb * 512, (tb + 1) * 512)
                    pst, pk = kb.ps("mm")
                    for kc in range(8):
                        kb.mm(pst, wv[:, kc, m * 128:(m + 1) * 128], hT[:, kc, sl], kc == 0, kc == 7, (sk, HK(kc, tb)), (pk,))
                    if isq:
                        kb.act(QM[:, m, sl], pst, AF.Copy, (pk,), [("QM", m, tb * 4 + i) for i in range(4)], scale=0.125)
                    else:
                        kb.cp("dve", KTna[:, m, sl], pst, (pk,), (("KTna", m, tb),))
        pg.barrier()
        kb.top = MARK_O2
        kb.rings = {"mm": (0, 6), "acc": (6, 2), "tr": (7, 1)}
        PT = kb.bf([6, 512]); nab = [kb.bf([2, 5, 128]) for _ in range(2)]; rDs = kb.f32([2, 128])
        units = [(j, i, hh) for j in range(4) for i in range(16) for hh in range(2)]
        st = {}

        def na_S(k):
            j, i, hh = units[k]
            if hh == 0:
                nb_ = nab[(k // 2) % 2]; nk = ("nab", (k // 2) % 2)
                kb.dma("pool", nb_, I["nabias"][e, VAR_OF[i]][:, 2 * j:2 * j + 2, :, :], (), (nk,))
                start = min(max(i - 2, 0), 11)
                tiles = [(start + c, c) for c in range(5)] + [(16 + c, None) for c in range(4)]
                accb, ak = kb.ps("acc")
                st[(j, i)] = (nb_, nk, tiles, accb, ak)
            nb_, nk, tiles, accb, ak = st[(j, i)]
            hp = slice(hh * 64, hh * 64 + 64)
            banks = []
            for t, (kblk, c) in enumerate(tiles):
                if t % 4 == 0:
                    banks.append(kb.ps("mm"))
                pst, pk = banks[-1]; col = slice((t % 4) * 128, (t % 4) * 128 + 128)
                kb.mm(pst[:, col], KTna[hp, j, kblk * 128:(kblk + 1) * 128], QM[hp, j, i * 128:(i + 1) * 128], True, True,
                      (("QM", j, i),), (pk,))
            kb.tt("dve", banks[0][0], banks[0][0], nb_[:, hh, 0:4, :].rearrange("p a b -> p (a b)"), ALU.add, (banks[0][1], nk), (banks[0][1],))
            kb.tt("dve", banks[1][0][:, 0:128], banks[1][0][:, 0:128], nb_[:, hh, 4, :], ALU.add, (banks[1][1], nk), (banks[1][1],))
            slots = []
            for bi, (pst, pk) in enumerate(banks):
                ncol = min(4, 9 - bi * 4) * 128
                sl_ = (k * 3 + bi) % 6
                kb.act(PT[:, sl_, 0:ncol], pst[:, 0:ncol], AF.Exp, (pk,), (("PT", sl_),))
                slots.append(sl_)
            st[(j, i, hh)] = slots

        def na_OD(k):
            j, i, hh = units[k]
            nb_, nk, tiles, accb, ak = st[(j, i)]
            slots = st[(j, i, hh)]
            Oh = accb[:, hh * 128:(hh + 1) * 128]; Dh = accb[:, 256 + hh * 128:256 + (hh + 1) * 128]
            for t, (kblk, c) in enumerate(tiles):
                sl_ = slots[t // 4]
                kb.mm(Oh, Vna[:, kblk, j * 128:(j + 1) * 128], PT[:, sl_, (t % 4) * 128:(t % 4) * 128 + 128], t == 0, t == 8,
                      (("PT", sl_),), (ak,))
            for t, (kblk, c) in enumerate(tiles):
                sl_ = slots[t // 4]
                kb.mm(Dh, ones_bf if c is not None else ctxones, PT[:, sl_, (t % 4) * 128:(t % 4) * 128 + 128], t == 0, t == 8,
                      (("PT", sl_), "ones", "ctxones"), (ak,))
            if hh == 1:
                for h2 in range(2):
                    hp = slice(h2 * 64, h2 * 64 + 64)
                    rd = rDs[:, (k + h2) % 2, :]; rk = ("rD", (k + h2) % 2)
                    kb.recip(rd[hp, :], accb[hp, 256 + h2 * 128:256 + (h2 + 1) * 128], (ak,), (rk,))
                    kb.tt("dve", QM[hp, j, i * 128:(i + 1) * 128], accb[hp, h2 * 128:(h2 + 1) * 128], rd[hp, :], ALU.mult, (ak, rk), (("QM", j, i),))

        na_S(0)
        for k in range(len(units)):
            if k + 1 < len(units): na_S(k + 1)
            na_OD(k)
        pg.barrier()
        kb.rings = dict(kb.RINGS_DEFAULT)
        kb.top = MARK_O
        KTsw = kb.bf([NK]); Vsw = kb.bf([20, 128])
        MARK_S = kb.top
        kb.ring_setup(3, 8 * 512)
        stg = kb.f32([2, 256]); ctxk = kb.bf([4, 128]); t1 = kb.f32([512]); t2 = kb.f32([512])
        tabO = [kb.f32([2, 512]) for _ in range(2)]
        kb.dma("pool", Vsw[:, 16:20, :], I["c_sw"][e][:, 1, :].rearrange("(b p) n -> p b n", p=128), (), ("Vsw_ctx",))
        kb.dma("pool", ctxk, I["c_sw"][e][:, 0, :].rearrange("(b p) n -> p b n", p=128), (), ("ctxk",))
        slot, sk = kb.ring_next(); wv = wview(slot, 8, 512)
        kb.wload(wv[:, :, 0:256], I["wio_swkv"][e], sk)
        kb.dma("pool", wv[:, :, 256:384], I["wio_ksw"][e].rearrange("(kc p) n -> p kc n", p=128), (), (sk,))
        kb.dma("pool", wv[:, :, 384:512], I["wio_ksws"][e].rearrange("(kc p) n -> p kc n", p=128), (), (sk,))
        for j in range(16):
            pst, pk = kb.ps("mm")
            for kc in range(8):
                kb.mm(pst[:, 0:256], hT[:, kc, j * 128:(j + 1) * 128], wv[:, kc, 0:256], kc == 0, kc == 7, (sk, HK(kc, j // 4)), (pk,))
            kb.cp("act", stg[:, j % 2, :], pst[:, 0:256], (pk,), (("stg", j % 2),))
            kb.dma("sp", O["swkv"][e, j * 128:(j + 1) * 128, :], stg[:, j % 2, :], (("stg", j % 2),), ())
            kb.cp("dve", Vsw[:, j, :], pst[:, 128:256], (pk,), (("Vsw", j),))
        ptr, tk = kb.ps("tr"); pb = bfv(ptr)
        for blk in range(4):
            kb.tr(pb[:, blk * 128:(blk + 1) * 128], ctxk[:, blk, :], ident_bf, ("ctxk", "ident"), (tk,))
        kb.cp("act", KTsw[:, T:NK], pb, (tk,), ("KTsw_ctx",))
        slot, skq = kb.ring_next(); wq = wview(slot, 8, 512)
        kb.wload(wq, I["wio_qsw"][e], skq)
        slot, skqs = kb.ring_next(); wqs = wview(slot, 8, 512)
        kb.wload(wqs, I["wio_qsws"][e], skqs)
        for tb in range(4):
            sl = slice(tb * 512, (tb + 1) * 512)
            tab = tabO[tb % 2]; tabk = ("tabO", tb % 2)
            kb.dma("sp", tab, I["ropeO"][:, :, sl].rearrange("a p t -> p a t"), (), (tabk,))
            pa, pak = kb.ps("mm"); pb_, pbk = kb.ps("mm")
            for kc in range(8):
                kb.mm(pa, wv[:, kc, 256:384], hT[:, kc, sl], kc == 0, kc == 7, (sk, HK(kc, tb)), (pak,))
            for kc in range(8):
                kb.mm(pb_, wv[:, kc, 384:512], hT[:, kc, sl], kc == 0, kc == 7, (sk, HK(kc, tb)), (pbk,))
            rope_combine(KTsw[:, sl], ("KTsw", tb), pa, pak, pb_, pbk, tab, tabk, t1, t2, 128)
            for m in range(4):
                pa, pak = kb.ps("mm"); pb_, pbk = kb.ps("mm")
                for kc in range(8):
                    kb.mm(pa, wq[:, kc, m * 128:(m + 1) * 128], hT[:, kc, sl], kc == 0, kc == 7, (skq, HK(kc, tb)), (pak,))
                for kc in range(8):
                    kb.mm(pb_, wqs[:, kc, m * 128:(m + 1) * 128], hT[:, kc, sl], kc == 0, kc == 7, (skqs, HK(kc, tb)), (pbk,))
                rope_combine(QM[:, 4 + m, sl], ("QMs", m, tb), pa, pak, pb_, pbk, tab, tabk, t1, t2, 128, pre=0.125)
        pg.barrier()
        kb.top = MARK_S
        kb.rings = {"mm": (0, 4), "acc": (4, 4), "tr": (7, 1)}
        PT = kb.bf([14, 512]); swb = kb.bf([6, 3, 128]); esink = kb.bf([2, 512]); rDs = kb.f32([2, 512])
        kb.dma("pool", swb, I["swbias"].rearrange("v k c q -> k v c q"), (), ("swb",))
        so = VOFF["sink"][0] + e * 8
        for g in range(2):
            for hd in range(4):
                kb.act(esink[0:1, g, hd * 128:(hd + 1) * 128], zrow[0:1, 0:128], AF.Exp, ("zrow", "vecs"), ("esink",),
                       bias=vecs[0:1, so + 4 * g + hd:so + 4 * g + hd + 1])
        sunits = [(i, g) for i in range(16) for g in range(2)]
        sst = {}

        def sw_S(k):
            i, g = sunits[k]
            hp = slice(g * 64, g * 64 + 64)
            chunks = [(min(max(i - 1 + c, 0), 15), c) for c in range(3)] + [(16 + c, None) for c in range(4)]
            slots = []
            for t, (kblk, c) in enumerate(chunks):
                pst, pk = kb.ps("mm")
                kb.mm(pst, KTsw[hp, kblk * 128:(kblk + 1) * 128], QM[hp, 4:8, i * 128:(i + 1) * 128], True, True, (), (pk,))
                if c is not None:
                    p3 = pst.rearrange("p (a b) -> p a b", a=4)
                    kb.tt("dve", p3, p3, swb[:, VAR_OF[i], c, :].unsqueeze(1).to_broadcast([128, 4, 128]), ALU.add, (pk, "swb"), (pk,))
                sl_ = (k * 7 + t) % 14
                kb.act(PT[:, sl_, :], pst, AF.Exp, (pk,), (("PT", sl_),))
                slots.append(sl_)
            sst[k] = (chunks, slots)

        def sw_OD(k):
            i, g = sunits[k]
            hp = slice(g * 64, g * 64 + 64)
            chunks, slots = sst[k]
            accO, ok_ = kb.ps("acc"); accD, dk_ = kb.ps("acc")
            for t, (kblk, c) in enumerate(chunks):
                kb.mm(accO, Vsw[:, kblk, :], PT[:, slots[t], :], t == 0, t == 6, (("PT", slots[t]),), (ok_,))
            for t, (kblk, c) in enumerate(chunks):
                kb.mm(accD, ones_bf if c is not None else ctxones, PT[:, slots[t], :], t == 0, False, (("PT", slots[t]), "ones", "ctxones"), (dk_,))
            kb.mm(accD, ones_bf[0:1, :], esink[0:1, g, :], False, True, ("esink", "ones"), (dk_,))
            rd = rDs[:, k % 2, :]; rk = ("rD", k % 2)
            kb.recip(rd[hp, :], accD[hp, :], (dk_,), (rk,))
            kb.tt("dve", QM[hp, 4:8, i * 128:(i + 1) * 128], accO[hp, :].rearrange("p (a b) -> p a b", a=4),
                  rd[hp, :].rearrange("p (a b) -> p a b", a=4), ALU.mult, (ok_, rk), (("QMo", i, g),))

        sw_S(0)
        for k in range(len(sunits)):
            if k + 1 < len(sunits): sw_S(k + 1)
            sw_OD(k)
        pg.barrier()
        kb.rings = dict(kb.RINGS_DEFAULT)
        kb.top = MARK_O
        out_proj("w_out_odd", e, None, src=QM)

    return {"even": even, "odd": odd}


_CACHE = {}


def kernel(**inputs):
    maps = _host_inputs(inputs)
    for core, m in enumerate(maps):
        sample = core >= 4
        fl = m.pop("flags")
        m["vecs"] = _pack_vecs(inputs, m.pop("cvec"), fl)
        for k in ("b_mod", "norm_mix", "norm_ffn", "norm_final", "mla_q_norm", "mla_kv_norm", "pool_scale",
                  "swa_sink", "conv_w", "conv_b"):
            m.pop(k, None)
    if "nc" not in _CACHE:
        _CACHE["nc"] = build_program()[0]
    nc = _CACHE["nc"]
    res = run_bass_kernel_spmd(nc, maps, core_ids=list(range(8)))
    R = res.results
    y_prompt = np.concatenate([np.asarray(R[c]["yT"]).T.reshape(8, 256, D) for c in range(4)], 0)
    y_sample = np.stack([np.asarray(R[4 + b]["yT"]).T for b in range(4)], 0)
    lat = np.concatenate([np.asarray(R[c]["lat"]).reshape(2, 8, 256, 288).transpose(1, 0, 2, 3) for c in range(4)], 0)
    na = np.concatenate([np.asarray(R[c]["nakv"]).reshape(2, 8, 256, 2, 8, 64).transpose(1, 0, 2, 3, 4, 5) for c in range(4)], 0)
    sw = np.concatenate([np.asarray(R[c]["swkv"]).reshape(2, 8, 256, 2, 2, 64).transpose(1, 0, 2, 3, 4, 5) for c in range(4)], 0)
    f = lambda a: np.ascontiguousarray(a, dtype=np.float32)
    return (f(y_prompt), f(y_sample), f(lat), f(na), f(sw))
```

```python
import numpy as np
import concourse.bass as bass
import concourse.mybir as mybir
from concourse.bass_utils import run_bass_kernel_spmd
from contextlib import ExitStack

F32, BF16 = mybir.dt.float32, mybir.dt.bfloat16
AF, ALU = mybir.ActivationFunctionType, mybir.AluOpType

D = 1024; T = 2048; DEPTH = 4; PAST = 512; NK = T + PAST
DFF = 2816; NFC = 22
EPS = 1e-6
MLA_SCALE = 96 ** -0.5
BM = 1024.0
NEGB = -30000.0
STAGES = {"even": True, "odd": True, "ffn": True}
STOP_AT = None
EMBED_WAITS = True


class _Stop(Exception):
    pass


def ck(name):
    if STOP_AT == name:
        raise _Stop()
NLAYERS = DEPTH


class Op:
    __slots__ = ("eng", "fn", "deps", "dma", "sem", "target", "pre", "need", "ticket")

    def __init__(self, eng, fn, dma):
        self.eng = eng; self.fn = fn; self.dma = dma; self.deps = []
        self.sem = None; self.target = 0; self.pre = None; self.need = False; self.ticket = 0


class Prog:
    ENGS = ("pe", "act", "dve", "pool", "sp")
    NDS = {"sp": 40, "pool": 40}

    def __init__(self):
        self.q = {e: [] for e in self.ENGS}
        self.lw = {}; self.rd = {}
        self.dsem_next = {e: 0 for e in self.NDS}
        self.dsem_tgt = {}
        self.dma_since = []

    def add(self, eng, fn, reads=(), writes=(), dma=False):
        op = Op(eng, fn, dma)
        deps = []
        for k in reads:
            w = self.lw.get(k)
            if w is not None: deps.append(w)
            if isinstance(k, tuple) and k[0] == "ps":
                for r in self.rd.get(k, ()):
                    if r.eng != eng: deps.append(r)
        for k in writes:
            w = self.lw.get(k)
            if w is not None: deps.append(w)
            for r in self.rd.get(k, ()): deps.append(r)
        seen = set(); dl = []
        for d in deps:
            if d is op or id(d) in seen: continue
            seen.add(id(d))
            if (not d.dma) and d.eng == eng and eng == "pe": continue
            dl.append(d)
        op.deps = dl
        if dma:
            i = self.dsem_next[eng]; self.dsem_next[eng] = (i + 1) % self.NDS[eng]
            key = (eng, i)
            prev = self.dsem_tgt.get(key, 0)
            op.sem = key; op.pre = prev; op.target = prev + 16
            self.dsem_tgt[key] = op.target
            self.dma_since.append(op)
        for k in reads:
            self.rd.setdefault(k, []).append(op)
        for k in writes:
            self.lw[k] = op; self.rd[k] = []
        self.q[eng].append(op)
        return op

    def barrier(self):
        col = Op("dve", "nop", False)
        for e in self.ENGS:
            if self.q[e]:
                last = None
                for o in reversed(self.q[e]):
                    if o.fn is not None and not o.dma:
                        last = o; break
                if last is not None: col.deps.append(last)
        col.deps.extend(self.dma_since)
        self.dma_since = []
        self.q["dve"].append(col)
        for e in self.ENGS:
            if e == "dve": continue
            w = Op(e, None, False); w.deps = [col]
            self.q[e].append(w)
        self.lw = {}; self.rd = {}

    def emit(self, nc, block, csem, dsems):
        for e in self.ENGS:
            for op in self.q[e]:
                for d in op.deps:
                    if not d.dma: d.need = True
        for e in self.ENGS:
            t = 0
            for op in self.q[e]:
                if op.need and not op.dma:
                    t += 1; op.ticket = t
        engobj = {"pe": nc.tensor, "act": nc.scalar, "dve": nc.vector, "pool": nc.gpsimd, "sp": nc.sync}
        self.nwaits = 0

        def run(e, eo):
            waited = {}

            def wait(key, sem, val):
                if val <= 0 or waited.get(key, 0) >= val: return
                waited[key] = val
                eo.wait_ge(sem, val); self.nwaits += 1

            for op in self.q[e]:
                need = []
                for d in op.deps:
                    key, sem, val = (d.sem, dsems[d.sem], d.target) if d.dma else (d.eng, csem[d.eng], d.ticket)
                    if val > 0 and waited.get(key, 0) < val:
                        waited[key] = val
                        need = [w for w in need if w[0] is not sem] + [(sem, val)]
                embed = None
                if EMBED_WAITS and e == "pe" and need and (not op.dma) and op.fn is not None and op.fn != "nop":
                    embed = need.pop()
                for sem, val in need:
                    eo.wait_ge(sem, val); self.nwaits += 1
                if op.dma:
                    wait(op.sem, dsems[op.sem], op.pre)
                    op.fn(eo).then_inc(dsems[op.sem], 16)
                elif op.fn is None:
                    pass
                else:
                    ins = eo.nop() if op.fn == "nop" else op.fn(eo)
                    if embed is not None: ins._wait_ge(embed[0], embed[1])
                    if op.need: ins.then_inc(csem[e], 1)
            for key, tgt in self.dsem_tgt.items():
                if key[0] == e: wait(key, dsems[key], tgt)

        block.tensor(lambda eo: run("pe", eo))
        block.scalar(lambda eo: run("act", eo))
        block.vector(lambda eo: run("dve", eo))
        block.gpsimd(lambda eo: run("pool", eo))
        block.sync(lambda eo: run("sp", eo))


def _bf(x):
    import ml_dtypes
    return np.asarray(x, np.float32).astype(ml_dtypes.bfloat16).astype(np.float32)


def _rope_tables(R, sample):
    half = R // 2; nf = half // 2
    t = np.arange(T)
    freqs = (10000.0 ** (-np.arange(nf, dtype=np.float32) / nf)).astype(np.float32)
    C = np.ones((R, T), np.float32); S = np.zeros((R, T), np.float32)
    if sample:
        for hi, pos in enumerate((t // 64, t % 64)):
            ang = pos.astype(np.float32)[None, :] * freqs[:, None]
            c, s = np.cos(ang).astype(np.float32), np.sin(ang).astype(np.float32)
            b = hi * half
            C[b:b + nf] = c; C[b + nf:b + 2 * nf] = c
            S[b:b + nf] = -s; S[b + nf:b + 2 * nf] = s
    return C, S


def _rope_perm(R):
    half = R // 2; nf = half // 2
    p = np.arange(R)
    for b in (0, half):
        p[b:b + nf] = np.arange(b + nf, b + 2 * nf)
        p[b + nf:b + 2 * nf] = np.arange(b, b + nf)
    return p


VAR_OF = [0, 1] + [2, 3] * 6 + [4, 5]
VAR_REP = [0, 1, 2, 3, 14, 15]


def _struct_consts(sample):
    c = {}
    c["ident"] = np.eye(128, dtype=np.float32)
    Ce, Se = _rope_tables(32, sample)
    c["ropeE"] = np.stack([np.tile(Ce, (4, 1)), np.tile(Se, (4, 1))], 0)
    Co, So = _rope_tables(64, sample)
    c["ropeO"] = np.stack([np.tile(Co, (2, 1)), np.tile(So, (2, 1))], 0)
    km = np.zeros((9, NK), np.float32); qm = np.zeros((9, T), np.float32)
    km[8, :] = 1.0; qm[8, :] = -BM
    if sample:
        km[0, :] = 1.0; qm[0, :] = BM
    else:
        for s in range(8):
            km[s, s * 256:(s + 1) * 256] = 1.0
            qm[s, s * 256:(s + 1) * 256] = BM
    c["kmaskE"] = km; c["qmaskE"] = qm
    fl = 1.0 if sample else 0.0
    c["flags"] = np.tile(np.array([[fl, 1.0 - fl]], np.float32), (128, 1))
    n = T if sample else 256
    Pm = np.zeros((16, 128, 4, 3, 128), np.float32)
    tt = np.arange(T); tl = tt % n; base = tt - tl
    for g, w in enumerate((2, 4, 8, 16)):
        lo = np.clip(tl - w // 2, 0, n); hi = np.clip(tl + w // 2, 0, n)
        cnt = (hi - lo).astype(np.float32)
        for t in range(T):
            j = t // 128
            for s in range(base[t] + lo[t], base[t] + hi[t]):
                sb = s // 128 - (j - 1)
                Pm[j, s % 128, g, sb, t % 128] += 1.0 / cnt[t]
            Pm[j, t % 128, g, 1, t % 128] -= 1.0
    c["Pm"] = Pm
    swb = np.full((6, 128, 3, 128), NEGB, np.float32)
    for v, i in enumerate(VAR_REP):
        tq = i * 128 + np.arange(128)
        for cc in range(3):
            kb = i - 1 + cc
            if kb < 0 or kb > 15: continue
            ks = kb * 128 + np.arange(128)
            if sample:
                ok = np.abs(tq[None, :] - ks[:, None]) <= 128
            else:
                ok = (tq[None, :] // 256) == (ks[:, None] // 256)
            swb[v, :, cc, :] = np.where(ok, 0.0, NEGB)
    c["swbias"] = swb
    return c


def _na_bias(sample, rpb):
    out = np.full((2, 6, 128, 8, 5, 128), NEGB, np.float32)
    for v, i in enumerate(VAR_REP):
        start = min(max(i - 2, 0), 11)
        tq = i * 128 + np.arange(128)
        r, cq = tq // 64, tq % 64
        for cc in range(5):
            ks = (start + cc) * 128 + np.arange(128)
            if sample:
                rp, cp = ks // 64, ks % 64
                r0 = np.clip(r - 4, 0, 24); c0 = np.clip(cq - 8, 0, 48)
                ok = ((rp[:, None] >= r0[None, :]) & (rp[:, None] < r0[None, :] + 8) &
                      (cp[:, None] >= c0[None, :]) & (cp[:, None] < c0[None, :] + 16))
                dr = np.clip(rp[:, None] - r[None, :] + 7, 0, 14)
                dc = np.clip(cp[:, None] - cq[None, :] + 15, 0, 30)
                for l in range(2):
                    g = rpb[l][:, dr, dc]
                    out[l, v, :, :, cc, :] = np.where(ok[:, None, :], g.transpose(1, 0, 2), NEGB)
            else:
                ok = (tq[None, :] // 256) == (ks[:, None] // 256)
                out[:, v, :, :, cc, :] = np.where(ok, 0.0, NEGB)[None, :, None, :]
    return out


def _host_inputs(inp):
    f = lambda a: np.ascontiguousarray(np.asarray(a, np.float32))
    w = {}
    for k in ("w_mod", "b_mod", "norm_mix", "norm_ffn", "norm_final", "mla_q_norm", "mla_kv_norm",
              "pool_scale", "w_out_even", "w_out_odd", "swa_sink", "w_up", "conv_w", "conv_b", "w_down"):
        w[k] = f(inp[k])
    rowperm = np.concatenate([np.arange(512)] + [np.concatenate([512 + jj * 64 + np.arange(64), 512 + (4 + jj) * 64 + np.arange(64)])
                                                  for jj in range(4)])
    w["w_out_odd"] = f(w["w_out_odd"][:, rowperm, :])
    wie = f(inp["w_in_even"])
    w["wie_qa"] = f(wie[:, :, 0:384]); w["wie_tm"] = f(wie[:, :, 384:928])
    w["wie_kr"] = f(wie[:, :, 640:672]); w["wie_krs"] = f(wie[:, :, 640 + _rope_perm(32)])
    wuq = f(inp["w_uq"]).reshape(2, 384, 12, 96)
    w["wuq_n"] = f(wuq[..., 0:64].reshape(2, 384, 768))
    w["wuq_r"] = f(wuq[..., 64:96].reshape(2, 384, 384))
    w["wuq_rs"] = f(wuq[..., 64 + _rope_perm(32)].reshape(2, 384, 384))
    wukv = f(inp["w_ukv"]).reshape(2, 256, 12, 128)
    w["wukv_k"] = f(wukv[..., 0:64].reshape(2, 256, 768)); w["wukv_v"] = f(wukv[..., 64:128].reshape(2, 256, 768))
    wp = f(inp["w_pool"]); bd = np.zeros((2, 2, 128, 128), np.float32)
    for e in range(2):
        for pr in range(2):
            bd[e, pr, 0:64, 0:64] = wp[e, 2 * pr]; bd[e, pr, 64:128, 64:128] = wp[e, 2 * pr + 1]
    w["wpool_bd"] = bd
    wio = f(inp["w_in_odd"])
    w["wio_qna"] = f(wio[:, :, 0:512]); w["wio_kna"] = f(wio[:, :, 512:1024])
    w["wio_nakv"] = f(wio[:, :, 512:1536]); w["wio_swkv"] = f(wio[:, :, 2048:2304])
    qsw = wio[:, :, 1536:2048].reshape(2, 1024, 8, 64)
    order = [0, 4, 1, 5, 2, 6, 3, 7]
    w["wio_qsw"] = f(qsw[:, :, order, :].reshape(2, 1024, 512))
    w["wio_qsws"] = f(qsw[:, :, order, :][..., _rope_perm(64)].reshape(2, 1024, 512))
    ksw = wio[:, :, 2048:2176].reshape(2, 1024, 2, 64)
    w["wio_ksw"] = f(ksw.reshape(2, 1024, 128)); w["wio_ksws"] = f(ksw[..., _rope_perm(64)].reshape(2, 1024, 128))
    sc = {True: _struct_consts(True), False: _struct_consts(False)}
    rpb = f(inp["na_rpb"])
    nab = {True: _na_bias(True, rpb), False: _na_bias(False, rpb)}
    xp = f(inp["x_prompt"]); xs = f(inp["x_sample"])
    maps = []
    for core in range(8):
        sample = core >= 4
        m = dict(w)
        m.update(sc[sample]); m["nabias"] = nab[sample]
        if sample:
            b = core - 4
            m["xT"] = f(xs[b].T)
            m["cvec"] = f(inp["c"])[b]
            m["c_mla"] = f(inp["cache_mla_latent"])[b]
            m["c_na"] = f(inp["cache_na_kv"])[b].reshape(2, 512, 2, 512)
            m["c_sw"] = f(inp["cache_swa_kv"])[b].reshape(2, 512, 2, 128)
        else:
            m["xT"] = f(xp[8 * core:8 * core + 8].reshape(T, D).T)
            m["cvec"] = f(inp["c_ctx"])
            m["c_mla"] = np.zeros((2, 512, 288), np.float32)
            m["c_na"] = np.zeros((2, 512, 2, 512), np.float32)
            m["c_sw"] = np.zeros((2, 512, 2, 128), np.float32)
        maps.append(m)
    return maps


IN_SHAPES = {
    "xT": [D, T], "cvec": [D], "c_mla": [2, 512, 288], "c_na": [2, 512, 2, 512], "c_sw": [2, 512, 2, 128],
    "w_mod": [4, D, 6 * D], "b_mod": [4, 6 * D], "norm_mix": [4, D], "norm_ffn": [4, D], "norm_final": [D],
    "mla_q_norm": [2, 384], "mla_kv_norm": [2, 256], "pool_scale": [2, 256],
    "w_out_even": [2, D, D], "w_out_odd": [2, D, D], "swa_sink": [2, 8],
    "w_up": [4, D, 2 * DFF], "conv_w": [4, 3, 2 * DFF], "conv_b": [4, 2 * DFF], "w_down": [4, DFF, D],
    "wie_qa": [2, D, 384], "wie_tm": [2, D, 544], "wie_kr": [2, D, 32], "wie_krs": [2, D, 32],
    "wuq_n": [2, 384, 768], "wuq_r": [2, 384, 384], "wuq_rs": [2, 384, 384],
    "wukv_k": [2, 256, 768], "wukv_v": [2, 256, 768], "wpool_bd": [2, 2, 128, 128],
    "wio_qna": [2, D, 512], "wio_kna": [2, D, 512], "wio_nakv": [2, D, 1024], "wio_swkv": [2, D, 256],
    "wio_qsw": [2, D, 512], "wio_qsws": [2, D, 512], "wio_ksw": [2, D, 128], "wio_ksws": [2, D, 128],
    "ident": [128, 128], "ropeE": [2, 128, T], "ropeO": [2, 128, T], "kmaskE": [9, NK], "qmaskE": [9, T],
    "flags": [128, 2], "Pm": [16, 128, 4, 3, 128], "swbias": [6, 128, 3, 128], "nabias": [2, 6, 128, 8, 5, 128],
}
OUT_SHAPES = {"yT": [D, T], "lat": [2, T, 288], "nakv": [2, T, 1024], "swkv": [2, T, 256]}


class KB:
    def __init__(self, nc, big, psums, ins, outs):
        self.nc = nc; self.big = big; self.P = psums; self.I = ins; self.O = outs
        self.pg = Prog()
        self.top = 0
        self.rr = {"mm": 0, "acc": 0, "tr": 0}
        self.RINGS_DEFAULT = {"mm": (0, 4), "acc": (4, 3), "tr": (7, 1)}
        self.rings = dict(self.RINGS_DEFAULT)
        self.uid = 0

    def alloc(self, nbytes):
        off = self.top; self.top += (nbytes + 7) // 8 * 2
        assert self.top * 4 <= 212800, f"SBUF overflow {self.top * 4}"
        return off

    def _shape(self, v, shape):
        if len(shape) == 1: return v
        if len(shape) == 2: return v.rearrange("p (a b) -> p a b", a=shape[0])
        if len(shape) == 3: return v.rearrange("p (a b c) -> p a b c", a=shape[0], b=shape[1])
        return v.rearrange("p (a b c d) -> p a b c d", a=shape[0], b=shape[1], c=shape[2])

    def f32(self, shape):
        n = int(np.prod(shape)); off = self.alloc(n * 4)
        return self._shape(self.big[:, off:off + n], shape)

    def bf(self, shape):
        n = int(np.prod(shape)); off = self.alloc(n * 2)
        return self._shape(self.big[:, off:off + (n + 1) // 2].bitcast(BF16)[:, 0:n], shape)

    def key(self, name):
        self.uid += 1
        return (name, self.uid)

    def ps(self, ring):
        lo, n = self.rings[ring]
        i = lo + self.rr[ring] % n; self.rr[ring] += 1
        return self.P[i], ("ps", i)

    def mm(self, out, lhsT, rhs, start, stop, reads, writes):
        return self.pg.add("pe", lambda e: e.matmul(out, lhsT, rhs, start=start, stop=stop), reads, writes)

    def tr(self, out, in_, ident, reads, writes):
        return self.pg.add("pe", lambda e: e.transpose(out, in_, ident), reads, writes)

    def act(self, out, in_, func, reads, writes, scale=1.0, bias=0.0, accum=None):
        if accum is None:
            return self.pg.add("act", lambda e: e.activation(out=out, in_=in_, func=func, bias=bias, scale=scale), reads, writes)
        return self.pg.add("act", lambda e: e.activation(out=out, in_=in_, func=func, bias=bias, scale=scale, accum_out=accum), reads, writes)

    def ts(self, eng, out, in0, s1, s2, op0, op1, reads, writes):
        if s2 is None:
            return self.pg.add(eng, lambda e: e.tensor_scalar(out=out, in0=in0, scalar1=s1, scalar2=None, op0=op0), reads, writes)
        return self.pg.add(eng, lambda e: e.tensor_scalar(out=out, in0=in0, scalar1=s1, scalar2=s2, op0=op0, op1=op1), reads, writes)

    def stt(self, eng, out, in0, scalar, in1, op0, op1, reads, writes):
        return self.pg.add(eng, lambda e: e.scalar_tensor_tensor(out=out, in0=in0, scalar=scalar, in1=in1, op0=op0, op1=op1), reads, writes)

    def tt(self, eng, out, in0, in1, op, reads, writes):
        return self.pg.add(eng, lambda e: e.tensor_tensor(out=out, in0=in0, in1=in1, op=op), reads, writes)

    def cp(self, eng, out, in_, reads, writes):
        if eng == "act":
            return self.pg.add("act", lambda e: e.copy(out=out, in_=in_), reads, writes)
        return self.pg.add(eng, lambda e: e.tensor_copy(out=out, in_=in_), reads, writes)

    def recip(self, out, in_, reads, writes):
        return self.pg.add("dve", lambda e: e.reciprocal(out=out, in_=in_), reads, writes)

    def memset(self, eng, ap, val, writes):
        return self.pg.add(eng, lambda e: e.memset(ap, val), (), writes)

    def dma(self, q, out, in_, reads, writes):
        return self.pg.add(q, lambda e: e.dma_start(out=out, in_=in_), reads, writes, dma=True)

    def wload(self, dst, src_ap, key):
        return self.dma("pool", dst, src_ap.rearrange("(kc p) n -> p kc n", p=128), (), (key,))


VOFF = {}
def _voff():
    o = 0
    for name, n in (("bmod", 192), ("nmix", 32), ("nffn", 32), ("nfin", 8), ("conv", 704), ("gq", 6),
                    ("pscale", 4), ("cvec", 8), ("gkv", 512), ("sink", 16), ("flags", 2)):
        VOFF[name] = (o, n); o += n
    return o
NV = _voff()


def _pack_vecs(inp, cvec, flags):
    v = np.zeros((128, NV), np.float32)
    def put(name, arr):
        o, n = VOFF[name]; v[:, o:o + n] = np.asarray(arr, np.float32).reshape(128, n)
    f = lambda a: np.asarray(a, np.float32)
    put("bmod", f(inp["b_mod"]).reshape(4, 48, 128).transpose(2, 0, 1))
    put("nmix", f(inp["norm_mix"]).reshape(4, 8, 128).transpose(2, 0, 1))
    put("nffn", f(inp["norm_ffn"]).reshape(4, 8, 128).transpose(2, 0, 1))
    put("nfin", f(inp["norm_final"]).reshape(8, 128).transpose(1, 0))
    cw = np.concatenate([f(inp["conv_w"]), f(inp["conv_b"])[:, None, :]], 1)
    put("conv", cw.reshape(4, 4, 44, 128).transpose(3, 0, 1, 2))
    put("gq", f(inp["mla_q_norm"]).reshape(2, 3, 128).transpose(2, 0, 1))
    put("pscale", f(inp["pool_scale"]).reshape(2, 2, 128).transpose(2, 0, 1))
    put("cvec", f(cvec).reshape(8, 128).transpose(1, 0))
    put("gkv", np.broadcast_to(f(inp["mla_kv_norm"]).reshape(1, 512), (128, 512)))
    put("sink", np.broadcast_to(f(inp["swa_sink"]).reshape(1, 16), (128, 16)))
    put("flags", flags)
    return v


def build_program():
    nc = bass.Bass("TRN2", target_bir_lowering=False)
    shapes = dict(IN_SHAPES); shapes["vecs"] = [128, NV]
    for k in ("cvec", "b_mod", "norm_mix", "norm_ffn", "norm_final", "mla_q_norm", "mla_kv_norm", "pool_scale",
              "swa_sink", "conv_w", "conv_b", "flags"):
        shapes.pop(k)
    I = {k: nc.dram_tensor(k, s, F32, kind="ExternalInput").ap() for k, s in shapes.items()}
    O = {k: nc.dram_tensor(k, s, F32, kind="ExternalOutput").ap() for k, s in OUT_SHAPES.items()}
    es = ExitStack()
    with es:
        big = es.enter_context(nc.sbuf_tensor("big", [128, 53200], F32))
        P = [es.enter_context(nc.psum_tensor(f"psb{i}", [128, 512], F32)) for i in range(8)]
        csem = {e: es.enter_context(nc.semaphore(f"c_{e}")) for e in Prog.ENGS}
        dsems = {(q, i): es.enter_context(nc.semaphore(f"d_{q}{i}")) for q in Prog.NDS for i in range(Prog.NDS[q])}
        block = es.enter_context(nc.Block())
        kb = KB(nc, big, [p[:, :] for p in P], I, O)
        _emit_all(kb)
        kb.pg.emit(nc, block, csem, dsems)
    return nc, kb


def _emit_all(kb):
    I, O, pg = kb.I, kb.O, kb.pg
    xres = kb.f32([8, T]); hT = kb.bf([8, T])
    vecs = kb.f32([NV])
    ones_bf = kb.bf([128]); ident_bf = kb.bf([128])
    scv = kb.bf([8]); modT_all = kb.f32([4, 48]); prm_all = kb.f32([4, 6, 8]); convx_all = kb.f32([4, 4, 44])
    prm = prm_all[:, 0]; convx = convx_all[:, 0]
    MARK = kb.top
    kb.xres, kb.hT, kb.vecs, kb.ones_bf, kb.ident_bf, kb.prm = xres, hT, vecs, ones_bf, ident_bf, prm
    kb.MARK = MARK

    def vv(name, l=None, per=None):
        o, n = VOFF[name]
        if l is None: return vecs[:, o:o + n]
        return vecs[:, o + l * per:o + (l + 1) * per]
    kb.vv = vv

    XK = lambda c, tb: ("x", c, tb)
    HK = lambda c, tb: ("h", c, tb)
    kb.XK, kb.HK = XK, HK
    kb.dma("sp", vecs, I["vecs"][:, :], (), ("vecs",))
    for c in range(8):
        kb.dma("sp", xres[:, c, :], I["xT"][c * 128:(c + 1) * 128, :], (), [XK(c, tb) for tb in range(4)])
    kb.dma("pool", ident_bf, I["ident"][:, :], (), ("ident",))
    kb.memset("dve", ones_bf, 1.0, ("ones",))
    kb.act(scv, vv("cvec"), AF.Silu, ("vecs",), ("scv",))

    def ring_setup(nslots, nel):
        kb.ring = [kb.bf([nel]) for _ in range(nslots)]
        kb.ring_i = 0; kb.ring_nel = nel

    def ring_next():
        i = kb.ring_i % len(kb.ring); kb.ring_i += 1
        return kb.ring[i], ("wr", i)
    kb.ring_setup, kb.ring_next = ring_setup, ring_next

    def wview(slot, kc, n):
        return slot[:, 0:kc * n].rearrange("p (a b) -> p a b", a=kc)
    kb.wview = wview

    def norm_mod(Acol, Bcol, sq, rs, tmp, dst_fn, final=False):
        nsq = sq.shape[1]; ntm = tmp.shape[1]
        two_rs = len(rs.shape) == 3
        rss = []
        for tb in range(4):
            sl = slice(tb * 512, (tb + 1) * 512)
            pst, pk = kb.ps("mm")
            for c in range(8):
                i = (tb * 8 + c) % nsq
                s = sq[:, i, :]
                kb.act(s, xres[:, c, sl], AF.Square, (XK(c, tb),), (("sq", i),))
                kb.mm(pst, ones_bf, s, c == 0, c == 7, (("sq", i), "ones"), (pk,))
            r = rs[:, tb % 2, :] if two_rs else rs
            rk = ("rs", tb % 2) if two_rs else "rs"
            kb.act(r, pst, AF.Sqrt, (pk,), (rk,), scale=1.0 / D, bias=EPS)
            kb.recip(r, r, (rk,), (rk,))
            for c in range(8):
                i = (tb * 8 + c) % ntm
                tm = tmp[:, i, :]
                kb.stt("dve", tm, xres[:, c, sl], Acol(c), r, ALU.mult, ALU.mult,
                       (XK(c, tb), rk, "prm", "vecs"), (("tmp", i),))
                dst_fn(c, tb, tm, ("tmp", i), Bcol(c) if Bcol else 0.0)
    kb.norm_mod = norm_mod

    def to_hT(c, tb, tm, tk, bias):
        kb.act(hT[:, c, tb * 512:(tb + 1) * 512], tm, AF.Identity, (tk, "prm"), (HK(c, tb),), bias=bias)

    def adaln_gen(layers, aring, pst, pk, pw=512):
        cnt = 0
        for l in layers:
            modT = modT_all[:, l]; prm = prm_all[:, l]; convx = convx_all[:, l]
            for piece in range(6144 // pw):
                slot, sk = aring[cnt % len(aring)], ("awr", cnt % len(aring)); cnt += 1
                wv = wview(slot, 8, pw)
                kb.wload(wv, I["w_mod"][l][:, piece * pw:(piece + 1) * pw], sk)
                for cc in range(pw // 128):
                    j = piece * (pw // 128) + cc
                    for kc in range(8):
                        kb.mm(pst[:, j:j + 1], wv[:, kc, cc * 128:(cc + 1) * 128], scv[:, kc:kc + 1], kc == 0, kc == 7,
                              (sk, "scv"), (pk,))
                yield
            mk = ("modT", l); pk_ = ("prm", l)
            kb.tt("dve", modT, pst[:, 0:48], vv("bmod", l, 48), ALU.add, (pk, "vecs"), (mk,))
            for (row, sc_i, g) in ((0, 1, "nmix"), (3, 4, "nffn")):
                kb.stt("dve", prm[:, row, :], modT[:, sc_i * 8:(sc_i + 1) * 8], 1.0, vv(g, l, 8), ALU.add, ALU.mult,
                       (mk, "vecs"), (pk_,))
            for (row, m_i) in ((1, 0), (2, 2), (4, 3), (5, 5)):
                kb.cp("dve", prm[:, row, :], modT[:, m_i * 8:(m_i + 1) * 8], (mk,), (pk_,))
            o, _ = VOFF["conv"]; cb = o + l * 176
            fo, _ = VOFF["flags"]
            w0 = vecs[:, cb:cb + 44]; w2 = vecs[:, cb + 88:cb + 132]
            kb.ts("dve", convx[:, 0, :], w0, vecs[:, fo:fo + 1], None, ALU.mult, None, ("vecs",), (("convx", l),))
            kb.ts("dve", convx[:, 1, :], w2, vecs[:, fo:fo + 1], None, ALU.mult, None, ("vecs",), (("convx", l),))
            kb.ts("dve", convx[:, 2, :], w0, vecs[:, fo + 1:fo + 2], -1.0, ALU.mult, ALU.mult, ("vecs",), (("convx", l),))
            kb.ts("dve", convx[:, 3, :], w2, vecs[:, fo + 1:fo + 2], -1.0, ALU.mult, ALU.mult, ("vecs",), (("convx", l),))
            yield

    def adaln_first():
        kb.top = MARK
        aring = [kb.bf([8 * 512]) for _ in range(3)]
        pst, pk = kb.ps("acc")
        for _ in adaln_gen([0], aring, pst, pk):
            pass
        pg.barrier()

    def norm_phase(row_a, row_b):
        kb.top = MARK
        sq = kb.bf([4, 512]); rs = kb.f32([2, 512]); tmp = kb.f32([4, 512])
        norm_mod(lambda c: kb.prm[:, row_a, c:c + 1], lambda c: kb.prm[:, row_b, c:c + 1], sq, rs, tmp, to_hT)
        pg.barrier()

    def ffn(l):
        kb.top = MARK
        o, _ = VOFF["conv"]; cb = o + l * 176
        cw = lambda k, c: vecs[:, cb + k * 44 + c:cb + k * 44 + c + 1]
        cx = lambda k, c: kb.convx[:, k, c:c + 1]
        ring_setup(2 if (l == 0 and NLAYERS > 1) else 3, 22 * 256)
        kb.rings = {"mm": (0, 7), "acc": (0, 7), "tr": (7, 1)}
        actb = kb.bf([NFC, 1024]); ta_r = kb.f32([4, 512]); tg_r = kb.f32([4, 512]); es = kb.f32([4, 2]); eh = kb.f32([44])
        agen = None
        if l == 0 and NLAYERS > 1:
            kb.rings = {"mm": (0, 6), "acc": (0, 6), "tr": (7, 1)}
            aring = [kb.bf([8 * 384]) for _ in range(2)]
            agen = adaln_gen(list(range(1, NLAYERS)), aring, kb.P[6], ("ps", 6), pw=384)
        for sb in range(2):
            for c in range(NFC):
                if c % 2 == 0:
                    slot, sk = ring_next(); wv = wview(slot, 8, 512)
                    kb.dma("pool", wv[:, :, 0:256], I["w_up"][l][:, c * 128:c * 128 + 256].rearrange("(kc p) n -> p kc n", p=128), (), (sk,))
                    kb.dma("pool", wv[:, :, 256:512], I["w_up"][l][:, DFF + c * 128:DFF + c * 128 + 256].rearrange("(kc p) n -> p kc n", p=128), (), (sk,))
                hp, hk = kb.ps("tr")
                tiles = {}; tts = {}
                for tb2 in range(2):
                    tb = sb * 2 + tb2; t0 = tb * 512
                    ri = (c % 2) * 2 + tb2
                    for gi in range(2):
                        col0 = gi * 256 + (c % 2) * 128
                        pst, pk = kb.ps("mm")
                        for kc in range(8):
                            kb.mm(pst, wv[:, kc, col0:col0 + 128], hT[:, kc, t0:t0 + 512], kc == 0, kc == 7,
                                  (sk, HK(kc, tb)), (pk,))
                        hcol = None
                        if sb == 0 and tb2 == 1: hcol = 1024
                        if hcol is not None:
                            for kc in range(8):
                                kb.mm(hp[:, gi:gi + 1], wv[:, kc, col0:col0 + 128], hT[:, kc, hcol:hcol + 1], kc == 0, kc == 7,
                                      (sk, HK(kc, hcol // 512)), (hk,))
                        tiles[(tb2, gi)] = (pst, pk)
                        tts[(tb2, gi)] = ((ta_r if gi == 0 else tg_r)[:, ri, :], ("tconv", gi, ri))
                    ccs = [gi * NFC + c for gi in range(2)]
                    for gi in range(2):
                        (pst, pk), (tt_, tk) = tiles[(tb2, gi)], tts[(tb2, gi)]
                        kb.act(tt_, pst, AF.Identity, (pk, "vecs"), (tk,), scale=cw(1, ccs[gi]), bias=cw(3, ccs[gi]))
                        if tb2 == 0:
                            kb.cp("act", es[:, (c % 2) * 2 + gi, 0:1], pst[:, 511:512], (pk,), (("es", (c % 2) * 2 + gi),))
                        if sb == 0 and tb2 == 1:
                            kb.cp("act", eh[:, ccs[gi]:ccs[gi] + 1], pst[:, 511:512], (pk,), (("eh", ccs[gi]),))
                    for gi in range(2):
                        (pst, pk), (tt_, tk) = tiles[(tb2, gi)], tts[(tb2, gi)]
                        kb.stt("dve", tt_[:, 1:512], pst[:, 0:511], cw(0, ccs[gi]), tt_[:, 1:512], ALU.mult, ALU.add, (pk, tk, "vecs"), (tk,))
                    for gi in range(2):
                        (pst, pk), (tt_, tk) = tiles[(tb2, gi)], tts[(tb2, gi)]
                        kb.stt("dve", tt_[:, 0:511], pst[:, 1:512], cw(2, ccs[gi]), tt_[:, 0:511], ALU.mult, ALU.add, (pk, tk, "vecs"), (tk,))
                    for gi in range(2):
                        (pst, pk), (tt_, tk) = tiles[(tb2, gi)], tts[(tb2, gi)]
                        kb.stt("dve", tt_[:, 256:257], pst[:, 255:256], cx(2, ccs[gi]), tt_[:, 256:257], ALU.mult, ALU.add, (pk, tk), (tk,))
                    for gi in range(2):
                        (pst, pk), (tt_, tk) = tiles[(tb2, gi)], tts[(tb2, gi)]
                        kb.stt("dve", tt_[:, 255:256], pst[:, 256:257], cx(3, ccs[gi]), tt_[:, 255:256], ALU.mult, ALU.add, (pk, tk), (tk,))
                    for gi in range(2):
                        (pst, pk), (tt_, tk) = tiles[(tb2, gi)], tts[(tb2, gi)]
                        if tb2 == 0 and sb == 1:
                            kb.act(tt_[:, 0:1], eh[:, ccs[gi]:ccs[gi] + 1], AF.Identity, (("eh", ccs[gi]), tk), (tk,), scale=cx(0, ccs[gi]), bias=tt_[:, 0:1])
                        if tb2 == 1:
                            ek = ("es", (c % 2) * 2 + gi)
                            kb.act(tt_[:, 0:1], es[:, (c % 2) * 2 + gi, 0:1], AF.Identity, (ek, tk), (tk,), scale=cx(0, ccs[gi]), bias=tt_[:, 0:1])
                        if tb2 == 1 and sb == 0:
                            kb.act(tt_[:, 511:512], hp[:, gi:gi + 1], AF.Identity, (hk, tk), (tk,), scale=cx(1, ccs[gi]), bias=tt_[:, 511:512])
                for gi in range(2):
                    (p1, k1), (t0_, tk0) = tiles[(1, gi)], tts[(0, gi)]
                    kb.act(t0_[:, 511:512], p1[:, 0:1], AF.Identity, (k1, tk0), (tk0,), scale=cx(1, gi * NFC + c), bias=t0_[:, 511:512])
                for tb2 in range(2):
                    (ta, tak), (tg, tgk) = tts[(tb2, 0)], tts[(tb2, 1)]
                    kb.act(tg, tg, AF.Silu, (tgk,), (tgk,))
                    kb.tt("dve", actb[:, c, tb2 * 512:(tb2 + 1) * 512], ta, tg, ALU.mult, (tak, tgk), (("actb", c, tb2),))
                if agen is not None and c % 2 == 1:
                    next(agen, None)
            for dp in range(4):
                slot, sk = ring_next(); wd = wview(slot, NFC, 256)
                kb.wload(wd, I["w_down"][l][:, dp * 256:(dp + 1) * 256], sk)
                for d2 in range(2):
                    dch = dp * 2 + d2
                    for tb2 in range(2):
                        tb = sb * 2 + tb2
                        pst, pk = kb.ps("acc")
                        for fc in range(NFC):
                            kb.mm(pst, wd[:, fc, d2 * 128:(d2 + 1) * 128], actb[:, fc, tb2 * 512:(tb2 + 1) * 512], fc == 0, fc == NFC - 1,
                                  (sk, ("actb", fc, tb2)), (pk,))
                        xs = xres[:, dch, tb * 512:(tb + 1) * 512]
                        kb.stt("dve", xs, pst, kb.prm[:, 5, dch:dch + 1], xs, ALU.mult, ALU.add, (pk, XK(dch, tb)), (XK(dch, tb),))
                if agen is not None:
                    next(agen, None)
        if agen is not None:
            for _ in agen:
                pass
        pg.barrier()
        kb.rings = dict(kb.RINGS_DEFAULT)

    def final_out():
        kb.top = MARK
        sq = kb.bf([4, 512]); rs = kb.f32([2, 512]); tmp = kb.f32([4, 512]); yst = kb.f32([4, 512])
        cnt = [0]
        def to_out(c, tb, tm, tk, bias):
            i = cnt[0] % 4; cnt[0] += 1
            kb.cp("act", yst[:, i, :], tm, (tk,), (("yst", i),))
            kb.dma("sp", O["yT"][c * 128:(c + 1) * 128, tb * 512:(tb + 1) * 512], yst[:, i, :], (("yst", i),), ())
        o, _ = VOFF["nfin"]
        norm_mod(lambda c: vecs[:, o + c:o + c + 1], None, sq, rs, tmp, to_out)

    from_mixers = _mixers(kb)
    adaln_first()
    try:
        for l in range(NLAYERS):
            kb.prm = prm_all[:, l]; kb.convx = convx_all[:, l]
            norm_phase(0, 1)
            if l % 2 == 0 and STAGES["even"]:
                from_mixers["even"](l, l // 2)
            if l % 2 == 1 and STAGES["odd"]:
                from_mixers["odd"](l, l // 2)
            if STAGES["ffn"]:
                norm_phase(3, 4)
                ffn(l)
    except _Stop:
        pg.barrier()
    final_out()


def _mixers(kb):
    I, O, pg = kb.I, kb.O, kb.pg
    xres, hT, vecs, ones_bf, ident_bf, prm = kb.xres, kb.hT, kb.vecs, kb.ones_bf, kb.ident_bf, kb.prm
    XK, HK, vv, MARK = kb.XK, kb.HK, kb.vv, kb.MARK
    wview = kb.wview
    fo = VOFF["flags"][0]

    def bfv(pst):
        return pst[:, 0:256].bitcast(BF16)

    def out_proj(wname, e, extra, src=None):
        src = hT if src is None else src
        kb.ring_setup(2, 8 * 512)
        for piece in range(2):
            slot, sk = kb.ring_next(); wo = wview(slot, 8, 512)
            kb.wload(wo, I[wname][e][:, piece * 512:(piece + 1) * 512], sk)
            for d4 in range(4):
                dch = piece * 4 + d4
                for tb in range(4):
                    sl = slice(tb * 512, (tb + 1) * 512)
                    pst, pk = kb.ps("acc")
                    for kc in range(8):
                        if extra is not None and kc >= 6:
                            rhs, rk = extra[:, kc - 6, sl], ("yp", kc - 6, tb)
                        else:
                            rhs, rk = src[:, kc, sl], HK(kc, tb)
                        kb.mm(pst, wo[:, kc, d4 * 128:(d4 + 1) * 128], rhs, kc == 0, kc == 7, (sk, rk), (pk,))
                    xs = xres[:, dch, sl]
                    kb.stt("dve", xs, pst, kb.prm[:, 2, dch:dch + 1], xs, ALU.mult, ALU.add, (pk, XK(dch, tb)), (XK(dch, tb),))
        pg.barrier()

    def rope_combine(dst, dk, pa, pak, pb, pbk, tab, tabk, t1, t2, np_, pre=1.0):
        kb.stt("dve", t1[0:np_, :], pa[0:np_, :], pre, tab[0:np_, 0, :], ALU.mult, ALU.mult, (pak, tabk), ("rt1",))
        kb.stt("dve", t2[0:np_, :], pb[0:np_, :], pre, tab[0:np_, 1, :], ALU.mult, ALU.mult, (pbk, tabk), ("rt2",))
        kb.tt("dve", dst, t1[0:np_, :], t2[0:np_, :], ALU.add, ("rt1", "rt2"), (dk,))

    def even(l, e):
        kb.top = MARK
        qnT = kb.bf([3, T]); qrT = kb.bf([3, T]); cT = kb.bf([2, NK]); krT = kb.bf([NK]); ypT = kb.bf([2, T])
        MARK_E = kb.top
        wtm = kb.bf([8, 544]); wpl = kb.bf([2, 128]); sk = "wtm"; skp = "wpl"
        xp_tok = kb.bf([16, 384]); pooledT = kb.bf([2, T]); lat_st = kb.f32([2, 288]); ctok = kb.bf([2, 256])
        sqt = kb.f32([2, 256]); ssq = kb.f32([2]); ctxl = kb.bf([4, 288]); Pms = [kb.bf([4, 3, 128]) for _ in range(2)]
        kb.memset("dve", xp_tok, 0.0, [("xp", j) for j in range(16)])
        kb.wload(wtm, I["wie_tm"][e], sk)
        kb.dma("pool", ctxl, I["c_mla"][e].rearrange("(b p) n -> p b n", p=128), (), ("ctxl",))
        for j in range(16):
            i = j % 2; tb = j // 4
            p1, k1 = kb.ps("mm"); p2, k2 = kb.ps("mm")
            for kc in range(8):
                lh = hT[:, kc, j * 128:(j + 1) * 128]
                kb.mm(p1[:, 0:288], lh, wtm[:, kc, 0:288], kc == 0, kc == 7, (sk, HK(kc, tb)), (k1,))
                kb.mm(p2[:, 0:256], lh, wtm[:, kc, 288:544], kc == 0, kc == 7, (sk, HK(kc, tb)), (k2,))
            kb.act(sqt[:, i, :], p1[:, 0:256], AF.Square, (k1,), (("sqt", i),))
            pg.add("dve", lambda en, o=ssq[:, i:i + 1], a=sqt[:, i, :]: en.reduce_sum(out=o, in_=a, axis=mybir.AxisListType.X),
                   (("sqt", i),), (("ssq", i),))
            kb.act(ssq[:, i:i + 1], ssq[:, i:i + 1], AF.Sqrt, (("ssq", i),), (("ssq", i),), scale=1.0 / 256, bias=EPS)
            kb.recip(ssq[:, i:i + 1], ssq[:, i:i + 1], (("ssq", i),), (("ssq", i),))
            go = VOFF["gkv"][0] + e * 256
            kb.stt("dve", lat_st[:, i, 0:256], p1[:, 0:256], ssq[:, i:i + 1], vecs[:, go:go + 256], ALU.mult, ALU.mult,
                   (k1, ("ssq", i), "vecs"), (("lat", i),))
            kb.cp("act", lat_st[:, i, 256:288], p1[:, 256:288], (k1,), (("lat", i),))
            kb.dma("sp", O["lat"][e, j * 128:(j + 1) * 128, :], lat_st[:, i, :], (("lat", i),), ())
            kb.cp("dve", ctok[:, i, :], lat_st[:, i, 0:256], (("lat", i),), (("ctok", i),))
            ptr, tk = kb.ps("tr"); pb = bfv(ptr)
            for cc in range(2):
                kb.tr(pb[:, cc * 128:(cc + 1) * 128], ctok[:, i, cc * 128:(cc + 1) * 128], ident_bf, (("ctok", i), "ident"), (tk,))
            kb.cp("act", cT[:, :, j * 128:(j + 1) * 128], pb[:, 0:256].rearrange("p (a b) -> p a b", a=2), (tk,), (("cT", j),))
            for half in range(2):
                dst = xp_tok[:, j, half * 192:(half + 1) * 192].rearrange("p (a b) -> p a b", a=3)[:, 0:3:2, :]
                src = p2[:, half * 128:(half + 1) * 128].rearrange("p (a b) -> p a b", a=2)
                kb.cp("act", dst, src, (k2,), (("xp", j),))
        ck("e_tm")
        for blk in range(4):
            ptr, tk = kb.ps("tr"); pb = bfv(ptr)
            for cc in range(2):
                kb.tr(pb[:, cc * 128:(cc + 1) * 128], ctxl[:, blk, cc * 128:(cc + 1) * 128], ident_bf, ("ctxl", "ident"), (tk,))
            kb.tr(pb[0:32, 256:384], ctxl[:, blk, 256:288], ident_bf, ("ctxl", "ident"), (tk,))
            kb.cp("act", cT[:, :, T + blk * 128:T + (blk + 1) * 128], pb[:, 0:256].rearrange("p (a b) -> p a b", a=2), (tk,), (("cT", 16 + blk),))
            kb.cp("dve", krT[0:32, T + blk * 128:T + (blk + 1) * 128], pb[0:32, 256:384], (tk,), (("krT", 4),))
        ck("e_ctx")
        kb.dma("pool", wpl, I["wpool_bd"][e].rearrange("a k m -> k a m"), (), (skp,))
        for j in range(16):
            pm = Pms[j % 2]; pmk = ("Pm", j % 2)
            kb.dma("pool", pm, I["Pm"][j], (), (pmk,))
            for pr in range(2):
                pst, pk = kb.ps("mm")
                todo = [(gg, sbi) for gg in range(2) for sbi in range(3) if 0 <= j - 1 + sbi <= 15]
                for n_, (gg, sbi) in enumerate(todo):
                    sbk = j - 1 + sbi; c0 = pr * 192 + gg * 64
                    kb.mm(pst[:, 0:128], xp_tok[:, sbk, c0:c0 + 128], pm[:, pr * 2 + gg, sbi, :], n_ == 0, n_ == len(todo) - 1,
                          (("xp", sbk), pmk), (pk,))
                kb.cp("act", pooledT[:, pr, j * 128:(j + 1) * 128], pst[:, 0:128], (pk,), (("pooled", pr, j // 4),))
        pso = VOFF["pscale"][0] + e * 2
        for pr in range(2):
            for tb in range(4):
                sl = slice(tb * 512, (tb + 1) * 512)
                pst, pk = kb.ps("mm")
                kb.mm(pst, wpl[:, pr, :], pooledT[:, pr, sl], True, True, (skp, ("pooled", pr, tb)), (pk,))
                kb.ts("dve", ypT[:, pr, sl], pst, vecs[:, pso + pr:pso + pr + 1], None, ALU.mult, None, (pk, "vecs"), (("yp", pr, tb),))
        ck("e_pool")
        pg.barrier()
        kb.top = MARK_E
        kb.ring_setup(2, 8 * 544)
        qa_f = kb.f32([3, 512]); sq = kb.bf([2, 512]); rs = kb.f32([512]); t1 = kb.f32([512]); t2 = kb.f32([512])
        tabE = [kb.f32([2, 512]) for _ in range(2)]
        slot, sk1 = kb.ring_next(); wkr = wview(slot, 8, 448)
        kb.dma("pool", wkr[:, :, 0:32], I["wie_kr"][e].rearrange("(kc p) n -> p kc n", p=128), (), (sk1,))
        kb.dma("pool", wkr[:, :, 32:64], I["wie_krs"][e].rearrange("(kc p) n -> p kc n", p=128), (), (sk1,))
        kb.dma("pool", wkr[:, :, 64:448], I["wie_qa"][e].rearrange("(kc p) n -> p kc n", p=128), (), (sk1,))
        slot, sk2 = kb.ring_next(); wqr = wview(slot, 3, 768)
        kb.dma("pool", wqr[:, :, 0:384], I["wuq_r"][e].rearrange("(kc p) n -> p kc n", p=128), (), (sk2,))
        kb.dma("pool", wqr[:, :, 384:768], I["wuq_rs"][e].rearrange("(kc p) n -> p kc n", p=128), (), (sk2,))
        gq0 = VOFF["gq"][0] + e * 3
        for tb in range(4):
            sl = slice(tb * 512, (tb + 1) * 512)
            tab = tabE[tb % 2]; tabk = ("tabE", tb % 2)
            kb.dma("sp", tab, I["ropeE"][:, :, sl].rearrange("a p t -> p a t"), (), (tabk,))
            pa, pak = kb.ps("mm"); pb_, pbk = kb.ps("mm")
            for kc in range(8):
                kb.mm(pa[0:32, :], wkr[:, kc, 0:32], hT[:, kc, sl], kc == 0, kc == 7, (sk1, HK(kc, tb)), (pak,))
            for kc in range(8):
                kb.mm(pb_[0:32, :], wkr[:, kc, 32:64], hT[:, kc, sl], kc == 0, kc == 7, (sk1, HK(kc, tb)), (pbk,))
            rope_combine(krT[0:32, sl], ("krT", tb), pa, pak, pb_, pbk, tab, tabk, t1, t2, 32)
            pn, pnk = kb.ps("acc")
            for m in range(3):
                pq, pqk = kb.ps("mm")
                for kc in range(8):
                    kb.mm(pq, wkr[:, kc, 64 + m * 128:64 + (m + 1) * 128], hT[:, kc, sl], kc == 0, kc == 7, (sk1, HK(kc, tb)), (pqk,))
                kb.cp("act", qa_f[:, m, :], pq, (pqk,), (("qa_f", m),))
                kb.act(sq[:, m % 2, :], qa_f[:, m, :], AF.Square, (("qa_f", m),), (("sq", m % 2),))
                kb.mm(pn, ones_bf, sq[:, m % 2, :], m == 0, m == 2, (("sq", m % 2), "ones"), (pnk,))
            kb.act(rs, pn, AF.Sqrt, (pnk,), ("rs",), scale=1.0 / 384, bias=EPS)
            kb.recip(rs, rs, ("rs",), ("rs",))
            for m in range(3):
                kb.stt("dve", qnT[:, m, sl], qa_f[:, m, :], vecs[:, gq0 + m:gq0 + m + 1], rs, ALU.mult, ALU.mult,
                       (("qa_f", m), "rs", "vecs"), (("qnT", m, tb),))
            for m in range(3):
                pa, pak = kb.ps("mm"); pb_, pbk = kb.ps("mm")
                for kc in range(3):
                    kb.mm(pa, wqr[:, kc, m * 128:(m + 1) * 128], qnT[:, kc, sl], kc == 0, kc == 2, (sk2, ("qnT", kc, tb)), (pak,))
                for kc in range(3):
                    kb.mm(pb_, wqr[:, kc, 384 + m * 128:384 + (m + 1) * 128], qnT[:, kc, sl], kc == 0, kc == 2, (sk2, ("qnT", kc, tb)), (pbk,))
                rope_combine(qrT[:, m, sl], ("qrT", m, tb), pa, pak, pb_, pbk, tab, tabk, t1, t2, 128)
        ck("e_ea2")
        pg.barrier()
        kb.top = MARK_E
        kb.rings = {"mm": (0, 4), "acc": (4, 4), "tr": (7, 1)}
        wqn = kb.bf([3, 768]); wk = kb.bf([2, 768]); wvv = kb.bf([2, 768])
        KT = kb.bf([2, NK]); QT = kb.bf([2, T]); Vh = kb.bf([2, 20, 128]); PT = kb.bf([6, 512])
        rDs = kb.f32([2, 512]); rDt = kb.f32([2, 512]); sel = kb.f32([2, 128]); fin_state = []
        kb.memset("dve", rDt, 0.0, ("rDt",)); kb.memset("dve", sel, 0.0, ("sel",))
        kb.memset("dve", sel[64:65, 0, 0:64], 1.0, ("sel",))
        kb.memset("dve", sel[32:33, 1, 64:128], 1.0, ("sel",))
        kb.wload(wqn, I["wuq_n"][e], "wqn"); kb.wload(wk, I["wukv_k"][e], "wk"); kb.wload(wvv, I["wukv_v"][e], "wvv")
        for b in range(2):
            kb.dma("pool", KT[96:105, b, :], I["kmaskE"][:, :], (), (("KTm", b),))
            kb.dma("pool", QT[96:105, b, :], I["qmaskE"][:, :], (), (("QTm", b),))
            kb.dma("sp", KT[64:96, b, :], krT[0:32, :], (), (("KTr", b),))
            kb.memset("dve", Vh[:, b, :, :], 0.0, (("Vh", b),))
            oc = 64 if b == 0 else 32
            kb.memset("dve", Vh[:, b, :, oc:oc + 1], 1.0, (("Vh", b),))

        def proj(h, b):
            kb.dma("sp", QT[64:96, b, :], qrT[(h % 4) * 32:(h % 4) * 32 + 32, h // 4, :], (), (("QTr", b),))
            for tb in range(4):
                sl = slice(tb * 512, (tb + 1) * 512)
                pst, pk = kb.ps("mm")
                for kc in range(3):
                    kb.mm(pst[0:64, :], wqn[:, kc, h * 64:(h + 1) * 64], qnT[:, kc, sl], kc == 0, kc == 2, ("wqn",), (pk,))
                kb.cp("dve", QT[0:64, b, sl], pst[0:64, :], (pk,), (("QTn", b),))
            for k5 in range(5):
                sl = slice(k5 * 512, (k5 + 1) * 512)
                pst, pk = kb.ps("mm")
                for cc in range(2):
                    kb.mm(pst[0:64, :], wk[:, cc, h * 64:(h + 1) * 64], cT[:, cc, sl], cc == 0, cc == 1, ("wk",), (pk,))
                kb.cp("dve", KT[0:64, b, sl], pst[0:64, :], (pk,), (("KTn", b),))
            for g0, nb in ((0, 8), (8, 8), (16, 4)):
                pst, pk = kb.ps("mm")
                for i in range(nb):
                    kblk = g0 + i
                    for cc in range(2):
                        kb.mm(pst[:, i * 64:(i + 1) * 64], cT[:, cc, kblk * 128:(kblk + 1) * 128], wvv[:, cc, h * 64:(h + 1) * 64],
                              cc == 0, cc == 1, ("wvv",), (pk,))
                kb.cp("dve", Vh[:, b, g0:g0 + nb, b * 64:b * 64 + 64], pst[:, 0:nb * 64].rearrange("p (a b) -> p a b", a=nb),
                      (pk,), (("Vh", b),))

        def attn(h, b):
            rd_q = (("QTn", b), ("QTr", b), ("QTm", b)); rd_k = (("KTn", b), ("KTr", b), ("KTm", b))
            hp = slice(b * 64, b * 64 + 64)
            p0 = 64 if b == 0 else 32
            for qc in range(4):
                accO, ok_ = kb.ps("acc")
                pend = []

                def pv(kc, slot):
                    kb.mm(accO, Vh[:, b, kc, :], PT[:, slot, :], kc == 0, kc == 19, (("PT", slot), ("Vh", b)), (ok_,))
                for kc in range(20):
                    slot = (qc * 20 + kc) % 6
                    pst, pk = kb.ps("mm")
                    kb.mm(pst, KT[0:105, b, kc * 128:(kc + 1) * 128], QT[0:105, b, qc * 512:(qc + 1) * 512], True, True, rd_q + rd_k, (pk,))
                    kb.act(PT[:, slot, :], pst, AF.Exp, (pk,), (("PT", slot),), scale=MLA_SCALE)
                    pend.append((kc, slot))
                    if len(pend) > 2:
                        pv(*pend.pop(0))
                    if kc == 4 and fin_state:
                        fin_state.pop(0)()
                while pend:
                    pv(*pend.pop(0))
                ri = (h * 4 + qc) % 2
                rd = rDs[:, ri, :]; rk = ("rD", ri)
                rdt = rDt[:, ri, :]; rtk = ("rDt", ri)
                kb.recip(rdt[p0:p0 + 1, :], accO[p0:p0 + 1, :], (ok_,), (rtk,))

                def fin(accO=accO, ok_=ok_, rd=rd, rk=rk, rdt=rdt, rtk=rtk, qc=qc):
                    bc, bk = kb.ps("acc")
                    kb.mm(bc, sel[:, b, :], rdt, True, True, ("sel", rtk), (bk,))
                    kb.cp("dve", rd[hp, :], bc[hp, :], (bk,), (rk,))
                    kb.tt("dve", hT[hp, h // 2, qc * 512:(qc + 1) * 512], accO[hp, :], rd[hp, :], ALU.mult, (ok_, rk), (HK(h // 2, qc),))
                fin_state.append(fin)

        ck("e_ebsetup")
        proj(0, 0)
        ck("e_proj0")
        for h in range(12):
            if h + 1 < 12: proj(h + 1, (h + 1) % 2)
            attn(h, h % 2)
            ck("e_attn%d" % h)
        while fin_state:
            fin_state.pop(0)()
        pg.barrier()
        kb.rings = dict(kb.RINGS_DEFAULT)
        kb.top = MARK_E
        out_proj("w_out_even", e, ypT)

    def odd(l, e):
        kb.top = MARK
        QM = kb.bf([8, T]); ctxones = kb.bf([128]); zrow = kb.f32([128])
        kb.memset("dve", ctxones, 1.0, ("ctxones",))
        kb.ts("dve", ctxones, ctxones, vecs[:, fo:fo + 1], None, ALU.mult, None, ("ctxones", "vecs"), ("ctxones",))
        kb.memset("dve", zrow, 0.0, ("zrow",))
        MARK_O = kb.top
        KTna = kb.bf([4, NK]); Vna = kb.bf([20, 512])
        MARK_O2 = kb.top
        kb.ring_setup(2, 8 * 512)
        stg = kb.f32([2, 512]); ctxl = kb.bf([4, 512])
        kb.dma("pool", Vna[:, 16:20, :], I["c_na"][e][:, 1, :].rearrange("(b p) n -> p b n", p=128), (), ("Vna_ctx",))
        kb.dma("pool", ctxl, I["c_na"][e][:, 0, :].rearrange("(b p) n -> p b n", p=128), (), ("ctxl",))
        for piece in range(2):
            slot, sk = kb.ring_next(); wv = wview(slot, 8, 512)
            kb.wload(wv, I["wio_nakv"][e][:, piece * 512:(piece + 1) * 512], sk)
            for j in range(16):
                pst, pk = kb.ps("mm")
                for kc in range(8):
                    kb.mm(pst, hT[:, kc, j * 128:(j + 1) * 128], wv[:, kc, :], kc == 0, kc == 7, (sk, HK(kc, j // 4)), (pk,))
                kb.cp("act", stg[:, j % 2, :], pst, (pk,), (("stg", j % 2),))
                kb.dma("sp", O["nakv"][e, j * 128:(j + 1) * 128, piece * 512:(piece + 1) * 512], stg[:, j % 2, :], (("stg", j % 2),), ())
                if piece == 1:
                    kb.cp("dve", Vna[:, j, :], pst, (pk,), (("Vna", j),))
        for blk in range(4):
            ptr, tk = kb.ps("tr"); pb = bfv(ptr)
            for m in range(4):
                kb.tr(pb[:, m * 128:(m + 1) * 128], ctxl[:, blk, m * 128:(m + 1) * 128], ident_bf, ("ctxl", "ident"), (tk,))
            kb.cp("act", KTna[:, :, T + blk * 128:T + (blk + 1) * 128], pb.rearrange("p (a b) -> p a b", a=4), (tk,), (("KTna_ctx", blk),))
        for wname, isq in (("wio_qna", True), ("wio_kna", False)):
            slot, sk = kb.ring_next(); wv = wview(slot, 8, 512)
            kb.wload(wv, I[wname][e], sk)
            for m in range(4):
                for tb in range(4):
                    sl = slice(tb * 512, (tb + 1) * 512)
                    pst, pk = kb.ps("mm")
                    for kc in range(8):
                        kb.mm(pst, wv[:, kc, m * 128:(m + 1) * 128], hT[:, kc, sl], kc == 0, kc == 7, (sk, HK(kc, tb)), (pk,))
                    if isq:
                        kb.act(QM[:, m, sl], pst, AF.Copy, (pk,), [("QM", m, tb * 4 + i) for i in range(4)], scale=0.125)
                    else:
                        kb.cp("dve", KTna[:, m, sl], pst, (pk,), (("KTna", m, tb),))
        pg.barrier()
        kb.top = MARK_O2
        kb.rings = {"mm": (0, 6), "acc": (6, 2), "tr": (7, 1)}
        PT = kb.bf([6, 512]); nab = [kb.bf([2, 5, 128]) for _ in range(2)]; rDs = kb.f32([2, 128])
        units = [(j, i, hh) for j in range(4) for i in range(16) for hh in range(2)]
        st = {}

        def na_S(k):
            j, i, hh = units[k]
            if hh == 0:
                nb_ = nab[(k // 2) % 2]; nk = ("nab", (k // 2) % 2)
                kb.dma("pool", nb_, I["nabias"][e, VAR_OF[i]][:, 2 * j:2 * j + 2, :, :], (), (nk,))
                start = min(max(i - 2, 0), 11)
                tiles = [(start + c, c) for c in range(5)] + [(16 + c, None) for c in range(4)]
                accb, ak = kb.ps("acc")
                st[(j, i)] = (nb_, nk, tiles, accb, ak)
            nb_, nk, tiles, accb, ak = st[(j, i)]
            hp = slice(hh * 64, hh * 64 + 64)
            banks = []
            for t, (kblk, c) in enumerate(tiles):
                if t % 4 == 0:
                    banks.append(kb.ps("mm"))
                pst, pk = banks[-1]; col = slice((t % 4) * 128, (t % 4) * 128 + 128)
                kb.mm(pst[:, col], KTna[hp, j, kblk * 128:(kblk + 1) * 128], QM[hp, j, i * 128:(i + 1) * 128], True, True,
                      (("QM", j, i),), (pk,))
            kb.tt("dve", banks[0][0], banks[0][0], nb_[:, hh, 0:4, :].rearrange("p a b -> p (a b)"), ALU.add, (banks[0][1], nk), (banks[0][1],))
            kb.tt("dve", banks[1][0][:, 0:128], banks[1][0][:, 0:128], nb_[:, hh, 4, :], ALU.add, (banks[1][1], nk), (banks[1][1],))
            slots = []
            for bi, (pst, pk) in enumerate(banks):
                ncol = min(4, 9 - bi * 4) * 128
                sl_ = (k * 3 + bi) % 6
                kb.act(PT[:, sl_, 0:ncol], pst[:, 0:ncol], AF.Exp, (pk,), (("PT", sl_),))
                slots.append(sl_)
            st[(j, i, hh)] = slots

        def na_OD(k):
            j, i, hh = units[k]
            nb_, nk, tiles, accb, ak = st[(j, i)]
            slots = st[(j, i, hh)]
            Oh = accb[:, hh * 128:(hh + 1) * 128]; Dh = accb[:, 256 + hh * 128:256 + (hh + 1) * 128]
            for t, (kblk, c) in enumerate(tiles):
                sl_ = slots[t // 4]
                kb.mm(Oh, Vna[:, kblk, j * 128:(j + 1) * 128], PT[:, sl_, (t % 4) * 128:(t % 4) * 128 + 128], t == 0, t == 8,
                      (("PT", sl_),), (ak,))
            for t, (kblk, c) in enumerate(tiles):
                sl_ = slots[t // 4]
                kb.mm(Dh, ones_bf if c is not None else ctxones, PT[:, sl_, (t % 4) * 128:(t % 4) * 128 + 128], t == 0, t == 8,
                      (("PT", sl_), "ones", "ctxones"), (ak,))
            if hh == 1:
                for h2 in range(2):
                    hp = slice(h2 * 64, h2 * 64 + 64)
                    rd = rDs[:, (k + h2) % 2, :]; rk = ("rD", (k + h2) % 2)
                    kb.recip(rd[hp, :], accb[hp, 256 + h2 * 128:256 + (h2 + 1) * 128], (ak,), (rk,))
                    kb.tt("dve", QM[hp, j, i * 128:(i + 1) * 128], accb[hp, h2 * 128:(h2 + 1) * 128], rd[hp, :], ALU.mult, (ak, rk), (("QM", j, i),))

        na_S(0)
        for k in range(len(units)):
            if k + 1 < len(units): na_S(k + 1)
            na_OD(k)
        pg.barrier()
        kb.rings = dict(kb.RINGS_DEFAULT)
        kb.top = MARK_O
        KTsw = kb.bf([NK]); Vsw = kb.bf([20, 128])
        MARK_S = kb.top
        kb.ring_setup(3, 8 * 512)
        stg = kb.f32([2, 256]); ctxk = kb.bf([4, 128]); t1 = kb.f32([512]); t2 = kb.f32([512])
        tabO = [kb.f32([2, 512]) for _ in range(2)]
        kb.dma("pool", Vsw[:, 16:20, :], I["c_sw"][e][:, 1, :].rearrange("(b p) n -> p b n", p=128), (), ("Vsw_ctx",))
        kb.dma("pool", ctxk, I["c_sw"][e][:, 0, :].rearrange("(b p) n -> p b n", p=128), (), ("ctxk",))
        slot, sk = kb.ring_next(); wv = wview(slot, 8, 512)
        kb.wload(wv[:, :, 0:256], I["wio_swkv"][e], sk)
        kb.dma("pool", wv[:, :, 256:384], I["wio_ksw"][e].rearrange("(kc p) n -> p kc n", p=128), (), (sk,))
        kb.dma("pool", wv[:, :, 384:512], I["wio_ksws"][e].rearrange("(kc p) n -> p kc n", p=128), (), (sk,))
        for j in range(16):
            pst, pk = kb.ps("mm")
            for kc in range(8):
                kb.mm(pst[:, 0:256], hT[:, kc, j * 128:(j + 1) * 128], wv[:, kc, 0:256], kc == 0, kc == 7, (sk, HK(kc, j // 4)), (pk,))
            kb.cp("act", stg[:, j % 2, :], pst[:, 0:256], (pk,), (("stg", j % 2),))
            kb.dma("sp", O["swkv"][e, j * 128:(j + 1) * 128, :], stg[:, j % 2, :], (("stg", j % 2),), ())
            kb.cp("dve", Vsw[:, j, :], pst[:, 128:256], (pk,), (("Vsw", j),))
        ptr, tk = kb.ps("tr"); pb = bfv(ptr)
        for blk in range(4):
            kb.tr(pb[:, blk * 128:(blk + 1) * 128], ctxk[:, blk, :], ident_bf, ("ctxk", "ident"), (tk,))
        kb.cp("act", KTsw[:, T:NK], pb, (tk,), ("KTsw_ctx",))
        slot, skq = kb.ring_next(); wq = wview(slot, 8, 512)
        kb.wload(wq, I["wio_qsw"][e], skq)
        slot, skqs = kb.ring_next(); wqs = wview(slot, 8, 512)
        kb.wload(wqs, I["wio_qsws"][e], skqs)
        for tb in range(4):
            sl = slice(tb * 512, (tb + 1) * 512)
            tab = tabO[tb % 2]; tabk = ("tabO", tb % 2)
            kb.dma("sp", tab, I["ropeO"][:, :, sl].rearrange("a p t -> p a t"), (), (tabk,))
            pa, pak = kb.ps("mm"); pb_, pbk = kb.ps("mm")
            for kc in range(8):
                kb.mm(pa, wv[:, kc, 256:384], hT[:, kc, sl], kc == 0, kc == 7, (sk, HK(kc, tb)), (pak,))
            for kc in range(8):
                kb.mm(pb_, wv[:, kc, 384:512], hT[:, kc, sl], kc == 0, kc == 7, (sk, HK(kc, tb)), (pbk,))
            rope_combine(KTsw[:, sl], ("KTsw", tb), pa, pak, pb_, pbk, tab, tabk, t1, t2, 128)
            for m in range(4):
                pa, pak = kb.ps("mm"); pb_, pbk = kb.ps("mm")
                for kc in range(8):
                    kb.mm(pa, wq[:, kc, m * 128:(m + 1) * 128], hT[:, kc, sl], kc == 0, kc == 7, (skq, HK(kc, tb)), (pak,))
                for kc in range(8):
                    kb.mm(pb_, wqs[:, kc, m * 128:(m + 1) * 128], hT[:, kc, sl], kc == 0, kc == 7, (skqs, HK(kc, tb)), (pbk,))
                rope_combine(QM[:, 4 + m, sl], ("QMs", m, tb), pa, pak, pb_, pbk, tab, tabk, t1, t2, 128, pre=0.125)
        pg.barrier()
        kb.top = MARK_S
        kb.rings = {"mm": (0, 4), "acc": (4, 4), "tr": (7, 1)}
        PT = kb.bf([14, 512]); swb = kb.bf([6, 3, 128]); esink = kb.bf([2, 512]); rDs = kb.f32([2, 512])
        kb.dma("pool", swb, I["swbias"].rearrange("v k c q -> k v c q"), (), ("swb",))
        so = VOFF["sink"][0] + e * 8
        for g in range(2):
            for hd in range(4):
                kb.act(esink[0:1, g, hd * 128:(hd + 1) * 128], zrow[0:1, 0:128], AF.Exp, ("zrow", "vecs"), ("esink",),
                       bias=vecs[0:1, so + 4 * g + hd:so + 4 * g + hd + 1])
        sunits = [(i, g) for i in range(16) for g in range(2)]
        sst = {}

        def sw_S(k):
            i, g = sunits[k]
            hp = slice(g * 64, g * 64 + 64)
            chunks = [(min(max(i - 1 + c, 0), 15), c) for c in range(3)] + [(16 + c, None) for c in range(4)]
            slots = []
            for t, (kblk, c) in enumerate(chunks):
                pst, pk = kb.ps("mm")
                kb.mm(pst, KTsw[hp, kblk * 128:(kblk + 1) * 128], QM[hp, 4:8, i * 128:(i + 1) * 128], True, True, (), (pk,))
                if c is not None:
                    p3 = pst.rearrange("p (a b) -> p a b", a=4)
                    kb.tt("dve", p3, p3, swb[:, VAR_OF[i], c, :].unsqueeze(1).to_broadcast([128, 4, 128]), ALU.add, (pk, "swb"), (pk,))
                sl_ = (k * 7 + t) % 14
                kb.act(PT[:, sl_, :], pst, AF.Exp, (pk,), (("PT", sl_),))
                slots.append(sl_)
            sst[k] = (chunks, slots)

        def sw_OD(k):
            i, g = sunits[k]
            hp = slice(g * 64, g * 64 + 64)
            chunks, slots = sst[k]
            accO, ok_ = kb.ps("acc"); accD, dk_ = kb.ps("acc")
            for t, (kblk, c) in enumerate(chunks):
                kb.mm(accO, Vsw[:, kblk, :], PT[:, slots[t], :], t == 0, t == 6, (("PT", slots[t]),), (ok_,))
            for t, (kblk, c) in enumerate(chunks):
                kb.mm(accD, ones_bf if c is not None else ctxones, PT[:, slots[t], :], t == 0, False, (("PT", slots[t]), "ones", "ctxones"), (dk_,))
            kb.mm(accD, ones_bf[0:1, :], esink[0:1, g, :], False, True, ("esink", "ones"), (dk_,))
            rd = rDs[:, k % 2, :]; rk = ("rD", k % 2)
            kb.recip(rd[hp, :], accD[hp, :], (dk_,), (rk,))
            kb.tt("dve", QM[hp, 4:8, i * 128:(i + 1) * 128], accO[hp, :].rearrange("p (a b) -> p a b", a=4),
                  rd[hp, :].rearrange("p (a b) -> p a b", a=4), ALU.mult, (ok_, rk), (("QMo", i, g),))

        sw_S(0)
        for k in range(len(sunits)):
            if k + 1 < len(sunits): sw_S(k + 1)
            sw_OD(k)
        pg.barrier()
        kb.rings = dict(kb.RINGS_DEFAULT)
        kb.top = MARK_O
        out_proj("w_out_odd", e, None, src=QM)

    return {"even": even, "odd": odd}


_CACHE = {}


def kernel(**inputs):
    maps = _host_inputs(inputs)
    for core, m in enumerate(maps):
        sample = core >= 4
        fl = m.pop("flags")
        m["vecs"] = _pack_vecs(inputs, m.pop("cvec"), fl)
        for k in ("b_mod", "norm_mix", "norm_ffn", "norm_final", "mla_q_norm", "mla_kv_norm", "pool_scale",
                  "swa_sink", "conv_w", "conv_b"):
            m.pop(k, None)
    if "nc" not in _CACHE:
        _CACHE["nc"] = build_program()[0]
    nc = _CACHE["nc"]
    res = run_bass_kernel_spmd(nc, maps, core_ids=list(range(8)))
    R = res.results
    y_prompt = np.concatenate([np.asarray(R[c]["yT"]).T.reshape(8, 256, D) for c in range(4)], 0)
    y_sample = np.stack([np.asarray(R[4 + b]["yT"]).T for b in range(4)], 0)
    lat = np.concatenate([np.asarray(R[c]["lat"]).reshape(2, 8, 256, 288).transpose(1, 0, 2, 3) for c in range(4)], 0)
    na = np.concatenate([np.asarray(R[c]["nakv"]).reshape(2, 8, 256, 2, 8, 64).transpose(1, 0, 2, 3, 4, 5) for c in range(4)], 0)
    sw = np.concatenate([np.asarray(R[c]["swkv"]).reshape(2, 8, 256, 2, 2, 64).transpose(1, 0, 2, 3, 4, 5) for c in range(4)], 0)
    f = lambda a: np.ascontiguousarray(a, dtype=np.float32)
    return (f(y_prompt), f(y_sample), f(lat), f(na), f(sw))
```

```python
import numpy as np
import concourse.bass as bass
import concourse.mybir as mybir
from concourse.bass_utils import run_bass_kernel_spmd
from contextlib import ExitStack

F32, BF16 = mybir.dt.float32, mybir.dt.bfloat16
AF, ALU = mybir.ActivationFunctionType, mybir.AluOpType

D = 1024; T = 2048; DEPTH = 4; PAST = 512; NK = T + PAST
DFF = 2816; NFC = 22
EPS = 1e-6
MLA_SCALE = 96 ** -0.5
BM = 1024.0
NEGB = -30000.0
STAGES = {"even": True, "odd": True, "ffn": True}
STOP_AT = None
EMBED_WAITS = True


class _Stop(Exception):
    pass


def ck(name):
    if STOP_AT == name:
        raise _Stop()
NLAYERS = DEPTH


class Op:
    __slots__ = ("eng", "fn", "deps", "dma", "sem", "target", "pre", "need", "ticket")

    def __init__(self, eng, fn, dma):
        self.eng = eng; self.fn = fn; self.dma = dma; self.deps = []
        self.sem = None; self.target = 0; self.pre = None; self.need = False; self.ticket = 0


class Prog:
    ENGS = ("pe", "act", "dve", "pool", "sp")
    NDS = {"sp": 40, "pool": 40}

    def __init__(self):
        self.q = {e: [] for e in self.ENGS}
        self.lw = {}; self.rd = {}
        self.dsem_next = {e: 0 for e in self.NDS}
        self.dsem_tgt = {}
        self.dma_since = []

    def add(self, eng, fn, reads=(), writes=(), dma=False):
        op = Op(eng, fn, dma)
        deps = []
        for k in reads:
            w = self.lw.get(k)
            if w is not None: deps.append(w)
            if isinstance(k, tuple) and k[0] == "ps":
                for r in self.rd.get(k, ()):
                    if r.eng != eng: deps.append(r)
        for k in writes:
            w = self.lw.get(k)
            if w is not None: deps.append(w)
            for r in self.rd.get(k, ()): deps.append(r)
        seen = set(); dl = []
        for d in deps:
            if d is op or id(d) in seen: continue
            seen.add(id(d))
            if (not d.dma) and d.eng == eng and eng == "pe": continue
            dl.append(d)
        op.deps = dl
        if dma:
            i = self.dsem_next[eng]; self.dsem_next[eng] = (i + 1) % self.NDS[eng]
            key = (eng, i)
            prev = self.dsem_tgt.get(key, 0)
            op.sem = key; op.pre = prev; op.target = prev + 16
            self.dsem_tgt[key] = op.target
            self.dma_since.append(op)
        for k in reads:
            self.rd.setdefault(k, []).append(op)
        for k in writes:
            self.lw[k] = op; self.rd[k] = []
        self.q[eng].append(op)
        return op

    def barrier(self):
        col = Op("dve", "nop", False)
        for e in self.ENGS:
            if self.q[e]:
                last = None
                for o in reversed(self.q[e]):
                    if o.fn is not None and not o.dma:
                        last = o; break
                if last is not None: col.deps.append(last)
        col.deps.extend(self.dma_since)
        self.dma_since = []
        self.q["dve"].append(col)
        for e in self.ENGS:
            if e == "dve": continue
            w = Op(e, None, False); w.deps = [col]
            self.q[e].append(w)
        self.lw = {}; self.rd = {}

    def emit(self, nc, block, csem, dsems):
        for e in self.ENGS:
            for op in self.q[e]:
                for d in op.deps:
                    if not d.dma: d.need = True
        for e in self.ENGS:
            t = 0
            for op in self.q[e]:
                if op.need and not op.dma:
                    t += 1; op.ticket = t
        engobj = {"pe": nc.tensor, "act": nc.scalar, "dve": nc.vector, "pool": nc.gpsimd, "sp": nc.sync}
        self.nwaits = 0

        def run(e, eo):
            waited = {}

            def wait(key, sem, val):
                if val <= 0 or waited.get(key, 0) >= val: return
                waited[key] = val
                eo.wait_ge(sem, val); self.nwaits += 1

            for op in self.q[e]:
                need = []
                for d in op.deps:
                    key, sem, val = (d.sem, dsems[d.sem], d.target) if d.dma else (d.eng, csem[d.eng], d.ticket)
                    if val > 0 and waited.get(key, 0) < val:
                        waited[key] = val
                        need = [w for w in need if w[0] is not sem] + [(sem, val)]
                embed = None
                if EMBED_WAITS and e in ("pe", "act", "dve") and need and (not op.dma) and op.fn is not None and op.fn != "nop":
                    embed = need.pop()
                for sem, val in need:
                    eo.wait_ge(sem, val); self.nwaits += 1
                if op.dma:
                    wait(op.sem, dsems[op.sem], op.pre)
                    op.fn(eo).then_inc(dsems[op.sem], 16)
                elif op.fn is None:
                    pass
                else:
                    ins = eo.nop() if op.fn == "nop" else op.fn(eo)
                    if embed is not None: ins._wait_ge(embed[0], embed[1])
                    if op.need: ins.then_inc(csem[e], 1)
            for key, tgt in self.dsem_tgt.items():
                if key[0] == e: wait(key, dsems[key], tgt)

        block.tensor(lambda eo: run("pe", eo))
        block.scalar(lambda eo: run("act", eo))
        block.vector(lambda eo: run("dve", eo))
        block.gpsimd(lambda eo: run("pool", eo))
        block.sync(lambda eo: run("sp", eo))


def _bf(x):
    import ml_dtypes
    return np.asarray(x, np.float32).astype(ml_dtypes.bfloat16).astype(np.float32)


def _rope_tables(R, sample):
    half = R // 2; nf = half // 2
    t = np.arange(T)
    freqs = (10000.0 ** (-np.arange(nf, dtype=np.float32) / nf)).astype(np.float32)
    C = np.ones((R, T), np.float32); S = np.zeros((R, T), np.float32)
    if sample:
        for hi, pos in enumerate((t // 64, t % 64)):
            ang = pos.astype(np.float32)[None, :] * freqs[:, None]
            c, s = np.cos(ang).astype(np.float32), np.sin(ang).astype(np.float32)
            b = hi * half
            C[b:b + nf] = c; C[b + nf:b + 2 * nf] = c
            S[b:b + nf] = -s; S[b + nf:b + 2 * nf] = s
    return C, S


def _rope_perm(R):
    half = R // 2; nf = half // 2
    p = np.arange(R)
    for b in (0, half):
        p[b:b + nf] = np.arange(b + nf, b + 2 * nf)
        p[b + nf:b + 2 * nf] = np.arange(b, b + nf)
    return p


VAR_OF = [0, 1] + [2, 3] * 6 + [4, 5]
VAR_REP = [0, 1, 2, 3, 14, 15]


def _struct_consts(sample):
    c = {}
    c["ident"] = np.eye(128, dtype=np.float32)
    Ce, Se = _rope_tables(32, sample)
    c["ropeE"] = np.stack([np.tile(Ce, (4, 1)), np.tile(Se, (4, 1))], 0)
    Co, So = _rope_tables(64, sample)
    c["ropeO"] = np.stack([np.tile(Co, (2, 1)), np.tile(So, (2, 1))], 0)
    km = np.zeros((9, NK), np.float32); qm = np.zeros((9, T), np.float32)
    km[8, :] = 1.0; qm[8, :] = -BM
    if sample:
        km[0, :] = 1.0; qm[0, :] = BM
    else:
        for s in range(8):
            km[s, s * 256:(s + 1) * 256] = 1.0
            qm[s, s * 256:(s + 1) * 256] = BM
    c["kmaskE"] = km; c["qmaskE"] = qm
    fl = 1.0 if sample else 0.0
    c["flags"] = np.tile(np.array([[fl, 1.0 - fl]], np.float32), (128, 1))
    n = T if sample else 256
    Pm = np.zeros((16, 128, 4, 3, 128), np.float32)
    tt = np.arange(T); tl = tt % n; base = tt - tl
    for g, w in enumerate((2, 4, 8, 16)):
        lo = np.clip(tl - w // 2, 0, n); hi = np.clip(tl + w // 2, 0, n)
        cnt = (hi - lo).astype(np.float32)
        for t in range(T):
            j = t // 128
            for s in range(base[t] + lo[t], base[t] + hi[t]):
                sb = s // 128 - (j - 1)
                Pm[j, s % 128, g, sb, t % 128] += 1.0 / cnt[t]
            Pm[j, t % 128, g, 1, t % 128] -= 1.0
    c["Pm"] = Pm
    swb = np.full((6, 128, 3, 128), NEGB, np.float32)
    for v, i in enumerate(VAR_REP):
        tq = i * 128 + np.arange(128)
        for cc in range(3):
            kb = i - 1 + cc
            if kb < 0 or kb > 15: continue
            ks = kb * 128 + np.arange(128)
            if sample:
                ok = np.abs(tq[None, :] - ks[:, None]) <= 128
            else:
                ok = (tq[None, :] // 256) == (ks[:, None] // 256)
            swb[v, :, cc, :] = np.where(ok, 0.0, NEGB)
    c["swbias"] = swb
    return c


def _na_bias(sample, rpb):
    out = np.full((2, 6, 128, 8, 5, 128), NEGB, np.float32)
    for v, i in enumerate(VAR_REP):
        start = min(max(i - 2, 0), 11)
        tq = i * 128 + np.arange(128)
        r, cq = tq // 64, tq % 64
        for cc in range(5):
            ks = (start + cc) * 128 + np.arange(128)
            if sample:
                rp, cp = ks // 64, ks % 64
                r0 = np.clip(r - 4, 0, 24); c0 = np.clip(cq - 8, 0, 48)
                ok = ((rp[:, None] >= r0[None, :]) & (rp[:, None] < r0[None, :] + 8) &
                      (cp[:, None] >= c0[None, :]) & (cp[:, None] < c0[None, :] + 16))
                dr = np.clip(rp[:, None] - r[None, :] + 7, 0, 14)
                dc = np.clip(cp[:, None] - cq[None, :] + 15, 0, 30)
                for l in range(2):
                    g = rpb[l][:, dr, dc]
                    out[l, v, :, :, cc, :] = np.where(ok[:, None, :], g.transpose(1, 0, 2), NEGB)
            else:
                ok = (tq[None, :] // 256) == (ks[:, None] // 256)
                out[:, v, :, :, cc, :] = np.where(ok, 0.0, NEGB)[None, :, None, :]
    return out


def _host_inputs(inp):
    f = lambda a: np.ascontiguousarray(np.asarray(a, np.float32))
    w = {}
    for k in ("w_mod", "b_mod", "norm_mix", "norm_ffn", "norm_final", "mla_q_norm", "mla_kv_norm",
              "pool_scale", "w_out_even", "w_out_odd", "swa_sink", "w_up", "conv_w", "conv_b", "w_down"):
        w[k] = f(inp[k])
    rowperm = np.concatenate([np.arange(512)] + [np.concatenate([512 + jj * 64 + np.arange(64), 512 + (4 + jj) * 64 + np.arange(64)])
                                                  for jj in range(4)])
    w["w_out_odd"] = f(w["w_out_odd"][:, rowperm, :])
    wie = f(inp["w_in_even"])
    w["wie_qa"] = f(wie[:, :, 0:384]); w["wie_tm"] = f(wie[:, :, 384:928])
    w["wie_kr"] = f(wie[:, :, 640:672]); w["wie_krs"] = f(wie[:, :, 640 + _rope_perm(32)])
    wuq = f(inp["w_uq"]).reshape(2, 384, 12, 96)
    w["wuq_n"] = f(wuq[..., 0:64].reshape(2, 384, 768))
    w["wuq_r"] = f(wuq[..., 64:96].reshape(2, 384, 384))
    w["wuq_rs"] = f(wuq[..., 64 + _rope_perm(32)].reshape(2, 384, 384))
    wukv = f(inp["w_ukv"]).reshape(2, 256, 12, 128)
    w["wukv_k"] = f(wukv[..., 0:64].reshape(2, 256, 768)); w["wukv_v"] = f(wukv[..., 64:128].reshape(2, 256, 768))
    wp = f(inp["w_pool"]); bd = np.zeros((2, 2, 128, 128), np.float32)
    for e in range(2):
        for pr in range(2):
            bd[e, pr, 0:64, 0:64] = wp[e, 2 * pr]; bd[e, pr, 64:128, 64:128] = wp[e, 2 * pr + 1]
    w["wpool_bd"] = bd
    wio = f(inp["w_in_odd"])
    w["wio_qna"] = f(wio[:, :, 0:512]); w["wio_kna"] = f(wio[:, :, 512:1024])
    w["wio_nakv"] = f(wio[:, :, 512:1536]); w["wio_swkv"] = f(wio[:, :, 2048:2304])
    qsw = wio[:, :, 1536:2048].reshape(2, 1024, 8, 64)
    order = [0, 4, 1, 5, 2, 6, 3, 7]
    w["wio_qsw"] = f(qsw[:, :, order, :].reshape(2, 1024, 512))
    w["wio_qsws"] = f(qsw[:, :, order, :][..., _rope_perm(64)].reshape(2, 1024, 512))
    ksw = wio[:, :, 2048:2176].reshape(2, 1024, 2, 64)
    w["wio_ksw"] = f(ksw.reshape(2, 1024, 128)); w["wio_ksws"] = f(ksw[..., _rope_perm(64)].reshape(2, 1024, 128))
    sc = {True: _struct_consts(True), False: _struct_consts(False)}
    rpb = f(inp["na_rpb"])
    nab = {True: _na_bias(True, rpb), False: _na_bias(False, rpb)}
    xp = f(inp["x_prompt"]); xs = f(inp["x_sample"])
    maps = []
    for core in range(8):
        sample = core >= 4
        m = dict(w)
        m.update(sc[sample]); m["nabias"] = nab[sample]
        if sample:
            b = core - 4
            m["xT"] = f(xs[b].T)
            m["cvec"] = f(inp["c"])[b]
            m["c_mla"] = f(inp["cache_mla_latent"])[b]
            m["c_na"] = f(inp["cache_na_kv"])[b].reshape(2, 512, 2, 512)
            m["c_sw"] = f(inp["cache_swa_kv"])[b].reshape(2, 512, 2, 128)
        else:
            m["xT"] = f(xp[8 * core:8 * core + 8].reshape(T, D).T)
            m["cvec"] = f(inp["c_ctx"])
            m["c_mla"] = np.zeros((2, 512, 288), np.float32)
            m["c_na"] = np.zeros((2, 512, 2, 512), np.float32)
            m["c_sw"] = np.zeros((2, 512, 2, 128), np.float32)
        maps.append(m)
    return maps


IN_SHAPES = {
    "xT": [D, T], "cvec": [D], "c_mla": [2, 512, 288], "c_na": [2, 512, 2, 512], "c_sw": [2, 512, 2, 128],
    "w_mod": [4, D, 6 * D], "b_mod": [4, 6 * D], "norm_mix": [4, D], "norm_ffn": [4, D], "norm_final": [D],
    "mla_q_norm": [2, 384], "mla_kv_norm": [2, 256], "pool_scale": [2, 256],
    "w_out_even": [2, D, D], "w_out_odd": [2, D, D], "swa_sink": [2, 8],
    "w_up": [4, D, 2 * DFF], "conv_w": [4, 3, 2 * DFF], "conv_b": [4, 2 * DFF], "w_down": [4, DFF, D],
    "wie_qa": [2, D, 384], "wie_tm": [2, D, 544], "wie_kr": [2, D, 32], "wie_krs": [2, D, 32],
    "wuq_n": [2, 384, 768], "wuq_r": [2, 384, 384], "wuq_rs": [2, 384, 384],
    "wukv_k": [2, 256, 768], "wukv_v": [2, 256, 768], "wpool_bd": [2, 2, 128, 128],
    "wio_qna": [2, D, 512], "wio_kna": [2, D, 512], "wio_nakv": [2, D, 1024], "wio_swkv": [2, D, 256],
    "wio_qsw": [2, D, 512], "wio_qsws": [2, D, 512], "wio_ksw": [2, D, 128], "wio_ksws": [2, D, 128],
    "ident": [128, 128], "ropeE": [2, 128, T], "ropeO": [2, 128, T], "kmaskE": [9, NK], "qmaskE": [9, T],
    "flags": [128, 2], "Pm": [16, 128, 4, 3, 128], "swbias": [6, 128, 3, 128], "nabias": [2, 6, 128, 8, 5, 128],
}
OUT_SHAPES = {"yT": [D, T], "lat": [2, T, 288], "nakv": [2, T, 1024], "swkv": [2, T, 256]}


class KB:
    def __init__(self, nc, big, psums, ins, outs):
        self.nc = nc; self.big = big; self.P = psums; self.I = ins; self.O = outs
        self.pg = Prog()
        self.top = 0
        self.rr = {"mm": 0, "acc": 0, "tr": 0}
        self.RINGS_DEFAULT = {"mm": (0, 4), "acc": (4, 3), "tr": (7, 1)}
        self.rings = dict(self.RINGS_DEFAULT)
        self.uid = 0

    def alloc(self, nbytes):
        off = self.top; self.top += (nbytes + 7) // 8 * 2
        assert self.top * 4 <= 212800, f"SBUF overflow {self.top * 4}"
        return off

    def _shape(self, v, shape):
        if len(shape) == 1: return v
        if len(shape) == 2: return v.rearrange("p (a b) -> p a b", a=shape[0])
        if len(shape) == 3: return v.rearrange("p (a b c) -> p a b c", a=shape[0], b=shape[1])
        return v.rearrange("p (a b c d) -> p a b c d", a=shape[0], b=shape[1], c=shape[2])

    def f32(self, shape):
        n = int(np.prod(shape)); off = self.alloc(n * 4)
        return self._shape(self.big[:, off:off + n], shape)

    def bf(self, shape):
        n = int(np.prod(shape)); off = self.alloc(n * 2)
        return self._shape(self.big[:, off:off + (n + 1) // 2].bitcast(BF16)[:, 0:n], shape)

    def key(self, name):
        self.uid += 1
        return (name, self.uid)

    def ps(self, ring):
        lo, n = self.rings[ring]
        i = lo + self.rr[ring] % n; self.rr[ring] += 1
        return self.P[i], ("ps", i)

    def mm(self, out, lhsT, rhs, start, stop, reads, writes):
        return self.pg.add("pe", lambda e: e.matmul(out, lhsT, rhs, start=start, stop=stop), reads, writes)

    def tr(self, out, in_, ident, reads, writes):
        return self.pg.add("pe", lambda e: e.transpose(out, in_, ident), reads, writes)

    def act(self, out, in_, func, reads, writes, scale=1.0, bias=0.0, accum=None):
        if accum is None:
            return self.pg.add("act", lambda e: e.activation(out=out, in_=in_, func=func, bias=bias, scale=scale), reads, writes)
        return self.pg.add("act", lambda e: e.activation(out=out, in_=in_, func=func, bias=bias, scale=scale, accum_out=accum), reads, writes)

    def ts(self, eng, out, in0, s1, s2, op0, op1, reads, writes):
        if s2 is None:
            return self.pg.add(eng, lambda e: e.tensor_scalar(out=out, in0=in0, scalar1=s1, scalar2=None, op0=op0), reads, writes)
        return self.pg.add(eng, lambda e: e.tensor_scalar(out=out, in0=in0, scalar1=s1, scalar2=s2, op0=op0, op1=op1), reads, writes)

    def stt(self, eng, out, in0, scalar, in1, op0, op1, reads, writes):
        return self.pg.add(eng, lambda e: e.scalar_tensor_tensor(out=out, in0=in0, scalar=scalar, in1=in1, op0=op0, op1=op1), reads, writes)

    def tt(self, eng, out, in0, in1, op, reads, writes):
        return self.pg.add(eng, lambda e: e.tensor_tensor(out=out, in0=in0, in1=in1, op=op), reads, writes)

    def cp(self, eng, out, in_, reads, writes):
        if eng == "act":
            return self.pg.add("act", lambda e: e.copy(out=out, in_=in_), reads, writes)
        return self.pg.add(eng, lambda e: e.tensor_copy(out=out, in_=in_), reads, writes)

    def recip(self, out, in_, reads, writes):
        return self.pg.add("dve", lambda e: e.reciprocal(out=out, in_=in_), reads, writes)

    def memset(self, eng, ap, val, writes):
        return self.pg.add(eng, lambda e: e.memset(ap, val), (), writes)

    def dma(self, q, out, in_, reads, writes):
        return self.pg.add(q, lambda e: e.dma_start(out=out, in_=in_), reads, writes, dma=True)

    def wload(self, dst, src_ap, key):
        return self.dma("pool", dst, src_ap.rearrange("(kc p) n -> p kc n", p=128), (), (key,))


VOFF = {}
def _voff():
    o = 0
    for name, n in (("bmod", 192), ("nmix", 32), ("nffn", 32), ("nfin", 8), ("conv", 704), ("gq", 6),
                    ("pscale", 4), ("cvec", 8), ("gkv", 512), ("sink", 16), ("flags", 2)):
        VOFF[name] = (o, n); o += n
    return o
NV = _voff()


def _pack_vecs(inp, cvec, flags):
    v = np.zeros((128, NV), np.float32)
    def put(name, arr):
        o, n = VOFF[name]; v[:, o:o + n] = np.asarray(arr, np.float32).reshape(128, n)
    f = lambda a: np.asarray(a, np.float32)
    put("bmod", f(inp["b_mod"]).reshape(4, 48, 128).transpose(2, 0, 1))
    put("nmix", f(inp["norm_mix"]).reshape(4, 8, 128).transpose(2, 0, 1))
    put("nffn", f(inp["norm_ffn"]).reshape(4, 8, 128).transpose(2, 0, 1))
    put("nfin", f(inp["norm_final"]).reshape(8, 128).transpose(1, 0))
    cw = np.concatenate([f(inp["conv_w"]), f(inp["conv_b"])[:, None, :]], 1)
    put("conv", cw.reshape(4, 4, 44, 128).transpose(3, 0, 1, 2))
    put("gq", f(inp["mla_q_norm"]).reshape(2, 3, 128).transpose(2, 0, 1))
    put("pscale", f(inp["pool_scale"]).reshape(2, 2, 128).transpose(2, 0, 1))
    put("cvec", f(cvec).reshape(8, 128).transpose(1, 0))
    put("gkv", np.broadcast_to(f(inp["mla_kv_norm"]).reshape(1, 512), (128, 512)))
    put("sink", np.broadcast_to(f(inp["swa_sink"]).reshape(1, 16), (128, 16)))
    put("flags", flags)
    return v


def build_program():
    nc = bass.Bass("TRN2", target_bir_lowering=False)
    shapes = dict(IN_SHAPES); shapes["vecs"] = [128, NV]
    for k in ("cvec", "b_mod", "norm_mix", "norm_ffn", "norm_final", "mla_q_norm", "mla_kv_norm", "pool_scale",
              "swa_sink", "conv_w", "conv_b", "flags"):
        shapes.pop(k)
    I = {k: nc.dram_tensor(k, s, F32, kind="ExternalInput").ap() for k, s in shapes.items()}
    O = {k: nc.dram_tensor(k, s, F32, kind="ExternalOutput").ap() for k, s in OUT_SHAPES.items()}
    es = ExitStack()
    with es:
        big = es.enter_context(nc.sbuf_tensor("big", [128, 53200], F32))
        P = [es.enter_context(nc.psum_tensor(f"psb{i}", [128, 512], F32)) for i in range(8)]
        csem = {e: es.enter_context(nc.semaphore(f"c_{e}")) for e in Prog.ENGS}
        dsems = {(q, i): es.enter_context(nc.semaphore(f"d_{q}{i}")) for q in Prog.NDS for i in range(Prog.NDS[q])}
        block = es.enter_context(nc.Block())
        kb = KB(nc, big, [p[:, :] for p in P], I, O)
        _emit_all(kb)
        kb.pg.emit(nc, block, csem, dsems)
    return nc, kb


def _emit_all(kb):
    I, O, pg = kb.I, kb.O, kb.pg
    xres = kb.f32([8, T]); hT = kb.bf([8, T])
    vecs = kb.f32([NV])
    ones_bf = kb.bf([128]); ident_bf = kb.bf([128])
    scv = kb.bf([8]); modT_all = kb.f32([4, 48]); prm_all = kb.f32([4, 6, 8]); convx_all = kb.f32([4, 4, 44])
    prm = prm_all[:, 0]; convx = convx_all[:, 0]
    MARK = kb.top
    kb.xres, kb.hT, kb.vecs, kb.ones_bf, kb.ident_bf, kb.prm = xres, hT, vecs, ones_bf, ident_bf, prm
    kb.MARK = MARK

    def vv(name, l=None, per=None):
        o, n = VOFF[name]
        if l is None: return vecs[:, o:o + n]
        return vecs[:, o + l * per:o + (l + 1) * per]
    kb.vv = vv

    XK = lambda c, tb: ("x", c, tb)
    HK = lambda c, tb: ("h", c, tb)
    kb.XK, kb.HK = XK, HK
    kb.dma("sp", vecs, I["vecs"][:, :], (), ("vecs",))
    for c in range(8):
        kb.dma("sp", xres[:, c, :], I["xT"][c * 128:(c + 1) * 128, :], (), [XK(c, tb) for tb in range(4)])
    kb.dma("pool", ident_bf, I["ident"][:, :], (), ("ident",))
    kb.memset("dve", ones_bf, 1.0, ("ones",))
    kb.act(scv, vv("cvec"), AF.Silu, ("vecs",), ("scv",))

    def ring_setup(nslots, nel):
        kb.ring = [kb.bf([nel]) for _ in range(nslots)]
        kb.ring_i = 0; kb.ring_nel = nel

    def ring_next():
        i = kb.ring_i % len(kb.ring); kb.ring_i += 1
        return kb.ring[i], ("wr", i)
    kb.ring_setup, kb.ring_next = ring_setup, ring_next

    def wview(slot, kc, n):
        return slot[:, 0:kc * n].rearrange("p (a b) -> p a b", a=kc)
    kb.wview = wview

    def norm_mod(Acol, Bcol, sq, rs, tmp, dst_fn, final=False):
        nsq = sq.shape[1]; ntm = tmp.shape[1]
        two_rs = len(rs.shape) == 3
        rss = []
        for tb in range(4):
            sl = slice(tb * 512, (tb + 1) * 512)
            pst, pk = kb.ps("mm")
            for c in range(8):
                i = (tb * 8 + c) % nsq
                s = sq[:, i, :]
                kb.act(s, xres[:, c, sl], AF.Square, (XK(c, tb),), (("sq", i),))
                kb.mm(pst, ones_bf, s, c == 0, c == 7, (("sq", i), "ones"), (pk,))
            r = rs[:, tb % 2, :] if two_rs else rs
            rk = ("rs", tb % 2) if two_rs else "rs"
            kb.act(r, pst, AF.Sqrt, (pk,), (rk,), scale=1.0 / D, bias=EPS)
            kb.recip(r, r, (rk,), (rk,))
            for c in range(8):
                i = (tb * 8 + c) % ntm
                tm = tmp[:, i, :]
                kb.stt("dve", tm, xres[:, c, sl], Acol(c), r, ALU.mult, ALU.mult,
                       (XK(c, tb), rk, "prm", "vecs"), (("tmp", i),))
                dst_fn(c, tb, tm, ("tmp", i), Bcol(c) if Bcol else 0.0)
    kb.norm_mod = norm_mod

    def to_hT(c, tb, tm, tk, bias):
        kb.act(hT[:, c, tb * 512:(tb + 1) * 512], tm, AF.Identity, (tk, "prm"), (HK(c, tb),), bias=bias)

    def adaln_gen(layers, aring, pst, pk, pw=512):
        cnt = 0
        for l in layers:
            modT = modT_all[:, l]; prm = prm_all[:, l]; convx = convx_all[:, l]
            for piece in range(6144 // pw):
                slot, sk = aring[cnt % len(aring)], ("awr", cnt % len(aring)); cnt += 1
                wv = wview(slot, 8, pw)
                kb.wload(wv, I["w_mod"][l][:, piece * pw:(piece + 1) * pw], sk)
                for cc in range(pw // 128):
                    j = piece * (pw // 128) + cc
                    for kc in range(8):
                        kb.mm(pst[:, j:j + 1], wv[:, kc, cc * 128:(cc + 1) * 128], scv[:, kc:kc + 1], kc == 0, kc == 7,
                              (sk, "scv"), (pk,))
                yield
            mk = ("modT", l); pk_ = ("prm", l)
            kb.tt("dve", modT, pst[:, 0:48], vv("bmod", l, 48), ALU.add, (pk, "vecs"), (mk,))
            for (row, sc_i, g) in ((0, 1, "nmix"), (3, 4, "nffn")):
                kb.stt("dve", prm[:, row, :], modT[:, sc_i * 8:(sc_i + 1) * 8], 1.0, vv(g, l, 8), ALU.add, ALU.mult,
                       (mk, "vecs"), (pk_,))
            for (row, m_i) in ((1, 0), (2, 2), (4, 3), (5, 5)):
                kb.cp("dve", prm[:, row, :], modT[:, m_i * 8:(m_i + 1) * 8], (mk,), (pk_,))
            o, _ = VOFF["conv"]; cb = o + l * 176
            fo, _ = VOFF["flags"]
            w0 = vecs[:, cb:cb + 44]; w2 = vecs[:, cb + 88:cb + 132]
            kb.ts("dve", convx[:, 0, :], w0, vecs[:, fo:fo + 1], None, ALU.mult, None, ("vecs",), (("convx", l),))
            kb.ts("dve", convx[:, 1, :], w2, vecs[:, fo:fo + 1], None, ALU.mult, None, ("vecs",), (("convx", l),))
            kb.ts("dve", convx[:, 2, :], w0, vecs[:, fo + 1:fo + 2], -1.0, ALU.mult, ALU.mult, ("vecs",), (("convx", l),))
            kb.ts("dve", convx[:, 3, :], w2, vecs[:, fo + 1:fo + 2], -1.0, ALU.mult, ALU.mult, ("vecs",), (("convx", l),))
            yield

    def adaln_first():
        kb.top = MARK
        aring = [kb.bf([8 * 512]) for _ in range(3)]
        pst, pk = kb.ps("acc")
        for _ in adaln_gen([0], aring, pst, pk):
            pass
        pg.barrier()

    def norm_phase(row_a, row_b):
        kb.top = MARK
        sq = kb.bf([4, 512]); rs = kb.f32([2, 512]); tmp = kb.f32([4, 512])
        norm_mod(lambda c: kb.prm[:, row_a, c:c + 1], lambda c: kb.prm[:, row_b, c:c + 1], sq, rs, tmp, to_hT)
        pg.barrier()

    def ffn(l):
        kb.top = MARK
        o, _ = VOFF["conv"]; cb = o + l * 176
        cw = lambda k, c: vecs[:, cb + k * 44 + c:cb + k * 44 + c + 1]
        cx = lambda k, c: kb.convx[:, k, c:c + 1]
        ring_setup(2 if (l == 0 and NLAYERS > 1) else 3, 22 * 256)
        kb.rings = {"mm": (0, 7), "acc": (0, 7), "tr": (7, 1)}
        actb = kb.bf([NFC, 1024]); ta_r = kb.f32([4, 512]); tg_r = kb.f32([4, 512]); es = kb.f32([4, 2]); eh = kb.f32([44])
        agen = None
        if l == 0 and NLAYERS > 1:
            kb.rings = {"mm": (0, 6), "acc": (0, 6), "tr": (7, 1)}
            aring = [kb.bf([8 * 384]) for _ in range(2)]
            agen = adaln_gen(list(range(1, NLAYERS)), aring, kb.P[6], ("ps", 6), pw=384)
        for sb in range(2):
            for c in range(NFC):
                if c % 2 == 0:
                    slot, sk = ring_next(); wv = wview(slot, 8, 512)
                    kb.dma("pool", wv[:, :, 0:256], I["w_up"][l][:, c * 128:c * 128 + 256].rearrange("(kc p) n -> p kc n", p=128), (), (sk,))
                    kb.dma("pool", wv[:, :, 256:512], I["w_up"][l][:, DFF + c * 128:DFF + c * 128 + 256].rearrange("(kc p) n -> p kc n", p=128), (), (sk,))
                hp, hk = kb.ps("tr")
                tiles = {}; tts = {}
                for tb2 in range(2):
                    tb = sb * 2 + tb2; t0 = tb * 512
                    ri = (c % 2) * 2 + tb2
                    for gi in range(2):
                        col0 = gi * 256 + (c % 2) * 128
                        pst, pk = kb.ps("mm")
                        for kc in range(8):
                            kb.mm(pst, wv[:, kc, col0:col0 + 128], hT[:, kc, t0:t0 + 512], kc == 0, kc == 7,
                                  (sk, HK(kc, tb)), (pk,))
                        hcol = None
                        if sb == 0 and tb2 == 1: hcol = 1024
                        if hcol is not None:
                            for kc in range(8):
                                kb.mm(hp[:, gi:gi + 1], wv[:, kc, col0:col0 + 128], hT[:, kc, hcol:hcol + 1], kc == 0, kc == 7,
                                      (sk, HK(kc, hcol // 512)), (hk,))
                        tiles[(tb2, gi)] = (pst, pk)
                        tts[(tb2, gi)] = ((ta_r if gi == 0 else tg_r)[:, ri, :], ("tconv", gi, ri))
                    ccs = [gi * NFC + c for gi in range(2)]
                    for gi in range(2):
                        (pst, pk), (tt_, tk) = tiles[(tb2, gi)], tts[(tb2, gi)]
                        kb.act(tt_, pst, AF.Identity, (pk, "vecs"), (tk,), scale=cw(1, ccs[gi]), bias=cw(3, ccs[gi]))
                        if tb2 == 0:
                            kb.cp("act", es[:, (c % 2) * 2 + gi, 0:1], pst[:, 511:512], (pk,), (("es", (c % 2) * 2 + gi),))
                        if sb == 0 and tb2 == 1:
                            kb.cp("act", eh[:, ccs[gi]:ccs[gi] + 1], pst[:, 511:512], (pk,), (("eh", ccs[gi]),))
                    for gi in range(2):
                        (pst, pk), (tt_, tk) = tiles[(tb2, gi)], tts[(tb2, gi)]
                        kb.stt("dve", tt_[:, 1:512], pst[:, 0:511], cw(0, ccs[gi]), tt_[:, 1:512], ALU.mult, ALU.add, (pk, tk, "vecs"), (tk,))
                    for gi in range(2):
                        (pst, pk), (tt_, tk) = tiles[(tb2, gi)], tts[(tb2, gi)]
                        kb.stt("dve", tt_[:, 0:511], pst[:, 1:512], cw(2, ccs[gi]), tt_[:, 0:511], ALU.mult, ALU.add, (pk, tk, "vecs"), (tk,))
                    for gi in range(2):
                        (pst, pk), (tt_, tk) = tiles[(tb2, gi)], tts[(tb2, gi)]
                        kb.stt("dve", tt_[:, 256:257], pst[:, 255:256], cx(2, ccs[gi]), tt_[:, 256:257], ALU.mult, ALU.add, (pk, tk), (tk,))
                    for gi in range(2):
                        (pst, pk), (tt_, tk) = tiles[(tb2, gi)], tts[(tb2, gi)]
                        kb.stt("dve", tt_[:, 255:256], pst[:, 256:257], cx(3, ccs[gi]), tt_[:, 255:256], ALU.mult, ALU.add, (pk, tk), (tk,))
                    for gi in range(2):
                        (pst, pk), (tt_, tk) = tiles[(tb2, gi)], tts[(tb2, gi)]
                        if tb2 == 0 and sb == 1:
                            kb.act(tt_[:, 0:1], eh[:, ccs[gi]:ccs[gi] + 1], AF.Identity, (("eh", ccs[gi]), tk), (tk,), scale=cx(0, ccs[gi]), bias=tt_[:, 0:1])
                        if tb2 == 1:
                            ek = ("es", (c % 2) * 2 + gi)
                            kb.act(tt_[:, 0:1], es[:, (c % 2) * 2 + gi, 0:1], AF.Identity, (ek, tk), (tk,), scale=cx(0, ccs[gi]), bias=tt_[:, 0:1])
                        if tb2 == 1 and sb == 0:
                            kb.act(tt_[:, 511:512], hp[:, gi:gi + 1], AF.Identity, (hk, tk), (tk,), scale=cx(1, ccs[gi]), bias=tt_[:, 511:512])
                for gi in range(2):
                    (p1, k1), (t0_, tk0) = tiles[(1, gi)], tts[(0, gi)]
                    kb.act(t0_[:, 511:512], p1[:, 0:1], AF.Identity, (k1, tk0), (tk0,), scale=cx(1, gi * NFC + c), bias=t0_[:, 511:512])
                for tb2 in range(2):
                    (ta, tak), (tg, tgk) = tts[(tb2, 0)], tts[(tb2, 1)]
                    kb.act(tg, tg, AF.Silu, (tgk,), (tgk,))
                    kb.tt("dve", actb[:, c, tb2 * 512:(tb2 + 1) * 512], ta, tg, ALU.mult, (tak, tgk), (("actb", c, tb2),))
                if agen is not None and c % 2 == 1:
                    next(agen, None)
            for dp in range(4):
                slot, sk = ring_next(); wd = wview(slot, NFC, 256)
                kb.wload(wd, I["w_down"][l][:, dp * 256:(dp + 1) * 256], sk)
                for d2 in range(2):
                    dch = dp * 2 + d2
                    for tb2 in range(2):
                        tb = sb * 2 + tb2
                        pst, pk = kb.ps("acc")
                        for fc in range(NFC):
                            kb.mm(pst, wd[:, fc, d2 * 128:(d2 + 1) * 128], actb[:, fc, tb2 * 512:(tb2 + 1) * 512], fc == 0, fc == NFC - 1,
                                  (sk, ("actb", fc, tb2)), (pk,))
                        xs = xres[:, dch, tb * 512:(tb + 1) * 512]
                        kb.stt("dve", xs, pst, kb.prm[:, 5, dch:dch + 1], xs, ALU.mult, ALU.add, (pk, XK(dch, tb)), (XK(dch, tb),))
                if agen is not None:
                    next(agen, None)
        if agen is not None:
            for _ in agen:
                pass
        pg.barrier()
        kb.rings = dict(kb.RINGS_DEFAULT)

    def final_out():
        kb.top = MARK
        sq = kb.bf([4, 512]); rs = kb.f32([2, 512]); tmp = kb.f32([4, 512]); yst = kb.f32([4, 512])
        cnt = [0]
        def to_out(c, tb, tm, tk, bias):
            i = cnt[0] % 4; cnt[0] += 1
            kb.cp("act", yst[:, i, :], tm, (tk,), (("yst", i),))
            kb.dma("sp", O["yT"][c * 128:(c + 1) * 128, tb * 512:(tb + 1) * 512], yst[:, i, :], (("yst", i),), ())
        o, _ = VOFF["nfin"]
        norm_mod(lambda c: vecs[:, o + c:o + c + 1], None, sq, rs, tmp, to_out)

    from_mixers = _mixers(kb)
    adaln_first()
    try:
        for l in range(NLAYERS):
            kb.prm = prm_all[:, l]; kb.convx = convx_all[:, l]
            norm_phase(0, 1)
            if l % 2 == 0 and STAGES["even"]:
                from_mixers["even"](l, l // 2)
            if l % 2 == 1 and STAGES["odd"]:
                from_mixers["odd"](l, l // 2)
            if STAGES["ffn"]:
                norm_phase(3, 4)
                ffn(l)
    except _Stop:
        pg.barrier()
    final_out()


def _mixers(kb):
    I, O, pg = kb.I, kb.O, kb.pg
    xres, hT, vecs, ones_bf, ident_bf, prm = kb.xres, kb.hT, kb.vecs, kb.ones_bf, kb.ident_bf, kb.prm
    XK, HK, vv, MARK = kb.XK, kb.HK, kb.vv, kb.MARK
    wview = kb.wview
    fo = VOFF["flags"][0]

    def bfv(pst):
        return pst[:, 0:256].bitcast(BF16)

    def out_proj(wname, e, extra, src=None):
        src = hT if src is None else src
        kb.ring_setup(2, 8 * 512)
        for piece in range(2):
            slot, sk = kb.ring_next(); wo = wview(slot, 8, 512)
            kb.wload(wo, I[wname][e][:, piece * 512:(piece + 1) * 512], sk)
            for d4 in range(4):
                dch = piece * 4 + d4
                for tb in range(4):
                    sl = slice(tb * 512, (tb + 1) * 512)
                    pst, pk = kb.ps("acc")
                    for kc in range(8):
                        if extra is not None and kc >= 6:
                            rhs, rk = extra[:, kc - 6, sl], ("yp", kc - 6, tb)
                        else:
                            rhs, rk = src[:, kc, sl], HK(kc, tb)
                        kb.mm(pst, wo[:, kc, d4 * 128:(d4 + 1) * 128], rhs, kc == 0, kc == 7, (sk, rk), (pk,))
                    xs = xres[:, dch, sl]
                    kb.stt("dve", xs, pst, kb.prm[:, 2, dch:dch + 1], xs, ALU.mult, ALU.add, (pk, XK(dch, tb)), (XK(dch, tb),))
        pg.barrier()

    def rope_combine(dst, dk, pa, pak, pb, pbk, tab, tabk, t1, t2, np_, pre=1.0):
        kb.stt("dve", t1[0:np_, :], pa[0:np_, :], pre, tab[0:np_, 0, :], ALU.mult, ALU.mult, (pak, tabk), ("rt1",))
        kb.stt("dve", t2[0:np_, :], pb[0:np_, :], pre, tab[0:np_, 1, :], ALU.mult, ALU.mult, (pbk, tabk), ("rt2",))
        kb.tt("dve", dst, t1[0:np_, :], t2[0:np_, :], ALU.add, ("rt1", "rt2"), (dk,))

    def even(l, e):
        kb.top = MARK
        qnT = kb.bf([3, T]); qrT = kb.bf([3, T]); cT = kb.bf([2, NK]); krT = kb.bf([NK]); ypT = kb.bf([2, T])
        MARK_E = kb.top
        wtm = kb.bf([8, 544]); wpl = kb.bf([2, 128]); sk = "wtm"; skp = "wpl"
        xp_tok = kb.bf([16, 384]); pooledT = kb.bf([2, T]); lat_st = kb.f32([2, 288]); ctok = kb.bf([2, 256])
        sqt = kb.f32([2, 256]); ssq = kb.f32([2]); ctxl = kb.bf([4, 288]); Pms = [kb.bf([4, 3, 128]) for _ in range(2)]
        kb.memset("dve", xp_tok, 0.0, [("xp", j) for j in range(16)])
        kb.wload(wtm, I["wie_tm"][e], sk)
        kb.dma("pool", ctxl, I["c_mla"][e].rearrange("(b p) n -> p b n", p=128), (), ("ctxl",))
        for j in range(16):
            i = j % 2; tb = j // 4
            p1, k1 = kb.ps("mm"); p2, k2 = kb.ps("mm")
            for kc in range(8):
                lh = hT[:, kc, j * 128:(j + 1) * 128]
                kb.mm(p1[:, 0:288], lh, wtm[:, kc, 0:288], kc == 0, kc == 7, (sk, HK(kc, tb)), (k1,))
                kb.mm(p2[:, 0:256], lh, wtm[:, kc, 288:544], kc == 0, kc == 7, (sk, HK(kc, tb)), (k2,))
            kb.act(sqt[:, i, :], p1[:, 0:256], AF.Square, (k1,), (("sqt", i),))
            pg.add("dve", lambda en, o=ssq[:, i:i + 1], a=sqt[:, i, :]: en.reduce_sum(out=o, in_=a, axis=mybir.AxisListType.X),
                   (("sqt", i),), (("ssq", i),))
            kb.act(ssq[:, i:i + 1], ssq[:, i:i + 1], AF.Sqrt, (("ssq", i),), (("ssq", i),), scale=1.0 / 256, bias=EPS)
            kb.recip(ssq[:, i:i + 1], ssq[:, i:i + 1], (("ssq", i),), (("ssq", i),))
            go = VOFF["gkv"][0] + e * 256
            kb.stt("dve", lat_st[:, i, 0:256], p1[:, 0:256], ssq[:, i:i + 1], vecs[:, go:go + 256], ALU.mult, ALU.mult,
                   (k1, ("ssq", i), "vecs"), (("lat", i),))
            kb.cp("act", lat_st[:, i, 256:288], p1[:, 256:288], (k1,), (("lat", i),))
            kb.dma("sp", O["lat"][e, j * 128:(j + 1) * 128, :], lat_st[:, i, :], (("lat", i),), ())
            kb.cp("dve", ctok[:, i, :], lat_st[:, i, 0:256], (("lat", i),), (("ctok", i),))
            ptr, tk = kb.ps("tr"); pb = bfv(ptr)
            for cc in range(2):
                kb.tr(pb[:, cc * 128:(cc + 1) * 128], ctok[:, i, cc * 128:(cc + 1) * 128], ident_bf, (("ctok", i), "ident"), (tk,))
            kb.cp("act", cT[:, :, j * 128:(j + 1) * 128], pb[:, 0:256].rearrange("p (a b) -> p a b", a=2), (tk,), (("cT", j),))
            for half in range(2):
                dst = xp_tok[:, j, half * 192:(half + 1) * 192].rearrange("p (a b) -> p a b", a=3)[:, 0:3:2, :]
                src = p2[:, half * 128:(half + 1) * 128].rearrange("p (a b) -> p a b", a=2)
                kb.cp("act", dst, src, (k2,), (("xp", j),))
        ck("e_tm")
        for blk in range(4):
            ptr, tk = kb.ps("tr"); pb = bfv(ptr)
            for cc in range(2):
                kb.tr(pb[:, cc * 128:(cc + 1) * 128], ctxl[:, blk, cc * 128:(cc + 1) * 128], ident_bf, ("ctxl", "ident"), (tk,))
            kb.tr(pb[0:32, 256:384], ctxl[:, blk, 256:288], ident_bf, ("ctxl", "ident"), (tk,))
            kb.cp("act", cT[:, :, T + blk * 128:T + (blk + 1) * 128], pb[:, 0:256].rearrange("p (a b) -> p a b", a=2), (tk,), (("cT", 16 + blk),))
            kb.cp("dve", krT[0:32, T + blk * 128:T + (blk + 1) * 128], pb[0:32, 256:384], (tk,), (("krT", 4),))
        ck("e_ctx")
        kb.dma("pool", wpl, I["wpool_bd"][e].rearrange("a k m -> k a m"), (), (skp,))
        for j in range(16):
            pm = Pms[j % 2]; pmk = ("Pm", j % 2)
            kb.dma("pool", pm, I["Pm"][j], (), (pmk,))
            for pr in range(2):
                pst, pk = kb.ps("mm")
                todo = [(gg, sbi) for gg in range(2) for sbi in range(3) if 0 <= j - 1 + sbi <= 15]
                for n_, (gg, sbi) in enumerate(todo):
                    sbk = j - 1 + sbi; c0 = pr * 192 + gg * 64
                    kb.mm(pst[:, 0:128], xp_tok[:, sbk, c0:c0 + 128], pm[:, pr * 2 + gg, sbi, :], n_ == 0, n_ == len(todo) - 1,
                          (("xp", sbk), pmk), (pk,))
                kb.cp("act", pooledT[:, pr, j * 128:(j + 1) * 128], pst[:, 0:128], (pk,), (("pooled", pr, j // 4),))
        pso = VOFF["pscale"][0] + e * 2
        for pr in range(2):
            for tb in range(4):
                sl = slice(tb * 512, (tb + 1) * 512)
                pst, pk = kb.ps("mm")
                kb.mm(pst, wpl[:, pr, :], pooledT[:, pr, sl], True, True, (skp, ("pooled", pr, tb)), (pk,))
                kb.ts("dve", ypT[:, pr, sl], pst, vecs[:, pso + pr:pso + pr + 1], None, ALU.mult, None, (pk, "vecs"), (("yp", pr, tb),))
        ck("e_pool")
        pg.barrier()
        kb.top = MARK_E
        kb.ring_setup(2, 8 * 544)
        qa_f = kb.f32([3, 512]); sq = kb.bf([2, 512]); rs = kb.f32([512]); t1 = kb.f32([512]); t2 = kb.f32([512])
        tabE = [kb.f32([2, 512]) for _ in range(2)]
        slot, sk1 = kb.ring_next(); wkr = wview(slot, 8, 448)
        kb.dma("pool", wkr[:, :, 0:32], I["wie_kr"][e].rearrange("(kc p) n -> p kc n", p=128), (), (sk1,))
        kb.dma("pool", wkr[:, :, 32:64], I["wie_krs"][e].rearrange("(kc p) n -> p kc n", p=128), (), (sk1,))
        kb.dma("pool", wkr[:, :, 64:448], I["wie_qa"][e].rearrange("(kc p) n -> p kc n", p=128), (), (sk1,))
        slot, sk2 = kb.ring_next(); wqr = wview(slot, 3, 768)
        kb.dma("pool", wqr[:, :, 0:384], I["wuq_r"][e].rearrange("(kc p) n -> p kc n", p=128), (), (sk2,))
        kb.dma("pool", wqr[:, :, 384:768], I["wuq_rs"][e].rearrange("(kc p) n -> p kc n", p=128), (), (sk2,))
        gq0 = VOFF["gq"][0] + e * 3
        for tb in range(4):
            sl = slice(tb * 512, (tb + 1) * 512)
            tab = tabE[tb % 2]; tabk = ("tabE", tb % 2)
            kb.dma("sp", tab, I["ropeE"][:, :, sl].rearrange("a p t -> p a t"), (), (tabk,))
            pa, pak = kb.ps("mm"); pb_, pbk = kb.ps("mm")
            for kc in range(8):
                kb.mm(pa[0:32, :], wkr[:, kc, 0:32], hT[:, kc, sl], kc == 0, kc == 7, (sk1, HK(kc, tb)), (pak,))
            for kc in range(8):
                kb.mm(pb_[0:32, :], wkr[:, kc, 32:64], hT[:, kc, sl], kc == 0, kc == 7, (sk1, HK(kc, tb)), (pbk,))
            rope_combine(krT[0:32, sl], ("krT", tb), pa, pak, pb_, pbk, tab, tabk, t1, t2, 32)
            pn, pnk = kb.ps("acc")
            for m in range(3):
                pq, pqk = kb.ps("mm")
                for kc in range(8):
                    kb.mm(pq, wkr[:, kc, 64 + m * 128:64 + (m + 1) * 128], hT[:, kc, sl], kc == 0, kc == 7, (sk1, HK(kc, tb)), (pqk,))
                kb.cp("act", qa_f[:, m, :], pq, (pqk,), (("qa_f", m),))
                kb.act(sq[:, m % 2, :], qa_f[:, m, :], AF.Square, (("qa_f", m),), (("sq", m % 2),))
                kb.mm(pn, ones_bf, sq[:, m % 2, :], m == 0, m == 2, (("sq", m % 2), "ones"), (pnk,))
            kb.act(rs, pn, AF.Sqrt, (pnk,), ("rs",), scale=1.0 / 384, bias=EPS)
            kb.recip(rs, rs, ("rs",), ("rs",))
            for m in range(3):
                kb.stt("dve", qnT[:, m, sl], qa_f[:, m, :], vecs[:, gq0 + m:gq0 + m + 1], rs, ALU.mult, ALU.mult,
                       (("qa_f", m), "rs", "vecs"), (("qnT", m, tb),))
            for m in range(3):
                pa, pak = kb.ps("mm"); pb_, pbk = kb.ps("mm")
                for kc in range(3):
                    kb.mm(pa, wqr[:, kc, m * 128:(m + 1) * 128], qnT[:, kc, sl], kc == 0, kc == 2, (sk2, ("qnT", kc, tb)), (pak,))
                for kc in range(3):
                    kb.mm(pb_, wqr[:, kc, 384 + m * 128:384 + (m + 1) * 128], qnT[:, kc, sl], kc == 0, kc == 2, (sk2, ("qnT", kc, tb)), (pbk,))
                rope_combine(qrT[:, m, sl], ("qrT", m, tb), pa, pak, pb_, pbk, tab, tabk, t1, t2, 128)
        ck("e_ea2")
        pg.barrier()
        kb.top = MARK_E
        kb.rings = {"mm": (0, 4), "acc": (4, 4), "tr": (7, 1)}
        wqn = kb.bf([3, 768]); wk = kb.bf([2, 768]); wvv = kb.bf([2, 768])
        KT = kb.bf([2, NK]); QT = kb.bf([2, T]); Vh = kb.bf([2, 20, 128]); PT = kb.bf([6, 512])
        rDs = kb.f32([2, 512]); rDt = kb.f32([2, 512]); sel = kb.f32([2, 128]); fin_state = []
        kb.memset("dve", rDt, 0.0, ("rDt",)); kb.memset("dve", sel, 0.0, ("sel",))
        kb.memset("dve", sel[64:65, 0, 0:64], 1.0, ("sel",))
        kb.memset("dve", sel[32:33, 1, 64:128], 1.0, ("sel",))
        kb.wload(wqn, I["wuq_n"][e], "wqn"); kb.wload(wk, I["wukv_k"][e], "wk"); kb.wload(wvv, I["wukv_v"][e], "wvv")
        for b in range(2):
            kb.dma("pool", KT[96:105, b, :], I["kmaskE"][:, :], (), (("KTm", b),))
            kb.dma("pool", QT[96:105, b, :], I["qmaskE"][:, :], (), (("QTm", b),))
            kb.dma("sp", KT[64:96, b, :], krT[0:32, :], (), (("KTr", b),))
            kb.memset("dve", Vh[:, b, :, :], 0.0, (("Vh", b),))
            oc = 64 if b == 0 else 32
            kb.memset("dve", Vh[:, b, :, oc:oc + 1], 1.0, (("Vh", b),))

        def proj(h, b):
            kb.dma("sp", QT[64:96, b, :], qrT[(h % 4) * 32:(h % 4) * 32 + 32, h // 4, :], (), (("QTr", b),))
            for tb in range(4):
                sl = slice(tb * 512, (tb + 1) * 512)
                pst, pk = kb.ps("mm")
                for kc in range(3):
                    kb.mm(pst[0:64, :], wqn[:, kc, h * 64:(h + 1) * 64], qnT[:, kc, sl], kc == 0, kc == 2, ("wqn",), (pk,))
                kb.cp("dve", QT[0:64, b, sl], pst[0:64, :], (pk,), (("QTn", b),))
            for k5 in range(5):
                sl = slice(k5 * 512, (k5 + 1) * 512)
                pst, pk = kb.ps("mm")
                for cc in range(2):
                    kb.mm(pst[0:64, :], wk[:, cc, h * 64:(h + 1) * 64], cT[:, cc, sl], cc == 0, cc == 1, ("wk",), (pk,))
                kb.cp("dve", KT[0:64, b, sl], pst[0:64, :], (pk,), (("KTn", b),))
            for g0, nb in ((0, 8), (8, 8), (16, 4)):
                pst, pk = kb.ps("mm")
                for i in range(nb):
                    kblk = g0 + i
                    for cc in range(2):
                        kb.mm(pst[:, i * 64:(i + 1) * 64], cT[:, cc, kblk * 128:(kblk + 1) * 128], wvv[:, cc, h * 64:(h + 1) * 64],
                              cc == 0, cc == 1, ("wvv",), (pk,))
                kb.cp("dve", Vh[:, b, g0:g0 + nb, b * 64:b * 64 + 64], pst[:, 0:nb * 64].rearrange("p (a b) -> p a b", a=nb),
                      (pk,), (("Vh", b),))

        def attn(h, b):
            rd_q = (("QTn", b), ("QTr", b), ("QTm", b)); rd_k = (("KTn", b), ("KTr", b), ("KTm", b))
            hp = slice(b * 64, b * 64 + 64)
            p0 = 64 if b == 0 else 32
            for qc in range(4):
                accO, ok_ = kb.ps("acc")
                pend = []

                def pv(kc, slot):
                    kb.mm(accO, Vh[:, b, kc, :], PT[:, slot, :], kc == 0, kc == 19, (("PT", slot), ("Vh", b)), (ok_,))
                for kc in range(20):
                    slot = (qc * 20 + kc) % 6
                    pst, pk = kb.ps("mm")
                    kb.mm(pst, KT[0:105, b, kc * 128:(kc + 1) * 128], QT[0:105, b, qc * 512:(qc + 1) * 512], True, True, rd_q + rd_k, (pk,))
                    kb.act(PT[:, slot, :], pst, AF.Exp, (pk,), (("PT", slot),), scale=MLA_SCALE)
                    pend.append((kc, slot))
                    if len(pend) > 2:
                        pv(*pend.pop(0))
                    if kc == 4 and fin_state:
                        fin_state.pop(0)()
                while pend:
                    pv(*pend.pop(0))
                ri = (h * 4 + qc) % 2
                rd = rDs[:, ri, :]; rk = ("rD", ri)
                rdt = rDt[:, ri, :]; rtk = ("rDt", ri)
                kb.recip(rdt[p0:p0 + 1, :], accO[p0:p0 + 1, :], (ok_,), (rtk,))

                def fin(accO=accO, ok_=ok_, rd=rd, rk=rk, rdt=rdt, rtk=rtk, qc=qc):
                    bc, bk = kb.ps("acc")
                    kb.mm(bc, sel[:, b, :], rdt, True, True, ("sel", rtk), (bk,))
                    kb.cp("dve", rd[hp, :], bc[hp, :], (bk,), (rk,))
                    kb.tt("dve", hT[hp, h // 2, qc * 512:(qc + 1) * 512], accO[hp, :], rd[hp, :], ALU.mult, (ok_, rk), (HK(h // 2, qc),))
                fin_state.append(fin)

        ck("e_ebsetup")
        proj(0, 0)
        ck("e_proj0")
        for h in range(12):
            if h + 1 < 12: proj(h + 1, (h + 1) % 2)
            attn(h, h % 2)
            ck("e_attn%d" % h)
        while fin_state:
            fin_state.pop(0)()
        pg.barrier()
        kb.rings = dict(kb.RINGS_DEFAULT)
        kb.top = MARK_E
        out_proj("w_out_even", e, ypT)

    def odd(l, e):
        kb.top = MARK
        QM = kb.bf([8, T]); ctxones = kb.bf([128]); zrow = kb.f32([128])
        kb.memset("dve", ctxones, 1.0, ("ctxones",))
        kb.ts("dve", ctxones, ctxones, vecs[:, fo:fo + 1], None, ALU.mult, None, ("ctxones", "vecs"), ("ctxones",))
        kb.memset("dve", zrow, 0.0, ("zrow",))
        MARK_O = kb.top
        KTna = kb.bf([4, NK]); Vna = kb.bf([20, 512])
        MARK_O2 = kb.top
        kb.ring_setup(2, 8 * 512)
        stg = kb.f32([2, 512]); ctxl = kb.bf([4, 512])
        kb.dma("pool", Vna[:, 16:20, :], I["c_na"][e][:, 1, :].rearrange("(b p) n -> p b n", p=128), (), ("Vna_ctx",))
        kb.dma("pool", ctxl, I["c_na"][e][:, 0, :].rearrange("(b p) n -> p b n", p=128), (), ("ctxl",))
        for piece in range(2):
            slot, sk = kb.ring_next(); wv = wview(slot, 8, 512)
            kb.wload(wv, I["wio_nakv"][e][:, piece * 512:(piece + 1) * 512], sk)
            for j in range(16):
                pst, pk = kb.ps("mm")
                for kc in range(8):
                    kb.mm(pst, hT[:, kc, j * 128:(j + 1) * 128], wv[:, kc, :], kc == 0, kc == 7, (sk, HK(kc, j // 4)), (pk,))
                kb.cp("act", stg[:, j % 2, :], pst, (pk,), (("stg", j % 2),))
                kb.dma("sp", O["nakv"][e, j * 128:(j + 1) * 128, piece * 512:(piece + 1) * 512], stg[:, j % 2, :], (("stg", j % 2),), ())
                if piece == 1:
                    kb.cp("dve", Vna[:, j, :], pst, (pk,), (("Vna", j),))
        for blk in range(4):
            ptr, tk = kb.ps("tr"); pb = bfv(ptr)
            for m in range(4):
                kb.tr(pb[:, m * 128:(m + 1) * 128], ctxl[:, blk, m * 128:(m + 1) * 128], ident_bf, ("ctxl", "ident"), (tk,))
            kb.cp("act", KTna[:, :, T + blk * 128:T + (blk + 1) * 128], pb.rearrange("p (a b) -> p a b", a=4), (tk,), (("KTna_ctx", blk),))
        for wname, isq in (("wio_qna", True), ("wio_kna", False)):
            slot, sk = kb.ring_next(); wv = wview(slot, 8, 512)
            kb.wload(wv, I[wname][e], sk)
            for m in range(4):
                for tb in range(4):
                    sl = slice(tb * 512, (tb + 1) * 512)
                    pst, pk = kb.ps("mm")
                    for kc in range(8):
                        kb.mm(pst, wv[:, kc, m * 128:(m + 1) * 128], hT[:, kc, sl], kc == 0, kc == 7, (sk, HK(kc, tb)), (pk,))
                    if isq:
                        kb.act(QM[:, m, sl], pst, AF.Copy, (pk,), [("QM", m, tb * 4 + i) for i in range(4)], scale=0.125)
                    else:
                        kb.cp("dve", KTna[:, m, sl], pst, (pk,), (("KTna", m, tb),))
        pg.barrier()
        kb.top = MARK_O2
        kb.rings = {"mm": (0, 6), "acc": (6, 2), "tr": (7, 1)}
        PT = kb.bf([6, 512]); nab = [kb.bf([2, 5, 128]) for _ in range(2)]; rDs = kb.f32([2, 128])
        units = [(j, i, hh) for j in range(4) for i in range(16) for hh in range(2)]
        st = {}

        def na_S(k):
            j, i, hh = units[k]
            if hh == 0:
                nb_ = nab[(k // 2) % 2]; nk = ("nab", (k // 2) % 2)
                kb.dma("pool", nb_, I["nabias"][e, VAR_OF[i]][:, 2 * j:2 * j + 2, :, :], (), (nk,))
                start = min(max(i - 2, 0), 11)
                tiles = [(start + c, c) for c in range(5)] + [(16 + c, None) for c in range(4)]
                accb, ak = kb.ps("acc")
                st[(j, i)] = (nb_, nk, tiles, accb, ak)
            nb_, nk, tiles, accb, ak = st[(j, i)]
            hp = slice(hh * 64, hh * 64 + 64)
            banks = []
            for t, (kblk, c) in enumerate(tiles):
                if t % 4 == 0:
                    banks.append(kb.ps("mm"))
                pst, pk = banks[-1]; col = slice((t % 4) * 128, (t % 4) * 128 + 128)
                kb.mm(pst[:, col], KTna[hp, j, kblk * 128:(kblk + 1) * 128], QM[hp, j, i * 128:(i + 1) * 128], True, True,
                      (("QM", j, i),), (pk,))
            kb.tt("dve", banks[0][0], banks[0][0], nb_[:, hh, 0:4, :].rearrange("p a b -> p (a b)"), ALU.add, (banks[0][1], nk), (banks[0][1],))
            kb.tt("dve", banks[1][0][:, 0:128], banks[1][0][:, 0:128], nb_[:, hh, 4, :], ALU.add, (banks[1][1], nk), (banks[1][1],))
            slots = []
            for bi, (pst, pk) in enumerate(banks):
                ncol = min(4, 9 - bi * 4) * 128
                sl_ = (k * 3 + bi) % 6
                kb.act(PT[:, sl_, 0:ncol], pst[:, 0:ncol], AF.Exp, (pk,), (("PT", sl_),))
                slots.append(sl_)
            st[(j, i, hh)] = slots

        def na_OD(k):
            j, i, hh = units[k]
            nb_, nk, tiles, accb, ak = st[(j, i)]
            slots = st[(j, i, hh)]
            Oh = accb[:, hh * 128:(hh + 1) * 128]; Dh = accb[:, 256 + hh * 128:256 + (hh + 1) * 128]
            for t, (kblk, c) in enumerate(tiles):
                sl_ = slots[t // 4]
                kb.mm(Oh, Vna[:, kblk, j * 128:(j + 1) * 128], PT[:, sl_, (t % 4) * 128:(t % 4) * 128 + 128], t == 0, t == 8,
                      (("PT", sl_),), (ak,))
            for t, (kblk, c) in enumerate(tiles):
                sl_ = slots[t // 4]
                kb.mm(Dh, ones_bf if c is not None else ctxones, PT[:, sl_, (t % 4) * 128:(t % 4) * 128 + 128], t == 0, t == 8,
                      (("PT", sl_), "ones", "ctxones"), (ak,))
            if hh == 1:
                for h2 in range(2):
                    hp = slice(h2 * 64, h2 * 64 + 64)
                    rd = rDs[:, (k + h2) % 2, :]; rk = ("rD", (k + h2) % 2)
                    kb.recip(rd[hp, :], accb[hp, 256 + h2 * 128:256 + (h2 + 1) * 128], (ak,), (rk,))
                    kb.tt("dve", QM[hp, j, i * 128:(i + 1) * 128], accb[hp, h2 * 128:(h2 + 1) * 128], rd[hp, :], ALU.mult, (ak, rk), (("QM", j, i),))

        na_S(0)
        for k in range(len(units)):
            if k + 1 < len(units): na_S(k + 1)
            na_OD(k)
        pg.barrier()
        kb.rings = dict(kb.RINGS_DEFAULT)
        kb.top = MARK_O
        KTsw = kb.bf([NK]); Vsw = kb.bf([20, 128])
        MARK_S = kb.top
        kb.ring_setup(3, 8 * 512)
        stg = kb.f32([2, 256]); ctxk = kb.bf([4, 128]); t1 = kb.f32([512]); t2 = kb.f32([512])
        tabO = [kb.f32([2, 512]) for _ in range(2)]
        kb.dma("pool", Vsw[:, 16:20, :], I["c_sw"][e][:, 1, :].rearrange("(b p) n -> p b n", p=128), (), ("Vsw_ctx",))
        kb.dma("pool", ctxk, I["c_sw"][e][:, 0, :].rearrange("(b p) n -> p b n", p=128), (), ("ctxk",))
        slot, sk = kb.ring_next(); wv = wview(slot, 8, 512)
        kb.wload(wv[:, :, 0:256], I["wio_swkv"][e], sk)
        kb.dma("pool", wv[:, :, 256:384], I["wio_ksw"][e].rearrange("(kc p) n -> p kc n", p=128), (), (sk,))
        kb.dma("pool", wv[:, :, 384:512], I["wio_ksws"][e].rearrange("(kc p) n -> p kc n", p=128), (), (sk,))
        for j in range(16):
            pst, pk = kb.ps("mm")
            for kc in range(8):
                kb.mm(pst[:, 0:256], hT[:, kc, j * 128:(j + 1) * 128], wv[:, kc, 0:256], kc == 0, kc == 7, (sk, HK(kc, j // 4)), (pk,))
            kb.cp("act", stg[:, j % 2, :], pst[:, 0:256], (pk,), (("stg", j % 2),))
            kb.dma("sp", O["swkv"][e, j * 128:(j + 1) * 128, :], stg[:, j % 2, :], (("stg", j % 2),), ())
            kb.cp("dve", Vsw[:, j, :], pst[:, 128:256], (pk,), (("Vsw", j),))
        ptr, tk = kb.ps("tr"); pb = bfv(ptr)
        for blk in range(4):
            kb.tr(pb[:, blk * 128:(blk + 1) * 128], ctxk[:, blk, :], ident_bf, ("ctxk", "ident"), (tk,))
        kb.cp("act", KTsw[:, T:NK], pb, (tk,), ("KTsw_ctx",))
        slot, skq = kb.ring_next(); wq = wview(slot, 8, 512)
        kb.wload(wq, I["wio_qsw"][e], skq)
        slot, skqs = kb.ring_next(); wqs = wview(slot, 8, 512)
        kb.wload(wqs, I["wio_qsws"][e], skqs)
        for tb in range(4):
            sl = slice(tb * 512, (tb + 1) * 512)
            tab = tabO[tb % 2]; tabk = ("tabO", tb % 2)
            kb.dma("sp", tab, I["ropeO"][:, :, sl].rearrange("a p t -> p a t"), (), (tabk,))
            pa, pak = kb.ps("mm"); pb_, pbk = kb.ps("mm")
            for kc in range(8):
                kb.mm(pa, wv[:, kc, 256:384], hT[:, kc, sl], kc == 0, kc == 7, (sk, HK(kc, tb)), (pak,))
            for kc in range(8):
                kb.mm(pb_, wv[:, kc, 384:512], hT[:, kc, sl], kc == 0, kc == 7, (sk, HK(kc, tb)), (pbk,))
            rope_combine(KTsw[:, sl], ("KTsw", tb), pa, pak, pb_, pbk, tab, tabk, t1, t2, 128)
            for m in range(4):
                pa, pak = kb.ps("mm"); pb_, pbk = kb.ps("mm")
                for kc in range(8):
                    kb.mm(pa, wq[:, kc, m * 128:(m + 1) * 128], hT[:, kc, sl], kc == 0, kc == 7, (skq, HK(kc, tb)), (pak,))
                for kc in range(8):
                    kb.mm(pb_, wqs[:, kc, m * 128:(m + 1) * 128], hT[:, kc, sl], kc == 0, kc == 7, (skqs, HK(kc, tb)), (pbk,))
                rope_combine(QM[:, 4 + m, sl], ("QMs", m, tb), pa, pak, pb_, pbk, tab, tabk, t1, t2, 128, pre=0.125)
        pg.barrier()
        kb.top = MARK_S
        kb.rings = {"mm": (0, 4), "acc": (4, 4), "tr": (7, 1)}
        PT = kb.bf([14, 512]); swb = kb.bf([6, 3, 128]); esink = kb.bf([2, 512]); rDs = kb.f32([2, 512])
        kb.dma("pool", swb, I["swbias"].rearrange("v k c q -> k v c q"), (), ("swb",))
        so = VOFF["sink"][0] + e * 8
        for g in range(2):
            for hd in range(4):
                kb.act(esink[0:1, g, hd * 128:(hd + 1) * 128], zrow[0:1, 0:128], AF.Exp, ("zrow", "vecs"), ("esink",),
                       bias=vecs[0:1, so + 4 * g + hd:so + 4 * g + hd + 1])
        sunits = [(i, g) for i in range(16) for g in range(2)]
        sst = {}

        def sw_S(k):
            i, g = sunits[k]
            hp = slice(g * 64, g * 64 + 64)
            chunks = [(min(max(i - 1 + c, 0), 15), c) for c in range(3)] + [(16 + c, None) for c in range(4)]
            slots = []
            for t, (kblk, c) in enumerate(chunks):
                pst, pk = kb.ps("mm")
                kb.mm(pst, KTsw[hp, kblk * 128:(kblk + 1) * 128], QM[hp, 4:8, i * 128:(i + 1) * 128], True, True, (), (pk,))
                if c is not None:
                    p3 = pst.rearrange("p (a b) -> p a b", a=4)
                    kb.tt("dve", p3, p3, swb[:, VAR_OF[i], c, :].unsqueeze(1).to_broadcast([128, 4, 128]), ALU.add, (pk, "swb"), (pk,))
                sl_ = (k * 7 + t) % 14
                kb.act(PT[:, sl_, :], pst, AF.Exp, (pk,), (("PT", sl_),))
                slots.append(sl_)
            sst[k] = (chunks, slots)

        def sw_OD(k):
            i, g = sunits[k]
            hp = slice(g * 64, g * 64 + 64)
            chunks, slots = sst[k]
            accO, ok_ = kb.ps("acc"); accD, dk_ = kb.ps("acc")
            for t, (kblk, c) in enumerate(chunks):
                kb.mm(accO, Vsw[:, kblk, :], PT[:, slots[t], :], t == 0, t == 6, (("PT", slots[t]),), (ok_,))
            for t, (kblk, c) in enumerate(chunks):
                kb.mm(accD, ones_bf if c is not None else ctxones, PT[:, slots[t], :], t == 0, False, (("PT", slots[t]), "ones", "ctxones"), (dk_,))
            kb.mm(accD, ones_bf[0:1, :], esink[0:1, g, :], False, True, ("esink", "ones"), (dk_,))
            rd = rDs[:, k % 2, :]; rk = ("rD", k % 2)
            kb.recip(rd[hp, :], accD[hp, :], (dk_,), (rk,))
            kb.tt("dve", QM[hp, 4:8, i * 128:(i + 1) * 128], accO[hp, :].rearrange("p (a b) -> p a b", a=4),
                  rd[hp, :].rearrange("p (a b) -> p a b", a=4), ALU.mult, (ok_, rk), (("QMo", i, g),))

        sw_S(0)
        for k in range(len(sunits)):
            if k + 1 < len(sunits): sw_S(k + 1)
            sw_OD(k)
        pg.barrier()
        kb.rings = dict(kb.RINGS_DEFAULT)
        kb.top = MARK_O
        out_proj("w_out_odd", e, None, src=QM)

    return {"even": even, "odd": odd}


_CACHE = {}


def kernel(**inputs):
    maps = _host_inputs(inputs)
    for core, m in enumerate(maps):
        sample = core >= 4
        fl = m.pop("flags")
        m["vecs"] = _pack_vecs(inputs, m.pop("cvec"), fl)
        for k in ("b_mod", "norm_mix", "norm_ffn", "norm_final", "mla_q_norm", "mla_kv_norm", "pool_scale",
                  "swa_sink", "conv_w", "conv_b"):
            m.pop(k, None)
    if "nc" not in _CACHE:
        _CACHE["nc"] = build_program()[0]
    nc = _CACHE["nc"]
    res = run_bass_kernel_spmd(nc, maps, core_ids=list(range(8)))
    R = res.results
    y_prompt = np.concatenate([np.asarray(R[c]["yT"]).T.reshape(8, 256, D) for c in range(4)], 0)
    y_sample = np.stack([np.asarray(R[4 + b]["yT"]).T for b in range(4)], 0)
    lat = np.concatenate([np.asarray(R[c]["lat"]).reshape(2, 8, 256, 288).transpose(1, 0, 2, 3) for c in range(4)], 0)
    na = np.concatenate([np.asarray(R[c]["nakv"]).reshape(2, 8, 256, 2, 8, 64).transpose(1, 0, 2, 3, 4, 5) for c in range(4)], 0)
    sw = np.concatenate([np.asarray(R[c]["swkv"]).reshape(2, 8, 256, 2, 2, 64).transpose(1, 0, 2, 3, 4, 5) for c in range(4)], 0)
    f = lambda a: np.ascontiguousarray(a, dtype=np.float32)
    return (f(y_prompt), f(y_sample), f(lat), f(na), f(sw))
```

```python
import numpy as np
import concourse.bass as bass
import concourse.mybir as mybir
from concourse.bass_utils import run_bass_kernel_spmd
from contextlib import ExitStack

F32, BF16 = mybir.dt.float32, mybir.dt.bfloat16
AF, ALU = mybir.ActivationFunctionType, mybir.AluOpType

D = 1024; T = 2048; DEPTH = 4; PAST = 512; NK = T + PAST
DFF = 2816; NFC = 22
EPS = 1e-6
MLA_SCALE = 96 ** -0.5
BM = 1024.0
NEGB = -30000.0
STAGES = {"even": True, "odd": True, "ffn": True}
STOP_AT = None
EMBED_WAITS = True


class _Stop(Exception):
    pass


def ck(name):
    if STOP_AT == name:
        raise _Stop()
NLAYERS = DEPTH


class Op:
    __slots__ = ("eng", "fn", "deps", "dma", "sem", "target", "pre", "need", "ticket", "idx", "waits", "cov")
    _ctr = [0]

    def __init__(self, eng, fn, dma):
        self.eng = eng; self.fn = fn; self.dma = dma; self.deps = []
        self.sem = None; self.target = 0; self.pre = None; self.need = False; self.ticket = 0
        Op._ctr[0] += 1; self.idx = Op._ctr[0]; self.waits = None; self.cov = None


class Prog:
    ENGS = ("pe", "act", "dve", "pool", "sp")
    NDS = {"sp": 40, "pool": 40}

    def __init__(self):
        self.q = {e: [] for e in self.ENGS}
        self.lw = {}; self.rd = {}
        self.dsem_next = {e: 0 for e in self.NDS}
        self.dsem_tgt = {}
        self.dma_since = []

    def add(self, eng, fn, reads=(), writes=(), dma=False):
        op = Op(eng, fn, dma)
        deps = []
        for k in reads:
            w = self.lw.get(k)
            if w is not None: deps.append(w)
            if isinstance(k, tuple) and k[0] == "ps":
                for r in self.rd.get(k, ()):
                    if r.eng != eng: deps.append(r)
        for k in writes:
            w = self.lw.get(k)
            if w is not None: deps.append(w)
            for r in self.rd.get(k, ()): deps.append(r)
        seen = set(); dl = []
        for d in deps:
            if d is op or id(d) in seen: continue
            seen.add(id(d))
            if (not d.dma) and d.eng == eng and eng == "pe": continue
            dl.append(d)
        op.deps = dl
        if dma:
            i = self.dsem_next[eng]; self.dsem_next[eng] = (i + 1) % self.NDS[eng]
            key = (eng, i)
            prev = self.dsem_tgt.get(key, 0)
            op.sem = key; op.pre = prev; op.target = prev + 16
            self.dsem_tgt[key] = op.target
            self.dma_since.append(op)
        for k in reads:
            self.rd.setdefault(k, []).append(op)
        for k in writes:
            self.lw[k] = op; self.rd[k] = []
        self.q[eng].append(op)
        return op

    def barrier(self):
        col = Op("dve", "nop", False)
        for e in self.ENGS:
            if self.q[e]:
                last = None
                for o in reversed(self.q[e]):
                    if o.fn is not None and not o.dma:
                        last = o; break
                if last is not None: col.deps.append(last)
        col.deps.extend(self.dma_since)
        self.dma_since = []
        self.q["dve"].append(col)
        for e in self.ENGS:
            if e == "dve": continue
            w = Op(e, None, False); w.deps = [col]
            self.q[e].append(w)
        self.lw = {}; self.rd = {}

    def emit(self, nc, block, csem, dsems):
        for e in self.ENGS:
            for op in self.q[e]:
                for d in op.deps:
                    if not d.dma: d.need = True
        for e in self.ENGS:
            t = 0
            for op in self.q[e]:
                if op.need and not op.dma:
                    t += 1; op.ticket = t
        engobj = {"pe": nc.tensor, "act": nc.scalar, "dve": nc.vector, "pool": nc.gpsimd, "sp": nc.sync}
        self.nwaits = 0
        allops = sorted((op for e in self.ENGS for op in self.q[e]), key=lambda o: o.idx)
        eng_cov = {e: {} for e in self.ENGS}
        for op in allops:
            cov = eng_cov[op.eng]
            ws = {}
            for d in op.deps:
                key, val = (d.sem, d.target) if d.dma else (d.eng, d.ticket)
                if val <= 0 or cov.get(key, 0) >= val: continue
                ws[key] = val; cov[key] = val
                for k2, v2 in d.cov.items():
                    if cov.get(k2, 0) < v2: cov[k2] = v2
            if op.dma and op.pre > 0 and cov.get(op.sem, 0) < op.pre:
                ws[op.sem] = max(ws.get(op.sem, 0), op.pre); cov[op.sem] = op.pre
            op.waits = [(k, v) for k, v in ws.items() if cov.get(k, 0) <= v or True]
            op.cov = dict(cov)

        def run(e, eo):
            waited = {}

            def wait(key, sem, val):
                if val <= 0 or waited.get(key, 0) >= val: return
                waited[key] = val
                eo.wait_ge(sem, val); self.nwaits += 1

            for op in self.q[e]:
                need = [((dsems[k] if isinstance(k, tuple) else csem[k]), v) for k, v in op.waits]
                for k, v in op.waits:
                    waited[k] = max(waited.get(k, 0), v)
                embed = None
                if EMBED_WAITS and e in ("pe", "act", "dve") and need and (not op.dma) and op.fn is not None and op.fn != "nop":
                    embed = need.pop()
                for sem, val in need:
                    eo.wait_ge(sem, val); self.nwaits += 1
                if op.dma:
                    op.fn(eo).then_inc(dsems[op.sem], 16)
                elif op.fn is None:
                    pass
                else:
                    ins = eo.nop() if op.fn == "nop" else op.fn(eo)
                    if embed is not None: ins._wait_ge(embed[0], embed[1])
                    if op.need: ins.then_inc(csem[e], 1)
            for key, tgt in self.dsem_tgt.items():
                if key[0] == e: wait(key, dsems[key], tgt)

        block.tensor(lambda eo: run("pe", eo))
        block.scalar(lambda eo: run("act", eo))
        block.vector(lambda eo: run("dve", eo))
        block.gpsimd(lambda eo: run("pool", eo))
        block.sync(lambda eo: run("sp", eo))


def _bf(x):
    import ml_dtypes
    return np.asarray(x, np.float32).astype(ml_dtypes.bfloat16).astype(np.float32)


def _rope_tables(R, sample):
    half = R // 2; nf = half // 2
    t = np.arange(T)
    freqs = (10000.0 ** (-np.arange(nf, dtype=np.float32) / nf)).astype(np.float32)
    C = np.ones((R, T), np.float32); S = np.zeros((R, T), np.float32)
    if sample:
        for hi, pos in enumerate((t // 64, t % 64)):
            ang = pos.astype(np.float32)[None, :] * freqs[:, None]
            c, s = np.cos(ang).astype(np.float32), np.sin(ang).astype(np.float32)
            b = hi * half
            C[b:b + nf] = c; C[b + nf:b + 2 * nf] = c
            S[b:b + nf] = -s; S[b + nf:b + 2 * nf] = s
    return C, S


def _rope_perm(R):
    half = R // 2; nf = half // 2
    p = np.arange(R)
    for b in (0, half):
        p[b:b + nf] = np.arange(b + nf, b + 2 * nf)
        p[b + nf:b + 2 * nf] = np.arange(b, b + nf)
    return p


VAR_OF = [0, 1] + [2, 3] * 6 + [4, 5]
VAR_REP = [0, 1, 2, 3, 14, 15]


def _struct_consts(sample):
    c = {}
    c["ident"] = np.eye(128, dtype=np.float32)
    Ce, Se = _rope_tables(32, sample)
    c["ropeE"] = np.stack([np.tile(Ce, (4, 1)), np.tile(Se, (4, 1))], 0)
    Co, So = _rope_tables(64, sample)
    c["ropeO"] = np.stack([np.tile(Co, (2, 1)), np.tile(So, (2, 1))], 0)
    km = np.zeros((9, NK), np.float32); qm = np.zeros((9, T), np.float32)
    km[8, :] = 1.0; qm[8, :] = -BM
    if sample:
        km[0, :] = 1.0; qm[0, :] = BM
    else:
        for s in range(8):
            km[s, s * 256:(s + 1) * 256] = 1.0
            qm[s, s * 256:(s + 1) * 256] = BM
    c["kmaskE"] = km; c["qmaskE"] = qm
    fl = 1.0 if sample else 0.0
    c["flags"] = np.tile(np.array([[fl, 1.0 - fl]], np.float32), (128, 1))
    n = T if sample else 256
    Pm = np.zeros((16, 128, 4, 3, 128), np.float32)
    tt = np.arange(T); tl = tt % n; base = tt - tl
    for g, w in enumerate((2, 4, 8, 16)):
        lo = np.clip(tl - w // 2, 0, n); hi = np.clip(tl + w // 2, 0, n)
        cnt = (hi - lo).astype(np.float32)
        for t in range(T):
            j = t // 128
            for s in range(base[t] + lo[t], base[t] + hi[t]):
                sb = s // 128 - (j - 1)
                Pm[j, s % 128, g, sb, t % 128] += 1.0 / cnt[t]
            Pm[j, t % 128, g, 1, t % 128] -= 1.0
    c["Pm"] = Pm
    swb = np.full((6, 128, 3, 128), NEGB, np.float32)
    for v, i in enumerate(VAR_REP):
        tq = i * 128 + np.arange(128)
        for cc in range(3):
            kb = i - 1 + cc
            if kb < 0 or kb > 15: continue
            ks = kb * 128 + np.arange(128)
            if sample:
                ok = np.abs(tq[None, :] - ks[:, None]) <= 128
            else:
                ok = (tq[None, :] // 256) == (ks[:, None] // 256)
            swb[v, :, cc, :] = np.where(ok, 0.0, NEGB)
    c["swbias"] = swb
    return c


def _na_bias(sample, rpb):
    out = np.full((2, 6, 128, 8, 5, 128), NEGB, np.float32)
    for v, i in enumerate(VAR_REP):
        start = min(max(i - 2, 0), 11)
        tq = i * 128 + np.arange(128)
        r, cq = tq // 64, tq % 64
        for cc in range(5):
            ks = (start + cc) * 128 + np.arange(128)
            if sample:
                rp, cp = ks // 64, ks % 64
                r0 = np.clip(r - 4, 0, 24); c0 = np.clip(cq - 8, 0, 48)
                ok = ((rp[:, None] >= r0[None, :]) & (rp[:, None] < r0[None, :] + 8) &
                      (cp[:, None] >= c0[None, :]) & (cp[:, None] < c0[None, :] + 16))
                dr = np.clip(rp[:, None] - r[None, :] + 7, 0, 14)
                dc = np.clip(cp[:, None] - cq[None, :] + 15, 0, 30)
                for l in range(2):
                    g = rpb[l][:, dr, dc]
                    out[l, v, :, :, cc, :] = np.where(ok[:, None, :], g.transpose(1, 0, 2), NEGB)
            else:
                ok = (tq[None, :] // 256) == (ks[:, None] // 256)
                out[:, v, :, :, cc, :] = np.where(ok, 0.0, NEGB)[None, :, None, :]
    return out


def _host_inputs(inp):
    f = lambda a: np.ascontiguousarray(np.asarray(a, np.float32))
    w = {}
    for k in ("w_mod", "b_mod", "norm_mix", "norm_ffn", "norm_final", "mla_q_norm", "mla_kv_norm",
              "pool_scale", "w_out_even", "w_out_odd", "swa_sink", "w_up", "conv_w", "conv_b", "w_down"):
        w[k] = f(inp[k])
    rowperm = np.concatenate([np.arange(512)] + [np.concatenate([512 + jj * 64 + np.arange(64), 512 + (4 + jj) * 64 + np.arange(64)])
                                                  for jj in range(4)])
    w["w_out_odd"] = f(w["w_out_odd"][:, rowperm, :])
    wie = f(inp["w_in_even"])
    w["wie_qa"] = f(wie[:, :, 0:384]); w["wie_tm"] = f(wie[:, :, 384:928])
    w["wie_kr"] = f(wie[:, :, 640:672]); w["wie_krs"] = f(wie[:, :, 640 + _rope_perm(32)])
    wuq = f(inp["w_uq"]).reshape(2, 384, 12, 96)
    w["wuq_n"] = f(wuq[..., 0:64].reshape(2, 384, 768))
    w["wuq_r"] = f(wuq[..., 64:96].reshape(2, 384, 384))
    w["wuq_rs"] = f(wuq[..., 64 + _rope_perm(32)].reshape(2, 384, 384))
    wukv = f(inp["w_ukv"]).reshape(2, 256, 12, 128)
    w["wukv_k"] = f(wukv[..., 0:64].reshape(2, 256, 768)); w["wukv_v"] = f(wukv[..., 64:128].reshape(2, 256, 768))
    wp = f(inp["w_pool"]); bd = np.zeros((2, 2, 128, 128), np.float32)
    for e in range(2):
        for pr in range(2):
            bd[e, pr, 0:64, 0:64] = wp[e, 2 * pr]; bd[e, pr, 64:128, 64:128] = wp[e, 2 * pr + 1]
    w["wpool_bd"] = bd
    wio = f(inp["w_in_odd"])
    w["wio_qna"] = f(wio[:, :, 0:512]); w["wio_kna"] = f(wio[:, :, 512:1024])
    w["wio_nakv"] = f(wio[:, :, 512:1536]); w["wio_swkv"] = f(wio[:, :, 2048:2304])
    qsw = wio[:, :, 1536:2048].reshape(2, 1024, 8, 64)
    order = [0, 4, 1, 5, 2, 6, 3, 7]
    w["wio_qsw"] = f(qsw[:, :, order, :].reshape(2, 1024, 512))
    w["wio_qsws"] = f(qsw[:, :, order, :][..., _rope_perm(64)].reshape(2, 1024, 512))
    ksw = wio[:, :, 2048:2176].reshape(2, 1024, 2, 64)
    w["wio_ksw"] = f(ksw.reshape(2, 1024, 128)); w["wio_ksws"] = f(ksw[..., _rope_perm(64)].reshape(2, 1024, 128))
    sc = {True: _struct_consts(True), False: _struct_consts(False)}
    rpb = f(inp["na_rpb"])
    nab = {True: _na_bias(True, rpb), False: _na_bias(False, rpb)}
    xp = f(inp["x_prompt"]); xs = f(inp["x_sample"])
    maps = []
    for core in range(8):
        sample = core >= 4
        m = dict(w)
        m.update(sc[sample]); m["nabias"] = nab[sample]
        if sample:
            b = core - 4
            m["xT"] = f(xs[b].T)
            m["cvec"] = f(inp["c"])[b]
            m["c_mla"] = f(inp["cache_mla_latent"])[b]
            m["c_na"] = f(inp["cache_na_kv"])[b].reshape(2, 512, 2, 512)
            m["c_sw"] = f(inp["cache_swa_kv"])[b].reshape(2, 512, 2, 128)
        else:
            m["xT"] = f(xp[8 * core:8 * core + 8].reshape(T, D).T)
            m["cvec"] = f(inp["c_ctx"])
            m["c_mla"] = np.zeros((2, 512, 288), np.float32)
            m["c_na"] = np.zeros((2, 512, 2, 512), np.float32)
            m["c_sw"] = np.zeros((2, 512, 2, 128), np.float32)
        maps.append(m)
    return maps


IN_SHAPES = {
    "xT": [D, T], "cvec": [D], "c_mla": [2, 512, 288], "c_na": [2, 512, 2, 512], "c_sw": [2, 512, 2, 128],
    "w_mod": [4, D, 6 * D], "b_mod": [4, 6 * D], "norm_mix": [4, D], "norm_ffn": [4, D], "norm_final": [D],
    "mla_q_norm": [2, 384], "mla_kv_norm": [2, 256], "pool_scale": [2, 256],
    "w_out_even": [2, D, D], "w_out_odd": [2, D, D], "swa_sink": [2, 8],
    "w_up": [4, D, 2 * DFF], "conv_w": [4, 3, 2 * DFF], "conv_b": [4, 2 * DFF], "w_down": [4, DFF, D],
    "wie_qa": [2, D, 384], "wie_tm": [2, D, 544], "wie_kr": [2, D, 32], "wie_krs": [2, D, 32],
    "wuq_n": [2, 384, 768], "wuq_r": [2, 384, 384], "wuq_rs": [2, 384, 384],
    "wukv_k": [2, 256, 768], "wukv_v": [2, 256, 768], "wpool_bd": [2, 2, 128, 128],
    "wio_qna": [2, D, 512], "wio_kna": [2, D, 512], "wio_nakv": [2, D, 1024], "wio_swkv": [2, D, 256],
    "wio_qsw": [2, D, 512], "wio_qsws": [2, D, 512], "wio_ksw": [2, D, 128], "wio_ksws": [2, D, 128],
    "ident": [128, 128], "ropeE": [2, 128, T], "ropeO": [2, 128, T], "kmaskE": [9, NK], "qmaskE": [9, T],
    "flags": [128, 2], "Pm": [16, 128, 4, 3, 128], "swbias": [6, 128, 3, 128], "nabias": [2, 6, 128, 8, 5, 128],
}
OUT_SHAPES = {"yT": [D, T], "lat": [2, T, 288], "nakv": [2, T, 1024], "swkv": [2, T, 256]}


class KB:
    def __init__(self, nc, big, psums, ins, outs):
        self.nc = nc; self.big = big; self.P = psums; self.I = ins; self.O = outs
        self.pg = Prog()
        self.top = 0
        self.rr = {"mm": 0, "acc": 0, "tr": 0}
        self.RINGS_DEFAULT = {"mm": (0, 4), "acc": (4, 3), "tr": (7, 1)}
        self.rings = dict(self.RINGS_DEFAULT)
        self.uid = 0

    def alloc(self, nbytes):
        off = self.top; self.top += (nbytes + 7) // 8 * 2
        assert self.top * 4 <= 212800, f"SBUF overflow {self.top * 4}"
        return off

    def _shape(self, v, shape):
        if len(shape) == 1: return v
        if len(shape) == 2: return v.rearrange("p (a b) -> p a b", a=shape[0])
        if len(shape) == 3: return v.rearrange("p (a b c) -> p a b c", a=shape[0], b=shape[1])
        return v.rearrange("p (a b c d) -> p a b c d", a=shape[0], b=shape[1], c=shape[2])

    def f32(self, shape):
        n = int(np.prod(shape)); off = self.alloc(n * 4)
        return self._shape(self.big[:, off:off + n], shape)

    def bf(self, shape):
        n = int(np.prod(shape)); off = self.alloc(n * 2)
        return self._shape(self.big[:, off:off + (n + 1) // 2].bitcast(BF16)[:, 0:n], shape)

    def key(self, name):
        self.uid += 1
        return (name, self.uid)

    def ps(self, ring):
        lo, n = self.rings[ring]
        i = lo + self.rr[ring] % n; self.rr[ring] += 1
        return self.P[i], ("ps", i)

    def mm(self, out, lhsT, rhs, start, stop, reads, writes):
        return self.pg.add("pe", lambda e: e.matmul(out, lhsT, rhs, start=start, stop=stop), reads, writes)

    def tr(self, out, in_, ident, reads, writes):
        return self.pg.add("pe", lambda e: e.transpose(out, in_, ident), reads, writes)

    def act(self, out, in_, func, reads, writes, scale=1.0, bias=0.0, accum=None):
        if accum is None:
            return self.pg.add("act", lambda e: e.activation(out=out, in_=in_, func=func, bias=bias, scale=scale), reads, writes)
        return self.pg.add("act", lambda e: e.activation(out=out, in_=in_, func=func, bias=bias, scale=scale, accum_out=accum), reads, writes)

    def ts(self, eng, out, in0, s1, s2, op0, op1, reads, writes):
        if s2 is None:
            return self.pg.add(eng, lambda e: e.tensor_scalar(out=out, in0=in0, scalar1=s1, scalar2=None, op0=op0), reads, writes)
        return self.pg.add(eng, lambda e: e.tensor_scalar(out=out, in0=in0, scalar1=s1, scalar2=s2, op0=op0, op1=op1), reads, writes)

    def stt(self, eng, out, in0, scalar, in1, op0, op1, reads, writes):
        return self.pg.add(eng, lambda e: e.scalar_tensor_tensor(out=out, in0=in0, scalar=scalar, in1=in1, op0=op0, op1=op1), reads, writes)

    def tt(self, eng, out, in0, in1, op, reads, writes):
        return self.pg.add(eng, lambda e: e.tensor_tensor(out=out, in0=in0, in1=in1, op=op), reads, writes)

    def cp(self, eng, out, in_, reads, writes):
        if eng == "act":
            return self.pg.add("act", lambda e: e.copy(out=out, in_=in_), reads, writes)
        return self.pg.add(eng, lambda e: e.tensor_copy(out=out, in_=in_), reads, writes)

    def recip(self, out, in_, reads, writes):
        return self.pg.add("dve", lambda e: e.reciprocal(out=out, in_=in_), reads, writes)

    def memset(self, eng, ap, val, writes):
        return self.pg.add(eng, lambda e: e.memset(ap, val), (), writes)

    def dma(self, q, out, in_, reads, writes):
        return self.pg.add(q, lambda e: e.dma_start(out=out, in_=in_), reads, writes, dma=True)

    def wload(self, dst, src_ap, key):
        return self.dma("pool", dst, src_ap.rearrange("(kc p) n -> p kc n", p=128), (), (key,))


VOFF = {}
def _voff():
    o = 0
    for name, n in (("bmod", 192), ("nmix", 32), ("nffn", 32), ("nfin", 8), ("conv", 704), ("gq", 6),
                    ("pscale", 4), ("cvec", 8), ("gkv", 512), ("sink", 16), ("flags", 2)):
        VOFF[name] = (o, n); o += n
    return o
NV = _voff()


def _pack_vecs(inp, cvec, flags):
    v = np.zeros((128, NV), np.float32)
    def put(name, arr):
        o, n = VOFF[name]; v[:, o:o + n] = np.asarray(arr, np.float32).reshape(128, n)
    f = lambda a: np.asarray(a, np.float32)
    put("bmod", f(inp["b_mod"]).reshape(4, 48, 128).transpose(2, 0, 1))
    put("nmix", f(inp["norm_mix"]).reshape(4, 8, 128).transpose(2, 0, 1))
    put("nffn", f(inp["norm_ffn"]).reshape(4, 8, 128).transpose(2, 0, 1))
    put("nfin", f(inp["norm_final"]).reshape(8, 128).transpose(1, 0))
    cw = np.concatenate([f(inp["conv_w"]), f(inp["conv_b"])[:, None, :]], 1)
    put("conv", cw.reshape(4, 4, 44, 128).transpose(3, 0, 1, 2))
    put("gq", f(inp["mla_q_norm"]).reshape(2, 3, 128).transpose(2, 0, 1))
    put("pscale", f(inp["pool_scale"]).reshape(2, 2, 128).transpose(2, 0, 1))
    put("cvec", f(cvec).reshape(8, 128).transpose(1, 0))
    put("gkv", np.broadcast_to(f(inp["mla_kv_norm"]).reshape(1, 512), (128, 512)))
    put("sink", np.broadcast_to(f(inp["swa_sink"]).reshape(1, 16), (128, 16)))
    put("flags", flags)
    return v


def build_program():
    nc = bass.Bass("TRN2", target_bir_lowering=False)
    shapes = dict(IN_SHAPES); shapes["vecs"] = [128, NV]
    for k in ("cvec", "b_mod", "norm_mix", "norm_ffn", "norm_final", "mla_q_norm", "mla_kv_norm", "pool_scale",
              "swa_sink", "conv_w", "conv_b", "flags"):
        shapes.pop(k)
    I = {k: nc.dram_tensor(k, s, F32, kind="ExternalInput").ap() for k, s in shapes.items()}
    O = {k: nc.dram_tensor(k, s, F32, kind="ExternalOutput").ap() for k, s in OUT_SHAPES.items()}
    es = ExitStack()
    with es:
        big = es.enter_context(nc.sbuf_tensor("big", [128, 53200], F32))
        P = [es.enter_context(nc.psum_tensor(f"psb{i}", [128, 512], F32)) for i in range(8)]
        csem = {e: es.enter_context(nc.semaphore(f"c_{e}")) for e in Prog.ENGS}
        dsems = {(q, i): es.enter_context(nc.semaphore(f"d_{q}{i}")) for q in Prog.NDS for i in range(Prog.NDS[q])}
        block = es.enter_context(nc.Block())
        kb = KB(nc, big, [p[:, :] for p in P], I, O)
        _emit_all(kb)
        kb.pg.emit(nc, block, csem, dsems)
    return nc, kb


def _emit_all(kb):
    I, O, pg = kb.I, kb.O, kb.pg
    xres = kb.f32([8, T]); hT = kb.bf([8, T])
    vecs = kb.f32([NV])
    ones_bf = kb.bf([128]); ident_bf = kb.bf([128])
    scv = kb.bf([8]); modT_all = kb.f32([4, 48]); prm_all = kb.f32([4, 6, 8]); convx_all = kb.f32([4, 4, 44])
    prm = prm_all[:, 0]; convx = convx_all[:, 0]
    MARK = kb.top
    kb.xres, kb.hT, kb.vecs, kb.ones_bf, kb.ident_bf, kb.prm = xres, hT, vecs, ones_bf, ident_bf, prm
    kb.MARK = MARK

    def vv(name, l=None, per=None):
        o, n = VOFF[name]
        if l is None: return vecs[:, o:o + n]
        return vecs[:, o + l * per:o + (l + 1) * per]
    kb.vv = vv

    XK = lambda c, tb: ("x", c, tb)
    HK = lambda c, tb: ("h", c, tb)
    kb.XK, kb.HK = XK, HK
    kb.dma("sp", vecs, I["vecs"][:, :], (), ("vecs",))
    for c in range(8):
        kb.dma("sp", xres[:, c, :], I["xT"][c * 128:(c + 1) * 128, :], (), [XK(c, tb) for tb in range(4)])
    kb.dma("pool", ident_bf, I["ident"][:, :], (), ("ident",))
    kb.memset("dve", ones_bf, 1.0, ("ones",))
    kb.act(scv, vv("cvec"), AF.Silu, ("vecs",), ("scv",))

    def ring_setup(nslots, nel):
        kb.ring = [kb.bf([nel]) for _ in range(nslots)]
        kb.ring_i = 0; kb.ring_nel = nel

    def ring_next():
        i = kb.ring_i % len(kb.ring); kb.ring_i += 1
        return kb.ring[i], ("wr", i)
    kb.ring_setup, kb.ring_next = ring_setup, ring_next

    def wview(slot, kc, n):
        return slot[:, 0:kc * n].rearrange("p (a b) -> p a b", a=kc)
    kb.wview = wview

    def norm_mod(Acol, Bcol, sq, rs, tmp, dst_fn, final=False):
        nsq = sq.shape[1]; ntm = tmp.shape[1]
        two_rs = len(rs.shape) == 3
        rss = []
        for tb in range(4):
            sl = slice(tb * 512, (tb + 1) * 512)
            pst, pk = kb.ps("mm")
            for c in range(8):
                i = (tb * 8 + c) % nsq
                s = sq[:, i, :]
                kb.act(s, xres[:, c, sl], AF.Square, (XK(c, tb),), (("sq", i),))
                kb.mm(pst, ones_bf, s, c == 0, c == 7, (("sq", i), "ones"), (pk,))
            r = rs[:, tb % 2, :] if two_rs else rs
            rk = ("rs", tb % 2) if two_rs else "rs"
            kb.act(r, pst, AF.Sqrt, (pk,), (rk,), scale=1.0 / D, bias=EPS)
            kb.recip(r, r, (rk,), (rk,))
            for c in range(8):
                i = (tb * 8 + c) % ntm
                tm = tmp[:, i, :]
                kb.stt("dve", tm, xres[:, c, sl], Acol(c), r, ALU.mult, ALU.mult,
                       (XK(c, tb), rk, "prm", "vecs"), (("tmp", i),))
                dst_fn(c, tb, tm, ("tmp", i), Bcol(c) if Bcol else 0.0)
    kb.norm_mod = norm_mod

    def to_hT(c, tb, tm, tk, bias):
        kb.act(hT[:, c, tb * 512:(tb + 1) * 512], tm, AF.Identity, (tk, "prm"), (HK(c, tb),), bias=bias)

    def adaln_gen(layers, aring, pst, pk, pw=512):
        cnt = 0
        for l in layers:
            modT = modT_all[:, l]; prm = prm_all[:, l]; convx = convx_all[:, l]
            for piece in range(6144 // pw):
                slot, sk = aring[cnt % len(aring)], ("awr", cnt % len(aring)); cnt += 1
                wv = wview(slot, 8, pw)
                kb.wload(wv, I["w_mod"][l][:, piece * pw:(piece + 1) * pw], sk)
                for cc in range(pw // 128):
                    j = piece * (pw // 128) + cc
                    for kc in range(8):
                        kb.mm(pst[:, j:j + 1], wv[:, kc, cc * 128:(cc + 1) * 128], scv[:, kc:kc + 1], kc == 0, kc == 7,
                              (sk, "scv"), (pk,))
                yield
            mk = ("modT", l); pk_ = ("prm", l)
            kb.tt("dve", modT, pst[:, 0:48], vv("bmod", l, 48), ALU.add, (pk, "vecs"), (mk,))
            for (row, sc_i, g) in ((0, 1, "nmix"), (3, 4, "nffn")):
                kb.stt("dve", prm[:, row, :], modT[:, sc_i * 8:(sc_i + 1) * 8], 1.0, vv(g, l, 8), ALU.add, ALU.mult,
                       (mk, "vecs"), (pk_,))
            for (row, m_i) in ((1, 0), (2, 2), (4, 3), (5, 5)):
                kb.cp("dve", prm[:, row, :], modT[:, m_i * 8:(m_i + 1) * 8], (mk,), (pk_,))
            o, _ = VOFF["conv"]; cb = o + l * 176
            fo, _ = VOFF["flags"]
            w0 = vecs[:, cb:cb + 44]; w2 = vecs[:, cb + 88:cb + 132]
            kb.ts("dve", convx[:, 0, :], w0, vecs[:, fo:fo + 1], None, ALU.mult, None, ("vecs",), (("convx", l),))
            kb.ts("dve", convx[:, 1, :], w2, vecs[:, fo:fo + 1], None, ALU.mult, None, ("vecs",), (("convx", l),))
            kb.ts("dve", convx[:, 2, :], w0, vecs[:, fo + 1:fo + 2], -1.0, ALU.mult, ALU.mult, ("vecs",), (("convx", l),))
            kb.ts("dve", convx[:, 3, :], w2, vecs[:, fo + 1:fo + 2], -1.0, ALU.mult, ALU.mult, ("vecs",), (("convx", l),))
            yield

    def adaln_first():
        kb.top = MARK
        aring = [kb.bf([8 * 512]) for _ in range(3)]
        pst, pk = kb.ps("acc")
        for _ in adaln_gen([0], aring, pst, pk):
            pass
        pg.barrier()

    def norm_phase(row_a, row_b):
        kb.top = MARK
        sq = kb.bf([4, 512]); rs = kb.f32([2, 512]); tmp = kb.f32([4, 512])
        norm_mod(lambda c: kb.prm[:, row_a, c:c + 1], lambda c: kb.prm[:, row_b, c:c + 1], sq, rs, tmp, to_hT)
        pg.barrier()

    def ffn(l):
        kb.top = MARK
        o, _ = VOFF["conv"]; cb = o + l * 176
        cw = lambda k, c: vecs[:, cb + k * 44 + c:cb + k * 44 + c + 1]
        cx = lambda k, c: kb.convx[:, k, c:c + 1]
        ring_setup(2 if (l == 0 and NLAYERS > 1) else 3, 22 * 256)
        kb.rings = {"mm": (0, 7), "acc": (0, 7), "tr": (7, 1)}
        actb = kb.bf([NFC, 1024]); ta_r = kb.f32([4, 512]); tg_r = kb.f32([4, 512]); es = kb.f32([4, 2]); eh = kb.f32([44])
        agen = None
        if l == 0 and NLAYERS > 1:
            kb.rings = {"mm": (0, 6), "acc": (0, 6), "tr": (7, 1)}
            aring = [kb.bf([8 * 384]) for _ in range(2)]
            agen = adaln_gen(list(range(1, NLAYERS)), aring, kb.P[6], ("ps", 6), pw=384)
        for sb in range(2):
            for c in range(NFC):
                if c % 2 == 0:
                    slot, sk = ring_next(); wv = wview(slot, 8, 512)
                    kb.dma("pool", wv[:, :, 0:256], I["w_up"][l][:, c * 128:c * 128 + 256].rearrange("(kc p) n -> p kc n", p=128), (), (sk,))
                    kb.dma("pool", wv[:, :, 256:512], I["w_up"][l][:, DFF + c * 128:DFF + c * 128 + 256].rearrange("(kc p) n -> p kc n", p=128), (), (sk,))
                hp, hk = kb.ps("tr")
                tiles = {}; tts = {}
                for tb2 in range(2):
                    tb = sb * 2 + tb2; t0 = tb * 512
                    ri = (c % 2) * 2 + tb2
                    for gi in range(2):
                        col0 = gi * 256 + (c % 2) * 128
                        pst, pk = kb.ps("mm")
                        for kc in range(8):
                            kb.mm(pst, wv[:, kc, col0:col0 + 128], hT[:, kc, t0:t0 + 512], kc == 0, kc == 7,
                                  (sk, HK(kc, tb)), (pk,))
                        hcol = None
                        if sb == 0 and tb2 == 1: hcol = 1024
                        if hcol is not None:
                            for kc in range(8):
                                kb.mm(hp[:, gi:gi + 1], wv[:, kc, col0:col0 + 128], hT[:, kc, hcol:hcol + 1], kc == 0, kc == 7,
                                      (sk, HK(kc, hcol // 512)), (hk,))
                        tiles[(tb2, gi)] = (pst, pk)
                        tts[(tb2, gi)] = ((ta_r if gi == 0 else tg_r)[:, ri, :], ("tconv", gi, ri))
                    ccs = [gi * NFC + c for gi in range(2)]
                    for gi in range(2):
                        (pst, pk), (tt_, tk) = tiles[(tb2, gi)], tts[(tb2, gi)]
                        kb.act(tt_, pst, AF.Identity, (pk, "vecs"), (tk,), scale=cw(1, ccs[gi]), bias=cw(3, ccs[gi]))
                        if tb2 == 0:
                            kb.cp("act", es[:, (c % 2) * 2 + gi, 0:1], pst[:, 511:512], (pk,), (("es", (c % 2) * 2 + gi),))
                        if sb == 0 and tb2 == 1:
                            kb.cp("act", eh[:, ccs[gi]:ccs[gi] + 1], pst[:, 511:512], (pk,), (("eh", ccs[gi]),))
                    for gi in range(2):
                        (pst, pk), (tt_, tk) = tiles[(tb2, gi)], tts[(tb2, gi)]
                        kb.stt("dve", tt_[:, 1:512], pst[:, 0:511], cw(0, ccs[gi]), tt_[:, 1:512], ALU.mult, ALU.add, (pk, tk, "vecs"), (tk,))
                    for gi in range(2):
                        (pst, pk), (tt_, tk) = tiles[(tb2, gi)], tts[(tb2, gi)]
                        kb.stt("dve", tt_[:, 0:511], pst[:, 1:512], cw(2, ccs[gi]), tt_[:, 0:511], ALU.mult, ALU.add, (pk, tk, "vecs"), (tk,))
                    for gi in range(2):
                        (pst, pk), (tt_, tk) = tiles[(tb2, gi)], tts[(tb2, gi)]
                        kb.stt("dve", tt_[:, 256:257], pst[:, 255:256], cx(2, ccs[gi]), tt_[:, 256:257], ALU.mult, ALU.add, (pk, tk), (tk,))
                    for gi in range(2):
                        (pst, pk), (tt_, tk) = tiles[(tb2, gi)], tts[(tb2, gi)]
                        kb.stt("dve", tt_[:, 255:256], pst[:, 256:257], cx(3, ccs[gi]), tt_[:, 255:256], ALU.mult, ALU.add, (pk, tk), (tk,))
                    for gi in range(2):
                        (pst, pk), (tt_, tk) = tiles[(tb2, gi)], tts[(tb2, gi)]
                        if tb2 == 0 and sb == 1:
                            kb.act(tt_[:, 0:1], eh[:, ccs[gi]:ccs[gi] + 1], AF.Identity, (("eh", ccs[gi]), tk), (tk,), scale=cx(0, ccs[gi]), bias=tt_[:, 0:1])
                        if tb2 == 1:
                            ek = ("es", (c % 2) * 2 + gi)
                            kb.act(tt_[:, 0:1], es[:, (c % 2) * 2 + gi, 0:1], AF.Identity, (ek, tk), (tk,), scale=cx(0, ccs[gi]), bias=tt_[:, 0:1])
                        if tb2 == 1 and sb == 0:
                            kb.act(tt_[:, 511:512], hp[:, gi:gi + 1], AF.Identity, (hk, tk), (tk,), scale=cx(1, ccs[gi]), bias=tt_[:, 511:512])
                for gi in range(2):
                    (p1, k1), (t0_, tk0) = tiles[(1, gi)], tts[(0, gi)]
                    kb.act(t0_[:, 511:512], p1[:, 0:1], AF.Identity, (k1, tk0), (tk0,), scale=cx(1, gi * NFC + c), bias=t0_[:, 511:512])
                for tb2 in range(2):
                    (ta, tak), (tg, tgk) = tts[(tb2, 0)], tts[(tb2, 1)]
                    kb.act(tg, tg, AF.Silu, (tgk,), (tgk,))
                    kb.tt("dve", actb[:, c, tb2 * 512:(tb2 + 1) * 512], ta, tg, ALU.mult, (tak, tgk), (("actb", c, tb2),))
                if agen is not None and c % 2 == 1:
                    next(agen, None)
            for dp in range(4):
                slot, sk = ring_next(); wd = wview(slot, NFC, 256)
                kb.wload(wd, I["w_down"][l][:, dp * 256:(dp + 1) * 256], sk)
                for d2 in range(2):
                    dch = dp * 2 + d2
                    for tb2 in range(2):
                        tb = sb * 2 + tb2
                        pst, pk = kb.ps("acc")
                        for fc in range(NFC):
                            kb.mm(pst, wd[:, fc, d2 * 128:(d2 + 1) * 128], actb[:, fc, tb2 * 512:(tb2 + 1) * 512], fc == 0, fc == NFC - 1,
                                  (sk, ("actb", fc, tb2)), (pk,))
                        xs = xres[:, dch, tb * 512:(tb + 1) * 512]
                        kb.stt("dve", xs, pst, kb.prm[:, 5, dch:dch + 1], xs, ALU.mult, ALU.add, (pk, XK(dch, tb)), (XK(dch, tb),))
                if agen is not None:
                    next(agen, None)
        if agen is not None:
            for _ in agen:
                pass
        pg.barrier()
        kb.rings = dict(kb.RINGS_DEFAULT)

    def final_out():
        kb.top = MARK
        sq = kb.bf([4, 512]); rs = kb.f32([2, 512]); tmp = kb.f32([4, 512]); yst = kb.f32([4, 512])
        cnt = [0]
        def to_out(c, tb, tm, tk, bias):
            i = cnt[0] % 4; cnt[0] += 1
            kb.cp("act", yst[:, i, :], tm, (tk,), (("yst", i),))
            kb.dma("sp", O["yT"][c * 128:(c + 1) * 128, tb * 512:(tb + 1) * 512], yst[:, i, :], (("yst", i),), ())
        o, _ = VOFF["nfin"]
        norm_mod(lambda c: vecs[:, o + c:o + c + 1], None, sq, rs, tmp, to_out)

    from_mixers = _mixers(kb)
    adaln_first()
    try:
        for l in range(NLAYERS):
            kb.prm = prm_all[:, l]; kb.convx = convx_all[:, l]
            norm_phase(0, 1)
            if l % 2 == 0 and STAGES["even"]:
                from_mixers["even"](l, l // 2)
            if l % 2 == 1 and STAGES["odd"]:
                from_mixers["odd"](l, l // 2)
            if STAGES["ffn"]:
                norm_phase(3, 4)
                ffn(l)
    except _Stop:
        pg.barrier()
    final_out()


def _mixers(kb):
    I, O, pg = kb.I, kb.O, kb.pg
    xres, hT, vecs, ones_bf, ident_bf, prm = kb.xres, kb.hT, kb.vecs, kb.ones_bf, kb.ident_bf, kb.prm
    XK, HK, vv, MARK = kb.XK, kb.HK, kb.vv, kb.MARK
    wview = kb.wview
    fo = VOFF["flags"][0]

    def bfv(pst):
        return pst[:, 0:256].bitcast(BF16)

    def out_proj(wname, e, extra, src=None):
        src = hT if src is None else src
        kb.ring_setup(2, 8 * 512)
        for piece in range(2):
            slot, sk = kb.ring_next(); wo = wview(slot, 8, 512)
            kb.wload(wo, I[wname][e][:, piece * 512:(piece + 1) * 512], sk)
            for d4 in range(4):
                dch = piece * 4 + d4
                for tb in range(4):
                    sl = slice(tb * 512, (tb + 1) * 512)
                    pst, pk = kb.ps("acc")
                    for kc in range(8):
                        if extra is not None and kc >= 6:
                            rhs, rk = extra[:, kc - 6, sl], ("yp", kc - 6, tb)
                        else:
                            rhs, rk = src[:, kc, sl], HK(kc, tb)
                        kb.mm(pst, wo[:, kc, d4 * 128:(d4 + 1) * 128], rhs, kc == 0, kc == 7, (sk, rk), (pk,))
                    xs = xres[:, dch, sl]
                    kb.stt("dve", xs, pst, kb.prm[:, 2, dch:dch + 1], xs, ALU.mult, ALU.add, (pk, XK(dch, tb)), (XK(dch, tb),))
        pg.barrier()

    def rope_combine(dst, dk, pa, pak, pb, pbk, tab, tabk, t1, t2, np_, pre=1.0):
        kb.stt("dve", t1[0:np_, :], pa[0:np_, :], pre, tab[0:np_, 0, :], ALU.mult, ALU.mult, (pak, tabk), ("rt1",))
        kb.stt("dve", t2[0:np_, :], pb[0:np_, :], pre, tab[0:np_, 1, :], ALU.mult, ALU.mult, (pbk, tabk), ("rt2",))
        kb.tt("dve", dst, t1[0:np_, :], t2[0:np_, :], ALU.add, ("rt1", "rt2"), (dk,))

    def even(l, e):
        kb.top = MARK
        qnT = kb.bf([3, T]); qrT = kb.bf([3, T]); cT = kb.bf([2, NK]); krT = kb.bf([NK]); ypT = kb.bf([2, T])
        MARK_E = kb.top
        wtm = kb.bf([8, 544]); wpl = kb.bf([2, 128]); sk = "wtm"; skp = "wpl"
        xp_tok = kb.bf([16, 384]); pooledT = kb.bf([2, T]); lat_st = kb.f32([2, 288]); ctok = kb.bf([2, 256])
        sqt = kb.f32([2, 256]); ssq = kb.f32([2]); ctxl = kb.bf([4, 288]); Pms = [kb.bf([4, 3, 128]) for _ in range(2)]
        kb.memset("dve", xp_tok, 0.0, [("xp", j) for j in range(16)])
        kb.wload(wtm, I["wie_tm"][e], sk)
        kb.dma("pool", ctxl, I["c_mla"][e].rearrange("(b p) n -> p b n", p=128), (), ("ctxl",))
        for j in range(16):
            i = j % 2; tb = j // 4
            p1, k1 = kb.ps("mm"); p2, k2 = kb.ps("mm")
            for kc in range(8):
                lh = hT[:, kc, j * 128:(j + 1) * 128]
                kb.mm(p1[:, 0:288], lh, wtm[:, kc, 0:288], kc == 0, kc == 7, (sk, HK(kc, tb)), (k1,))
                kb.mm(p2[:, 0:256], lh, wtm[:, kc, 288:544], kc == 0, kc == 7, (sk, HK(kc, tb)), (k2,))
            kb.act(sqt[:, i, :], p1[:, 0:256], AF.Square, (k1,), (("sqt", i),))
            pg.add("dve", lambda en, o=ssq[:, i:i + 1], a=sqt[:, i, :]: en.reduce_sum(out=o, in_=a, axis=mybir.AxisListType.X),
                   (("sqt", i),), (("ssq", i),))
            kb.act(ssq[:, i:i + 1], ssq[:, i:i + 1], AF.Sqrt, (("ssq", i),), (("ssq", i),), scale=1.0 / 256, bias=EPS)
            kb.recip(ssq[:, i:i + 1], ssq[:, i:i + 1], (("ssq", i),), (("ssq", i),))
            go = VOFF["gkv"][0] + e * 256
            kb.stt("dve", lat_st[:, i, 0:256], p1[:, 0:256], ssq[:, i:i + 1], vecs[:, go:go + 256], ALU.mult, ALU.mult,
                   (k1, ("ssq", i), "vecs"), (("lat", i),))
            kb.cp("act", lat_st[:, i, 256:288], p1[:, 256:288], (k1,), (("lat", i),))
            kb.dma("sp", O["lat"][e, j * 128:(j + 1) * 128, :], lat_st[:, i, :], (("lat", i),), ())
            kb.cp("dve", ctok[:, i, :], lat_st[:, i, 0:256], (("lat", i),), (("ctok", i),))
            ptr, tk = kb.ps("tr"); pb = bfv(ptr)
            for cc in range(2):
                kb.tr(pb[:, cc * 128:(cc + 1) * 128], ctok[:, i, cc * 128:(cc + 1) * 128], ident_bf, (("ctok", i), "ident"), (tk,))
            kb.cp("act", cT[:, :, j * 128:(j + 1) * 128], pb[:, 0:256].rearrange("p (a b) -> p a b", a=2), (tk,), (("cT", j),))
            for half in range(2):
                dst = xp_tok[:, j, half * 192:(half + 1) * 192].rearrange("p (a b) -> p a b", a=3)[:, 0:3:2, :]
                src = p2[:, half * 128:(half + 1) * 128].rearrange("p (a b) -> p a b", a=2)
                kb.cp("act", dst, src, (k2,), (("xp", j),))
        ck("e_tm")
        for blk in range(4):
            ptr, tk = kb.ps("tr"); pb = bfv(ptr)
            for cc in range(2):
                kb.tr(pb[:, cc * 128:(cc + 1) * 128], ctxl[:, blk, cc * 128:(cc + 1) * 128], ident_bf, ("ctxl", "ident"), (tk,))
            kb.tr(pb[0:32, 256:384], ctxl[:, blk, 256:288], ident_bf, ("ctxl", "ident"), (tk,))
            kb.cp("act", cT[:, :, T + blk * 128:T + (blk + 1) * 128], pb[:, 0:256].rearrange("p (a b) -> p a b", a=2), (tk,), (("cT", 16 + blk),))
            kb.cp("dve", krT[0:32, T + blk * 128:T + (blk + 1) * 128], pb[0:32, 256:384], (tk,), (("krT", 4),))
        ck("e_ctx")
        kb.dma("pool", wpl, I["wpool_bd"][e].rearrange("a k m -> k a m"), (), (skp,))
        for j in range(16):
            pm = Pms[j % 2]; pmk = ("Pm", j % 2)
            kb.dma("pool", pm, I["Pm"][j], (), (pmk,))
            for pr in range(2):
                pst, pk = kb.ps("mm")
                todo = [(gg, sbi) for gg in range(2) for sbi in range(3) if 0 <= j - 1 + sbi <= 15]
                for n_, (gg, sbi) in enumerate(todo):
                    sbk = j - 1 + sbi; c0 = pr * 192 + gg * 64
                    kb.mm(pst[:, 0:128], xp_tok[:, sbk, c0:c0 + 128], pm[:, pr * 2 + gg, sbi, :], n_ == 0, n_ == len(todo) - 1,
                          (("xp", sbk), pmk), (pk,))
                kb.cp("act", pooledT[:, pr, j * 128:(j + 1) * 128], pst[:, 0:128], (pk,), (("pooled", pr, j // 4),))
        pso = VOFF["pscale"][0] + e * 2
        for pr in range(2):
            for tb in range(4):
                sl = slice(tb * 512, (tb + 1) * 512)
                pst, pk = kb.ps("mm")
                kb.mm(pst, wpl[:, pr, :], pooledT[:, pr, sl], True, True, (skp, ("pooled", pr, tb)), (pk,))
                kb.ts("dve", ypT[:, pr, sl], pst, vecs[:, pso + pr:pso + pr + 1], None, ALU.mult, None, (pk, "vecs"), (("yp", pr, tb),))
        ck("e_pool")
        pg.barrier()
        kb.top = MARK_E
        kb.ring_setup(2, 8 * 544)
        qa_f = kb.f32([3, 512]); sq = kb.bf([2, 512]); rs = kb.f32([512]); t1 = kb.f32([512]); t2 = kb.f32([512])
        tabE = [kb.f32([2, 512]) for _ in range(2)]
        slot, sk1 = kb.ring_next(); wkr = wview(slot, 8, 448)
        kb.dma("pool", wkr[:, :, 0:32], I["wie_kr"][e].rearrange("(kc p) n -> p kc n", p=128), (), (sk1,))
        kb.dma("pool", wkr[:, :, 32:64], I["wie_krs"][e].rearrange("(kc p) n -> p kc n", p=128), (), (sk1,))
        kb.dma("pool", wkr[:, :, 64:448], I["wie_qa"][e].rearrange("(kc p) n -> p kc n", p=128), (), (sk1,))
        slot, sk2 = kb.ring_next(); wqr = wview(slot, 3, 768)
        kb.dma("pool", wqr[:, :, 0:384], I["wuq_r"][e].rearrange("(kc p) n -> p kc n", p=128), (), (sk2,))
        kb.dma("pool", wqr[:, :, 384:768], I["wuq_rs"][e].rearrange("(kc p) n -> p kc n", p=128), (), (sk2,))
        gq0 = VOFF["gq"][0] + e * 3
        for tb in range(4):
            sl = slice(tb * 512, (tb + 1) * 512)
            tab = tabE[tb % 2]; tabk = ("tabE", tb % 2)
            kb.dma("sp", tab, I["ropeE"][:, :, sl].rearrange("a p t -> p a t"), (), (tabk,))
            pa, pak = kb.ps("mm"); pb_, pbk = kb.ps("mm")
            for kc in range(8):
                kb.mm(pa[0:32, :], wkr[:, kc, 0:32], hT[:, kc, sl], kc == 0, kc == 7, (sk1, HK(kc, tb)), (pak,))
            for kc in range(8):
                kb.mm(pb_[0:32, :], wkr[:, kc, 32:64], hT[:, kc, sl], kc == 0, kc == 7, (sk1, HK(kc, tb)), (pbk,))
            rope_combine(krT[0:32, sl], ("krT", tb), pa, pak, pb_, pbk, tab, tabk, t1, t2, 32)
            pn, pnk = kb.ps("acc")
            for m in range(3):
                pq, pqk = kb.ps("mm")
                for kc in range(8):
                    kb.mm(pq, wkr[:, kc, 64 + m * 128:64 + (m + 1) * 128], hT[:, kc, sl], kc == 0, kc == 7, (sk1, HK(kc, tb)), (pqk,))
                kb.cp("act", qa_f[:, m, :], pq, (pqk,), (("qa_f", m),))
                kb.act(sq[:, m % 2, :], qa_f[:, m, :], AF.Square, (("qa_f", m),), (("sq", m % 2),))
                kb.mm(pn, ones_bf, sq[:, m % 2, :], m == 0, m == 2, (("sq", m % 2), "ones"), (pnk,))
            kb.act(rs, pn, AF.Sqrt, (pnk,), ("rs",), scale=1.0 / 384, bias=EPS)
            kb.recip(rs, rs, ("rs",), ("rs",))
            for m in range(3):
                kb.stt("dve", qnT[:, m, sl], qa_f[:, m, :], vecs[:, gq0 + m:gq0 + m + 1], rs, ALU.mult, ALU.mult,
                       (("qa_f", m), "rs", "vecs"), (("qnT", m, tb),))
            for m in range(3):
                pa, pak = kb.ps("mm"); pb_, pbk = kb.ps("mm")
                for kc in range(3):
                    kb.mm(pa, wqr[:, kc, m * 128:(m + 1) * 128], qnT[:, kc, sl], kc == 0, kc == 2, (sk2, ("qnT", kc, tb)), (pak,))
                for kc in range(3):
                    kb.mm(pb_, wqr[:, kc, 384 + m * 128:384 + (m + 1) * 128], qnT[:, kc, sl], kc == 0, kc == 2, (sk2, ("qnT", kc, tb)), (pbk,))
                rope_combine(qrT[:, m, sl], ("qrT", m, tb), pa, pak, pb_, pbk, tab, tabk, t1, t2, 128)
        ck("e_ea2")
        pg.barrier()
        kb.top = MARK_E
        kb.rings = {"mm": (0, 4), "acc": (4, 4), "tr": (7, 1)}
        wqn = kb.bf([3, 768]); wk = kb.bf([2, 768]); wvv = kb.bf([2, 768])
        KT = kb.bf([2, NK]); QT = kb.bf([2, T]); Vh = kb.bf([2, 20, 128]); PT = kb.bf([6, 512])
        rDs = kb.f32([2, 512]); rDt = kb.f32([2, 512]); sel = kb.f32([2, 128]); fin_state = []
        kb.memset("dve", rDt, 0.0, ("rDt",)); kb.memset("dve", sel, 0.0, ("sel",))
        kb.memset("dve", sel[64:65, 0, 0:64], 1.0, ("sel",))
        kb.memset("dve", sel[32:33, 1, 64:128], 1.0, ("sel",))
        kb.wload(wqn, I["wuq_n"][e], "wqn"); kb.wload(wk, I["wukv_k"][e], "wk"); kb.wload(wvv, I["wukv_v"][e], "wvv")
        for b in range(2):
            kb.dma("pool", KT[96:105, b, :], I["kmaskE"][:, :], (), (("KTm", b),))
            kb.dma("pool", QT[96:105, b, :], I["qmaskE"][:, :], (), (("QTm", b),))
            kb.dma("sp", KT[64:96, b, :], krT[0:32, :], (), (("KTr", b),))
            kb.memset("dve", Vh[:, b, :, :], 0.0, (("Vh", b),))
            oc = 64 if b == 0 else 32
            kb.memset("dve", Vh[:, b, :, oc:oc + 1], 1.0, (("Vh", b),))

        def proj(h, b):
            kb.dma("sp", QT[64:96, b, :], qrT[(h % 4) * 32:(h % 4) * 32 + 32, h // 4, :], (), (("QTr", b),))
            for tb in range(4):
                sl = slice(tb * 512, (tb + 1) * 512)
                pst, pk = kb.ps("mm")
                for kc in range(3):
                    kb.mm(pst[0:64, :], wqn[:, kc, h * 64:(h + 1) * 64], qnT[:, kc, sl], kc == 0, kc == 2, ("wqn",), (pk,))
                kb.cp("dve", QT[0:64, b, sl], pst[0:64, :], (pk,), (("QTn", b),))
            for k5 in range(5):
                sl = slice(k5 * 512, (k5 + 1) * 512)
                pst, pk = kb.ps("mm")
                for cc in range(2):
                    kb.mm(pst[0:64, :], wk[:, cc, h * 64:(h + 1) * 64], cT[:, cc, sl], cc == 0, cc == 1, ("wk",), (pk,))
                kb.cp("dve", KT[0:64, b, sl], pst[0:64, :], (pk,), (("KTn", b),))
            for g0, nb in ((0, 8), (8, 8), (16, 4)):
                pst, pk = kb.ps("mm")
                for i in range(nb):
                    kblk = g0 + i
                    for cc in range(2):
                        kb.mm(pst[:, i * 64:(i + 1) * 64], cT[:, cc, kblk * 128:(kblk + 1) * 128], wvv[:, cc, h * 64:(h + 1) * 64],
                              cc == 0, cc == 1, ("wvv",), (pk,))
                kb.cp("dve", Vh[:, b, g0:g0 + nb, b * 64:b * 64 + 64], pst[:, 0:nb * 64].rearrange("p (a b) -> p a b", a=nb),
                      (pk,), (("Vh", b),))

        def attn(h, b):
            rd_q = (("QTn", b), ("QTr", b), ("QTm", b)); rd_k = (("KTn", b), ("KTr", b), ("KTm", b))
            hp = slice(b * 64, b * 64 + 64)
            p0 = 64 if b == 0 else 32
            for qc in range(4):
                accO, ok_ = kb.ps("acc")
                pend = []

                def pv(kc, slot):
                    kb.mm(accO, Vh[:, b, kc, :], PT[:, slot, :], kc == 0, kc == 19, (("PT", slot), ("Vh", b)), (ok_,))
                for kc in range(20):
                    slot = (qc * 20 + kc) % 6
                    pst, pk = kb.ps("mm")
                    kb.mm(pst, KT[0:105, b, kc * 128:(kc + 1) * 128], QT[0:105, b, qc * 512:(qc + 1) * 512], True, True, rd_q + rd_k, (pk,))
                    kb.act(PT[:, slot, :], pst, AF.Exp, (pk,), (("PT", slot),), scale=MLA_SCALE)
                    pend.append((kc, slot))
                    if len(pend) > 2:
                        pv(*pend.pop(0))
                    if kc == 4 and fin_state:
                        fin_state.pop(0)()
                while pend:
                    pv(*pend.pop(0))
                ri = (h * 4 + qc) % 2
                rd = rDs[:, ri, :]; rk = ("rD", ri)
                rdt = rDt[:, ri, :]; rtk = ("rDt", ri)
                kb.recip(rdt[p0:p0 + 1, :], accO[p0:p0 + 1, :], (ok_,), (rtk,))

                def fin(accO=accO, ok_=ok_, rd=rd, rk=rk, rdt=rdt, rtk=rtk, qc=qc):
                    bc, bk = kb.ps("acc")
                    kb.mm(bc, sel[:, b, :], rdt, True, True, ("sel", rtk), (bk,))
                    kb.cp("dve", rd[hp, :], bc[hp, :], (bk,), (rk,))
                    kb.tt("dve", hT[hp, h // 2, qc * 512:(qc + 1) * 512], accO[hp, :], rd[hp, :], ALU.mult, (ok_, rk), (HK(h // 2, qc),))
                fin_state.append(fin)

        ck("e_ebsetup")
        proj(0, 0)
        ck("e_proj0")
        for h in range(12):
            if h + 1 < 12: proj(h + 1, (h + 1) % 2)
            attn(h, h % 2)
            ck("e_attn%d" % h)
        while fin_state:
            fin_state.pop(0)()
        pg.barrier()
        kb.rings = dict(kb.RINGS_DEFAULT)
        kb.top = MARK_E
        out_proj("w_out_even", e, ypT)

    def odd(l, e):
        kb.top = MARK
        QM = kb.bf([8, T]); ctxones = kb.bf([128]); zrow = kb.f32([128])
        kb.memset("dve", ctxones, 1.0, ("ctxones",))
        kb.ts("dve", ctxones, ctxones, vecs[:, fo:fo + 1], None, ALU.mult, None, ("ctxones", "vecs"), ("ctxones",))
        kb.memset("dve", zrow, 0.0, ("zrow",))
        MARK_O = kb.top
        KTna = kb.bf([4, NK]); Vna = kb.bf([20, 512])
        MARK_O2 = kb.top
        kb.ring_setup(2, 8 * 512)
        stg = kb.f32([2, 512]); ctxl = kb.bf([4, 512])
        kb.dma("pool", Vna[:, 16:20, :], I["c_na"][e][:, 1, :].rearrange("(b p) n -> p b n", p=128), (), ("Vna_ctx",))
        kb.dma("pool", ctxl, I["c_na"][e][:, 0, :].rearrange("(b p) n -> p b n", p=128), (), ("ctxl",))
        for piece in range(2):
            slot, sk = kb.ring_next(); wv = wview(slot, 8, 512)
            kb.wload(wv, I["wio_nakv"][e][:, piece * 512:(piece + 1) * 512], sk)
            for j in range(16):
                pst, pk = kb.ps("mm")
                for kc in range(8):
                    kb.mm(pst, hT[:, kc, j * 128:(j + 1) * 128], wv[:, kc, :], kc == 0, kc == 7, (sk, HK(kc, j // 4)), (pk,))
                kb.cp("act", stg[:, j % 2, :], pst, (pk,), (("stg", j % 2),))
                kb.dma("sp", O["nakv"][e, j * 128:(j + 1) * 128, piece * 512:(piece + 1) * 512], stg[:, j % 2, :], (("stg", j % 2),), ())
                if piece == 1:
                    kb.cp("dve", Vna[:, j, :], pst, (pk,), (("Vna", j),))
        for blk in range(4):
            ptr, tk = kb.ps("tr"); pb = bfv(ptr)
            for m in range(4):
                kb.tr(pb[:, m * 128:(m + 1) * 128], ctxl[:, blk, m * 128:(m + 1) * 128], ident_bf, ("ctxl", "ident"), (tk,))
            kb.cp("act", KTna[:, :, T + blk * 128:T + (blk + 1) * 128], pb.rearrange("p (a b) -> p a b", a=4), (tk,), (("KTna_ctx", blk),))
        for wname, isq in (("wio_qna", True), ("wio_kna", False)):
            slot, sk = kb.ring_next(); wv = wview(slot, 8, 512)
            kb.wload(wv, I[wname][e], sk)
            for m in range(4):
                for tb in range(4):
                    sl = slice(tb * 512, (tb + 1) * 512)
                    pst, pk = kb.ps("mm")
                    for kc in range(8):
                        kb.mm(pst, wv[:, kc, m * 128:(m + 1) * 128], hT[:, kc, sl], kc == 0, kc == 7, (sk, HK(kc, tb)), (pk,))
                    if isq:
                        kb.act(QM[:, m, sl], pst, AF.Copy, (pk,), [("QM", m, tb * 4 + i) for i in range(4)], scale=0.125)
                    else:
                        kb.cp("dve", KTna[:, m, sl], pst, (pk,), (("KTna", m, tb),))
        pg.barrier()
        kb.top = MARK_O2
        kb.rings = {"mm": (0, 6), "acc": (6, 2), "tr": (7, 1)}
        PT = kb.bf([6, 512]); nab = [kb.bf([2, 5, 128]) for _ in range(2)]; rDs = kb.f32([2, 128])
        units = [(j, i, hh) for j in range(4) for i in range(16) for hh in range(2)]
        st = {}

        def na_S(k):
            j, i, hh = units[k]
            if hh == 0:
                nb_ = nab[(k // 2) % 2]; nk = ("nab", (k // 2) % 2)
                kb.dma("pool", nb_, I["nabias"][e, VAR_OF[i]][:, 2 * j:2 * j + 2, :, :], (), (nk,))
                start = min(max(i - 2, 0), 11)
                tiles = [(start + c, c) for c in range(5)] + [(16 + c, None) for c in range(4)]
                accb, ak = kb.ps("acc")
                st[(j, i)] = (nb_, nk, tiles, accb, ak)
            nb_, nk, tiles, accb, ak = st[(j, i)]
            hp = slice(hh * 64, hh * 64 + 64)
            banks = []
            for t, (kblk, c) in enumerate(tiles):
                if t % 4 == 0:
                    banks.append(kb.ps("mm"))
                pst, pk = banks[-1]; col = slice((t % 4) * 128, (t % 4) * 128 + 128)
                kb.mm(pst[:, col], KTna[hp, j, kblk * 128:(kblk + 1) * 128], QM[hp, j, i * 128:(i + 1) * 128], True, True,
                      (("QM", j, i),), (pk,))
            kb.tt("dve", banks[0][0], banks[0][0], nb_[:, hh, 0:4, :].rearrange("p a b -> p (a b)"), ALU.add, (banks[0][1], nk), (banks[0][1],))
            kb.tt("dve", banks[1][0][:, 0:128], banks[1][0][:, 0:128], nb_[:, hh, 4, :], ALU.add, (banks[1][1], nk), (banks[1][1],))
            slots = []
            for bi, (pst, pk) in enumerate(banks):
                ncol = min(4, 9 - bi * 4) * 128
                sl_ = (k * 3 + bi) % 6
                kb.act(PT[:, sl_, 0:ncol], pst[:, 0:ncol], AF.Exp, (pk,), (("PT", sl_),))
                slots.append(sl_)
            st[(j, i, hh)] = slots

        def na_OD(k):
            j, i, hh = units[k]
            nb_, nk, tiles, accb, ak = st[(j, i)]
            slots = st[(j, i, hh)]
            Oh = accb[:, hh * 128:(hh + 1) * 128]; Dh = accb[:, 256 + hh * 128:256 + (hh + 1) * 128]
            for t, (kblk, c) in enumerate(tiles):
                sl_ = slots[t // 4]
                kb.mm(Oh, Vna[:, kblk, j * 128:(j + 1) * 128], PT[:, sl_, (t % 4) * 128:(t % 4) * 128 + 128], t == 0, t == 8,
                      (("PT", sl_),), (ak,))
            for t, (kblk, c) in enumerate(tiles):
                sl_ = slots[t // 4]
                kb.mm(Dh, ones_bf if c is not None else ctxones, PT[:, sl_, (t % 4) * 128:(t % 4) * 128 + 128], t == 0, t == 8,
                      (("PT", sl_), "ones", "ctxones"), (ak,))
            if hh == 1:
                for h2 in range(2):
                    hp = slice(h2 * 64, h2 * 64 + 64)
                    rd = rDs[:, (k + h2) % 2, :]; rk = ("rD", (k + h2) % 2)
                    kb.recip(rd[hp, :], accb[hp, 256 + h2 * 128:256 + (h2 + 1) * 128], (ak,), (rk,))
                    kb.tt("dve", QM[hp, j, i * 128:(i + 1) * 128], accb[hp, h2 * 128:(h2 + 1) * 128], rd[hp, :], ALU.mult, (ak, rk), (("QM", j, i),))

        na_S(0)
        for k in range(len(units)):
            if k + 1 < len(units): na_S(k + 1)
            na_OD(k)
        pg.barrier()
        kb.rings = dict(kb.RINGS_DEFAULT)
        kb.top = MARK_O
        KTsw = kb.bf([NK]); Vsw = kb.bf([20, 128])
        MARK_S = kb.top
        kb.ring_setup(3, 8 * 512)
        stg = kb.f32([2, 256]); ctxk = kb.bf([4, 128]); t1 = kb.f32([512]); t2 = kb.f32([512])
        tabO = [kb.f32([2, 512]) for _ in range(2)]
        kb.dma("pool", Vsw[:, 16:20, :], I["c_sw"][e][:, 1, :].rearrange("(b p) n -> p b n", p=128), (), ("Vsw_ctx",))
        kb.dma("pool", ctxk, I["c_sw"][e][:, 0, :].rearrange("(b p) n -> p b n", p=128), (), ("ctxk",))
        slot, sk = kb.ring_next(); wv = wview(slot, 8, 512)
        kb.wload(wv[:, :, 0:256], I["wio_swkv"][e], sk)
        kb.dma("pool", wv[:, :, 256:384], I["wio_ksw"][e].rearrange("(kc p) n -> p kc n", p=128), (), (sk,))
        kb.dma("pool", wv[:, :, 384:512], I["wio_ksws"][e].rearrange("(kc p) n -> p kc n", p=128), (), (sk,))
        for j in range(16):
            pst, pk = kb.ps("mm")
            for kc in range(8):
                kb.mm(pst[:, 0:256], hT[:, kc, j * 128:(j + 1) * 128], wv[:, kc, 0:256], kc == 0, kc == 7, (sk, HK(kc, j // 4)), (pk,))
            kb.cp("act", stg[:, j % 2, :], pst[:, 0:256], (pk,), (("stg", j % 2),))
            kb.dma("sp", O["swkv"][e, j * 128:(j + 1) * 128, :], stg[:, j % 2, :], (("stg", j % 2),), ())
            kb.cp("dve", Vsw[:, j, :], pst[:, 128:256], (pk,), (("Vsw", j),))
        ptr, tk = kb.ps("tr"); pb = bfv(ptr)
        for blk in range(4):
            kb.tr(pb[:, blk * 128:(blk + 1) * 128], ctxk[:, blk, :], ident_bf, ("ctxk", "ident"), (tk,))
        kb.cp("act", KTsw[:, T:NK], pb, (tk,), ("KTsw_ctx",))
        slot, skq = kb.ring_next(); wq = wview(slot, 8, 512)
        kb.wload(wq, I["wio_qsw"][e], skq)
        slot, skqs = kb.ring_next(); wqs = wview(slot, 8, 512)
        kb.wload(wqs, I["wio_qsws"][e], skqs)
        for tb in range(4):
            sl = slice(tb * 512, (tb + 1) * 512)
            tab = tabO[tb % 2]; tabk = ("tabO", tb % 2)
            kb.dma("sp", tab, I["ropeO"][:, :, sl].rearrange("a p t -> p a t"), (), (tabk,))
            pa, pak = kb.ps("mm"); pb_, pbk = kb.ps("mm")
            for kc in range(8):
                kb.mm(pa, wv[:, kc, 256:384], hT[:, kc, sl], kc == 0, kc == 7, (sk, HK(kc, tb)), (pak,))
            for kc in range(8):
                kb.mm(pb_, wv[:, kc, 384:512], hT[:, kc, sl], kc == 0, kc == 7, (sk, HK(kc, tb)), (pbk,))
            rope_combine(KTsw[:, sl], ("KTsw", tb), pa, pak, pb_, pbk, tab, tabk, t1, t2, 128)
            for m in range(4):
                pa, pak = kb.ps("mm"); pb_, pbk = kb.ps("mm")
                for kc in range(8):
                    kb.mm(pa, wq[:, kc, m * 128:(m + 1) * 128], hT[:, kc, sl], kc == 0, kc == 7, (skq, HK(kc, tb)), (pak,))
                for kc in range(8):
                    kb.mm(pb_, wqs[:, kc, m * 128:(m + 1) * 128], hT[:, kc, sl], kc == 0, kc == 7, (skqs, HK(kc, tb)), (pbk,))
                rope_combine(QM[:, 4 + m, sl], ("QMs", m, tb), pa, pak, pb_, pbk, tab, tabk, t1, t2, 128, pre=0.125)
        pg.barrier()
        kb.top = MARK_S
        kb.rings = {"mm": (0, 4), "acc": (4, 4), "tr": (7, 1)}
        PT = kb.bf([14, 512]); swb = kb.bf([6, 3, 128]); esink = kb.bf([2, 512]); rDs = kb.f32([2, 512])
        kb.dma("pool", swb, I["swbias"].rearrange("v k c q -> k v c q"), (), ("swb",))
        so = VOFF["sink"][0] + e * 8
        for g in range(2):
            for hd in range(4):
                kb.act(esink[0:1, g, hd * 128:(hd + 1) * 128], zrow[0:1, 0:128], AF.Exp, ("zrow", "vecs"), ("esink",),
                       bias=vecs[0:1, so + 4 * g + hd:so + 4 * g + hd + 1])
        sunits = [(i, g) for i in range(16) for g in range(2)]
        sst = {}

        def sw_S(k):
            i, g = sunits[k]
            hp = slice(g * 64, g * 64 + 64)
            chunks = [(min(max(i - 1 + c, 0), 15), c) for c in range(3)] + [(16 + c, None) for c in range(4)]
            slots = []
            for t, (kblk, c) in enumerate(chunks):
                pst, pk = kb.ps("mm")
                kb.mm(pst, KTsw[hp, kblk * 128:(kblk + 1) * 128], QM[hp, 4:8, i * 128:(i + 1) * 128], True, True, (), (pk,))
                if c is not None:
                    p3 = pst.rearrange("p (a b) -> p a b", a=4)
                    kb.tt("dve", p3, p3, swb[:, VAR_OF[i], c, :].unsqueeze(1).to_broadcast([128, 4, 128]), ALU.add, (pk, "swb"), (pk,))
                sl_ = (k * 7 + t) % 14
                kb.act(PT[:, sl_, :], pst, AF.Exp, (pk,), (("PT", sl_),))
                slots.append(sl_)
            sst[k] = (chunks, slots)

        def sw_OD(k):
            i, g = sunits[k]
            hp = slice(g * 64, g * 64 + 64)
            chunks, slots = sst[k]
            accO, ok_ = kb.ps("acc"); accD, dk_ = kb.ps("acc")
            for t, (kblk, c) in enumerate(chunks):
                kb.mm(accO, Vsw[:, kblk, :], PT[:, slots[t], :], t == 0, t == 6, (("PT", slots[t]),), (ok_,))
            for t, (kblk, c) in enumerate(chunks):
                kb.mm(accD, ones_bf if c is not None else ctxones, PT[:, slots[t], :], t == 0, False, (("PT", slots[t]), "ones", "ctxones"), (dk_,))
            kb.mm(accD, ones_bf[0:1, :], esink[0:1, g, :], False, True, ("esink", "ones"), (dk_,))
            rd = rDs[:, k % 2, :]; rk = ("rD", k % 2)
            kb.recip(rd[hp, :], accD[hp, :], (dk_,), (rk,))
            kb.tt("dve", QM[hp, 4:8, i * 128:(i + 1) * 128], accO[hp, :].rearrange("p (a b) -> p a b", a=4),
                  rd[hp, :].rearrange("p (a b) -> p a b", a=4), ALU.mult, (ok_, rk), (("QMo", i, g),))

        sw_S(0)
        for k in range(len(sunits)):
            if k + 1 < len(sunits): sw_S(k + 1)
            sw_OD(k)
        pg.barrier()
        kb.rings = dict(kb.RINGS_DEFAULT)
        kb.top = MARK_O
        out_proj("w_out_odd", e, None, src=QM)

    return {"even": even, "odd": odd}


_CACHE = {}


def kernel(**inputs):
    maps = _host_inputs(inputs)
    for core, m in enumerate(maps):
        sample = core >= 4
        fl = m.pop("flags")
        m["vecs"] = _pack_vecs(inputs, m.pop("cvec"), fl)
        for k in ("b_mod", "norm_mix", "norm_ffn", "norm_final", "mla_q_norm", "mla_kv_norm", "pool_scale",
                  "swa_sink", "conv_w", "conv_b"):
            m.pop(k, None)
    if "nc" not in _CACHE:
        _CACHE["nc"] = build_program()[0]
    nc = _CACHE["nc"]
    res = run_bass_kernel_spmd(nc, maps, core_ids=list(range(8)))
    R = res.results
    y_prompt = np.concatenate([np.asarray(R[c]["yT"]).T.reshape(8, 256, D) for c in range(4)], 0)
    y_sample = np.stack([np.asarray(R[4 + b]["yT"]).T for b in range(4)], 0)
    lat = np.concatenate([np.asarray(R[c]["lat"]).reshape(2, 8, 256, 288).transpose(1, 0, 2, 3) for c in range(4)], 0)
    na = np.concatenate([np.asarray(R[c]["nakv"]).reshape(2, 8, 256, 2, 8, 64).transpose(1, 0, 2, 3, 4, 5) for c in range(4)], 0)
    sw = np.concatenate([np.asarray(R[c]["swkv"]).reshape(2, 8, 256, 2, 2, 64).transpose(1, 0, 2, 3, 4, 5) for c in range(4)], 0)
    f = lambda a: np.ascontiguousarray(a, dtype=np.float32)
    return (f(y_prompt), f(y_sample), f(lat), f(na), f(sw))
```
